# Optimizing a Trainium2 kernel written in Bass

```python
import jax, jax.numpy as jnp
from jax import lax
import numpy as np

D_MODEL = 1024
BATCH = 2
SEQ = 8192
DEPTH = 2

N_A_LAYERS = DEPTH // 2
N_B_LAYERS = DEPTH - N_A_LAYERS

SSM_EXPAND = 2
D_INNER = SSM_EXPAND * D_MODEL
SSM_HEAD_DIM = 64
SSM_HEADS = D_INNER // SSM_HEAD_DIM
SSM_GROUPS = 4
SSM_STATE = 128
CONV_WIDTH = 4
CHUNK = 256
CONV_DIM = D_INNER + 2 * SSM_GROUPS * SSM_STATE
SSM_IN_DIM = D_INNER + CONV_DIM + SSM_HEADS

MLA_HEADS = 16
QK_NOPE = 64
QK_ROPE = 32
V_HEAD = 64
Q_LORA = 384
KV_LORA = 256
ROPE_BASE = 10000.0
Q_BLOCK = 128
MLA_IN_DIM = Q_LORA + MLA_HEADS * V_HEAD

EPS = 1e-6

kernel_name = "yoco_mamba2_mla_hybrid"


def rms_norm(x, g):
    xf = x.astype(jnp.float32)
    y = xf * lax.rsqrt(jnp.mean(xf * xf, axis=-1, keepdims=True) + EPS)
    return (y * g.astype(jnp.float32)).astype(x.dtype)


def rope_tables(positions):
    inv = ROPE_BASE ** (-jnp.arange(0, QK_ROPE, 2, dtype=jnp.float32) / QK_ROPE)
    ang = positions.astype(jnp.float32)[..., None] * inv
    return jnp.cos(ang), jnp.sin(ang)


def apply_rope(x, cos, sin):
    x1, x2 = jnp.split(x.astype(jnp.float32), 2, axis=-1)
    out = jnp.concatenate([x1 * cos - x2 * sin, x1 * sin + x2 * cos], axis=-1)
    return out.astype(x.dtype)


def causal_depthwise_conv(u, w, b):
    K = w.shape[0]
    S = u.shape[1]
    up = jnp.pad(u, ((0, 0), (K - 1, 0), (0, 0)))
    out = b
    for k in range(K):
        out = out + up[:, k:k + S, :] * w[k]
    return out


def ssd_chunked_scan(xh, dt, A, Bm, Cm):
    Bsz, S, H, P = xh.shape
    G, N = Bm.shape[2], Bm.shape[3]
    Hg = H // G
    nc = -(-S // CHUNK)
    pad = nc * CHUNK - S

    def chunkify(t):
        t = jnp.pad(t, [(0, 0), (0, pad)] + [(0, 0)] * (t.ndim - 2))
        t = t.reshape((Bsz, nc, CHUNK) + t.shape[2:])
        return jnp.moveaxis(t, 1, 0)

    xc = chunkify(xh.reshape(Bsz, S, G, Hg, P))
    dtc = chunkify(dt.reshape(Bsz, S, G, Hg))
    Bc = chunkify(Bm)
    Cc = chunkify(Cm)
    A_g = A.reshape(G, Hg)
    causal = jnp.tril(jnp.ones((CHUNK, CHUNK), dtype=bool))[None, :, :, None, None]

    def step(state, inp):
        x_c, dt_c, B_c, C_c = inp
        cum = jnp.cumsum(dt_c * A_g, axis=1)
        seg = cum[:, :, None] - cum[:, None, :]
        L = jnp.exp(jnp.where(causal, seg, -jnp.inf))
        CB = jnp.einsum('btgn,bsgn->btsg', C_c, B_c)
        y_intra = jnp.einsum('btsg,btsgh,bsgh,bsghp->btghp', CB, L, dt_c, x_c)
        y_inter = jnp.einsum('btgn,bghpn,btgh->btghp', C_c, state, jnp.exp(cum))
        w_end = jnp.exp(cum[:, -1:] - cum) * dt_c
        new_state = state * jnp.exp(cum[:, -1])[..., None, None] + \
            jnp.einsum('bsgn,bsgh,bsghp->bghpn', B_c, w_end, x_c)
        return new_state, y_intra + y_inter

    state0 = jnp.zeros((Bsz, G, Hg, P, N), jnp.float32)
    _, yc = lax.scan(step, state0, (xc, dtc, Bc, Cc))
    return jnp.moveaxis(yc, 0, 1).reshape(Bsz, nc * CHUNK, H, P)[:, :S]


def mamba2_mixer(h, w_in, conv_w, conv_b, dt_bias, A_log, D_skip, g_out, w_out):
    Bsz, S, _ = h.shape
    proj = h @ w_in
    z, xBC, dt_raw = jnp.split(proj, [D_INNER, D_INNER + CONV_DIM], axis=-1)
    xBC = jax.nn.silu(causal_depthwise_conv(xBC, conv_w, conv_b))
    xs, Bm, Cm = jnp.split(xBC, [D_INNER, D_INNER + SSM_GROUPS * SSM_STATE], axis=-1)
    dt = jax.nn.softplus((dt_raw + dt_bias).astype(jnp.float32))
    A = -jnp.exp(A_log.astype(jnp.float32))
    xh = xs.reshape(Bsz, S, SSM_HEADS, SSM_HEAD_DIM).astype(jnp.float32)
    y = ssd_chunked_scan(
        xh, dt, A,
        Bm.reshape(Bsz, S, SSM_GROUPS, SSM_STATE).astype(jnp.float32),
        Cm.reshape(Bsz, S, SSM_GROUPS, SSM_STATE).astype(jnp.float32))
    y = y + D_skip.astype(jnp.float32)[:, None] * xh
    y = y.reshape(Bsz, S, D_INNER) * jax.nn.silu(z.astype(jnp.float32))
    y = y.reshape(Bsz, S, SSM_GROUPS, D_INNER // SSM_GROUPS)
    y = y * lax.rsqrt(jnp.mean(y * y, axis=-1, keepdims=True) + EPS)
    y = y.reshape(Bsz, S, D_INNER) * g_out.astype(jnp.float32)
    return y.astype(h.dtype) @ w_out


def mla_shared_kv(h, g_kv_in, w_dkv, g_ckv, w_ukv, cos, sin):
    Bsz, S, _ = h.shape
    hn = rms_norm(h, g_kv_in)
    ckv, k_rope = jnp.split(hn @ w_dkv, [KV_LORA], axis=-1)
    ckv = rms_norm(ckv, g_ckv)
    kv = (ckv @ w_ukv).reshape(Bsz, S, MLA_HEADS, QK_NOPE + V_HEAD)
    k_nope, v = jnp.split(kv, [QK_NOPE], axis=-1)
    k_rope = apply_rope(k_rope, cos, sin)
    return k_nope, k_rope, v


def causal_block_attention(q_nope, q_rope, k_nope, k_rope, v):
    S = q_nope.shape[1]
    scale = (QK_NOPE + QK_ROPE) ** -0.5
    local = jnp.arange(Q_BLOCK)
    outs = []
    for i in range(S // Q_BLOCK):
        q0 = i * Q_BLOCK
        kv_len = q0 + Q_BLOCK
        s = jnp.einsum('bqhd,bkhd->bhqk', q_nope[:, q0:kv_len], k_nope[:, :kv_len]) + \
            jnp.einsum('bqhr,bkr->bhqk', q_rope[:, q0:kv_len], k_rope[:, :kv_len])
        s = s.astype(jnp.float32) * scale
        mask = (q0 + local)[:, None] >= jnp.arange(kv_len)[None, :]
        p = jax.nn.softmax(jnp.where(mask, s, -jnp.inf), axis=-1).astype(v.dtype)
        outs.append(jnp.einsum('bhqk,bkhd->bqhd', p, v[:, :kv_len]))
    return jnp.concatenate(outs, axis=1)


def mla_mixer(h, k_nope, k_rope, v, cos, sin, w_in, g_q, w_uq, w_out):
    Bsz, S, _ = h.shape
    cq, gate = jnp.split(h @ w_in, [Q_LORA], axis=-1)
    q = (rms_norm(cq, g_q) @ w_uq).reshape(Bsz, S, MLA_HEADS, QK_NOPE + QK_ROPE)
    q_nope, q_rope = jnp.split(q, [QK_NOPE], axis=-1)
    q_rope = apply_rope(q_rope, cos[:, :, None, :], sin[:, :, None, :])
    o = causal_block_attention(q_nope, q_rope, k_nope, k_rope, v)
    o = o.reshape(Bsz, S, MLA_HEADS * V_HEAD) * jax.nn.silu(gate)
    return o @ w_out


def setup_inputs(seed: int = 0) -> dict:
    key = jax.random.key(seed)
    ks = jax.random.split(key, 24)
    f32 = jnp.float32

    def nrm(k, shape, fan_in):
        return jax.random.normal(k, shape, f32) * (fan_in ** -0.5)

    def gain(k, shape):
        return 1.0 + 0.01 * jax.random.normal(k, shape, f32)

    x = jax.random.normal(ks[0], (BATCH, SEQ, D_MODEL), f32)
    offset = jax.random.randint(ks[1], (BATCH, 1), 0, 4096, dtype=jnp.int32)
    positions = offset + jnp.arange(SEQ, dtype=jnp.int32)[None, :]

    g_pre = gain(ks[2], (DEPTH, D_MODEL))
    ssm_w_in = nrm(ks[3], (N_A_LAYERS, D_MODEL, SSM_IN_DIM), D_MODEL)
    ssm_conv_w = nrm(ks[4], (N_A_LAYERS, CONV_WIDTH, CONV_DIM), CONV_WIDTH)
    ssm_conv_b = 0.01 * jax.random.normal(ks[5], (N_A_LAYERS, CONV_DIM), f32)
    dt0 = jnp.exp(jax.random.uniform(ks[6], (N_A_LAYERS, SSM_HEADS), f32,
                                     np.log(1e-3), np.log(1e-1)))
    ssm_dt_bias = dt0 + jnp.log(-jnp.expm1(-dt0))
    ssm_A_log = jnp.log(jax.random.uniform(ks[7], (N_A_LAYERS, SSM_HEADS), f32, 1.0, 16.0))
    ssm_D = 1.0 + 0.1 * jax.random.normal(ks[8], (N_A_LAYERS, SSM_HEADS), f32)
    ssm_g_out = gain(ks[9], (N_A_LAYERS, D_INNER))
    ssm_w_out = nrm(ks[10], (N_A_LAYERS, D_INNER, D_MODEL), D_INNER)
    kv_g_in = gain(ks[11], (D_MODEL,))
    kv_w_down = nrm(ks[12], (D_MODEL, KV_LORA + QK_ROPE), D_MODEL)
    kv_g_latent = gain(ks[13], (KV_LORA,))
    kv_w_up = nrm(ks[14], (KV_LORA, MLA_HEADS * (QK_NOPE + V_HEAD)), KV_LORA)
    mla_w_in = nrm(ks[15], (N_B_LAYERS, D_MODEL, MLA_IN_DIM), D_MODEL)
    mla_g_q = gain(ks[16], (N_B_LAYERS, Q_LORA))
    mla_w_uq = nrm(ks[17], (N_B_LAYERS, Q_LORA, MLA_HEADS * (QK_NOPE + QK_ROPE)), Q_LORA)
    mla_w_out = nrm(ks[18], (N_B_LAYERS, MLA_HEADS * V_HEAD, D_MODEL), MLA_HEADS * V_HEAD)
    g_final = gain(ks[19], (D_MODEL,))
    return {
        'x': x, 'positions': positions, 'g_pre': g_pre,
        'ssm_w_in': ssm_w_in, 'ssm_conv_w': ssm_conv_w, 'ssm_conv_b': ssm_conv_b,
        'ssm_dt_bias': ssm_dt_bias, 'ssm_A_log': ssm_A_log, 'ssm_D': ssm_D,
        'ssm_g_out': ssm_g_out, 'ssm_w_out': ssm_w_out,
        'kv_g_in': kv_g_in, 'kv_w_down': kv_w_down, 'kv_g_latent': kv_g_latent, 'kv_w_up': kv_w_up,
        'mla_w_in': mla_w_in, 'mla_g_q': mla_g_q, 'mla_w_uq': mla_w_uq, 'mla_w_out': mla_w_out,
        'g_final': g_final,
    }


def reference(x, positions, g_pre, ssm_w_in, ssm_conv_w, ssm_conv_b, ssm_dt_bias, ssm_A_log,
              ssm_D, ssm_g_out, ssm_w_out, kv_g_in, kv_w_down, kv_g_latent, kv_w_up,
              mla_w_in, mla_g_q, mla_w_uq, mla_w_out, g_final):
    cos, sin = rope_tables(positions)
    h = x
    shared_kv = None
    for l in range(DEPTH):
        hn = rms_norm(h, g_pre[l])
        if l < N_A_LAYERS:
            h = h + mamba2_mixer(hn, ssm_w_in[l], ssm_conv_w[l], ssm_conv_b[l], ssm_dt_bias[l],
                                 ssm_A_log[l], ssm_D[l], ssm_g_out[l], ssm_w_out[l])
        else:
            if shared_kv is None:
                shared_kv = mla_shared_kv(h, kv_g_in, kv_w_down, kv_g_latent, kv_w_up, cos, sin)
            k_nope, k_rope, v = shared_kv
            j = l - N_A_LAYERS
            h = h + mla_mixer(hn, k_nope, k_rope, v, cos, sin,
                              mla_w_in[j], mla_g_q[j], mla_w_uq[j], mla_w_out[j])
    return rms_norm(h, g_final)
```

```python
import contextlib
import math
from concourse.bass_utils import run_bass_kernel_spmd
import numpy as np
import concourse.bass as bass
import concourse.mybir as mybir

F32 = mybir.dt.float32
BF16 = mybir.dt.bfloat16
I32 = mybir.dt.int32
AF = mybir.ActivationFunctionType
ALU = mybir.AluOpType
AX = mybir.AxisListType


class Prog:
    def __init__(self, nc):
        self.nc = nc
        self.ops = []
        self.lastw = {}
        self.readers = {}
        self.dma_sems = {}

    def add(self, eng, fn, r=(), w=(), dma=None, group=False):
        deps = set()
        for k in r:
            if k in self.lastw:
                deps.add(self.lastw[k])
            if k[0] == 'B' and k[1:].isdigit():
                for j in self.readers.get(k, ()):
                    if self.ops[j]['eng'] != eng:
                        deps.add(j)
        for k in w:
            if k in self.lastw:
                deps.add(self.lastw[k])
            deps.update(self.readers.get(k, ()))
        i = len(self.ops)
        self.ops.append(dict(eng=eng, fn=fn, deps=deps, dma=dma, group=group, has_dep=False))
        for k in r:
            self.readers.setdefault(k, []).append(i)
        for k in w:
            self.lastw[k] = i
            self.readers[k] = []
        return i

    def pe(self, fn, r=(), w=()):
        return self.add('pe', fn, r, w)

    def act(self, fn, r=(), w=()):
        return self.add('act', fn, r, w)

    def dve(self, fn, r=(), w=()):
        return self.add('dve', fn, r, w)

    def pool(self, fn, r=(), w=()):
        return self.add('pool', fn, r, w)

    def dma(self, eng, fn, r=(), w=(), sem=None, group=False):
        assert sem is not None
        return self.add(eng, fn, r, w, dma=sem, group=group)

    def wait_all(self, eng, keys):
        return self.add(eng, None, r=keys, w=())

    def emit(self):
        nc = self.nc
        ops = self.ops
        engs = ['sp', 'act', 'dve', 'pool', 'pe']
        for o in ops:
            for d in o['deps']:
                if ops[d]['eng'] == 'pe' and o['eng'] == 'pe' and ops[d]['dma'] is None and o['dma'] is None:
                    continue
                ops[d]['has_dep'] = True
        esem = {e: nc.alloc_semaphore(name=f"s_{e}") for e in engs}
        group_tot = {}
        for o in ops:
            if o['dma'] is not None:
                if o['dma'] not in self.dma_sems:
                    self.dma_sems[o['dma']] = nc.alloc_semaphore(name=f"d_{o['dma']}")
                group_tot[o['dma']] = group_tot.get(o['dma'], 0) + 1
        cnt = {e: 0 for e in engs}
        dcnt = {}
        for o in ops:
            if o['fn'] is None:
                o['tok'] = None
            elif o['dma'] is not None:
                k = o['dma']
                dcnt[k] = dcnt.get(k, 0) + 1
                v = group_tot[k] if o['group'] else dcnt[k]
                o['tok'] = (('d', k), 16 * v)
            elif o['has_dep']:
                cnt[o['eng']] += 1
                o['tok'] = (('e', o['eng']), cnt[o['eng']])
            else:
                o['tok'] = None
        known = {e: {} for e in engs}
        for o in ops:
            e = o['eng']
            kn = known[e]
            waits = []
            for d in sorted(o['deps'], reverse=True):
                od = ops[d]
                if od['tok'] is None:
                    continue
                if od['eng'] == 'pe' and e == 'pe' and od['dma'] is None and o['dma'] is None:
                    continue
                s, v = od['tok']
                if kn.get(s, 0) < v:
                    waits.append((s, v))
                    kn[s] = v
                    for s2, v2 in od['clock'].items():
                        if kn.get(s2, 0) < v2:
                            kn[s2] = v2
            wm = {}
            for s, v in waits:
                wm[s] = max(wm.get(s, 0), v)
            o['waits'] = wm
            o['clock'] = dict(kn)

        def semof(s):
            return esem[s[1]] if s[0] == 'e' else self.dma_sems[s[1]]

        def run(ename, eng):
            for o in ops:
                if o['eng'] != ename:
                    continue
                for s, v in o['waits'].items():
                    eng.wait_ge(semof(s), v)
                if o['fn'] is None:
                    continue
                inst = o['fn'](eng)
                if o['tok'] is not None:
                    s, v = o['tok']
                    inst.then_inc(semof(s), 16 if s[0] == 'd' else 1)

        with nc.Block() as block:
            @block.sync
            def _(e):
                run('sp', e)

            @block.scalar
            def _(e):
                run('act', e)

            @block.vector
            def _(e):
                run('dve', e)

            @block.gpsimd
            def _(e):
                run('pool', e)

            @block.tensor
            def _(e):
                run('pe', e)
        n = {e: sum(1 for o in ops if o['eng'] == e) for e in engs}
        nw = sum(len(o['waits']) for o in ops)
        print("PROG ops", n, "waits", nw, "sems", 5 + len(self.dma_sems), flush=True)


def _kw(**k):
    return {a: b for a, b in k.items() if b is not None}


class P2(Prog):
    def mm(self, out, lhsT, rhs, start=True, stop=True, r=(), w=()):
        return self.add('pe', lambda e: e.matmul(out, lhsT=lhsT, rhs=rhs, start=start, stop=stop), r, w)

    def tr(self, out, in_, ident, r=(), w=()):
        return self.add('pe', lambda e: e.transpose(out, in_, ident), r, w)

    def actv(self, out, in_, func, bias=None, scale=None, accum=None, r=(), w=()):
        kw = _kw(bias=bias, scale=scale, accum_out=accum)
        return self.add('act', lambda e: e.activation(out=out, in_=in_, func=func, **kw), r, w)

    def ts(self, eng, out, in0, s1, s2=None, op0=ALU.mult, op1=None, r=(), w=()):
        kw = _kw(op1=op1)
        return self.add(eng, lambda e: e.tensor_scalar(out=out, in0=in0, scalar1=s1, scalar2=s2, op0=op0, **kw), r, w)

    def tt(self, eng, out, in0, in1, op, r=(), w=()):
        return self.add(eng, lambda e: e.tensor_tensor(out=out, in0=in0, in1=in1, op=op), r, w)

    def stt(self, out, in0, scalar, in1, op0, op1, r=(), w=()):
        return self.add('dve', lambda e: e.scalar_tensor_tensor(out=out, in0=in0, scalar=scalar, in1=in1, op0=op0, op1=op1), r, w)

    def cp(self, eng, out, in_, r=(), w=()):
        if eng == 'act':
            return self.add('act', lambda e: e.activation(out=out, in_=in_, func=AF.Copy), r, w)
        return self.add(eng, lambda e: e.tensor_copy(out=out, in_=in_), r, w)

    def ms(self, eng, ap, val, w=()):
        return self.add(eng, lambda e: e.memset(ap, val), (), w)

    def ld(self, out, in_, w, sem, eng='sp', group=False, r=()):
        return self.dma(eng, lambda e: e.dma_start(out=out, in_=in_), r=r, w=w, sem=sem, group=group)

SEQ = 8192
DM = 1024
NCH = SEQ // 256
EPS = 1e-6
WCOLS = 1288


def build_stageA(nch=NCH):
    nc = bass.Bass("TRN2", target_bir_lowering=False)
    x_d = nc.dram_tensor("x", [SEQ, DM], F32, kind="ExternalInput").ap()
    w_d = nc.dram_tensor("w", [DM, WCOLS], F32, kind="ExternalInput").ap()
    gpre_d = nc.dram_tensor("gpre", [128, 8], F32, kind="ExternalInput").ap()
    cw_d = nc.dram_tensor("cw", [128, 24], F32, kind="ExternalInput").ap()
    cb_d = nc.dram_tensor("cb", [128, 6], F32, kind="ExternalInput").ap()
    dtb_d = nc.dram_tensor("dtb", [1, 8], F32, kind="ExternalInput").ap()
    alog_d = nc.dram_tensor("alog", [1, 8], F32, kind="ExternalInput").ap()
    dsk_d = nc.dram_tensor("dsk", [1, 8], F32, kind="ExternalInput").ap()
    gout_d = nc.dram_tensor("gout", [1, 512], F32, kind="ExternalInput").ap()
    yn_d = nc.dram_tensor("yn", [SEQ, 512], BF16, kind="ExternalOutput").ap()

    P = P2(nc)
    es = contextlib.ExitStack()

    def S(name, shape, dt):
        return es.enter_context(nc.sbuf_tensor(name, shape, dt))

    banks = [es.enter_context(nc.psum_tensor(f"bank{i}", [128, 512], F32)) for i in range(8)]

    W = S("W", [128, 8, WCOLS], BF16)
    wst = [S(f"wst{i}", [128, WCOLS], F32) for i in range(2)]
    gpre = S("gpre_s", [128, 8], F32)
    cw = S("cw_s", [128, 24], F32)
    cb = S("cb_s", [128, 6], F32)
    dtb_bc = S("dtb_bc", [128, 8], F32)
    A_bc = S("A_bc", [128, 8], F32)
    D_bc = S("D_bc", [128, 8], F32)
    gout_bc = S("gout_bc", [128, 512], F32)
    identf = S("identf", [128, 128], F32)
    identb = S("identb", [128, 128], BF16)
    onesf = S("onesf", [128, 128], F32)
    onesb = S("onesb", [128, 128], BF16)
    trif = S("trif", [128, 128], F32)
    triw = S("triw", [128, 256], BF16)
    SU = S("SU", [128, 128], BF16)
    cdiag = S("cdiag", [128, 24, 128], BF16)
    Dident = S("Dident", [128, 8, 128], BF16)
    xin = [S(f"xin{i}", [128, 2, DM], F32) for i in range(2)]
    junk = [S(f"junk{i}", [128, DM], BF16) for i in range(2)]
    ss = S("ss", [128, 2], F32)
    rt = S("rt", [128, 2], F32)
    rstd = S("rstd", [128, 2], F32)
    hn = S("hn", [128, 2, DM], BF16)
    hnT = S("hnT", [128, 8, 256], BF16)
    ubuf = S("ubuf", [128, 6, 259], BF16)
    xc = S("xc", [128, 6, 256], BF16)
    xtok = S("xtok", [128, 2, 640], BF16)
    dtr = S("dtr", [128, 2, 8], F32)
    e1 = S("e1", [128, 2, 8], F32)
    dtk = S("dtk", [128, 2, 8], F32)
    dtA = S("dtA", [128, 2, 8], F32)
    cend = S("cend", [128, 8], F32)
    ecum = S("ecum", [128, 2, 8], F32)
    wtmp = S("wtmp", [128, 2, 8], F32)
    dec = S("dec", [128, 8], F32)
    UV = [S(f"UV{i}", [128, 3, 128], BF16) for i in range(2)]
    CBm = S("CBm", [128, 384], BF16)
    xdt = S("xdt", [128, 2, 512], BF16)
    Lb = [S(f"Lb{i}", [128, 384], BF16) for i in range(2)]
    MT = S("MT", [128, 8, 384], BF16)
    state = S("state", [128, 512], F32)
    state_bf = S("state_bf", [128, 512], BF16)
    yi = S("yi", [128, 512], F32)
    t1 = S("t1", [128, 512], F32)
    ysb = S("ysb", [128, 512], F32)
    zs = S("zs", [128, 512], F32)
    yg = S("yg", [128, 512], F32)
    junk2 = S("junk2", [128, 512], BF16)
    ss2 = S("ss2", [128, 1], F32)
    rt2 = S("rt2", [128, 1], F32)
    rstd2 = S("rstd2", [128, 1], F32)
    yn = [S(f"yn{i}", [128, 512], BF16) for i in range(2)]
    wx = S("wx", [128, 2, 512], BF16)

    def bfview(bank):
        return bank[:].bitcast(BF16)

    ptr = [bfview(banks[t]).rearrange("p (k t) -> p k t", k=8) for t in range(2)]
    ptx = bfview(banks[0])[:, 0:640]
    pCB = banks[1][:, 0:384]
    pseg = [banks[2][:, 0:384], banks[3][:, 0:384]]
    pdtk = banks[6][:, 0:16].rearrange("p (t c) -> p t c", t=2)
    pcum = banks[6][:, 16:32].rearrange("p (t c) -> p t c", t=2)
    pce = banks[6][:, 32:40]
    pyi = banks[6][:, :]
    pst = banks[6][:, :]
    pz = banks[7][:, :]
    py = [banks[4][:, :], banks[5][:, :]]

    P.ld(gpre[:], gpre_d, ['gpre'], 'c0')
    P.ld(cw[:], cw_d, ['cw'], 'c1')
    P.ld(cb[:], cb_d, ['cb'], 'c2')
    P.ld(dtb_bc[:], dtb_d.partition_broadcast(128), ['dtb_bc'], 'c3')
    P.ld(A_bc[:], alog_d.partition_broadcast(128), ['A_bc'], 'c4')
    P.ld(D_bc[:], dsk_d.partition_broadcast(128), ['D_bc'], 'c5')
    P.ld(gout_bc[:], gout_d.partition_broadcast(128), ['gout_bc'], 'c6')
    P.ms('pool', identf[:], 1.0, ['identf'])
    P.add('pool', lambda e: e.affine_select(out=identf[:], in_=identf[:], pattern=[[-1, 128]], compare_op=ALU.is_equal,
                                            fill=0.0, base=0, channel_multiplier=1), r=['identf'], w=['identf'])
    P.cp('dve', identb[:], identf[:], r=['identf'], w=['identb'])
    P.ms('pool', onesf[:], 1.0, ['onesf'])
    P.ms('pool', onesb[:], 1.0, ['onesb'])
    P.ms('pool', triw[:], 1.0, ['triw'])
    P.add('pool', lambda e: e.affine_select(out=triw[:, 0:128], in_=triw[:, 0:128], pattern=[[1, 128]], compare_op=ALU.is_ge,
                                            fill=0.0, base=0, channel_multiplier=-1), r=['triw'], w=['triw'])
    P.cp('dve', trif[:], triw[:, 0:128], r=['triw'], w=['trif'])
    P.ms('pool', SU[:], 1.0, ['SU'])
    P.add('pool', lambda e: e.affine_select(out=SU[:], in_=SU[:], pattern=[[-1, 128]], compare_op=ALU.is_gt,
                                            fill=0.0, base=0, channel_multiplier=1), r=['SU'], w=['SU'])
    P.ms('pool', ubuf[:], 0.0, ['ubuf%d' % i for i in range(3)])
    P.ms('pool', state[:], 0.0, ['state'])
    P.ms('pool', state_bf[:], 0.0, ['state_bf'])
    for kt in range(8):
        P.ld(wst[kt % 2][:], w_d[kt * 128:(kt + 1) * 128, :], [f'wst{kt % 2}'], f'wst{kt % 2}')
        P.ts('dve' if kt % 2 == 0 else 'pool', W[:, kt, :], wst[kt % 2][:], gpre[:, kt:kt + 1], None, ALU.mult,
             r=[f'wst{kt % 2}', 'gpre'], w=[f'W{kt}'])
    Wk = [f'W{kt}' for kt in range(8)]
    HNT = ['hnT0', 'hnT1']
    for i in range(24):
        P.ts('dve', cdiag[:, i, :], identf[:], cw[:, i:i + 1], None, ALU.mult, r=['identf', 'cw'], w=['cdiag'])
    for h in range(8):
        P.ts('dve', Dident[:, h, :], identf[:], D_bc[:, h:h + 1], None, ALU.mult, r=['identf', 'D_bc'], w=['Dident'])
    P.actv(A_bc[:], A_bc[:], AF.Exp, r=['A_bc'], w=['A_bc'])
    P.ts('dve', A_bc[:], A_bc[:], -1.0, None, ALU.mult, r=['A_bc'], w=['A_bc'])

    def load_x(c):
        sl = c % 2
        P.ld(xin[sl][:], x_d[c * 256:(c + 1) * 256, :].rearrange("(t p) d -> p t d", p=128), [f'xin{sl}'], f'xin{sl}')

    load_x(0)
    for c in range(nch):
        sl = c % 2
        if c + 1 < nch:
            load_x(c + 1)
        xk = f'xin{sl}'
        for t in range(2):
            P.actv(junk[t][:], xin[sl][:, t, :], AF.Square, accum=ss[:, t:t + 1], r=[xk], w=[f'ss{t}', f'junk{t}'])
        P.actv(rt[:], ss[:], AF.Sqrt, bias=EPS, scale=1.0 / DM, r=['ss0', 'ss1'], w=['rt'])
        P.add('dve', lambda e: e.reciprocal(out=rstd[:], in_=rt[:]), r=['rt'], w=['rstd'])
        for t in range(2):
            P.ts('pool', hn[:, t, :], xin[sl][:, t, :], rstd[:, t:t + 1], None, ALU.mult, r=[xk, 'rstd'], w=[f'hn{t}'])
        for t in range(2):
            for kt in range(8):
                P.tr(ptr[t][:, kt, :], hn[:, t, kt * 128:(kt + 1) * 128], identb[:], r=[f'hn{t}', 'identb'], w=[f'B{t}'])
            P.cp('dve' if t == 0 else 'act', hnT[:, :, t * 128:(t + 1) * 128], ptr[t], r=[f'B{t}'], w=[f'hnT{t}'])
        for pr in range(3):
            bx = banks[2 + pr % 2]
            bxk = f'B{2 + pr % 2}'
            for j in range(2):
                ct = 2 * pr + j
                for kt in range(8):
                    P.mm(bx[:, j * 256:(j + 1) * 256], W[:, kt, ct * 128:(ct + 1) * 128], hnT[:, kt, :], start=(kt == 0), stop=(kt == 7),
                         r=HNT + [Wk[kt]], w=[bxk])
            P.cp('act', ubuf[:, 2 * pr:2 * pr + 2, 3:259], bx[:, :].rearrange("p (j t) -> p j t", j=2), r=[bxk], w=[f'ubuf{pr}'])
            bc = banks[4 + pr % 2]
            bck = f'B{4 + pr % 2}'
            for j in range(2):
                ct = 2 * pr + j
                for k in range(4):
                    P.mm(bc[:, j * 256:(j + 1) * 256], cdiag[:, ct * 4 + k, :], ubuf[:, ct, k:k + 256], start=(k == 0), stop=(k == 3),
                         r=['cdiag', f'ubuf{pr}'], w=[bck])
            for j in range(2):
                ct = 2 * pr + j
                P.actv(xc[:, ct, :], bc[:, j * 256:(j + 1) * 256], AF.Silu, bias=cb[:, ct:ct + 1], r=[bck, 'cb'], w=[f'xc{ct}'])
            P.cp('pool', ubuf[:, 2 * pr:2 * pr + 2, 0:3], ubuf[:, 2 * pr:2 * pr + 2, 256:259], r=[f'ubuf{pr}'], w=[f'ubuf{pr}'])
        for t in range(2):
            for kt in range(8):
                P.mm(pdtk[:, t, :], hnT[:, kt, t * 128:(t + 1) * 128], W[:, kt, 768:776], start=(kt == 0), stop=(kt == 7),
                     r=HNT + [Wk[kt]], w=['B6'])
        P.tt('dve', dtr[:], pdtk, dtb_bc[:].unsqueeze(1).to_broadcast([128, 2, 8]), ALU.add, r=['B6', 'dtb_bc'], w=['dtr'])
        P.actv(e1[:], dtr[:], AF.Exp, r=['dtr'], w=['e1'])
        P.actv(dtk[:], e1[:], AF.Ln, bias=1.0, r=['e1'], w=['dtk'])
        P.tt('dve', dtA[:], dtk[:], A_bc[:].unsqueeze(1).to_broadcast([128, 2, 8]), ALU.mult, r=['dtk', 'A_bc'], w=['dtA'])
        P.mm(pcum[:, 0, :], trif[:], dtA[:, 0, :], r=['trif', 'dtA'], w=['B6'])
        P.mm(pcum[:, 1, :], onesf[:], dtA[:, 0, :], start=True, stop=False, r=['onesf', 'dtA'], w=['B6'])
        P.mm(pcum[:, 1, :], trif[:], dtA[:, 1, :], start=False, stop=True, r=['trif', 'dtA'], w=['B6'])
        P.mm(pce, onesf[:], dtA[:, 0, :], start=True, stop=False, r=['onesf', 'dtA'], w=['B6'])
        P.mm(pce, onesf[:], dtA[:, 1, :], start=False, stop=True, r=['onesf', 'dtA'], w=['B6'])
        P.actv(ecum[:], pcum, AF.Exp, r=['B6'], w=['ecum'])
        P.actv(dec[:], pce, AF.Exp, r=['B6'], w=['dec'])
        P.cp('act', cend[:], pce, r=['B6'], w=['cend'])
        P.tt('dve', wtmp[:], cend[:].unsqueeze(1).to_broadcast([128, 2, 8]), pcum, ALU.subtract, r=['cend', 'B6'], w=['wtmp'])
        P.actv(wtmp[:], wtmp[:], AF.Exp, r=['wtmp'], w=['wtmp'])
        for t in range(2):
            for ct in range(5):
                P.tr(ptx[:, ct * 128:(ct + 1) * 128], xc[:, ct, t * 128:(t + 1) * 128], identb[:],
                     r=[f'xc{ct}', 'identb'], w=['B0'])
            P.cp('dve' if t == 0 else 'act', xtok[:, t, :], ptx, r=['B0'], w=[f'xtok{t}'])
            P.tt('pool', xdt[:, t, :].rearrange("p (h c) -> p h c", h=8), xtok[:, t, 0:512].rearrange("p (h c) -> p h c", h=8),
                 dtk[:, t, :].unsqueeze(2).to_broadcast([128, 8, 64]), ALU.mult, r=[f'xtok{t}', 'dtk'], w=[f'xdt{t}'])
        P.mm(pCB[:, 0:256], xc[:, 4, 0:128], xc[:, 5, 0:256], r=['xc4', 'xc5'], w=['B1'])
        P.mm(pCB[:, 256:384], xc[:, 4, 128:256], xc[:, 5, 128:256], r=['xc4', 'xc5'], w=['B1'])
        P.cp('act', CBm[:], pCB, r=['B1'], w=['CBm'])
        for off in (0, 256):
            blk = CBm[:, off:off + 128]
            P.add('pool', (lambda blk: (lambda e: e.affine_select(out=blk, in_=blk, pattern=[[1, 128]], compare_op=ALU.is_ge,
                                                                  fill=0.0, base=0, channel_multiplier=-1)))(blk),
                  r=['CBm'], w=['CBm'])
        for h in range(8):
            ps = pseg[h % 2]
            psk = f'B{2 + h % 2}'
            L = Lb[h % 2]
            Lk = f'Lb{h % 2}'
            uv = UV[h % 2]
            uk = f'UV{h % 2}'
            P.ts('pool', uv[:, 0, :], SU[:], dtA[:, 0, h:h + 1], None, ALU.mult, r=['SU', 'dtA'], w=[uk])
            P.ts('pool', uv[:, 1, :], SU[:], dtA[:, 1, h:h + 1], None, ALU.mult, r=['SU', 'dtA'], w=[uk])
            P.ts('pool', uv[:, 2, :], triw[:, 0:128], dtA[:, 1, h:h + 1], None, ALU.mult, r=['triw', 'dtA'], w=[uk])
            P.mm(ps[:, 0:128], uv[:, 0, :], triw[:, 0:128], start=True, stop=True, r=[uk, 'triw'], w=[psk])
            P.mm(ps[:, 128:256], uv[:, 0, :], triw[:, 128:256], start=True, stop=False, r=[uk, 'triw'], w=[psk])
            P.mm(ps[:, 128:256], onesb[:], uv[:, 2, :], start=False, stop=True, r=[uk, 'onesb'], w=[psk])
            P.mm(ps[:, 256:384], uv[:, 1, :], triw[:, 0:128], start=True, stop=True, r=[uk, 'triw'], w=[psk])
            P.actv(L[:], ps, AF.Exp, r=[psk], w=[Lk])
            P.tt('dve', MT[:, h, :], L[:], CBm[:], ALU.mult, r=[Lk, 'CBm'], w=[f'MT{h}'])
        for t in range(2):
            for h in range(8):
                hc = slice(h * 64, (h + 1) * 64)
                P.mm(py[t][:, hc], MT[:, h, t * 128:(t + 1) * 128], xdt[:, 0, hc], start=True, stop=False,
                     r=[f'MT{h}', 'xdt0'], w=[f'B{4 + t}'])
                if t == 1:
                    P.mm(py[t][:, hc], MT[:, h, 256:384], xdt[:, 1, hc], start=False, stop=False,
                         r=[f'MT{h}', 'xdt1'], w=[f'B{4 + t}'])
                P.mm(py[t][:, hc], Dident[:, h, :], xtok[:, t, hc], start=False, stop=True,
                     r=['Dident', f'xtok{t}'], w=[f'B{4 + t}'])
            P.mm(pyi, xc[:, 5, t * 128:(t + 1) * 128], state_bf[:], r=['xc5', 'state_bf'], w=['B6'])
            P.cp('act', yi[:], pyi, r=['B6'], w=['yi'])
            P.tt('pool', t1[:].rearrange("p (h c) -> p h c", h=8), yi[:].rearrange("p (h c) -> p h c", h=8),
                 ecum[:, t, :].unsqueeze(2).to_broadcast([128, 8, 64]), ALU.mult, r=['yi', 'ecum'], w=['t1'])
            P.tt('dve', ysb[:], t1[:], py[t], ALU.add, r=['t1', f'B{4 + t}'], w=['ysb'])
            for kt in range(8):
                P.mm(pz, hnT[:, kt, t * 128:(t + 1) * 128], W[:, kt, 776:1288], start=(kt == 0), stop=(kt == 7),
                     r=HNT + [Wk[kt]], w=['B7'])
            P.actv(zs[:], pz, AF.Silu, r=['B7'], w=['zs'])
            P.tt('pool', yg[:], ysb[:], zs[:], ALU.mult, r=['ysb', 'zs'], w=['yg'])
            P.actv(junk2[:], yg[:], AF.Square, accum=ss2[:], r=['yg'], w=['ss2', 'junk2'])
            P.actv(rt2[:], ss2[:], AF.Sqrt, bias=EPS, scale=1.0 / 512, r=['ss2'], w=['rt2'])
            P.add('dve', lambda e: e.reciprocal(out=rstd2[:], in_=rt2[:]), r=['rt2'], w=['rstd2'])
            P.stt(yn[t][:], yg[:], rstd2[:, 0:1], gout_bc[:], ALU.mult, ALU.mult, r=['yg', 'rstd2', 'gout_bc'], w=[f'yn{t}'])
            P.ld(yn_d[c * 256 + t * 128: c * 256 + (t + 1) * 128, :], yn[t][:], w=[f'ynd{t}'], sem=f'st{t}', r=[f'yn{t}'])
        for st in range(2):
            P.tt('pool', wx[:, st, :].rearrange("p (h c) -> p h c", h=8), xdt[:, st, :].rearrange("p (h c) -> p h c", h=8),
                 wtmp[:, st, :].unsqueeze(2).to_broadcast([128, 8, 64]), ALU.mult, r=[f'xdt{st}', 'wtmp'], w=[f'wx{st}'])
        for st in range(2):
            P.mm(pst, xtok[:, st, 512:640], wx[:, st, :], start=(st == 0), stop=(st == 1), r=[f'xtok{st}', f'wx{st}'], w=['B6'])
        P.tt('dve', state[:].rearrange("p (h c) -> p h c", h=8), state[:].rearrange("p (h c) -> p h c", h=8),
             dec[:].unsqueeze(2).to_broadcast([128, 8, 64]), ALU.mult, r=['state', 'dec'], w=['state'])
        P.tt('dve', state[:], state[:], pst, ALU.add, r=['state', 'B6'], w=['state'])
        P.cp('pool', state_bf[:], state[:], r=['state'], w=['state_bf'])
    P.wait_all('sp', ['ynd0', 'ynd1'])
    P.emit()
    es.close()
    return nc

NTOK = 2048
NTT = NTOK // 128
INV_FREQ = [float(np.float32(10000.0) ** np.float32(-(2 * i) / 32.0)) for i in range(16)]
TWO_PI = 2.0 * math.pi
CW1 = 6.28125
CW2 = TWO_PI - CW1


def build_stageB(ntt=NTT):
    nc = bass.Bass("TRN2", target_bir_lowering=False)
    x_d = nc.dram_tensor("x", [NTOK, 1024], F32, kind="ExternalInput").ap()
    yn_d = nc.dram_tensor("yn", [NTOK, 2048], BF16, kind="ExternalInput").ap()
    pos_d = nc.dram_tensor("pos", [128, NTT], I32, kind="ExternalInput").ap()
    invf_d = nc.dram_tensor("invf", [1, 16], F32, kind="ExternalInput").ap()
    wout_d = nc.dram_tensor("wout", [2048, 1024], F32, kind="ExternalInput").ap()
    wdn_d = nc.dram_tensor("wdn", [1024, 288], F32, kind="ExternalInput").ap()
    wup_d = nc.dram_tensor("wup", [256, 2048], F32, kind="ExternalInput").ap()
    win_d = nc.dram_tensor("win", [1024, 1408], F32, kind="ExternalInput").ap()
    wuq_d = nc.dram_tensor("wuq", [384, 1536], F32, kind="ExternalInput").ap()
    gkv_d = nc.dram_tensor("gkv", [128, 8], F32, kind="ExternalInput").ap()
    gpre_d = nc.dram_tensor("gpre", [128, 8], F32, kind="ExternalInput").ap()
    glat_d = nc.dram_tensor("glat", [128, 2], F32, kind="ExternalInput").ap()
    gq_d = nc.dram_tensor("gq", [128, 3], F32, kind="ExternalInput").ap()
    h1_d = nc.dram_tensor("h1", [NTOK, 1024], F32, kind="ExternalOutput").ap()
    sg_d = nc.dram_tensor("sg", [128, 8, NTOK], BF16, kind="ExternalOutput").ap()
    kn_d = nc.dram_tensor("kn", [128, 8, NTOK], BF16, kind="ExternalOutput").ap()
    kr_d = nc.dram_tensor("kr", [32, NTOK], BF16, kind="ExternalOutput").ap()
    v_d = nc.dram_tensor("v", [NTOK, 1024], BF16, kind="ExternalOutput").ap()
    qT_d = nc.dram_tensor("qT", [96, 16, NTOK], BF16, kind="ExternalOutput").ap()

    P = P2(nc)
    es = contextlib.ExitStack()

    def S(name, shape, dt):
        return es.enter_context(nc.sbuf_tensor(name, shape, dt))

    banks = [es.enter_context(nc.psum_tensor(f"bank{i}", [128, 512], F32)) for i in range(8)]
    bctr = [0]

    def nb():
        i = bctr[0] % 8
        bctr[0] += 1
        return banks[i], f'B{i}'

    def bfv(bank):
        return bank[:].bitcast(BF16)

    wout = S("wout_s", [128, 16, 1024], BF16)
    wdn = S("wdn_s", [128, 8, 288], BF16)
    wkn = S("wkn_s", [128, 2, 1024], BF16)
    wv = S("wv_s", [128, 2, 1024], BF16)
    win = S("win_s", [128, 8, 1408], BF16)
    wuq = S("wuq_s", [128, 3, 1536], BF16)
    wst = [S(f"wst{i}", [128, 2048], F32) for i in range(2)]
    gkv = S("gkv_s", [128, 8], F32)
    gpre = S("gpre_s", [128, 8], F32)
    glat = S("glat_s", [128, 2], F32)
    gq = S("gq_s", [128, 3], F32)
    identf = S("identf", [128, 128], F32)
    identb = S("identb", [128, 128], BF16)
    posi = S("posi", [128, NTT], I32)
    posf = S("posf", [128, NTT], F32)
    invf = S("invf_s", [128, 16], F32)
    ang = S("ang", [128, NTT, 16], F32)
    uu = S("uu", [128, NTT, 16], F32)
    ki = S("ki", [128, NTT, 16], I32)
    kf = S("kf", [128, NTT, 16], F32)
    gg = S("gg", [128, NTT, 16], F32)
    m1 = S("m1", [128, NTT, 16], F32)
    gc = S("gc", [128, NTT, 16], F32)
    sinT = S("sinT", [128, NTT, 16], F32)
    cosT = S("cosT", [128, NTT, 16], F32)
    xin = [S(f"xin{i}", [128, 1024], F32) for i in range(2)]
    ynin = [S(f"ynin{i}", [128, 2048], BF16) for i in range(2)]
    ynT = S("ynT", [128, 16, 128], BF16)
    h1 = [S(f"h1_{i}", [128, 1024], F32) for i in range(2)]
    junk = S("junk", [128, 1024], BF16)
    ss = S("ss", [128, 1], F32)
    rt = S("rt", [128, 1], F32)
    rstd = S("rstd", [128, 1], F32)
    hnb = S("hnb", [128, 1024], BF16)
    hT = S("hT", [128, 8, 128], BF16)
    junk2 = S("junk2", [128, 384], BF16)
    ssc = S("ssc", [128, 1], F32)
    rtc = S("rtc", [128, 1], F32)
    rstdc = S("rstdc", [128, 1], F32)
    ckvn = S("ckvn", [128, 256], BF16)
    ra = S("ra", [128, 16], F32)
    rb = S("rb", [128, 16], F32)
    krb = S("krb", [128, 32], BF16)
    ckT = S("ckT", [128, 2, 128], BF16)
    krT = [S(f"krT{i}", [32, 128], BF16) for i in range(2)]
    knT = [S(f"knT{i}", [128, 8, 128], BF16) for i in range(2)]
    vsb = [S(f"vsb{i}", [128, 1024], BF16) for i in range(2)]
    ssq = S("ssq", [128, 1], F32)
    rtq = S("rtq", [128, 1], F32)
    rstdq = S("rstdq", [128, 1], F32)
    cqn = S("cqn", [128, 384], BF16)
    sg = [S(f"sg{i}", [128, 8, 128], BF16) for i in range(2)]
    cqT = S("cqT", [128, 3, 128], BF16)
    qtok = S("qtok", [128, 16, 96], BF16)
    qa = S("qa", [128, 16, 16], F32)
    qb = S("qb", [128, 16, 16], F32)
    qT = [S(f"qT{i}", [96, 16, 128], BF16) for i in range(2)]

    P.ld(gkv[:], gkv_d, ['gkv'], 'c0')
    P.ld(gpre[:], gpre_d, ['gpre'], 'c1')
    P.ld(glat[:], glat_d, ['glat'], 'c2')
    P.ld(gq[:], gq_d, ['gq'], 'c3')
    P.ld(posi[:], pos_d, ['posi'], 'c4')
    P.ld(invf[:], invf_d.partition_broadcast(128), ['invf'], 'c5')
    P.ms('pool', identf[:], 1.0, ['identf'])
    P.add('pool', lambda e: e.affine_select(out=identf[:], in_=identf[:], pattern=[[-1, 128]], compare_op=ALU.is_equal,
                                            fill=0.0, base=0, channel_multiplier=1), r=['identf'], w=['identf'])
    P.cp('dve', identb[:], identf[:], r=['identf'], w=['identb'])
    P.cp('dve', posf[:], posi[:], r=['posi'], w=['posf'])
    P.tt('dve', ang[:], posf[:].unsqueeze(2).to_broadcast([128, NTT, 16]), invf[:].unsqueeze(1).to_broadcast([128, NTT, 16]),
         ALU.mult, r=['posf', 'invf'], w=['ang'])
    P.ts('dve', uu[:], ang[:], 1.0 / TWO_PI, None, ALU.mult, r=['ang'], w=['uu'])
    P.cp('dve', ki[:], uu[:], r=['uu'], w=['ki'])
    P.cp('dve', kf[:], ki[:], r=['ki'], w=['kf'])
    P.stt(gg[:], kf[:], -CW1, ang[:], ALU.mult, ALU.add, r=['kf', 'ang'], w=['gg'])
    P.stt(gg[:], kf[:], -CW2, gg[:], ALU.mult, ALU.add, r=['kf', 'gg'], w=['gg'])
    P.ts('dve', gg[:], gg[:], 1.0 / TWO_PI, None, ALU.mult, r=['gg'], w=['gg'])

    def wrap():
        P.ts('dve', m1[:], gg[:], 0.5, None, ALU.is_gt, r=['gg'], w=['m1'])
        P.tt('dve', gg[:], gg[:], m1[:], ALU.subtract, r=['gg', 'm1'], w=['gg'])
        P.ts('dve', m1[:], gg[:], -0.5, None, ALU.is_lt, r=['gg'], w=['m1'])
        P.tt('dve', gg[:], gg[:], m1[:], ALU.add, r=['gg', 'm1'], w=['gg'])
        P.ts('dve', gg[:], gg[:], 0.4999995, -0.4999995, ALU.min, ALU.max, r=['gg'], w=['gg'])

    wrap()
    P.actv(sinT[:], gg[:], AF.Sin, scale=TWO_PI, r=['gg'], w=['sinT'])
    P.ts('dve', gg[:], gg[:], 0.25, None, ALU.add, r=['gg'], w=['gg'])
    wrap()
    P.actv(cosT[:], gg[:], AF.Sin, scale=TWO_PI, r=['gg'], w=['cosT'])

    wi = [0]

    def wload(dst_ap, src_ap, ncols, gain_ap, in_view=None):
        i = wi[0] % 2
        wi[0] += 1
        P.ld(wst[i][:, 0:ncols], src_ap, [f'wst{i}'], f'wst{i}')
        src = wst[i][:, 0:ncols] if in_view is None else in_view(wst[i])
        eng = 'dve' if i == 0 else 'pool'
        if gain_ap is None:
            P.cp(eng, dst_ap, src, r=[f'wst{i}'], w=[f'W{wi[0]}'])
        else:
            P.ts(eng, dst_ap, src, gain_ap, None, ALU.mult, r=[f'wst{i}', 'gkv', 'gpre', 'glat', 'gq'], w=[f'W{wi[0]}'])

    for kt in range(16):
        wload(wout[:, kt, :], wout_d[kt * 128:(kt + 1) * 128, :], 1024, None)
    for kt in range(8):
        wload(wdn[:, kt, :], wdn_d[kt * 128:(kt + 1) * 128, :], 288, gkv[:, kt:kt + 1])
    for kt in range(8):
        wload(win[:, kt, :], win_d[kt * 128:(kt + 1) * 128, :], 1408, gpre[:, kt:kt + 1])
    for kt in range(2):
        wload(wkn[:, kt, :].rearrange("p (h c) -> p h c", h=16), wup_d[kt * 128:(kt + 1) * 128, :], 2048, glat[:, kt:kt + 1],
              in_view=lambda t: t[:, 0:2048].rearrange("p (h c) -> p h c", h=16)[:, :, 0:64])
        wload(wv[:, kt, :].rearrange("p (h c) -> p h c", h=16), wup_d[kt * 128:(kt + 1) * 128, :], 2048, glat[:, kt:kt + 1],
              in_view=lambda t: t[:, 0:2048].rearrange("p (h c) -> p h c", h=16)[:, :, 64:128])
    for kt in range(3):
        wload(wuq[:, kt, 0:1024].rearrange("p (h c) -> p h c", h=16), wuq_d[kt * 128:(kt + 1) * 128, :], 1536, gq[:, kt:kt + 1],
              in_view=lambda t: t[:, 0:1536].rearrange("p (h c) -> p h c", h=16)[:, :, 0:64])
        wload(wuq[:, kt, 1024:1280].rearrange("p (h c) -> p h c", h=16), wuq_d[kt * 128:(kt + 1) * 128, :], 1536, gq[:, kt:kt + 1],
              in_view=lambda t: t[:, 0:1536].rearrange("p (h c) -> p h c", h=16)[:, :, 64:80])
        wload(wuq[:, kt, 1280:1536].rearrange("p (h c) -> p h c", h=16), wuq_d[kt * 128:(kt + 1) * 128, :], 1536, gq[:, kt:kt + 1],
              in_view=lambda t: t[:, 0:1536].rearrange("p (h c) -> p h c", h=16)[:, :, 80:96])

    ALLW = [f'W{i}' for i in range(1, wi[0] + 1)]

    def load_t(tt):
        sl = tt % 2
        P.ld(xin[sl][:], x_d[tt * 128:(tt + 1) * 128, :], [f'xin{sl}'], f'xin{sl}')
        P.ld(ynin[sl][:], yn_d[tt * 128:(tt + 1) * 128, :], [f'ynin{sl}'], f'ynin{sl}')

    load_t(0)
    for tt in range(ntt):
        sl = tt % 2
        tok = slice(tt * 128, (tt + 1) * 128)
        if tt + 1 < ntt:
            load_t(tt + 1)
        for half in range(2):
            bk, bkk = nb()
            pv = bfv(bk).rearrange("p (k t) -> p k t", k=8)
            for j in range(8):
                c = half * 8 + j
                P.tr(pv[:, j, :], ynin[sl][:, c * 128:(c + 1) * 128], identb[:], r=[f'ynin{sl}', 'identb'], w=[bkk])
            P.cp('dve' if half == 0 else 'act', ynT[:, half * 8:(half + 1) * 8, :], pv, r=[bkk], w=[f'ynT{half}'])
        for half in range(2):
            bk, bkk = nb()
            for c in range(16):
                P.mm(bk[:, :], ynT[:, c, :], wout[:, c, half * 512:(half + 1) * 512], start=(c == 0), stop=(c == 15),
                     r=['ynT0', 'ynT1', *ALLW], w=[bkk])
            P.tt('dve', h1[sl][:, half * 512:(half + 1) * 512], bk[:, :], xin[sl][:, half * 512:(half + 1) * 512], ALU.add,
                 r=[bkk, f'xin{sl}'], w=[f'h1_{sl}'])
        P.ld(h1_d[tok, :], h1[sl][:], w=[f'h1d{sl}'], sem=f'sth{sl}', r=[f'h1_{sl}'])
        P.actv(junk[:], h1[sl][:], AF.Square, accum=ss[:], r=[f'h1_{sl}'], w=['junk', 'ss'])
        P.actv(rt[:], ss[:], AF.Sqrt, bias=EPS, scale=1.0 / 1024, r=['ss'], w=['rt'])
        P.add('dve', lambda e: e.reciprocal(out=rstd[:], in_=rt[:]), r=['rt'], w=['rstd'])
        P.ts('pool', hnb[:], h1[sl][:], rstd[:, 0:1], None, ALU.mult, r=[f'h1_{sl}', 'rstd'], w=['hnb'])
        bk, bkk = nb()
        pv = bfv(bk).rearrange("p (k t) -> p k t", k=8)
        for kt in range(8):
            P.tr(pv[:, kt, :], hnb[:, kt * 128:(kt + 1) * 128], identb[:], r=['hnb', 'identb'], w=[bkk])
        P.cp('act', hT[:], pv, r=[bkk], w=['hT'])
        bk, bkk = nb()
        for kt in range(8):
            P.mm(bk[:, 0:288], hT[:, kt, :], wdn[:, kt, :], start=(kt == 0), stop=(kt == 7), r=['hT', *ALLW], w=[bkk])
        P.actv(junk2[:, 0:256], bk[:, 0:256], AF.Square, accum=ssc[:], r=[bkk], w=['junk2', 'ssc'])
        P.actv(rtc[:], ssc[:], AF.Sqrt, bias=EPS, scale=1.0 / 256, r=['ssc'], w=['rtc'])
        P.add('dve', lambda e: e.reciprocal(out=rstdc[:], in_=rtc[:]), r=['rtc'], w=['rstdc'])
        P.tt('dve', ra[:], bk[:, 256:272], cosT[:, tt, :], ALU.mult, r=[bkk, 'cosT'], w=['ra'])
        P.tt('dve', rb[:], bk[:, 272:288], sinT[:, tt, :], ALU.mult, r=[bkk, 'sinT'], w=['rb'])
        P.tt('dve', krb[:, 0:16], ra[:], rb[:], ALU.subtract, r=['ra', 'rb'], w=['krb'])
        P.tt('dve', ra[:], bk[:, 256:272], sinT[:, tt, :], ALU.mult, r=[bkk, 'sinT', 'krb'], w=['ra'])
        P.tt('dve', rb[:], bk[:, 272:288], cosT[:, tt, :], ALU.mult, r=[bkk, 'cosT', 'krb'], w=['rb'])
        P.tt('dve', krb[:, 16:32], ra[:], rb[:], ALU.add, r=['ra', 'rb'], w=['krb'])
        P.ts('dve', ckvn[:], bk[:, 0:256], rstdc[:, 0:1], None, ALU.mult, r=[bkk, 'rstdc'], w=['ckvn'])
        bk, bkk = nb()
        pv = bfv(bk)
        for kt in range(2):
            P.tr(pv[:, kt * 128:(kt + 1) * 128], ckvn[:, kt * 128:(kt + 1) * 128], identb[:], r=['ckvn', 'identb'], w=[bkk])
        P.tr(pv[0:32, 256:384], krb[:], identb[:], r=['krb', 'identb'], w=[bkk])
        P.cp('act', ckT[:], pv[:, 0:256].rearrange("p (k t) -> p k t", k=2), r=[bkk], w=['ckT'])
        P.cp('act', krT[sl][:], pv[0:32, 256:384], r=[bkk], w=[f'krT{sl}'])
        P.ld(kr_d[:, tok], krT[sl][:], w=[f'krd{sl}'], sem=f'stkr{sl}', r=[f'krT{sl}'])
        for half in range(2):
            bk, bkk = nb()
            for j in range(4):
                pr = half * 4 + j
                for kt in range(2):
                    P.mm(bk[:, j * 128:(j + 1) * 128], wkn[:, kt, pr * 128:(pr + 1) * 128], ckT[:, kt, :], start=(kt == 0), stop=(kt == 1),
                         r=['ckT', *ALLW], w=[bkk])
            P.cp('act' if half == 0 else 'dve', knT[sl][:, half * 4:(half + 1) * 4, :], bk[:, :].rearrange("p (j t) -> p j t", j=4),
                 r=[bkk], w=[f'knT{sl}'])
        P.ld(kn_d[:, :, tok], knT[sl][:], w=[f'knd{sl}'], sem=f'stkn{sl}', r=[f'knT{sl}'])
        for half in range(2):
            bk, bkk = nb()
            for kt in range(2):
                P.mm(bk[:, :], ckT[:, kt, :], wv[:, kt, half * 512:(half + 1) * 512], start=(kt == 0), stop=(kt == 1),
                     r=['ckT', *ALLW], w=[bkk])
            P.cp('act' if half == 0 else 'dve', vsb[sl][:, half * 512:(half + 1) * 512], bk[:, :], r=[bkk], w=[f'vsb{sl}'])
        P.ld(v_d[tok, :], vsb[sl][:], w=[f'vd{sl}'], sem=f'stv{sl}', r=[f'vsb{sl}'])
        bk, bkk = nb()
        for kt in range(8):
            P.mm(bk[:, 0:384], hT[:, kt, :], win[:, kt, 0:384], start=(kt == 0), stop=(kt == 7), r=['hT', *ALLW], w=[bkk])
        P.actv(junk2[:], bk[:, 0:384], AF.Square, accum=ssq[:], r=[bkk], w=['junk2', 'ssq'])
        P.actv(rtq[:], ssq[:], AF.Sqrt, bias=EPS, scale=1.0 / 384, r=['ssq'], w=['rtq'])
        P.add('dve', lambda e: e.reciprocal(out=rstdq[:], in_=rtq[:]), r=['rtq'], w=['rstdq'])
        P.ts('dve', cqn[:], bk[:, 0:384], rstdq[:, 0:1], None, ALU.mult, r=[bkk, 'rstdq'], w=['cqn'])
        for half in range(2):
            bk, bkk = nb()
            for j in range(4):
                ct = half * 4 + j
                for kt in range(8):
                    P.mm(bk[:, j * 128:(j + 1) * 128], win[:, kt, 384 + ct * 128:384 + (ct + 1) * 128], hT[:, kt, :],
                         start=(kt == 0), stop=(kt == 7), r=['hT', *ALLW], w=[bkk])
            P.actv(sg[sl][:, half * 4:(half + 1) * 4, :], bk[:, :].rearrange("p (j t) -> p j t", j=4), AF.Silu, r=[bkk], w=[f'sg{sl}'])
        P.ld(sg_d[:, :, tok], sg[sl][:], w=[f'sgd{sl}'], sem=f'stsg{sl}', r=[f'sg{sl}'])
        bk, bkk = nb()
        pv = bfv(bk)
        for kt in range(3):
            P.tr(pv[:, kt * 128:(kt + 1) * 128], cqn[:, kt * 128:(kt + 1) * 128], identb[:], r=['cqn', 'identb'], w=[bkk])
        P.cp('act', cqT[:], pv[:, 0:384].rearrange("p (k t) -> p k t", k=3), r=[bkk], w=['cqT'])
        for blk in range(2):
            bk, bkk = nb()
            for kt in range(3):
                P.mm(bk[:, :], cqT[:, kt, :], wuq[:, kt, blk * 512:(blk + 1) * 512], start=(kt == 0), stop=(kt == 2),
                     r=['cqT', *ALLW], w=[bkk])
            P.cp('act', qtok[:, blk * 8:(blk + 1) * 8, 0:64], bk[:, :].rearrange("p (h c) -> p h c", h=8), r=[bkk], w=['qtok'])
        bk, bkk = nb()
        for kt in range(3):
            P.mm(bk[:, :], cqT[:, kt, :], wuq[:, kt, 1024:1536], start=(kt == 0), stop=(kt == 2), r=['cqT', *ALLW], w=[bkk])
        x1 = bk[:, 0:256].rearrange("p (h c) -> p h c", h=16)
        x2 = bk[:, 256:512].rearrange("p (h c) -> p h c", h=16)
        cb_ = cosT[:, tt, :].unsqueeze(1).to_broadcast([128, 16, 16])
        sb_ = sinT[:, tt, :].unsqueeze(1).to_broadcast([128, 16, 16])
        P.tt('dve', qa[:], x1, cb_, ALU.mult, r=[bkk, 'cosT'], w=['qa'])
        P.tt('dve', qb[:], x2, sb_, ALU.mult, r=[bkk, 'sinT'], w=['qb'])
        P.tt('dve', qtok[:, :, 64:80], qa[:], qb[:], ALU.subtract, r=['qa', 'qb'], w=['qtok'])
        P.tt('dve', qa[:], x1, sb_, ALU.mult, r=[bkk, 'sinT', 'qtok'], w=['qa'])
        P.tt('dve', qb[:], x2, cb_, ALU.mult, r=[bkk, 'cosT', 'qtok'], w=['qb'])
        P.tt('dve', qtok[:, :, 80:96], qa[:], qb[:], ALU.add, r=['qa', 'qb'], w=['qtok'])
        for half in range(2):
            bk, bkk = nb()
            pv = bfv(bk)[0:96, :].rearrange("p (h t) -> p h t", h=8)
            for j in range(8):
                P.tr(pv[:, j, :], qtok[:, half * 8 + j, :], identb[:], r=['qtok', 'identb'], w=[bkk])
            P.cp('act' if half == 0 else 'dve', qT[sl][:, half * 8:(half + 1) * 8, :], pv, r=[bkk], w=[f'qT{sl}'])
        P.ld(qT_d[:, :, tok], qT[sl][:], w=[f'qd{sl}'], sem=f'stq{sl}', r=[f'qT{sl}'])
    outk = []
    for sl in range(2):
        outk += [f'h1d{sl}', f'krd{sl}', f'knd{sl}', f'vd{sl}', f'sgd{sl}', f'qd{sl}']
    P.wait_all('sp', outk)
    P.emit()
    es.close()
    return nc
SCALE = 96.0 ** -0.5
LOOKAHEAD = 2


def build_stageC(nheads=4, nchunks=16):
    nc = bass.Bass("TRN2", target_bir_lowering=False)
    SQ = 8192
    LK = 8192
    NKT = LK // 128
    chunks = list(range(nchunks))
    kT_d = nc.dram_tensor("kT", [4, 96, 8192], BF16, kind="ExternalInput").ap()
    v_d = nc.dram_tensor("v", [4, 128, 64, 64], BF16, kind="ExternalInput").ap()
    qT_d = nc.dram_tensor("qT", [4, 96, SQ], BF16, kind="ExternalInput").ap()
    sg_d = nc.dram_tensor("sg", [4, 64, SQ], BF16, kind="ExternalInput").ap()
    og_d = nc.dram_tensor("og", [64, 4, SQ], BF16, kind="ExternalOutput").ap()

    P = P2(nc)
    es = contextlib.ExitStack()

    def S(name, shape, dt):
        return es.enter_context(nc.sbuf_tensor(name, shape, dt))

    banks = [es.enter_context(nc.psum_tensor(f"bank{i}", [128, 512], F32)) for i in range(8)]
    kT = [S(f"kT{i}", [96, LK], BF16) for i in range(2)]
    vh = [S(f"vh{i}", [128, NKT, 65], BF16) for i in range(2)]
    qh = [S(f"qh{i}", [96, SQ], BF16) for i in range(2)]
    sgh = [S(f"sgh{i}", [64, SQ], BF16) for i in range(2)]
    ogs = [S(f"ogs{i}", [64, 512], BF16) for i in range(2)]
    PT = [S(f"PT{i}", [128, 512], BF16) for i in range(4)]
    rrow = S("rrow", [65, 512], F32)
    onesr = S("onesr", [65, 64], F32)
    ot = [S(f"ot{i}", [64, 512], F32) for i in range(2)]
    tn = S("tn", [64, 512], BF16)

    P.ms('pool', onesr[:], 1.0, ['onesr'])
    for i in range(2):
        P.ms('pool', vh[i][:, :, 64:65], 1.0, [f'vh{i}'])

    def load_head(h):
        i = h % 2
        half = LK // 2
        P.ld(kT[i][:, 0:half], kT_d[h, :, 0:half], [f'kT{i}a'], f'kT{i}a')
        P.ld(kT[i][:, half:LK], kT_d[h, :, half:LK], [f'kT{i}b'], f'kT{i}b', eng='act')
        P.ld(vh[i][:, :, 0:64], v_d[h, :, 0:NKT, :], [f'vh{i}'], f'vh{i}', eng='pool')
        P.ld(qh[i][:], qT_d[h, :, :], [f'qh{i}'], f'qh{i}')
        P.ld(sgh[i][:], sg_d[h, :, :], [f'sgh{i}'], f'sgh{i}')

    load_head(0)
    sctr = [0]
    for h in range(nheads):
        i = h % 2
        if h + 1 < nheads:
            load_head(h + 1)
        KK = [f'kT{i}a', f'kT{i}b']
        for qi, cj in enumerate(chunks):
            nk = (cj + 1) * 4
            qsl = slice(qi * 512, (qi + 1) * 512)
            po = banks[4 + qi % 2]
            pok = f'B{4 + qi % 2}'
            tiles = []
            for kt in range(nk):
                d = kt - (nk - 4)
                c0 = 128 * d if d > 0 else 0
                tiles.append((kt, d, c0))

            def emit_S(kt, d, c0):
                sb = sctr[0] % 4
                sctr[0] += 1
                ps = banks[sb]
                P.mm(ps[:, c0:512], kT[i][:, kt * 128:(kt + 1) * 128], qh[i][:, qi * 512 + c0:(qi + 1) * 512],
                     r=KK + [f'qh{i}'], w=[f'B{sb}'])
                P.actv(PT[sb][:, c0:512], ps[:, c0:512], AF.Exp, scale=SCALE, r=[f'B{sb}'], w=[f'PT{sb}'])
                if d >= 0:
                    blk = PT[sb][:, c0:c0 + 128]
                    P.add('pool', (lambda blk: (lambda e: e.affine_select(out=blk, in_=blk, pattern=[[1, 128]], compare_op=ALU.is_ge,
                                                                          fill=0.0, base=0, channel_multiplier=-1)))(blk),
                          r=[f'PT{sb}'], w=[f'PT{sb}'])
                return sb

            pend = []
            for n, (kt, d, c0) in enumerate(tiles):
                pend.append((emit_S(kt, d, c0), kt, c0))
                if len(pend) > LOOKAHEAD:
                    sb, kt2, c02 = pend.pop(0)
                    P.mm(po[0:65, c02:512], vh[i][:, kt2, :], PT[sb][:, c02:512], start=(kt2 == 0), stop=(kt2 == nk - 1),
                         r=[f'vh{i}', f'PT{sb}'], w=[pok])
            while pend:
                sb, kt2, c02 = pend.pop(0)
                P.mm(po[0:65, c02:512], vh[i][:, kt2, :], PT[sb][:, c02:512], start=(kt2 == 0), stop=(kt2 == nk - 1),
                     r=[f'vh{i}', f'PT{sb}'], w=[pok])
            P.add('dve', lambda e, po=po: e.reciprocal(out=rrow[64:65, :], in_=po[64:65, :]), r=[pok], w=['rrow'])
            oti = ot[qi % 2]
            P.cp('act', oti[:], po[0:64, :], r=[pok], w=[f'ot{qi % 2}'])
            prb = banks[6]
            P.mm(prb[0:64, :], onesr[64:65, :], rrow[64:65, :], r=['onesr', 'rrow'], w=['B6'])
            P.tt('dve', tn[:], oti[:], prb[0:64, :], ALU.mult, r=[f'ot{qi % 2}', 'B6'], w=['tn'])
            ogi = ogs[qi % 2]
            P.tt('pool', ogi[:], tn[:], sgh[i][:, qsl], ALU.mult, r=['tn', f'sgh{i}'], w=[f'ogs{qi % 2}'])
            P.ld(og_d[:, h, qsl], ogi[:], w=[f'ogd{qi % 2}'], sem=f'sto{qi % 2}', r=[f'ogs{qi % 2}'])
    P.wait_all('sp', ['ogd0', 'ogd1'])
    P.emit()
    es.close()
    return nc


def build_stageD():
    nc = bass.Bass("TRN2", target_bir_lowering=False)
    og_d = nc.dram_tensor("og", [64, 16, NTOK], BF16, kind="ExternalInput").ap()
    h1_d = nc.dram_tensor("h1", [NTOK, 1024], F32, kind="ExternalInput").ap()
    wo_d = nc.dram_tensor("wo", [1024, 1024], F32, kind="ExternalInput").ap()
    gf_d = nc.dram_tensor("gf", [1, 1024], F32, kind="ExternalInput").ap()
    out_d = nc.dram_tensor("out", [NTOK, 1024], F32, kind="ExternalOutput").ap()
    P = P2(nc)
    es = contextlib.ExitStack()

    def S(name, shape, dt):
        return es.enter_context(nc.sbuf_tensor(name, shape, dt))

    banks = [es.enter_context(nc.psum_tensor(f"bank{i}", [128, 512], F32)) for i in range(8)]
    ogT = S("ogT", [64, 16, NTOK], BF16)
    wo = S("wo_s", [64, 16, 1024], BF16)
    wst = [S(f"wst{i}", [64, 1024], F32) for i in range(2)]
    gf_bc = S("gf_bc", [128, 1024], F32)
    h1t = [S(f"h1t{i}", [128, 1024], F32) for i in range(2)]
    h2 = S("h2", [128, 1024], F32)
    junk = S("junk", [128, 1024], BF16)
    ss = S("ss", [128, 1], F32)
    rt = S("rt", [128, 1], F32)
    rstd = S("rstd", [128, 1], F32)
    outt = [S(f"outt{i}", [128, 1024], F32) for i in range(2)]
    P.ld(gf_bc[:], gf_d.partition_broadcast(128), ['gf_bc'], 'c0')
    for q in range(4):
        P.ld(ogT[:, :, q * 512:(q + 1) * 512], og_d[:, :, q * 512:(q + 1) * 512], [f'ogT{q}'], f'og{q}')
    wkeys = []
    for h in range(16):
        i = h % 2
        P.ld(wst[i][:], wo_d[h * 64:(h + 1) * 64, :], [f'wst{i}'], f'wst{i}')
        P.cp('dve' if i == 0 else 'pool', wo[:, h, :], wst[i][:], r=[f'wst{i}'], w=[f'wo{h}'])
        wkeys.append(f'wo{h}')
    def load_h1(tt):
        P.ld(h1t[tt % 2][:], h1_d[tt * 128:(tt + 1) * 128, :], [f'h1t{tt % 2}'], f'h1t{tt % 2}')

    load_h1(0)
    for tt in range(NTT):
        sl = tt % 2
        if tt + 1 < NTT:
            load_h1(tt + 1)
        for half in range(2):
            bk = banks[half]
            for h in range(16):
                P.mm(bk[:, :], ogT[:, h, tt * 128:(tt + 1) * 128], wo[:, h, half * 512:(half + 1) * 512], start=(h == 0), stop=(h == 15),
                     r=[f'ogT{tt // 4}', wkeys[h]], w=[f'B{half}'])
            P.tt('dve', h2[:, half * 512:(half + 1) * 512], bk[:, :], h1t[sl][:, half * 512:(half + 1) * 512], ALU.add,
                 r=[f'B{half}', f'h1t{sl}'], w=[f'h2_{half}'])
        P.actv(junk[:], h2[:], AF.Square, accum=ss[:], r=['h2_0', 'h2_1'], w=['junk', 'ss'])
        P.actv(rt[:], ss[:], AF.Sqrt, bias=EPS, scale=1.0 / 1024, r=['ss'], w=['rt'])
        P.add('dve', lambda e: e.reciprocal(out=rstd[:], in_=rt[:]), r=['rt'], w=['rstd'])
        P.stt(outt[sl][:], h2[:], rstd[:, 0:1], gf_bc[:], ALU.mult, ALU.mult, r=['h2_0', 'h2_1', 'rstd', 'gf_bc'], w=[f'outt{sl}'])
        P.ld(out_d[tt * 128:(tt + 1) * 128, :], outt[sl][:], w=[f'od{sl}'], sem=f'sto{sl}', r=[f'outt{sl}'])
    P.wait_all('sp', ['od0', 'od1'])
    P.emit()
    es.close()
    return nc


def _prepA(inp, b, g):
    w_in = inp['ssm_w_in'][0]
    w = np.concatenate([w_in[:, 2048 + g * 512:2048 + (g + 1) * 512], w_in[:, 4096 + g * 128:4096 + (g + 1) * 128],
                        w_in[:, 4608 + g * 128:4608 + (g + 1) * 128], w_in[:, 5120 + g * 8:5120 + (g + 1) * 8],
                        w_in[:, g * 512:(g + 1) * 512]], axis=1)
    cidx = np.concatenate([np.arange(g * 512, (g + 1) * 512), 2048 + np.arange(g * 128, (g + 1) * 128),
                           2560 + np.arange(g * 128, (g + 1) * 128)])
    cwc = inp['ssm_conv_w'][0][:, cidx]
    cw = cwc.T.reshape(6, 128, 4).transpose(1, 0, 2).reshape(128, 24)
    cb = inp['ssm_conv_b'][0][cidx].reshape(6, 128).T
    hs = slice(g * 8, (g + 1) * 8)
    C = np.ascontiguousarray
    return dict(x=C(inp['x'][b]), w=C(w), gpre=C(inp['g_pre'][0].reshape(8, 128).T), cw=C(cw), cb=C(cb),
                dtb=C(inp['ssm_dt_bias'][0][hs].reshape(1, 8)), alog=C(inp['ssm_A_log'][0][hs].reshape(1, 8)),
                dsk=C(inp['ssm_D'][0][hs].reshape(1, 8)), gout=C(inp['ssm_g_out'][0][g * 512:(g + 1) * 512].reshape(1, 512)))


def _prepB(inp, yn_b, b, j):
    C = np.ascontiguousarray
    tok = slice(j * NTOK, (j + 1) * NTOK)
    pos = np.asarray(inp['positions'][b][tok]).astype(np.int32).reshape(NTT, 128).T
    return dict(x=C(inp['x'][b][tok]), yn=C(yn_b[tok]), pos=C(pos), invf=np.array(INV_FREQ, dtype=np.float32).reshape(1, 16),
                wout=C(inp['ssm_w_out'][0]), wdn=C(inp['kv_w_down']), wup=C(inp['kv_w_up']), win=C(inp['mla_w_in'][0]),
                wuq=C(inp['mla_w_uq'][0]), gkv=C(inp['kv_g_in'].reshape(8, 128).T), gpre=C(inp['g_pre'][1].reshape(8, 128).T),
                glat=C(inp['kv_g_latent'].reshape(2, 128).T), gq=C(inp['mla_g_q'][0].reshape(3, 128).T))


def kernel(**inputs):
    inp = {k: np.asarray(v) for k, v in inputs.items()}
    C = np.ascontiguousarray
    cores = list(range(8))
    ncA = build_stageA()
    rA = run_bass_kernel_spmd(ncA, [_prepA(inp, c // 4, c % 4) for c in cores], core_ids=cores).results
    yn = [np.concatenate([rA[b * 4 + g]['yn'] for g in range(4)], axis=1) for b in range(2)]
    ncB = build_stageB()
    rB = run_bass_kernel_spmd(ncB, [_prepB(inp, yn[c // 4], c // 4, c % 4) for c in cores], core_ids=cores).results
    imC = []
    for c in cores:
        b, hg = c // 4, c % 4
        kn = np.concatenate([rB[b * 4 + j]['kn'] for j in range(4)], axis=2)
        kr = np.concatenate([rB[b * 4 + j]['kr'] for j in range(4)], axis=1)
        vf = np.concatenate([rB[b * 4 + j]['v'] for j in range(4)], axis=0)
        qf = np.concatenate([rB[b * 4 + j]['qT'] for j in range(4)], axis=2)
        sf = np.concatenate([rB[b * 4 + j]['sg'] for j in range(4)], axis=2)
        kT = np.empty((4, 96, 8192), dtype=kn.dtype)
        v4 = np.empty((4, 128, 64, 64), dtype=vf.dtype)
        q4 = np.empty((4, 96, 8192), dtype=qf.dtype)
        s4 = np.empty((4, 64, 8192), dtype=sf.dtype)
        for hl in range(4):
            h = hg * 4 + hl
            kT[hl, 0:64] = kn[(h % 2) * 64:(h % 2) * 64 + 64, h // 2, :]
            kT[hl, 64:96] = kr
            v4[hl] = vf[:, h * 64:(h + 1) * 64].reshape(64, 128, 64).transpose(1, 0, 2)
            q4[hl] = qf[:, h, :]
            s4[hl] = sf[(h % 2) * 64:(h % 2) * 64 + 64, h // 2, :]
        imC.append(dict(kT=kT, v=v4, qT=q4, sg=s4))
    ncC = build_stageC()
    rC = run_bass_kernel_spmd(ncC, imC, core_ids=cores).results
    imD = []
    for c in cores:
        b, j = c // 4, c % 4
        tok = slice(j * NTOK, (j + 1) * NTOK)
        og = np.concatenate([rC[b * 4 + hg]['og'][:, :, tok] for hg in range(4)], axis=1)
        imD.append(dict(og=C(og), h1=rB[c]['h1'], wo=C(inp['mla_w_out'][0]), gf=C(inp['g_final'].reshape(1, 1024))))
    ncD = build_stageD()
    rD = run_bass_kernel_spmd(ncD, imD, core_ids=cores).results
    out = np.stack([np.concatenate([rD[b * 4 + j]['out'] for j in range(4)], axis=0) for b in range(2)], axis=0)
    return out.astype(np.float32)
```

```python
import contextlib
import math
from concourse.bass_utils import run_bass_kernel_spmd
import numpy as np
import concourse.bass as bass
import concourse.mybir as mybir

F32 = mybir.dt.float32
BF16 = mybir.dt.bfloat16
I32 = mybir.dt.int32
AF = mybir.ActivationFunctionType
ALU = mybir.AluOpType
AX = mybir.AxisListType


class Prog:
    def __init__(self, nc):
        self.nc = nc
        self.ops = []
        self.lastw = {}
        self.readers = {}
        self.dma_sems = {}

    def add(self, eng, fn, r=(), w=(), dma=None, group=False):
        deps = set()
        for k in r:
            if k in self.lastw:
                deps.add(self.lastw[k])
            if k[0] == 'B' and k[1:].isdigit():
                for j in self.readers.get(k, ()):
                    if self.ops[j]['eng'] != eng:
                        deps.add(j)
        for k in w:
            if k in self.lastw:
                deps.add(self.lastw[k])
            deps.update(self.readers.get(k, ()))
        i = len(self.ops)
        self.ops.append(dict(eng=eng, fn=fn, deps=deps, dma=dma, group=group, has_dep=False))
        for k in r:
            self.readers.setdefault(k, []).append(i)
        for k in w:
            self.lastw[k] = i
            self.readers[k] = []
        return i

    def pe(self, fn, r=(), w=()):
        return self.add('pe', fn, r, w)

    def act(self, fn, r=(), w=()):
        return self.add('act', fn, r, w)

    def dve(self, fn, r=(), w=()):
        return self.add('dve', fn, r, w)

    def pool(self, fn, r=(), w=()):
        return self.add('pool', fn, r, w)

    def dma(self, eng, fn, r=(), w=(), sem=None, group=False):
        assert sem is not None
        return self.add(eng, fn, r, w, dma=sem, group=group)

    def wait_all(self, eng, keys):
        return self.add(eng, None, r=keys, w=())

    def emit(self):
        nc = self.nc
        ops = self.ops
        engs = ['sp', 'act', 'dve', 'pool', 'pe']
        for o in ops:
            for d in o['deps']:
                if ops[d]['eng'] == 'pe' and o['eng'] == 'pe' and ops[d]['dma'] is None and o['dma'] is None:
                    continue
                ops[d]['has_dep'] = True
        esem = {e: nc.alloc_semaphore(name=f"s_{e}") for e in engs}
        group_tot = {}
        for o in ops:
            if o['dma'] is not None:
                if o['dma'] not in self.dma_sems:
                    self.dma_sems[o['dma']] = nc.alloc_semaphore(name=f"d_{o['dma']}")
                group_tot[o['dma']] = group_tot.get(o['dma'], 0) + 1
        cnt = {e: 0 for e in engs}
        dcnt = {}
        for o in ops:
            if o['fn'] is None:
                o['tok'] = None
            elif o['dma'] is not None:
                k = o['dma']
                dcnt[k] = dcnt.get(k, 0) + 1
                v = group_tot[k] if o['group'] else dcnt[k]
                o['tok'] = (('d', k), 16 * v)
            elif o['has_dep']:
                cnt[o['eng']] += 1
                o['tok'] = (('e', o['eng']), cnt[o['eng']])
            else:
                o['tok'] = None
        known = {e: {} for e in engs}
        for o in ops:
            e = o['eng']
            kn = known[e]
            waits = []
            for d in sorted(o['deps'], reverse=True):
                od = ops[d]
                if od['tok'] is None:
                    continue
                if od['eng'] == 'pe' and e == 'pe' and od['dma'] is None and o['dma'] is None:
                    continue
                s, v = od['tok']
                if kn.get(s, 0) < v:
                    waits.append((s, v))
                    kn[s] = v
                    for s2, v2 in od['clock'].items():
                        if kn.get(s2, 0) < v2:
                            kn[s2] = v2
            wm = {}
            for s, v in waits:
                wm[s] = max(wm.get(s, 0), v)
            o['waits'] = wm
            o['clock'] = dict(kn)

        def semof(s):
            return esem[s[1]] if s[0] == 'e' else self.dma_sems[s[1]]

        def run(ename, eng):
            for o in ops:
                if o['eng'] != ename:
                    continue
                for s, v in o['waits'].items():
                    eng.wait_ge(semof(s), v)
                if o['fn'] is None:
                    continue
                inst = o['fn'](eng)
                if o['tok'] is not None:
                    s, v = o['tok']
                    inst.then_inc(semof(s), 16 if s[0] == 'd' else 1)

        with nc.Block() as block:
            @block.sync
            def _(e):
                run('sp', e)

            @block.scalar
            def _(e):
                run('act', e)

            @block.vector
            def _(e):
                run('dve', e)

            @block.gpsimd
            def _(e):
                run('pool', e)

            @block.tensor
            def _(e):
                run('pe', e)
        n = {e: sum(1 for o in ops if o['eng'] == e) for e in engs}
        nw = sum(len(o['waits']) for o in ops)
        print("PROG ops", n, "waits", nw, "sems", 5 + len(self.dma_sems), flush=True)


def _kw(**k):
    return {a: b for a, b in k.items() if b is not None}


class P2(Prog):
    def mm(self, out, lhsT, rhs, start=True, stop=True, r=(), w=()):
        return self.add('pe', lambda e: e.matmul(out, lhsT=lhsT, rhs=rhs, start=start, stop=stop), r, w)

    def tr(self, out, in_, ident, r=(), w=()):
        return self.add('pe', lambda e: e.transpose(out, in_, ident), r, w)

    def actv(self, out, in_, func, bias=None, scale=None, accum=None, r=(), w=()):
        kw = _kw(bias=bias, scale=scale, accum_out=accum)
        return self.add('act', lambda e: e.activation(out=out, in_=in_, func=func, **kw), r, w)

    def ts(self, eng, out, in0, s1, s2=None, op0=ALU.mult, op1=None, r=(), w=()):
        kw = _kw(op1=op1)
        return self.add(eng, lambda e: e.tensor_scalar(out=out, in0=in0, scalar1=s1, scalar2=s2, op0=op0, **kw), r, w)

    def tt(self, eng, out, in0, in1, op, r=(), w=()):
        return self.add(eng, lambda e: e.tensor_tensor(out=out, in0=in0, in1=in1, op=op), r, w)

    def stt(self, out, in0, scalar, in1, op0, op1, r=(), w=()):
        return self.add('dve', lambda e: e.scalar_tensor_tensor(out=out, in0=in0, scalar=scalar, in1=in1, op0=op0, op1=op1), r, w)

    def cp(self, eng, out, in_, r=(), w=()):
        if eng == 'act':
            return self.add('act', lambda e: e.activation(out=out, in_=in_, func=AF.Copy), r, w)
        return self.add(eng, lambda e: e.tensor_copy(out=out, in_=in_), r, w)

    def ms(self, eng, ap, val, w=()):
        return self.add(eng, lambda e: e.memset(ap, val), (), w)

    def ld(self, out, in_, w, sem, eng='sp', group=False, r=()):
        return self.dma(eng, lambda e: e.dma_start(out=out, in_=in_), r=r, w=w, sem=sem, group=group)

SEQ = 8192
DM = 1024
NCH = SEQ // 256
EPS = 1e-6
WCOLS = 1288


def build_stageA(nch=NCH):
    nc = bass.Bass("TRN2", target_bir_lowering=False)
    x_d = nc.dram_tensor("x", [SEQ, DM], F32, kind="ExternalInput").ap()
    w_d = nc.dram_tensor("w", [DM, WCOLS], F32, kind="ExternalInput").ap()
    gpre_d = nc.dram_tensor("gpre", [128, 8], F32, kind="ExternalInput").ap()
    cw_d = nc.dram_tensor("cw", [128, 24], F32, kind="ExternalInput").ap()
    cb_d = nc.dram_tensor("cb", [128, 6], F32, kind="ExternalInput").ap()
    dtb_d = nc.dram_tensor("dtb", [1, 8], F32, kind="ExternalInput").ap()
    alog_d = nc.dram_tensor("alog", [1, 8], F32, kind="ExternalInput").ap()
    dsk_d = nc.dram_tensor("dsk", [1, 8], F32, kind="ExternalInput").ap()
    gout_d = nc.dram_tensor("gout", [1, 512], F32, kind="ExternalInput").ap()
    yn_d = nc.dram_tensor("yn", [SEQ, 512], BF16, kind="ExternalOutput").ap()

    P = P2(nc)
    es = contextlib.ExitStack()

    def S(name, shape, dt):
        return es.enter_context(nc.sbuf_tensor(name, shape, dt))

    banks = [es.enter_context(nc.psum_tensor(f"bank{i}", [128, 512], F32)) for i in range(8)]

    W = S("W", [128, 8, WCOLS], BF16)
    wst = [S(f"wst{i}", [128, WCOLS], F32) for i in range(2)]
    gpre = S("gpre_s", [128, 8], F32)
    cw = S("cw_s", [128, 24], F32)
    cb = S("cb_s", [128, 6], F32)
    dtb_bc = S("dtb_bc", [128, 8], F32)
    A_bc = S("A_bc", [128, 8], F32)
    D_bc = S("D_bc", [128, 8], F32)
    gout_bc = S("gout_bc", [128, 512], F32)
    identf = S("identf", [128, 128], F32)
    identb = S("identb", [128, 128], BF16)
    onesf = S("onesf", [128, 128], F32)
    onesb = S("onesb", [128, 128], BF16)
    trif = S("trif", [128, 128], F32)
    triw = S("triw", [128, 256], BF16)
    SU = S("SU", [128, 128], BF16)
    cdiag = S("cdiag", [128, 24, 128], BF16)
    Dident = S("Dident", [128, 8, 128], BF16)
    xin = [S(f"xin{i}", [128, 2, DM], F32) for i in range(2)]
    junk = [S(f"junk{i}", [128, DM], BF16) for i in range(2)]
    ss = S("ss", [128, 2], F32)
    rt = S("rt", [128, 2], F32)
    rstd = S("rstd", [128, 2], F32)
    hn = S("hn", [128, 2, DM], BF16)
    hnT = S("hnT", [128, 8, 256], BF16)
    ubuf = S("ubuf", [128, 6, 259], BF16)
    xc = S("xc", [128, 6, 256], BF16)
    xtok = S("xtok", [128, 2, 640], BF16)
    dtr = S("dtr", [128, 2, 8], F32)
    e1 = S("e1", [128, 2, 8], F32)
    dtk = S("dtk", [128, 2, 8], F32)
    dtA = S("dtA", [128, 2, 8], F32)
    cend = S("cend", [128, 8], F32)
    ecum = S("ecum", [128, 2, 8], F32)
    wtmp = S("wtmp", [128, 2, 8], F32)
    dec = S("dec", [128, 8], F32)
    W0 = S("W0", [128, 8, 256], BF16)
    V1 = S("V1", [128, 8, 128], BF16)
    CBm = S("CBm", [128, 384], BF16)
    xdt = S("xdt", [128, 2, 512], BF16)
    Lb = [S(f"Lb{i}", [128, 384], BF16) for i in range(2)]
    MT = S("MT", [128, 8, 384], BF16)
    state = S("state", [128, 512], F32)
    state_bf = S("state_bf", [128, 512], BF16)
    yi = S("yi", [128, 512], F32)
    t1 = S("t1", [128, 512], F32)
    ysb = S("ysb", [128, 512], F32)
    zs = S("zs", [128, 512], F32)
    yg = S("yg", [128, 512], F32)
    junk2 = S("junk2", [128, 512], BF16)
    ss2 = S("ss2", [128, 1], F32)
    rt2 = S("rt2", [128, 1], F32)
    rstd2 = S("rstd2", [128, 1], F32)
    yn = [S(f"yn{i}", [128, 512], BF16) for i in range(2)]
    wx = S("wx", [128, 2, 512], BF16)

    def bfview(bank):
        return bank[:].bitcast(BF16)

    ptr = [bfview(banks[t]).rearrange("p (k t) -> p k t", k=8) for t in range(2)]
    ptx = bfview(banks[0])[:, 0:640]
    pCB = banks[1][:, 0:384]
    pseg = [banks[2][:, 0:384], banks[3][:, 0:384]]
    pdtk = banks[6][:, 0:16].rearrange("p (t c) -> p t c", t=2)
    pcum = banks[6][:, 16:32].rearrange("p (t c) -> p t c", t=2)
    pce = banks[6][:, 32:40]
    pyi = banks[6][:, :]
    pst = banks[6][:, :]
    pz = banks[7][:, :]
    py = [banks[4][:, :], banks[5][:, :]]

    P.ld(gpre[:], gpre_d, ['gpre'], 'c0')
    P.ld(cw[:], cw_d, ['cw'], 'c1')
    P.ld(cb[:], cb_d, ['cb'], 'c2')
    P.ld(dtb_bc[:], dtb_d.partition_broadcast(128), ['dtb_bc'], 'c3')
    P.ld(A_bc[:], alog_d.partition_broadcast(128), ['A_bc'], 'c4')
    P.ld(D_bc[:], dsk_d.partition_broadcast(128), ['D_bc'], 'c5')
    P.ld(gout_bc[:], gout_d.partition_broadcast(128), ['gout_bc'], 'c6')
    P.ms('pool', identf[:], 1.0, ['identf'])
    P.add('pool', lambda e: e.affine_select(out=identf[:], in_=identf[:], pattern=[[-1, 128]], compare_op=ALU.is_equal,
                                            fill=0.0, base=0, channel_multiplier=1), r=['identf'], w=['identf'])
    P.cp('dve', identb[:], identf[:], r=['identf'], w=['identb'])
    P.ms('pool', onesf[:], 1.0, ['onesf'])
    P.ms('pool', onesb[:], 1.0, ['onesb'])
    P.ms('pool', triw[:], 1.0, ['triw'])
    P.add('pool', lambda e: e.affine_select(out=triw[:, 0:128], in_=triw[:, 0:128], pattern=[[1, 128]], compare_op=ALU.is_ge,
                                            fill=0.0, base=0, channel_multiplier=-1), r=['triw'], w=['triw'])
    P.cp('dve', trif[:], triw[:, 0:128], r=['triw'], w=['trif'])
    P.ms('pool', SU[:], 1.0, ['SU'])
    P.add('pool', lambda e: e.affine_select(out=SU[:], in_=SU[:], pattern=[[-1, 128]], compare_op=ALU.is_gt,
                                            fill=0.0, base=0, channel_multiplier=1), r=['SU'], w=['SU'])
    P.ms('pool', ubuf[:], 0.0, ['ubuf%d' % i for i in range(3)])
    P.ms('pool', state[:], 0.0, ['state'])
    P.ms('pool', state_bf[:], 0.0, ['state_bf'])
    for kt in range(8):
        P.ld(wst[kt % 2][:], w_d[kt * 128:(kt + 1) * 128, :], [f'wst{kt % 2}'], f'wst{kt % 2}')
        P.ts('dve' if kt % 2 == 0 else 'pool', W[:, kt, :], wst[kt % 2][:], gpre[:, kt:kt + 1], None, ALU.mult,
             r=[f'wst{kt % 2}', 'gpre'], w=[f'W{kt}'])
    Wk = [f'W{kt}' for kt in range(8)]
    HNT = ['hnT0', 'hnT1']
    for i in range(24):
        P.ts('dve', cdiag[:, i, :], identf[:], cw[:, i:i + 1], None, ALU.mult, r=['identf', 'cw'], w=['cdiag'])
    for h in range(8):
        P.ts('dve', Dident[:, h, :], identf[:], D_bc[:, h:h + 1], None, ALU.mult, r=['identf', 'D_bc'], w=['Dident'])
    P.actv(A_bc[:], A_bc[:], AF.Exp, r=['A_bc'], w=['A_bc'])
    P.ts('dve', A_bc[:], A_bc[:], -1.0, None, ALU.mult, r=['A_bc'], w=['A_bc'])

    def load_x(c):
        sl = c % 2
        P.ld(xin[sl][:], x_d[c * 256:(c + 1) * 256, :].rearrange("(t p) d -> p t d", p=128), [f'xin{sl}'], f'xin{sl}')

    load_x(0)
    for c in range(nch):
        sl = c % 2
        if c + 1 < nch:
            load_x(c + 1)
        xk = f'xin{sl}'
        for t in range(2):
            P.actv(junk[t][:], xin[sl][:, t, :], AF.Square, accum=ss[:, t:t + 1], r=[xk], w=[f'ss{t}', f'junk{t}'])
        P.actv(rt[:], ss[:], AF.Sqrt, bias=EPS, scale=1.0 / DM, r=['ss0', 'ss1'], w=['rt'])
        P.add('dve', lambda e: e.reciprocal(out=rstd[:], in_=rt[:]), r=['rt'], w=['rstd'])
        for t in range(2):
            P.ts('dve', hn[:, t, :], xin[sl][:, t, :], rstd[:, t:t + 1], None, ALU.mult, r=[xk, 'rstd'], w=[f'hn{t}'])
        for t in range(2):
            for kt in range(8):
                P.tr(ptr[t][:, kt, :], hn[:, t, kt * 128:(kt + 1) * 128], identb[:], r=[f'hn{t}', 'identb'], w=[f'B{t}'])
            P.cp('dve' if t == 0 else 'act', hnT[:, :, t * 128:(t + 1) * 128], ptr[t], r=[f'B{t}'], w=[f'hnT{t}'])
        for pr in range(3):
            bx = banks[2 + pr % 2]
            bxk = f'B{2 + pr % 2}'
            for j in range(2):
                ct = 2 * pr + j
                for kt in range(8):
                    P.mm(bx[:, j * 256:(j + 1) * 256], W[:, kt, ct * 128:(ct + 1) * 128], hnT[:, kt, :], start=(kt == 0), stop=(kt == 7),
                         r=HNT + [Wk[kt]], w=[bxk])
            P.cp('act', ubuf[:, 2 * pr:2 * pr + 2, 3:259], bx[:, :].rearrange("p (j t) -> p j t", j=2), r=[bxk], w=[f'ubuf{pr}'])
            bc = banks[4 + pr % 2]
            bck = f'B{4 + pr % 2}'
            for j in range(2):
                ct = 2 * pr + j
                for k in range(4):
                    P.mm(bc[:, j * 256:(j + 1) * 256], cdiag[:, ct * 4 + k, :], ubuf[:, ct, k:k + 256], start=(k == 0), stop=(k == 3),
                         r=['cdiag', f'ubuf{pr}'], w=[bck])
            for j in range(2):
                ct = 2 * pr + j
                P.actv(xc[:, ct, :], bc[:, j * 256:(j + 1) * 256], AF.Silu, bias=cb[:, ct:ct + 1], r=[bck, 'cb'], w=[f'xc{ct}'])
            P.cp('pool', ubuf[:, 2 * pr:2 * pr + 2, 0:3], ubuf[:, 2 * pr:2 * pr + 2, 256:259], r=[f'ubuf{pr}'], w=[f'ubuf{pr}'])
        for t in range(2):
            for kt in range(8):
                P.mm(pdtk[:, t, :], hnT[:, kt, t * 128:(t + 1) * 128], W[:, kt, 768:776], start=(kt == 0), stop=(kt == 7),
                     r=HNT + [Wk[kt]], w=['B6'])
        P.tt('dve', dtr[:], pdtk, dtb_bc[:].unsqueeze(1).to_broadcast([128, 2, 8]), ALU.add, r=['B6', 'dtb_bc'], w=['dtr'])
        P.actv(e1[:], dtr[:], AF.Exp, r=['dtr'], w=['e1'])
        P.actv(dtk[:], e1[:], AF.Ln, bias=1.0, r=['e1'], w=['dtk'])
        P.tt('dve', dtA[:], dtk[:], A_bc[:].unsqueeze(1).to_broadcast([128, 2, 8]), ALU.mult, r=['dtk', 'A_bc'], w=['dtA'])
        P.mm(pcum[:, 0, :], trif[:], dtA[:, 0, :], r=['trif', 'dtA'], w=['B6'])
        P.mm(pcum[:, 1, :], onesf[:], dtA[:, 0, :], start=True, stop=False, r=['onesf', 'dtA'], w=['B6'])
        P.mm(pcum[:, 1, :], trif[:], dtA[:, 1, :], start=False, stop=True, r=['trif', 'dtA'], w=['B6'])
        P.mm(pce, onesf[:], dtA[:, 0, :], start=True, stop=False, r=['onesf', 'dtA'], w=['B6'])
        P.mm(pce, onesf[:], dtA[:, 1, :], start=False, stop=True, r=['onesf', 'dtA'], w=['B6'])
        P.actv(ecum[:], pcum, AF.Exp, r=['B6'], w=['ecum'])
        P.actv(dec[:], pce, AF.Exp, r=['B6'], w=['dec'])
        P.cp('act', cend[:], pce, r=['B6'], w=['cend'])
        P.tt('dve', wtmp[:], cend[:].unsqueeze(1).to_broadcast([128, 2, 8]), pcum, ALU.subtract, r=['cend', 'B6'], w=['wtmp'])
        P.actv(wtmp[:], wtmp[:], AF.Exp, r=['wtmp'], w=['wtmp'])
        for t in range(2):
            for ct in range(5):
                P.tr(ptx[:, ct * 128:(ct + 1) * 128], xc[:, ct, t * 128:(t + 1) * 128], identb[:],
                     r=[f'xc{ct}', 'identb'], w=['B0'])
            P.cp('dve' if t == 0 else 'act', xtok[:, t, :], ptx, r=['B0'], w=[f'xtok{t}'])
            P.tt('pool', xdt[:, t, :].rearrange("p (h c) -> p h c", h=8), xtok[:, t, 0:512].rearrange("p (h c) -> p h c", h=8),
                 dtk[:, t, :].unsqueeze(2).to_broadcast([128, 8, 64]), ALU.mult, r=[f'xtok{t}', 'dtk'], w=[f'xdt{t}'])
        P.mm(pCB[:, 0:256], xc[:, 4, 0:128], xc[:, 5, 0:256], r=['xc4', 'xc5'], w=['B1'])
        P.mm(pCB[:, 256:384], xc[:, 4, 128:256], xc[:, 5, 128:256], r=['xc4', 'xc5'], w=['B1'])
        P.cp('act', CBm[:], pCB, r=['B1'], w=['CBm'])
        for off in (0, 256):
            blk = CBm[:, off:off + 128]
            P.add('pool', (lambda blk: (lambda e: e.affine_select(out=blk, in_=blk, pattern=[[1, 128]], compare_op=ALU.is_ge,
                                                                  fill=0.0, base=0, channel_multiplier=-1)))(blk),
                  r=['CBm'], w=['CBm'])
        P.tt('dve', W0[:], triw[:].unsqueeze(1).to_broadcast([128, 8, 256]), dtA[:, 0, :].unsqueeze(2).to_broadcast([128, 8, 256]),
             ALU.mult, r=['triw', 'dtA'], w=['W0'])
        P.tt('dve', V1[:], triw[:, 0:128].unsqueeze(1).to_broadcast([128, 8, 128]), dtA[:, 1, :].unsqueeze(2).to_broadcast([128, 8, 128]),
             ALU.mult, r=['triw', 'dtA'], w=['V1'])
        for h in range(8):
            ps = pseg[h % 2]
            psk = f'B{2 + h % 2}'
            L = Lb[h % 2]
            Lk = f'Lb{h % 2}'
            P.mm(ps[:, 0:128], SU[:], W0[:, h, 0:128], start=True, stop=True, r=['SU', 'W0'], w=[psk])
            P.mm(ps[:, 128:256], SU[:], W0[:, h, 128:256], start=True, stop=False, r=['SU', 'W0'], w=[psk])
            P.mm(ps[:, 128:256], onesb[:], V1[:, h, :], start=False, stop=True, r=['onesb', 'V1'], w=[psk])
            P.mm(ps[:, 256:384], SU[:], V1[:, h, :], start=True, stop=True, r=['SU', 'V1'], w=[psk])
            P.actv(L[:], ps, AF.Exp, r=[psk], w=[Lk])
            P.tt('dve', MT[:, h, :], L[:], CBm[:], ALU.mult, r=[Lk, 'CBm'], w=[f'MT{h}'])
        for t in range(2):
            for h in range(8):
                hc = slice(h * 64, (h + 1) * 64)
                P.mm(py[t][:, hc], MT[:, h, t * 128:(t + 1) * 128], xdt[:, 0, hc], start=True, stop=False,
                     r=[f'MT{h}', 'xdt0'], w=[f'B{4 + t}'])
                if t == 1:
                    P.mm(py[t][:, hc], MT[:, h, 256:384], xdt[:, 1, hc], start=False, stop=False,
                         r=[f'MT{h}', 'xdt1'], w=[f'B{4 + t}'])
                P.mm(py[t][:, hc], Dident[:, h, :], xtok[:, t, hc], start=False, stop=True,
                     r=['Dident', f'xtok{t}'], w=[f'B{4 + t}'])
            P.mm(pyi, xc[:, 5, t * 128:(t + 1) * 128], state_bf[:], r=['xc5', 'state_bf'], w=['B6'])
            P.cp('act', yi[:], pyi, r=['B6'], w=['yi'])
            P.tt('pool', t1[:].rearrange("p (h c) -> p h c", h=8), yi[:].rearrange("p (h c) -> p h c", h=8),
                 ecum[:, t, :].unsqueeze(2).to_broadcast([128, 8, 64]), ALU.mult, r=['yi', 'ecum'], w=['t1'])
            P.tt('dve', ysb[:], t1[:], py[t], ALU.add, r=['t1', f'B{4 + t}'], w=['ysb'])
            for kt in range(8):
                P.mm(pz, hnT[:, kt, t * 128:(t + 1) * 128], W[:, kt, 776:1288], start=(kt == 0), stop=(kt == 7),
                     r=HNT + [Wk[kt]], w=['B7'])
            P.actv(zs[:], pz, AF.Silu, r=['B7'], w=['zs'])
            P.tt('pool', yg[:], ysb[:], zs[:], ALU.mult, r=['ysb', 'zs'], w=['yg'])
            P.actv(junk2[:], yg[:], AF.Square, accum=ss2[:], r=['yg'], w=['ss2', 'junk2'])
            P.actv(rt2[:], ss2[:], AF.Sqrt, bias=EPS, scale=1.0 / 512, r=['ss2'], w=['rt2'])
            P.add('dve', lambda e: e.reciprocal(out=rstd2[:], in_=rt2[:]), r=['rt2'], w=['rstd2'])
            P.stt(yn[t][:], yg[:], rstd2[:, 0:1], gout_bc[:], ALU.mult, ALU.mult, r=['yg', 'rstd2', 'gout_bc'], w=[f'yn{t}'])
            P.ld(yn_d[c * 256 + t * 128: c * 256 + (t + 1) * 128, :], yn[t][:], w=[f'ynd{t}'], sem=f'st{t}', r=[f'yn{t}'])
        for st in range(2):
            P.tt('pool', wx[:, st, :].rearrange("p (h c) -> p h c", h=8), xdt[:, st, :].rearrange("p (h c) -> p h c", h=8),
                 wtmp[:, st, :].unsqueeze(2).to_broadcast([128, 8, 64]), ALU.mult, r=[f'xdt{st}', 'wtmp'], w=[f'wx{st}'])
        for st in range(2):
            P.mm(pst, xtok[:, st, 512:640], wx[:, st, :], start=(st == 0), stop=(st == 1), r=[f'xtok{st}', f'wx{st}'], w=['B6'])
        P.tt('dve', state[:].rearrange("p (h c) -> p h c", h=8), state[:].rearrange("p (h c) -> p h c", h=8),
             dec[:].unsqueeze(2).to_broadcast([128, 8, 64]), ALU.mult, r=['state', 'dec'], w=['state'])
        P.tt('dve', state[:], state[:], pst, ALU.add, r=['state', 'B6'], w=['state'])
        P.cp('pool', state_bf[:], state[:], r=['state'], w=['state_bf'])
    P.wait_all('sp', ['ynd0', 'ynd1'])
    P.emit()
    es.close()
    return nc

NTOK = 2048
NTT = NTOK // 128
INV_FREQ = [float(np.float32(10000.0) ** np.float32(-(2 * i) / 32.0)) for i in range(16)]
TWO_PI = 2.0 * math.pi
CW1 = 6.28125
CW2 = TWO_PI - CW1


def build_stageB(ntt=NTT):
    nc = bass.Bass("TRN2", target_bir_lowering=False)
    x_d = nc.dram_tensor("x", [NTOK, 1024], F32, kind="ExternalInput").ap()
    yn_d = nc.dram_tensor("yn", [NTOK, 2048], BF16, kind="ExternalInput").ap()
    pos_d = nc.dram_tensor("pos", [128, NTT], I32, kind="ExternalInput").ap()
    invf_d = nc.dram_tensor("invf", [1, 16], F32, kind="ExternalInput").ap()
    wout_d = nc.dram_tensor("wout", [2048, 1024], F32, kind="ExternalInput").ap()
    wdn_d = nc.dram_tensor("wdn", [1024, 288], F32, kind="ExternalInput").ap()
    wup_d = nc.dram_tensor("wup", [256, 2048], F32, kind="ExternalInput").ap()
    win_d = nc.dram_tensor("win", [1024, 1408], F32, kind="ExternalInput").ap()
    wuq_d = nc.dram_tensor("wuq", [384, 1536], F32, kind="ExternalInput").ap()
    gkv_d = nc.dram_tensor("gkv", [128, 8], F32, kind="ExternalInput").ap()
    gpre_d = nc.dram_tensor("gpre", [128, 8], F32, kind="ExternalInput").ap()
    glat_d = nc.dram_tensor("glat", [128, 2], F32, kind="ExternalInput").ap()
    gq_d = nc.dram_tensor("gq", [128, 3], F32, kind="ExternalInput").ap()
    h1_d = nc.dram_tensor("h1", [NTOK, 1024], F32, kind="ExternalOutput").ap()
    sg_d = nc.dram_tensor("sg", [128, 8, NTOK], BF16, kind="ExternalOutput").ap()
    kn_d = nc.dram_tensor("kn", [128, 8, NTOK], BF16, kind="ExternalOutput").ap()
    kr_d = nc.dram_tensor("kr", [32, NTOK], BF16, kind="ExternalOutput").ap()
    v_d = nc.dram_tensor("v", [NTOK, 1024], BF16, kind="ExternalOutput").ap()
    qT_d = nc.dram_tensor("qT", [96, 16, NTOK], BF16, kind="ExternalOutput").ap()

    P = P2(nc)
    es = contextlib.ExitStack()

    def S(name, shape, dt):
        return es.enter_context(nc.sbuf_tensor(name, shape, dt))

    banks = [es.enter_context(nc.psum_tensor(f"bank{i}", [128, 512], F32)) for i in range(8)]
    bctr = [0]

    def nb():
        i = bctr[0] % 8
        bctr[0] += 1
        return banks[i], f'B{i}'

    def bfv(bank):
        return bank[:].bitcast(BF16)

    wout = S("wout_s", [128, 16, 1024], BF16)
    wdn = S("wdn_s", [128, 8, 288], BF16)
    wkn = S("wkn_s", [128, 2, 1024], BF16)
    wv = S("wv_s", [128, 2, 1024], BF16)
    win = S("win_s", [128, 8, 1408], BF16)
    wuq = S("wuq_s", [128, 3, 1536], BF16)
    wst = [S(f"wst{i}", [128, 2048], F32) for i in range(2)]
    gkv = S("gkv_s", [128, 8], F32)
    gpre = S("gpre_s", [128, 8], F32)
    glat = S("glat_s", [128, 2], F32)
    gq = S("gq_s", [128, 3], F32)
    identf = S("identf", [128, 128], F32)
    identb = S("identb", [128, 128], BF16)
    posi = S("posi", [128, NTT], I32)
    posf = S("posf", [128, NTT], F32)
    invf = S("invf_s", [128, 16], F32)
    ang = S("ang", [128, NTT, 16], F32)
    uu = S("uu", [128, NTT, 16], F32)
    ki = S("ki", [128, NTT, 16], I32)
    kf = S("kf", [128, NTT, 16], F32)
    gg = S("gg", [128, NTT, 16], F32)
    m1 = S("m1", [128, NTT, 16], F32)
    gc = S("gc", [128, NTT, 16], F32)
    sinT = S("sinT", [128, NTT, 16], F32)
    cosT = S("cosT", [128, NTT, 16], F32)
    xin = [S(f"xin{i}", [128, 1024], F32) for i in range(2)]
    ynin = [S(f"ynin{i}", [128, 2048], BF16) for i in range(2)]
    ynT = S("ynT", [128, 16, 128], BF16)
    h1 = [S(f"h1_{i}", [128, 1024], F32) for i in range(2)]
    junk = S("junk", [128, 1024], BF16)
    ss = S("ss", [128, 1], F32)
    rt = S("rt", [128, 1], F32)
    rstd = S("rstd", [128, 1], F32)
    hnb = S("hnb", [128, 1024], BF16)
    hT = S("hT", [128, 8, 128], BF16)
    junk2 = S("junk2", [128, 384], BF16)
    ssc = S("ssc", [128, 1], F32)
    rtc = S("rtc", [128, 1], F32)
    rstdc = S("rstdc", [128, 1], F32)
    ckvn = S("ckvn", [128, 256], BF16)
    ra = S("ra", [128, 16], F32)
    rb = S("rb", [128, 16], F32)
    krb = S("krb", [128, 32], BF16)
    ckT = S("ckT", [128, 2, 128], BF16)
    krT = [S(f"krT{i}", [32, 128], BF16) for i in range(2)]
    knT = [S(f"knT{i}", [128, 8, 128], BF16) for i in range(2)]
    vsb = [S(f"vsb{i}", [128, 1024], BF16) for i in range(2)]
    ssq = S("ssq", [128, 1], F32)
    rtq = S("rtq", [128, 1], F32)
    rstdq = S("rstdq", [128, 1], F32)
    cqn = S("cqn", [128, 384], BF16)
    sg = [S(f"sg{i}", [128, 8, 128], BF16) for i in range(2)]
    cqT = S("cqT", [128, 3, 128], BF16)
    qtok = S("qtok", [128, 16, 96], BF16)
    qa = S("qa", [128, 16, 16], F32)
    qb = S("qb", [128, 16, 16], F32)
    qT = [S(f"qT{i}", [96, 16, 128], BF16) for i in range(2)]

    P.ld(gkv[:], gkv_d, ['gkv'], 'c0')
    P.ld(gpre[:], gpre_d, ['gpre'], 'c1')
    P.ld(glat[:], glat_d, ['glat'], 'c2')
    P.ld(gq[:], gq_d, ['gq'], 'c3')
    P.ld(posi[:], pos_d, ['posi'], 'c4')
    P.ld(invf[:], invf_d.partition_broadcast(128), ['invf'], 'c5')
    P.ms('pool', identf[:], 1.0, ['identf'])
    P.add('pool', lambda e: e.affine_select(out=identf[:], in_=identf[:], pattern=[[-1, 128]], compare_op=ALU.is_equal,
                                            fill=0.0, base=0, channel_multiplier=1), r=['identf'], w=['identf'])
    P.cp('dve', identb[:], identf[:], r=['identf'], w=['identb'])
    P.cp('dve', posf[:], posi[:], r=['posi'], w=['posf'])
    P.tt('dve', ang[:], posf[:].unsqueeze(2).to_broadcast([128, NTT, 16]), invf[:].unsqueeze(1).to_broadcast([128, NTT, 16]),
         ALU.mult, r=['posf', 'invf'], w=['ang'])
    P.ts('dve', uu[:], ang[:], 1.0 / TWO_PI, None, ALU.mult, r=['ang'], w=['uu'])
    P.cp('dve', ki[:], uu[:], r=['uu'], w=['ki'])
    P.cp('dve', kf[:], ki[:], r=['ki'], w=['kf'])
    P.stt(gg[:], kf[:], -CW1, ang[:], ALU.mult, ALU.add, r=['kf', 'ang'], w=['gg'])
    P.stt(gg[:], kf[:], -CW2, gg[:], ALU.mult, ALU.add, r=['kf', 'gg'], w=['gg'])
    P.ts('dve', gg[:], gg[:], 1.0 / TWO_PI, None, ALU.mult, r=['gg'], w=['gg'])

    def wrap():
        P.ts('dve', m1[:], gg[:], 0.5, None, ALU.is_gt, r=['gg'], w=['m1'])
        P.tt('dve', gg[:], gg[:], m1[:], ALU.subtract, r=['gg', 'm1'], w=['gg'])
        P.ts('dve', m1[:], gg[:], -0.5, None, ALU.is_lt, r=['gg'], w=['m1'])
        P.tt('dve', gg[:], gg[:], m1[:], ALU.add, r=['gg', 'm1'], w=['gg'])
        P.ts('dve', gg[:], gg[:], 0.4999995, -0.4999995, ALU.min, ALU.max, r=['gg'], w=['gg'])

    wrap()
    P.actv(sinT[:], gg[:], AF.Sin, scale=TWO_PI, r=['gg'], w=['sinT'])
    P.ts('dve', gg[:], gg[:], 0.25, None, ALU.add, r=['gg'], w=['gg'])
    wrap()
    P.actv(cosT[:], gg[:], AF.Sin, scale=TWO_PI, r=['gg'], w=['cosT'])

    wi = [0]

    def wload(dst_ap, src_ap, ncols, gain_ap, in_view=None):
        i = wi[0] % 2
        wi[0] += 1
        P.ld(wst[i][:, 0:ncols], src_ap, [f'wst{i}'], f'wst{i}')
        src = wst[i][:, 0:ncols] if in_view is None else in_view(wst[i])
        eng = 'dve' if i == 0 else 'pool'
        if gain_ap is None:
            P.cp(eng, dst_ap, src, r=[f'wst{i}'], w=[f'W{wi[0]}'])
        else:
            P.ts(eng, dst_ap, src, gain_ap, None, ALU.mult, r=[f'wst{i}', 'gkv', 'gpre', 'glat', 'gq'], w=[f'W{wi[0]}'])

    for kt in range(16):
        wload(wout[:, kt, :], wout_d[kt * 128:(kt + 1) * 128, :], 1024, None)
    for kt in range(8):
        wload(wdn[:, kt, :], wdn_d[kt * 128:(kt + 1) * 128, :], 288, gkv[:, kt:kt + 1])
    for kt in range(8):
        wload(win[:, kt, :], win_d[kt * 128:(kt + 1) * 128, :], 1408, gpre[:, kt:kt + 1])
    for kt in range(2):
        wload(wkn[:, kt, :].rearrange("p (h c) -> p h c", h=16), wup_d[kt * 128:(kt + 1) * 128, :], 2048, glat[:, kt:kt + 1],
              in_view=lambda t: t[:, 0:2048].rearrange("p (h c) -> p h c", h=16)[:, :, 0:64])
        wload(wv[:, kt, :].rearrange("p (h c) -> p h c", h=16), wup_d[kt * 128:(kt + 1) * 128, :], 2048, glat[:, kt:kt + 1],
              in_view=lambda t: t[:, 0:2048].rearrange("p (h c) -> p h c", h=16)[:, :, 64:128])
    for kt in range(3):
        wload(wuq[:, kt, 0:1024].rearrange("p (h c) -> p h c", h=16), wuq_d[kt * 128:(kt + 1) * 128, :], 1536, gq[:, kt:kt + 1],
              in_view=lambda t: t[:, 0:1536].rearrange("p (h c) -> p h c", h=16)[:, :, 0:64])
        wload(wuq[:, kt, 1024:1280].rearrange("p (h c) -> p h c", h=16), wuq_d[kt * 128:(kt + 1) * 128, :], 1536, gq[:, kt:kt + 1],
              in_view=lambda t: t[:, 0:1536].rearrange("p (h c) -> p h c", h=16)[:, :, 64:80])
        wload(wuq[:, kt, 1280:1536].rearrange("p (h c) -> p h c", h=16), wuq_d[kt * 128:(kt + 1) * 128, :], 1536, gq[:, kt:kt + 1],
              in_view=lambda t: t[:, 0:1536].rearrange("p (h c) -> p h c", h=16)[:, :, 80:96])

    ALLW = [f'W{i}' for i in range(1, wi[0] + 1)]

    def load_t(tt):
        sl = tt % 2
        P.ld(xin[sl][:], x_d[tt * 128:(tt + 1) * 128, :], [f'xin{sl}'], f'xin{sl}')
        P.ld(ynin[sl][:], yn_d[tt * 128:(tt + 1) * 128, :], [f'ynin{sl}'], f'ynin{sl}')

    load_t(0)
    for tt in range(ntt):
        sl = tt % 2
        tok = slice(tt * 128, (tt + 1) * 128)
        if tt + 1 < ntt:
            load_t(tt + 1)
        for half in range(2):
            bk, bkk = nb()
            pv = bfv(bk).rearrange("p (k t) -> p k t", k=8)
            for j in range(8):
                c = half * 8 + j
                P.tr(pv[:, j, :], ynin[sl][:, c * 128:(c + 1) * 128], identb[:], r=[f'ynin{sl}', 'identb'], w=[bkk])
            P.cp('dve' if half == 0 else 'act', ynT[:, half * 8:(half + 1) * 8, :], pv, r=[bkk], w=[f'ynT{half}'])
        for half in range(2):
            bk, bkk = nb()
            for c in range(16):
                P.mm(bk[:, :], ynT[:, c, :], wout[:, c, half * 512:(half + 1) * 512], start=(c == 0), stop=(c == 15),
                     r=['ynT0', 'ynT1', *ALLW], w=[bkk])
            P.tt('dve', h1[sl][:, half * 512:(half + 1) * 512], bk[:, :], xin[sl][:, half * 512:(half + 1) * 512], ALU.add,
                 r=[bkk, f'xin{sl}'], w=[f'h1_{sl}'])
        P.ld(h1_d[tok, :], h1[sl][:], w=[f'h1d{sl}'], sem=f'sth{sl}', r=[f'h1_{sl}'])
        P.actv(junk[:], h1[sl][:], AF.Square, accum=ss[:], r=[f'h1_{sl}'], w=['junk', 'ss'])
        P.actv(rt[:], ss[:], AF.Sqrt, bias=EPS, scale=1.0 / 1024, r=['ss'], w=['rt'])
        P.add('dve', lambda e: e.reciprocal(out=rstd[:], in_=rt[:]), r=['rt'], w=['rstd'])
        P.ts('pool', hnb[:], h1[sl][:], rstd[:, 0:1], None, ALU.mult, r=[f'h1_{sl}', 'rstd'], w=['hnb'])
        bk, bkk = nb()
        pv = bfv(bk).rearrange("p (k t) -> p k t", k=8)
        for kt in range(8):
            P.tr(pv[:, kt, :], hnb[:, kt * 128:(kt + 1) * 128], identb[:], r=['hnb', 'identb'], w=[bkk])
        P.cp('act', hT[:], pv, r=[bkk], w=['hT'])
        bk, bkk = nb()
        for kt in range(8):
            P.mm(bk[:, 0:288], hT[:, kt, :], wdn[:, kt, :], start=(kt == 0), stop=(kt == 7), r=['hT', *ALLW], w=[bkk])
        P.actv(junk2[:, 0:256], bk[:, 0:256], AF.Square, accum=ssc[:], r=[bkk], w=['junk2', 'ssc'])
        P.actv(rtc[:], ssc[:], AF.Sqrt, bias=EPS, scale=1.0 / 256, r=['ssc'], w=['rtc'])
        P.add('dve', lambda e: e.reciprocal(out=rstdc[:], in_=rtc[:]), r=['rtc'], w=['rstdc'])
        P.tt('dve', ra[:], bk[:, 256:272], cosT[:, tt, :], ALU.mult, r=[bkk, 'cosT'], w=['ra'])
        P.tt('dve', rb[:], bk[:, 272:288], sinT[:, tt, :], ALU.mult, r=[bkk, 'sinT'], w=['rb'])
        P.tt('dve', krb[:, 0:16], ra[:], rb[:], ALU.subtract, r=['ra', 'rb'], w=['krb'])
        P.tt('dve', ra[:], bk[:, 256:272], sinT[:, tt, :], ALU.mult, r=[bkk, 'sinT', 'krb'], w=['ra'])
        P.tt('dve', rb[:], bk[:, 272:288], cosT[:, tt, :], ALU.mult, r=[bkk, 'cosT', 'krb'], w=['rb'])
        P.tt('dve', krb[:, 16:32], ra[:], rb[:], ALU.add, r=['ra', 'rb'], w=['krb'])
        P.ts('dve', ckvn[:], bk[:, 0:256], rstdc[:, 0:1], None, ALU.mult, r=[bkk, 'rstdc'], w=['ckvn'])
        bk, bkk = nb()
        pv = bfv(bk)
        for kt in range(2):
            P.tr(pv[:, kt * 128:(kt + 1) * 128], ckvn[:, kt * 128:(kt + 1) * 128], identb[:], r=['ckvn', 'identb'], w=[bkk])
        P.tr(pv[0:32, 256:384], krb[:], identb[:], r=['krb', 'identb'], w=[bkk])
        P.cp('act', ckT[:], pv[:, 0:256].rearrange("p (k t) -> p k t", k=2), r=[bkk], w=['ckT'])
        P.cp('act', krT[sl][:], pv[0:32, 256:384], r=[bkk], w=[f'krT{sl}'])
        P.ld(kr_d[:, tok], krT[sl][:], w=[f'krd{sl}'], sem=f'stkr{sl}', r=[f'krT{sl}'])
        for half in range(2):
            bk, bkk = nb()
            for j in range(4):
                pr = half * 4 + j
                for kt in range(2):
                    P.mm(bk[:, j * 128:(j + 1) * 128], wkn[:, kt, pr * 128:(pr + 1) * 128], ckT[:, kt, :], start=(kt == 0), stop=(kt == 1),
                         r=['ckT', *ALLW], w=[bkk])
            P.cp('act' if half == 0 else 'dve', knT[sl][:, half * 4:(half + 1) * 4, :], bk[:, :].rearrange("p (j t) -> p j t", j=4),
                 r=[bkk], w=[f'knT{sl}'])
        P.ld(kn_d[:, :, tok], knT[sl][:], w=[f'knd{sl}'], sem=f'stkn{sl}', r=[f'knT{sl}'])
        for half in range(2):
            bk, bkk = nb()
            for kt in range(2):
                P.mm(bk[:, :], ckT[:, kt, :], wv[:, kt, half * 512:(half + 1) * 512], start=(kt == 0), stop=(kt == 1),
                     r=['ckT', *ALLW], w=[bkk])
            P.cp('act' if half == 0 else 'dve', vsb[sl][:, half * 512:(half + 1) * 512], bk[:, :], r=[bkk], w=[f'vsb{sl}'])
        P.ld(v_d[tok, :], vsb[sl][:], w=[f'vd{sl}'], sem=f'stv{sl}', r=[f'vsb{sl}'])
        bk, bkk = nb()
        for kt in range(8):
            P.mm(bk[:, 0:384], hT[:, kt, :], win[:, kt, 0:384], start=(kt == 0), stop=(kt == 7), r=['hT', *ALLW], w=[bkk])
        P.actv(junk2[:], bk[:, 0:384], AF.Square, accum=ssq[:], r=[bkk], w=['junk2', 'ssq'])
        P.actv(rtq[:], ssq[:], AF.Sqrt, bias=EPS, scale=1.0 / 384, r=['ssq'], w=['rtq'])
        P.add('dve', lambda e: e.reciprocal(out=rstdq[:], in_=rtq[:]), r=['rtq'], w=['rstdq'])
        P.ts('dve', cqn[:], bk[:, 0:384], rstdq[:, 0:1], None, ALU.mult, r=[bkk, 'rstdq'], w=['cqn'])
        for half in range(2):
            bk, bkk = nb()
            for j in range(4):
                ct = half * 4 + j
                for kt in range(8):
                    P.mm(bk[:, j * 128:(j + 1) * 128], win[:, kt, 384 + ct * 128:384 + (ct + 1) * 128], hT[:, kt, :],
                         start=(kt == 0), stop=(kt == 7), r=['hT', *ALLW], w=[bkk])
            P.actv(sg[sl][:, half * 4:(half + 1) * 4, :], bk[:, :].rearrange("p (j t) -> p j t", j=4), AF.Silu, r=[bkk], w=[f'sg{sl}'])
        P.ld(sg_d[:, :, tok], sg[sl][:], w=[f'sgd{sl}'], sem=f'stsg{sl}', r=[f'sg{sl}'])
        bk, bkk = nb()
        pv = bfv(bk)
        for kt in range(3):
            P.tr(pv[:, kt * 128:(kt + 1) * 128], cqn[:, kt * 128:(kt + 1) * 128], identb[:], r=['cqn', 'identb'], w=[bkk])
        P.cp('act', cqT[:], pv[:, 0:384].rearrange("p (k t) -> p k t", k=3), r=[bkk], w=['cqT'])
        for blk in range(2):
            bk, bkk = nb()
            for kt in range(3):
                P.mm(bk[:, :], cqT[:, kt, :], wuq[:, kt, blk * 512:(blk + 1) * 512], start=(kt == 0), stop=(kt == 2),
                     r=['cqT', *ALLW], w=[bkk])
            P.cp('act', qtok[:, blk * 8:(blk + 1) * 8, 0:64], bk[:, :].rearrange("p (h c) -> p h c", h=8), r=[bkk], w=['qtok'])
        bk, bkk = nb()
        for kt in range(3):
            P.mm(bk[:, :], cqT[:, kt, :], wuq[:, kt, 1024:1536], start=(kt == 0), stop=(kt == 2), r=['cqT', *ALLW], w=[bkk])
        x1 = bk[:, 0:256].rearrange("p (h c) -> p h c", h=16)
        x2 = bk[:, 256:512].rearrange("p (h c) -> p h c", h=16)
        cb_ = cosT[:, tt, :].unsqueeze(1).to_broadcast([128, 16, 16])
        sb_ = sinT[:, tt, :].unsqueeze(1).to_broadcast([128, 16, 16])
        P.tt('dve', qa[:], x1, cb_, ALU.mult, r=[bkk, 'cosT'], w=['qa'])
        P.tt('dve', qb[:], x2, sb_, ALU.mult, r=[bkk, 'sinT'], w=['qb'])
        P.tt('dve', qtok[:, :, 64:80], qa[:], qb[:], ALU.subtract, r=['qa', 'qb'], w=['qtok'])
        P.tt('dve', qa[:], x1, sb_, ALU.mult, r=[bkk, 'sinT', 'qtok'], w=['qa'])
        P.tt('dve', qb[:], x2, cb_, ALU.mult, r=[bkk, 'cosT', 'qtok'], w=['qb'])
        P.tt('dve', qtok[:, :, 80:96], qa[:], qb[:], ALU.add, r=['qa', 'qb'], w=['qtok'])
        for half in range(2):
            bk, bkk = nb()
            pv = bfv(bk)[0:96, :].rearrange("p (h t) -> p h t", h=8)
            for j in range(8):
                P.tr(pv[:, j, :], qtok[:, half * 8 + j, :], identb[:], r=['qtok', 'identb'], w=[bkk])
            P.cp('act' if half == 0 else 'dve', qT[sl][:, half * 8:(half + 1) * 8, :], pv, r=[bkk], w=[f'qT{sl}'])
        P.ld(qT_d[:, :, tok], qT[sl][:], w=[f'qd{sl}'], sem=f'stq{sl}', r=[f'qT{sl}'])
    outk = []
    for sl in range(2):
        outk += [f'h1d{sl}', f'krd{sl}', f'knd{sl}', f'vd{sl}', f'sgd{sl}', f'qd{sl}']
    P.wait_all('sp', outk)
    P.emit()
    es.close()
    return nc
SCALE = 96.0 ** -0.5
LOOKAHEAD = 2


def build_stageC(nheads=4, nchunks=16):
    nc = bass.Bass("TRN2", target_bir_lowering=False)
    SQ = 8192
    LK = 8192
    NKT = LK // 128
    chunks = list(range(nchunks))
    kT_d = nc.dram_tensor("kT", [4, 96, 8192], BF16, kind="ExternalInput").ap()
    v_d = nc.dram_tensor("v", [4, 128, 64, 64], BF16, kind="ExternalInput").ap()
    qT_d = nc.dram_tensor("qT", [4, 96, SQ], BF16, kind="ExternalInput").ap()
    sg_d = nc.dram_tensor("sg", [4, 64, SQ], BF16, kind="ExternalInput").ap()
    og_d = nc.dram_tensor("og", [64, 4, SQ], BF16, kind="ExternalOutput").ap()

    P = P2(nc)
    es = contextlib.ExitStack()

    def S(name, shape, dt):
        return es.enter_context(nc.sbuf_tensor(name, shape, dt))

    banks = [es.enter_context(nc.psum_tensor(f"bank{i}", [128, 512], F32)) for i in range(8)]
    kT = [S(f"kT{i}", [96, LK], BF16) for i in range(2)]
    vh = [S(f"vh{i}", [128, NKT, 65], BF16) for i in range(2)]
    qh = [S(f"qh{i}", [96, SQ], BF16) for i in range(2)]
    sgh = [S(f"sgh{i}", [64, SQ], BF16) for i in range(2)]
    ogs = [S(f"ogs{i}", [64, 512], BF16) for i in range(2)]
    PT = [S(f"PT{i}", [128, 512], BF16) for i in range(4)]
    rrow = S("rrow", [65, 512], F32)
    onesr = S("onesr", [65, 64], F32)
    ot = [S(f"ot{i}", [64, 512], F32) for i in range(2)]
    tn = S("tn", [64, 512], BF16)

    P.ms('pool', onesr[:], 1.0, ['onesr'])
    for i in range(2):
        P.ms('pool', vh[i][:, :, 64:65], 1.0, [f'vh{i}'])

    def load_head(h):
        i = h % 2
        half = LK // 2
        P.ld(kT[i][:, 0:half], kT_d[h, :, 0:half], [f'kT{i}a'], f'kT{i}a')
        P.ld(kT[i][:, half:LK], kT_d[h, :, half:LK], [f'kT{i}b'], f'kT{i}b', eng='act')
        P.ld(vh[i][:, :, 0:64], v_d[h, :, 0:NKT, :], [f'vh{i}'], f'vh{i}', eng='pool')
        P.ld(qh[i][:], qT_d[h, :, :], [f'qh{i}'], f'qh{i}')
        P.ld(sgh[i][:], sg_d[h, :, :], [f'sgh{i}'], f'sgh{i}')

    load_head(0)
    sctr = [0]
    for h in range(nheads):
        i = h % 2
        if h + 1 < nheads:
            load_head(h + 1)
        KK = [f'kT{i}a', f'kT{i}b']
        for qi, cj in enumerate(chunks):
            nk = (cj + 1) * 4
            qsl = slice(qi * 512, (qi + 1) * 512)
            po = banks[4 + qi % 2]
            pok = f'B{4 + qi % 2}'
            tiles = []
            for kt in range(nk):
                d = kt - (nk - 4)
                c0 = 128 * d if d > 0 else 0
                tiles.append((kt, d, c0))

            def emit_S(kt, d, c0):
                sb = sctr[0] % 4
                sctr[0] += 1
                ps = banks[sb]
                P.mm(ps[:, c0:512], kT[i][:, kt * 128:(kt + 1) * 128], qh[i][:, qi * 512 + c0:(qi + 1) * 512],
                     r=KK + [f'qh{i}'], w=[f'B{sb}'])
                P.actv(PT[sb][:, c0:512], ps[:, c0:512], AF.Exp, scale=SCALE, r=[f'B{sb}'], w=[f'PT{sb}'])
                if d >= 0:
                    blk = PT[sb][:, c0:c0 + 128]
                    P.add('pool', (lambda blk: (lambda e: e.affine_select(out=blk, in_=blk, pattern=[[1, 128]], compare_op=ALU.is_ge,
                                                                          fill=0.0, base=0, channel_multiplier=-1)))(blk),
                          r=[f'PT{sb}'], w=[f'PT{sb}'])
                return sb

            pend = []
            for n, (kt, d, c0) in enumerate(tiles):
                pend.append((emit_S(kt, d, c0), kt, c0))
                if len(pend) > LOOKAHEAD:
                    sb, kt2, c02 = pend.pop(0)
                    P.mm(po[0:65, c02:512], vh[i][:, kt2, :], PT[sb][:, c02:512], start=(kt2 == 0), stop=(kt2 == nk - 1),
                         r=[f'vh{i}', f'PT{sb}'], w=[pok])
            while pend:
                sb, kt2, c02 = pend.pop(0)
                P.mm(po[0:65, c02:512], vh[i][:, kt2, :], PT[sb][:, c02:512], start=(kt2 == 0), stop=(kt2 == nk - 1),
                     r=[f'vh{i}', f'PT{sb}'], w=[pok])
            P.add('dve', lambda e, po=po: e.reciprocal(out=rrow[64:65, :], in_=po[64:65, :]), r=[pok], w=['rrow'])
            oti = ot[qi % 2]
            P.cp('act', oti[:], po[0:64, :], r=[pok], w=[f'ot{qi % 2}'])
            prb = banks[6]
            P.mm(prb[0:64, :], onesr[64:65, :], rrow[64:65, :], r=['onesr', 'rrow'], w=['B6'])
            P.tt('dve', tn[:], oti[:], prb[0:64, :], ALU.mult, r=[f'ot{qi % 2}', 'B6'], w=['tn'])
            ogi = ogs[qi % 2]
            P.tt('pool', ogi[:], tn[:], sgh[i][:, qsl], ALU.mult, r=['tn', f'sgh{i}'], w=[f'ogs{qi % 2}'])
            P.ld(og_d[:, h, qsl], ogi[:], w=[f'ogd{qi % 2}'], sem=f'sto{qi % 2}', r=[f'ogs{qi % 2}'])
    P.wait_all('sp', ['ogd0', 'ogd1'])
    P.emit()
    es.close()
    return nc


def build_stageD():
    nc = bass.Bass("TRN2", target_bir_lowering=False)
    og_d = nc.dram_tensor("og", [64, 16, NTOK], BF16, kind="ExternalInput").ap()
    h1_d = nc.dram_tensor("h1", [NTOK, 1024], F32, kind="ExternalInput").ap()
    wo_d = nc.dram_tensor("wo", [1024, 1024], F32, kind="ExternalInput").ap()
    gf_d = nc.dram_tensor("gf", [1, 1024], F32, kind="ExternalInput").ap()
    out_d = nc.dram_tensor("out", [NTOK, 1024], F32, kind="ExternalOutput").ap()
    P = P2(nc)
    es = contextlib.ExitStack()

    def S(name, shape, dt):
        return es.enter_context(nc.sbuf_tensor(name, shape, dt))

    banks = [es.enter_context(nc.psum_tensor(f"bank{i}", [128, 512], F32)) for i in range(8)]
    ogT = S("ogT", [64, 16, NTOK], BF16)
    wo = S("wo_s", [64, 16, 1024], BF16)
    wst = [S(f"wst{i}", [64, 1024], F32) for i in range(2)]
    gf_bc = S("gf_bc", [128, 1024], F32)
    h1t = [S(f"h1t{i}", [128, 1024], F32) for i in range(2)]
    h2 = S("h2", [128, 1024], F32)
    junk = S("junk", [128, 1024], BF16)
    ss = S("ss", [128, 1], F32)
    rt = S("rt", [128, 1], F32)
    rstd = S("rstd", [128, 1], F32)
    outt = [S(f"outt{i}", [128, 1024], F32) for i in range(2)]
    P.ld(gf_bc[:], gf_d.partition_broadcast(128), ['gf_bc'], 'c0')
    for q in range(4):
        P.ld(ogT[:, :, q * 512:(q + 1) * 512], og_d[:, :, q * 512:(q + 1) * 512], [f'ogT{q}'], f'og{q}')
    wkeys = []
    for h in range(16):
        i = h % 2
        P.ld(wst[i][:], wo_d[h * 64:(h + 1) * 64, :], [f'wst{i}'], f'wst{i}')
        P.cp('dve' if i == 0 else 'pool', wo[:, h, :], wst[i][:], r=[f'wst{i}'], w=[f'wo{h}'])
        wkeys.append(f'wo{h}')
    def load_h1(tt):
        P.ld(h1t[tt % 2][:], h1_d[tt * 128:(tt + 1) * 128, :], [f'h1t{tt % 2}'], f'h1t{tt % 2}')

    load_h1(0)
    for tt in range(NTT):
        sl = tt % 2
        if tt + 1 < NTT:
            load_h1(tt + 1)
        for half in range(2):
            bk = banks[half]
            for h in range(16):
                P.mm(bk[:, :], ogT[:, h, tt * 128:(tt + 1) * 128], wo[:, h, half * 512:(half + 1) * 512], start=(h == 0), stop=(h == 15),
                     r=[f'ogT{tt // 4}', wkeys[h]], w=[f'B{half}'])
            P.tt('dve', h2[:, half * 512:(half + 1) * 512], bk[:, :], h1t[sl][:, half * 512:(half + 1) * 512], ALU.add,
                 r=[f'B{half}', f'h1t{sl}'], w=[f'h2_{half}'])
        P.actv(junk[:], h2[:], AF.Square, accum=ss[:], r=['h2_0', 'h2_1'], w=['junk', 'ss'])
        P.actv(rt[:], ss[:], AF.Sqrt, bias=EPS, scale=1.0 / 1024, r=['ss'], w=['rt'])
        P.add('dve', lambda e: e.reciprocal(out=rstd[:], in_=rt[:]), r=['rt'], w=['rstd'])
        P.stt(outt[sl][:], h2[:], rstd[:, 0:1], gf_bc[:], ALU.mult, ALU.mult, r=['h2_0', 'h2_1', 'rstd', 'gf_bc'], w=[f'outt{sl}'])
        P.ld(out_d[tt * 128:(tt + 1) * 128, :], outt[sl][:], w=[f'od{sl}'], sem=f'sto{sl}', r=[f'outt{sl}'])
    P.wait_all('sp', ['od0', 'od1'])
    P.emit()
    es.close()
    return nc


def _prepA(inp, b, g):
    w_in = inp['ssm_w_in'][0]
    w = np.concatenate([w_in[:, 2048 + g * 512:2048 + (g + 1) * 512], w_in[:, 4096 + g * 128:4096 + (g + 1) * 128],
                        w_in[:, 4608 + g * 128:4608 + (g + 1) * 128], w_in[:, 5120 + g * 8:5120 + (g + 1) * 8],
                        w_in[:, g * 512:(g + 1) * 512]], axis=1)
    cidx = np.concatenate([np.arange(g * 512, (g + 1) * 512), 2048 + np.arange(g * 128, (g + 1) * 128),
                           2560 + np.arange(g * 128, (g + 1) * 128)])
    cwc = inp['ssm_conv_w'][0][:, cidx]
    cw = cwc.T.reshape(6, 128, 4).transpose(1, 0, 2).reshape(128, 24)
    cb = inp['ssm_conv_b'][0][cidx].reshape(6, 128).T
    hs = slice(g * 8, (g + 1) * 8)
    C = np.ascontiguousarray
    return dict(x=C(inp['x'][b]), w=C(w), gpre=C(inp['g_pre'][0].reshape(8, 128).T), cw=C(cw), cb=C(cb),
                dtb=C(inp['ssm_dt_bias'][0][hs].reshape(1, 8)), alog=C(inp['ssm_A_log'][0][hs].reshape(1, 8)),
                dsk=C(inp['ssm_D'][0][hs].reshape(1, 8)), gout=C(inp['ssm_g_out'][0][g * 512:(g + 1) * 512].reshape(1, 512)))


def _prepB(inp, yn_b, b, j):
    C = np.ascontiguousarray
    tok = slice(j * NTOK, (j + 1) * NTOK)
    pos = np.asarray(inp['positions'][b][tok]).astype(np.int32).reshape(NTT, 128).T
    return dict(x=C(inp['x'][b][tok]), yn=C(yn_b[tok]), pos=C(pos), invf=np.array(INV_FREQ, dtype=np.float32).reshape(1, 16),
                wout=C(inp['ssm_w_out'][0]), wdn=C(inp['kv_w_down']), wup=C(inp['kv_w_up']), win=C(inp['mla_w_in'][0]),
                wuq=C(inp['mla_w_uq'][0]), gkv=C(inp['kv_g_in'].reshape(8, 128).T), gpre=C(inp['g_pre'][1].reshape(8, 128).T),
                glat=C(inp['kv_g_latent'].reshape(2, 128).T), gq=C(inp['mla_g_q'][0].reshape(3, 128).T))


def kernel(**inputs):
    inp = {k: np.asarray(v) for k, v in inputs.items()}
    C = np.ascontiguousarray
    cores = list(range(8))
    ncA = build_stageA()
    rA = run_bass_kernel_spmd(ncA, [_prepA(inp, c // 4, c % 4) for c in cores], core_ids=cores).results
    yn = [np.concatenate([rA[b * 4 + g]['yn'] for g in range(4)], axis=1) for b in range(2)]
    ncB = build_stageB()
    rB = run_bass_kernel_spmd(ncB, [_prepB(inp, yn[c // 4], c // 4, c % 4) for c in cores], core_ids=cores).results
    imC = []
    for c in cores:
        b, hg = c // 4, c % 4
        kn = np.concatenate([rB[b * 4 + j]['kn'] for j in range(4)], axis=2)
        kr = np.concatenate([rB[b * 4 + j]['kr'] for j in range(4)], axis=1)
        vf = np.concatenate([rB[b * 4 + j]['v'] for j in range(4)], axis=0)
        qf = np.concatenate([rB[b * 4 + j]['qT'] for j in range(4)], axis=2)
        sf = np.concatenate([rB[b * 4 + j]['sg'] for j in range(4)], axis=2)
        kT = np.empty((4, 96, 8192), dtype=kn.dtype)
        v4 = np.empty((4, 128, 64, 64), dtype=vf.dtype)
        q4 = np.empty((4, 96, 8192), dtype=qf.dtype)
        s4 = np.empty((4, 64, 8192), dtype=sf.dtype)
        for hl in range(4):
            h = hg * 4 + hl
            kT[hl, 0:64] = kn[(h % 2) * 64:(h % 2) * 64 + 64, h // 2, :]
            kT[hl, 64:96] = kr
            v4[hl] = vf[:, h * 64:(h + 1) * 64].reshape(64, 128, 64).transpose(1, 0, 2)
            q4[hl] = qf[:, h, :]
            s4[hl] = sf[(h % 2) * 64:(h % 2) * 64 + 64, h // 2, :]
        imC.append(dict(kT=kT, v=v4, qT=q4, sg=s4))
    ncC = build_stageC()
    rC = run_bass_kernel_spmd(ncC, imC, core_ids=cores).results
    imD = []
    for c in cores:
        b, j = c // 4, c % 4
        tok = slice(j * NTOK, (j + 1) * NTOK)
        og = np.concatenate([rC[b * 4 + hg]['og'][:, :, tok] for hg in range(4)], axis=1)
        imD.append(dict(og=C(og), h1=rB[c]['h1'], wo=C(inp['mla_w_out'][0]), gf=C(inp['g_final'].reshape(1, 1024))))
    ncD = build_stageD()
    rD = run_bass_kernel_spmd(ncD, imD, core_ids=cores).results
    out = np.stack([np.concatenate([rD[b * 4 + j]['out'] for j in range(4)], axis=0) for b in range(2)], axis=0)
    return out.astype(np.float32)
```

```python
import contextlib
import math
from concourse.bass_utils import run_bass_kernel_spmd
import numpy as np
import concourse.bass as bass
import concourse.mybir as mybir

F32 = mybir.dt.float32
BF16 = mybir.dt.bfloat16
I32 = mybir.dt.int32
AF = mybir.ActivationFunctionType
ALU = mybir.AluOpType
AX = mybir.AxisListType


class Prog:
    def __init__(self, nc):
        self.nc = nc
        self.ops = []
        self.lastw = {}
        self.readers = {}
        self.dma_sems = {}

    def add(self, eng, fn, r=(), w=(), dma=None, group=False):
        deps = set()
        for k in r:
            if k in self.lastw:
                deps.add(self.lastw[k])
            if k[0] == 'B' and k[1:].isdigit():
                for j in self.readers.get(k, ()):
                    if self.ops[j]['eng'] != eng:
                        deps.add(j)
        for k in w:
            if k in self.lastw:
                deps.add(self.lastw[k])
            deps.update(self.readers.get(k, ()))
        i = len(self.ops)
        self.ops.append(dict(eng=eng, fn=fn, deps=deps, dma=dma, group=group, has_dep=False))
        for k in r:
            self.readers.setdefault(k, []).append(i)
        for k in w:
            self.lastw[k] = i
            self.readers[k] = []
        return i

    def pe(self, fn, r=(), w=()):
        return self.add('pe', fn, r, w)

    def act(self, fn, r=(), w=()):
        return self.add('act', fn, r, w)

    def dve(self, fn, r=(), w=()):
        return self.add('dve', fn, r, w)

    def pool(self, fn, r=(), w=()):
        return self.add('pool', fn, r, w)

    def dma(self, eng, fn, r=(), w=(), sem=None, group=False):
        assert sem is not None
        return self.add(eng, fn, r, w, dma=sem, group=group)

    def wait_all(self, eng, keys):
        return self.add(eng, None, r=keys, w=())

    def emit(self):
        nc = self.nc
        ops = self.ops
        engs = ['sp', 'act', 'dve', 'pool', 'pe']
        for o in ops:
            for d in o['deps']:
                if ops[d]['eng'] == 'pe' and o['eng'] == 'pe' and ops[d]['dma'] is None and o['dma'] is None:
                    continue
                ops[d]['has_dep'] = True
        esem = {e: nc.alloc_semaphore(name=f"s_{e}") for e in engs}
        group_tot = {}
        for o in ops:
            if o['dma'] is not None:
                if o['dma'] not in self.dma_sems:
                    self.dma_sems[o['dma']] = nc.alloc_semaphore(name=f"d_{o['dma']}")
                group_tot[o['dma']] = group_tot.get(o['dma'], 0) + 1
        cnt = {e: 0 for e in engs}
        dcnt = {}
        for o in ops:
            if o['fn'] is None:
                o['tok'] = None
            elif o['dma'] is not None:
                k = o['dma']
                dcnt[k] = dcnt.get(k, 0) + 1
                v = group_tot[k] if o['group'] else dcnt[k]
                o['tok'] = (('d', k), 16 * v)
            elif o['has_dep']:
                cnt[o['eng']] += 1
                o['tok'] = (('e', o['eng']), cnt[o['eng']])
            else:
                o['tok'] = None
        known = {e: {} for e in engs}
        for o in ops:
            e = o['eng']
            kn = known[e]
            waits = []
            for d in sorted(o['deps'], reverse=True):
                od = ops[d]
                if od['tok'] is None:
                    continue
                if od['eng'] == 'pe' and e == 'pe' and od['dma'] is None and o['dma'] is None:
                    continue
                s, v = od['tok']
                if kn.get(s, 0) < v:
                    waits.append((s, v))
                    kn[s] = v
                    for s2, v2 in od['clock'].items():
                        if kn.get(s2, 0) < v2:
                            kn[s2] = v2
            wm = {}
            for s, v in waits:
                wm[s] = max(wm.get(s, 0), v)
            o['waits'] = wm
            o['clock'] = dict(kn)

        def semof(s):
            return esem[s[1]] if s[0] == 'e' else self.dma_sems[s[1]]

        def run(ename, eng):
            for o in ops:
                if o['eng'] != ename:
                    continue
                for s, v in o['waits'].items():
                    eng.wait_ge(semof(s), v)
                if o['fn'] is None:
                    continue
                inst = o['fn'](eng)
                if o['tok'] is not None:
                    s, v = o['tok']
                    inst.then_inc(semof(s), 16 if s[0] == 'd' else 1)

        with nc.Block() as block:
            @block.sync
            def _(e):
                run('sp', e)

            @block.scalar
            def _(e):
                run('act', e)

            @block.vector
            def _(e):
                run('dve', e)

            @block.gpsimd
            def _(e):
                run('pool', e)

            @block.tensor
            def _(e):
                run('pe', e)
        n = {e: sum(1 for o in ops if o['eng'] == e) for e in engs}
        nw = sum(len(o['waits']) for o in ops)
        print("PROG ops", n, "waits", nw, "sems", 5 + len(self.dma_sems), flush=True)


def _kw(**k):
    return {a: b for a, b in k.items() if b is not None}


class P2(Prog):
    def mm(self, out, lhsT, rhs, start=True, stop=True, r=(), w=()):
        return self.add('pe', lambda e: e.matmul(out, lhsT=lhsT, rhs=rhs, start=start, stop=stop), r, w)

    def tr(self, out, in_, ident, r=(), w=()):
        return self.add('pe', lambda e: e.transpose(out, in_, ident), r, w)

    def actv(self, out, in_, func, bias=None, scale=None, accum=None, r=(), w=()):
        kw = _kw(bias=bias, scale=scale, accum_out=accum)
        return self.add('act', lambda e: e.activation(out=out, in_=in_, func=func, **kw), r, w)

    def ts(self, eng, out, in0, s1, s2=None, op0=ALU.mult, op1=None, r=(), w=()):
        kw = _kw(op1=op1)
        return self.add(eng, lambda e: e.tensor_scalar(out=out, in0=in0, scalar1=s1, scalar2=s2, op0=op0, **kw), r, w)

    def tt(self, eng, out, in0, in1, op, r=(), w=()):
        return self.add(eng, lambda e: e.tensor_tensor(out=out, in0=in0, in1=in1, op=op), r, w)

    def stt(self, out, in0, scalar, in1, op0, op1, r=(), w=()):
        return self.add('dve', lambda e: e.scalar_tensor_tensor(out=out, in0=in0, scalar=scalar, in1=in1, op0=op0, op1=op1), r, w)

    def cp(self, eng, out, in_, r=(), w=()):
        if eng == 'act':
            return self.add('act', lambda e: e.activation(out=out, in_=in_, func=AF.Copy), r, w)
        return self.add(eng, lambda e: e.tensor_copy(out=out, in_=in_), r, w)

    def ms(self, eng, ap, val, w=()):
        return self.add(eng, lambda e: e.memset(ap, val), (), w)

    def ld(self, out, in_, w, sem, eng='sp', group=False, r=()):
        return self.dma(eng, lambda e: e.dma_start(out=out, in_=in_), r=r, w=w, sem=sem, group=group)

SEQ = 8192
DM = 1024
NCH = SEQ // 256
EPS = 1e-6
WCOLS = 1288


def build_stageA(nch=NCH):
    nc = bass.Bass("TRN2", target_bir_lowering=False)
    x_d = nc.dram_tensor("x", [SEQ, DM], F32, kind="ExternalInput").ap()
    w_d = nc.dram_tensor("w", [DM, WCOLS], F32, kind="ExternalInput").ap()
    gpre_d = nc.dram_tensor("gpre", [128, 8], F32, kind="ExternalInput").ap()
    cw_d = nc.dram_tensor("cw", [128, 24], F32, kind="ExternalInput").ap()
    cb_d = nc.dram_tensor("cb", [128, 6], F32, kind="ExternalInput").ap()
    dtb_d = nc.dram_tensor("dtb", [1, 8], F32, kind="ExternalInput").ap()
    alog_d = nc.dram_tensor("alog", [1, 8], F32, kind="ExternalInput").ap()
    dsk_d = nc.dram_tensor("dsk", [1, 8], F32, kind="ExternalInput").ap()
    gout_d = nc.dram_tensor("gout", [1, 512], F32, kind="ExternalInput").ap()
    yn_d = nc.dram_tensor("yn", [SEQ, 512], BF16, kind="ExternalOutput").ap()

    P = P2(nc)
    es = contextlib.ExitStack()

    def S(name, shape, dt):
        return es.enter_context(nc.sbuf_tensor(name, shape, dt))

    banks = [es.enter_context(nc.psum_tensor(f"bank{i}", [128, 512], F32)) for i in range(8)]

    W = S("W", [128, 8, WCOLS], BF16)
    wst = [S(f"wst{i}", [128, WCOLS], F32) for i in range(2)]
    gpre = S("gpre_s", [128, 8], F32)
    cw = S("cw_s", [128, 24], F32)
    cb = S("cb_s", [128, 6], F32)
    dtb_bc = S("dtb_bc", [128, 8], F32)
    A_bc = S("A_bc", [128, 8], F32)
    D_bc = S("D_bc", [128, 8], F32)
    gout_bc = S("gout_bc", [128, 512], F32)
    identf = S("identf", [128, 128], F32)
    identb = S("identb", [128, 128], BF16)
    onesf = S("onesf", [128, 128], F32)
    onesb = S("onesb", [128, 128], BF16)
    trif = S("trif", [128, 128], F32)
    triw = S("triw", [128, 256], BF16)
    SU = S("SU", [128, 128], BF16)
    cdiag = S("cdiag", [128, 24, 128], BF16)
    Dident = S("Dident", [128, 8, 128], BF16)
    xin = [S(f"xin{i}", [128, 2, DM], F32) for i in range(2)]
    junk = [S(f"junk{i}", [128, DM], BF16) for i in range(2)]
    ss = S("ss", [128, 2], F32)
    rt = S("rt", [128, 2], F32)
    rstd = S("rstd", [128, 2], F32)
    hn = S("hn", [128, 2, DM], BF16)
    hnT = S("hnT", [128, 8, 256], BF16)
    ubuf = S("ubuf", [128, 6, 259], BF16)
    xc = S("xc", [128, 6, 256], BF16)
    xtok = S("xtok", [128, 2, 640], BF16)
    dtr = S("dtr", [128, 2, 8], F32)
    e1 = S("e1", [128, 2, 8], F32)
    dtk = S("dtk", [128, 2, 8], F32)
    dtA = S("dtA", [128, 2, 8], F32)
    cend = S("cend", [128, 8], F32)
    ecum = S("ecum", [128, 2, 8], F32)
    wtmp = S("wtmp", [128, 2, 8], F32)
    dec = S("dec", [128, 8], F32)
    W0 = S("W0", [128, 8, 256], BF16)
    V1 = S("V1", [128, 8, 128], BF16)
    CBm = S("CBm", [128, 384], BF16)
    xdt = S("xdt", [128, 2, 512], BF16)
    Lb = [S(f"Lb{i}", [128, 384], BF16) for i in range(2)]
    MT = S("MT", [128, 8, 384], BF16)
    state = S("state", [128, 512], F32)
    state_bf = S("state_bf", [128, 512], BF16)
    yi = S("yi", [128, 512], F32)
    t1 = S("t1", [128, 512], F32)
    ysb = S("ysb", [128, 512], F32)
    zs = S("zs", [128, 512], F32)
    yg = S("yg", [128, 512], F32)
    junk2 = S("junk2", [128, 512], BF16)
    ss2 = S("ss2", [128, 1], F32)
    rt2 = S("rt2", [128, 1], F32)
    rstd2 = S("rstd2", [128, 1], F32)
    yn = [S(f"yn{i}", [128, 512], BF16) for i in range(2)]
    wx = S("wx", [128, 2, 512], BF16)

    def bfview(bank):
        return bank[:].bitcast(BF16)

    ptr = [bfview(banks[t]).rearrange("p (k t) -> p k t", k=8) for t in range(2)]
    ptx = bfview(banks[0])[:, 0:640]
    pCB = banks[1][:, 0:384]
    pseg = [banks[2][:, 0:384], banks[3][:, 0:384]]
    pdtk = banks[6][:, 0:16].rearrange("p (t c) -> p t c", t=2)
    pcum = banks[6][:, 16:32].rearrange("p (t c) -> p t c", t=2)
    pce = banks[6][:, 32:40]
    pyi = banks[6][:, :]
    pst = banks[6][:, :]
    pz = banks[7][:, :]
    py = [banks[4][:, :], banks[5][:, :]]

    P.ld(gpre[:], gpre_d, ['gpre'], 'c0')
    P.ld(cw[:], cw_d, ['cw'], 'c1')
    P.ld(cb[:], cb_d, ['cb'], 'c2')
    P.ld(dtb_bc[:], dtb_d.partition_broadcast(128), ['dtb_bc'], 'c3')
    P.ld(A_bc[:], alog_d.partition_broadcast(128), ['A_bc'], 'c4')
    P.ld(D_bc[:], dsk_d.partition_broadcast(128), ['D_bc'], 'c5')
    P.ld(gout_bc[:], gout_d.partition_broadcast(128), ['gout_bc'], 'c6')
    P.ms('pool', identf[:], 1.0, ['identf'])
    P.add('pool', lambda e: e.affine_select(out=identf[:], in_=identf[:], pattern=[[-1, 128]], compare_op=ALU.is_equal,
                                            fill=0.0, base=0, channel_multiplier=1), r=['identf'], w=['identf'])
    P.cp('dve', identb[:], identf[:], r=['identf'], w=['identb'])
    P.ms('pool', onesf[:], 1.0, ['onesf'])
    P.ms('pool', onesb[:], 1.0, ['onesb'])
    P.ms('pool', triw[:], 1.0, ['triw'])
    P.add('pool', lambda e: e.affine_select(out=triw[:, 0:128], in_=triw[:, 0:128], pattern=[[1, 128]], compare_op=ALU.is_ge,
                                            fill=0.0, base=0, channel_multiplier=-1), r=['triw'], w=['triw'])
    P.cp('dve', trif[:], triw[:, 0:128], r=['triw'], w=['trif'])
    P.ms('pool', SU[:], 1.0, ['SU'])
    P.add('pool', lambda e: e.affine_select(out=SU[:], in_=SU[:], pattern=[[-1, 128]], compare_op=ALU.is_gt,
                                            fill=0.0, base=0, channel_multiplier=1), r=['SU'], w=['SU'])
    P.ms('pool', ubuf[:], 0.0, ['ubuf%d' % i for i in range(3)])
    P.ms('pool', state[:], 0.0, ['state'])
    P.ms('pool', state_bf[:], 0.0, ['state_bf'])
    for kt in range(8):
        P.ld(wst[kt % 2][:], w_d[kt * 128:(kt + 1) * 128, :], [f'wst{kt % 2}'], f'wst{kt % 2}')
        P.ts('dve' if kt % 2 == 0 else 'pool', W[:, kt, :], wst[kt % 2][:], gpre[:, kt:kt + 1], None, ALU.mult,
             r=[f'wst{kt % 2}', 'gpre'], w=[f'W{kt}'])
    Wk = [f'W{kt}' for kt in range(8)]
    HNT = ['hnT0', 'hnT1']
    for i in range(24):
        P.ts('dve', cdiag[:, i, :], identf[:], cw[:, i:i + 1], None, ALU.mult, r=['identf', 'cw'], w=['cdiag'])
    for h in range(8):
        P.ts('dve', Dident[:, h, :], identf[:], D_bc[:, h:h + 1], None, ALU.mult, r=['identf', 'D_bc'], w=['Dident'])
    P.actv(A_bc[:], A_bc[:], AF.Exp, r=['A_bc'], w=['A_bc'])
    P.ts('dve', A_bc[:], A_bc[:], -1.0, None, ALU.mult, r=['A_bc'], w=['A_bc'])

    def load_x(c):
        sl = c % 2
        P.ld(xin[sl][:], x_d[c * 256:(c + 1) * 256, :].rearrange("(t p) d -> p t d", p=128), [f'xin{sl}'], f'xin{sl}')

    load_x(0)
    for c in range(nch):
        sl = c % 2
        if c + 1 < nch:
            load_x(c + 1)
        xk = f'xin{sl}'
        for t in range(2):
            P.actv(junk[t][:], xin[sl][:, t, :], AF.Square, accum=ss[:, t:t + 1], r=[xk], w=[f'ss{t}', f'junk{t}'])
        P.actv(rt[:], ss[:], AF.Sqrt, bias=EPS, scale=1.0 / DM, r=['ss0', 'ss1'], w=['rt'])
        P.add('dve', lambda e: e.reciprocal(out=rstd[:], in_=rt[:]), r=['rt'], w=['rstd'])
        for t in range(2):
            P.ts('dve', hn[:, t, :], xin[sl][:, t, :], rstd[:, t:t + 1], None, ALU.mult, r=[xk, 'rstd'], w=[f'hn{t}'])
        for t in range(2):
            for kt in range(8):
                P.tr(ptr[t][:, kt, :], hn[:, t, kt * 128:(kt + 1) * 128], identb[:], r=[f'hn{t}', 'identb'], w=[f'B{t}'])
            P.cp('dve' if t == 0 else 'act', hnT[:, :, t * 128:(t + 1) * 128], ptr[t], r=[f'B{t}'], w=[f'hnT{t}'])
        for pr in range(3):
            bx = banks[2 + pr % 2]
            bxk = f'B{2 + pr % 2}'
            for j in range(2):
                ct = 2 * pr + j
                for kt in range(8):
                    P.mm(bx[:, j * 256:(j + 1) * 256], W[:, kt, ct * 128:(ct + 1) * 128], hnT[:, kt, :], start=(kt == 0), stop=(kt == 7),
                         r=HNT + [Wk[kt]], w=[bxk])
            P.cp('act', ubuf[:, 2 * pr:2 * pr + 2, 3:259], bx[:, :].rearrange("p (j t) -> p j t", j=2), r=[bxk], w=[f'ubuf{pr}'])
            bc = banks[4 + pr % 2]
            bck = f'B{4 + pr % 2}'
            for j in range(2):
                ct = 2 * pr + j
                for k in range(4):
                    P.mm(bc[:, j * 256:(j + 1) * 256], cdiag[:, ct * 4 + k, :], ubuf[:, ct, k:k + 256], start=(k == 0), stop=(k == 3),
                         r=['cdiag', f'ubuf{pr}'], w=[bck])
            for j in range(2):
                ct = 2 * pr + j
                P.actv(xc[:, ct, :], bc[:, j * 256:(j + 1) * 256], AF.Silu, bias=cb[:, ct:ct + 1], r=[bck, 'cb'], w=[f'xc{ct}'])
            P.cp('pool', ubuf[:, 2 * pr:2 * pr + 2, 0:3], ubuf[:, 2 * pr:2 * pr + 2, 256:259], r=[f'ubuf{pr}'], w=[f'ubuf{pr}'])
        for t in range(2):
            for kt in range(8):
                P.mm(pdtk[:, t, :], hnT[:, kt, t * 128:(t + 1) * 128], W[:, kt, 768:776], start=(kt == 0), stop=(kt == 7),
                     r=HNT + [Wk[kt]], w=['B6'])
        P.tt('dve', dtr[:], pdtk, dtb_bc[:].unsqueeze(1).to_broadcast([128, 2, 8]), ALU.add, r=['B6', 'dtb_bc'], w=['dtr'])
        P.actv(e1[:], dtr[:], AF.Exp, r=['dtr'], w=['e1'])
        P.actv(dtk[:], e1[:], AF.Ln, bias=1.0, r=['e1'], w=['dtk'])
        P.tt('dve', dtA[:], dtk[:], A_bc[:].unsqueeze(1).to_broadcast([128, 2, 8]), ALU.mult, r=['dtk', 'A_bc'], w=['dtA'])
        P.mm(pcum[:, 0, :], trif[:], dtA[:, 0, :], r=['trif', 'dtA'], w=['B6'])
        P.mm(pcum[:, 1, :], onesf[:], dtA[:, 0, :], start=True, stop=False, r=['onesf', 'dtA'], w=['B6'])
        P.mm(pcum[:, 1, :], trif[:], dtA[:, 1, :], start=False, stop=True, r=['trif', 'dtA'], w=['B6'])
        P.mm(pce, onesf[:], dtA[:, 0, :], start=True, stop=False, r=['onesf', 'dtA'], w=['B6'])
        P.mm(pce, onesf[:], dtA[:, 1, :], start=False, stop=True, r=['onesf', 'dtA'], w=['B6'])
        P.actv(ecum[:], pcum, AF.Exp, r=['B6'], w=['ecum'])
        P.actv(dec[:], pce, AF.Exp, r=['B6'], w=['dec'])
        P.cp('act', cend[:], pce, r=['B6'], w=['cend'])
        P.tt('dve', wtmp[:], cend[:].unsqueeze(1).to_broadcast([128, 2, 8]), pcum, ALU.subtract, r=['cend', 'B6'], w=['wtmp'])
        P.actv(wtmp[:], wtmp[:], AF.Exp, r=['wtmp'], w=['wtmp'])
        for t in range(2):
            for ct in range(5):
                P.tr(ptx[:, ct * 128:(ct + 1) * 128], xc[:, ct, t * 128:(t + 1) * 128], identb[:],
                     r=[f'xc{ct}', 'identb'], w=['B0'])
            P.cp('dve' if t == 0 else 'act', xtok[:, t, :], ptx, r=['B0'], w=[f'xtok{t}'])
            P.tt('pool', xdt[:, t, :].rearrange("p (h c) -> p h c", h=8), xtok[:, t, 0:512].rearrange("p (h c) -> p h c", h=8),
                 dtk[:, t, :].unsqueeze(2).to_broadcast([128, 8, 64]), ALU.mult, r=[f'xtok{t}', 'dtk'], w=[f'xdt{t}'])
        P.mm(pCB[:, 0:256], xc[:, 4, 0:128], xc[:, 5, 0:256], r=['xc4', 'xc5'], w=['B1'])
        P.mm(pCB[:, 256:384], xc[:, 4, 128:256], xc[:, 5, 128:256], r=['xc4', 'xc5'], w=['B1'])
        P.cp('act', CBm[:], pCB, r=['B1'], w=['CBm'])
        for off in (0, 256):
            blk = CBm[:, off:off + 128]
            P.add('pool', (lambda blk: (lambda e: e.affine_select(out=blk, in_=blk, pattern=[[1, 128]], compare_op=ALU.is_ge,
                                                                  fill=0.0, base=0, channel_multiplier=-1)))(blk),
                  r=['CBm'], w=['CBm'])
        P.tt('dve', W0[:], triw[:].unsqueeze(1).to_broadcast([128, 8, 256]), dtA[:, 0, :].unsqueeze(2).to_broadcast([128, 8, 256]),
             ALU.mult, r=['triw', 'dtA'], w=['W0'])
        P.tt('dve', V1[:], triw[:, 0:128].unsqueeze(1).to_broadcast([128, 8, 128]), dtA[:, 1, :].unsqueeze(2).to_broadcast([128, 8, 128]),
             ALU.mult, r=['triw', 'dtA'], w=['V1'])
        for h in range(8):
            ps = pseg[h % 2]
            psk = f'B{2 + h % 2}'
            L = Lb[h % 2]
            Lk = f'Lb{h % 2}'
            P.mm(ps[:, 0:128], SU[:], W0[:, h, 0:128], start=True, stop=True, r=['SU', 'W0'], w=[psk])
            P.mm(ps[:, 128:256], SU[:], W0[:, h, 128:256], start=True, stop=False, r=['SU', 'W0'], w=[psk])
            P.mm(ps[:, 128:256], onesb[:], V1[:, h, :], start=False, stop=True, r=['onesb', 'V1'], w=[psk])
            P.mm(ps[:, 256:384], SU[:], V1[:, h, :], start=True, stop=True, r=['SU', 'V1'], w=[psk])
            P.actv(L[:], ps, AF.Exp, r=[psk], w=[Lk])
            P.tt('dve', MT[:, h, :], L[:], CBm[:], ALU.mult, r=[Lk, 'CBm'], w=[f'MT{h}'])
        for t in range(2):
            for h in range(8):
                hc = slice(h * 64, (h + 1) * 64)
                P.mm(py[t][:, hc], MT[:, h, t * 128:(t + 1) * 128], xdt[:, 0, hc], start=True, stop=False,
                     r=[f'MT{h}', 'xdt0'], w=[f'B{4 + t}'])
                if t == 1:
                    P.mm(py[t][:, hc], MT[:, h, 256:384], xdt[:, 1, hc], start=False, stop=False,
                         r=[f'MT{h}', 'xdt1'], w=[f'B{4 + t}'])
                P.mm(py[t][:, hc], Dident[:, h, :], xtok[:, t, hc], start=False, stop=True,
                     r=['Dident', f'xtok{t}'], w=[f'B{4 + t}'])
            P.mm(pyi, xc[:, 5, t * 128:(t + 1) * 128], state_bf[:], r=['xc5', 'state_bf'], w=['B6'])
            P.cp('act', yi[:], pyi, r=['B6'], w=['yi'])
            P.tt('pool', t1[:].rearrange("p (h c) -> p h c", h=8), yi[:].rearrange("p (h c) -> p h c", h=8),
                 ecum[:, t, :].unsqueeze(2).to_broadcast([128, 8, 64]), ALU.mult, r=['yi', 'ecum'], w=['t1'])
            P.tt('dve', ysb[:], t1[:], py[t], ALU.add, r=['t1', f'B{4 + t}'], w=['ysb'])
            for kt in range(8):
                P.mm(pz, hnT[:, kt, t * 128:(t + 1) * 128], W[:, kt, 776:1288], start=(kt == 0), stop=(kt == 7),
                     r=HNT + [Wk[kt]], w=['B7'])
            P.actv(zs[:], pz, AF.Silu, r=['B7'], w=['zs'])
            P.tt('pool', yg[:], ysb[:], zs[:], ALU.mult, r=['ysb', 'zs'], w=['yg'])
            P.actv(junk2[:], yg[:], AF.Square, accum=ss2[:], r=['yg'], w=['ss2', 'junk2'])
            P.actv(rt2[:], ss2[:], AF.Sqrt, bias=EPS, scale=1.0 / 512, r=['ss2'], w=['rt2'])
            P.add('dve', lambda e: e.reciprocal(out=rstd2[:], in_=rt2[:]), r=['rt2'], w=['rstd2'])
            P.stt(yn[t][:], yg[:], rstd2[:, 0:1], gout_bc[:], ALU.mult, ALU.mult, r=['yg', 'rstd2', 'gout_bc'], w=[f'yn{t}'])
            P.ld(yn_d[c * 256 + t * 128: c * 256 + (t + 1) * 128, :], yn[t][:], w=[f'ynd{t}'], sem=f'st{t}', r=[f'yn{t}'])
        for st in range(2):
            P.tt('pool', wx[:, st, :].rearrange("p (h c) -> p h c", h=8), xdt[:, st, :].rearrange("p (h c) -> p h c", h=8),
                 wtmp[:, st, :].unsqueeze(2).to_broadcast([128, 8, 64]), ALU.mult, r=[f'xdt{st}', 'wtmp'], w=[f'wx{st}'])
        for st in range(2):
            P.mm(pst, xtok[:, st, 512:640], wx[:, st, :], start=(st == 0), stop=(st == 1), r=[f'xtok{st}', f'wx{st}'], w=['B6'])
        P.tt('dve', state[:].rearrange("p (h c) -> p h c", h=8), state[:].rearrange("p (h c) -> p h c", h=8),
             dec[:].unsqueeze(2).to_broadcast([128, 8, 64]), ALU.mult, r=['state', 'dec'], w=['state'])
        P.tt('dve', state[:], state[:], pst, ALU.add, r=['state', 'B6'], w=['state'])
        P.cp('pool', state_bf[:], state[:], r=['state'], w=['state_bf'])
    P.wait_all('sp', ['ynd0', 'ynd1'])
    P.emit()
    es.close()
    return nc

NTOK = 2048
NTT = NTOK // 128
INV_FREQ = [float(np.float32(10000.0) ** np.float32(-(2 * i) / 32.0)) for i in range(16)]
TWO_PI = 2.0 * math.pi
CW1 = 6.28125
CW2 = TWO_PI - CW1


def build_stageB(ntt=NTT):
    nc = bass.Bass("TRN2", target_bir_lowering=False)
    x_d = nc.dram_tensor("x", [NTOK, 1024], F32, kind="ExternalInput").ap()
    yn_d = nc.dram_tensor("yn", [NTOK, 2048], BF16, kind="ExternalInput").ap()
    pos_d = nc.dram_tensor("pos", [128, NTT], I32, kind="ExternalInput").ap()
    invf_d = nc.dram_tensor("invf", [1, 16], F32, kind="ExternalInput").ap()
    wout_d = nc.dram_tensor("wout", [2048, 1024], F32, kind="ExternalInput").ap()
    wdn_d = nc.dram_tensor("wdn", [1024, 288], F32, kind="ExternalInput").ap()
    wup_d = nc.dram_tensor("wup", [256, 2048], F32, kind="ExternalInput").ap()
    win_d = nc.dram_tensor("win", [1024, 1408], F32, kind="ExternalInput").ap()
    wuq_d = nc.dram_tensor("wuq", [384, 1536], F32, kind="ExternalInput").ap()
    gkv_d = nc.dram_tensor("gkv", [128, 8], F32, kind="ExternalInput").ap()
    gpre_d = nc.dram_tensor("gpre", [128, 8], F32, kind="ExternalInput").ap()
    glat_d = nc.dram_tensor("glat", [128, 2], F32, kind="ExternalInput").ap()
    gq_d = nc.dram_tensor("gq", [128, 3], F32, kind="ExternalInput").ap()
    h1_d = nc.dram_tensor("h1", [NTOK, 1024], F32, kind="ExternalOutput").ap()
    sg_d = nc.dram_tensor("sg", [128, 8, NTOK], BF16, kind="ExternalOutput").ap()
    kn_d = nc.dram_tensor("kn", [128, 8, NTOK], BF16, kind="ExternalOutput").ap()
    kr_d = nc.dram_tensor("kr", [32, NTOK], BF16, kind="ExternalOutput").ap()
    v_d = nc.dram_tensor("v", [NTOK, 1024], BF16, kind="ExternalOutput").ap()
    qT_d = nc.dram_tensor("qT", [96, 16, NTOK], BF16, kind="ExternalOutput").ap()

    P = P2(nc)
    es = contextlib.ExitStack()

    def S(name, shape, dt):
        return es.enter_context(nc.sbuf_tensor(name, shape, dt))

    banks = [es.enter_context(nc.psum_tensor(f"bank{i}", [128, 512], F32)) for i in range(8)]
    bctr = [0]

    def nb():
        i = bctr[0] % 8
        bctr[0] += 1
        return banks[i], f'B{i}'

    def bfv(bank):
        return bank[:].bitcast(BF16)

    wout = S("wout_s", [128, 16, 1024], BF16)
    wdn = S("wdn_s", [128, 8, 288], BF16)
    wkn = S("wkn_s", [128, 2, 1024], BF16)
    wv = S("wv_s", [128, 2, 1024], BF16)
    win = S("win_s", [128, 8, 1408], BF16)
    wuq = S("wuq_s", [128, 3, 1536], BF16)
    wst = [S(f"wst{i}", [128, 2048], F32) for i in range(2)]
    gkv = S("gkv_s", [128, 8], F32)
    gpre = S("gpre_s", [128, 8], F32)
    glat = S("glat_s", [128, 2], F32)
    gq = S("gq_s", [128, 3], F32)
    identf = S("identf", [128, 128], F32)
    identb = S("identb", [128, 128], BF16)
    posi = S("posi", [128, NTT], I32)
    posf = S("posf", [128, NTT], F32)
    invf = S("invf_s", [128, 16], F32)
    ang = S("ang", [128, NTT, 16], F32)
    uu = S("uu", [128, NTT, 16], F32)
    ki = S("ki", [128, NTT, 16], I32)
    kf = S("kf", [128, NTT, 16], F32)
    gg = S("gg", [128, NTT, 16], F32)
    m1 = S("m1", [128, NTT, 16], F32)
    gc = S("gc", [128, NTT, 16], F32)
    sinT = S("sinT", [128, NTT, 16], F32)
    cosT = S("cosT", [128, NTT, 16], F32)
    xin = [S(f"xin{i}", [128, 1024], F32) for i in range(2)]
    ynin = [S(f"ynin{i}", [128, 2048], BF16) for i in range(2)]
    ynT = S("ynT", [128, 16, 128], BF16)
    h1 = [S(f"h1_{i}", [128, 1024], F32) for i in range(2)]
    junk = S("junk", [128, 1024], BF16)
    ss = S("ss", [128, 1], F32)
    rt = S("rt", [128, 1], F32)
    rstd = S("rstd", [128, 1], F32)
    hnb = S("hnb", [128, 1024], BF16)
    hT = S("hT", [128, 8, 128], BF16)
    junk2 = S("junk2", [128, 384], BF16)
    ssc = S("ssc", [128, 1], F32)
    rtc = S("rtc", [128, 1], F32)
    rstdc = S("rstdc", [128, 1], F32)
    ckvn = S("ckvn", [128, 256], BF16)
    ra = S("ra", [128, 16], F32)
    rb = S("rb", [128, 16], F32)
    krb = S("krb", [128, 32], BF16)
    ckT = S("ckT", [128, 2, 128], BF16)
    krT = [S(f"krT{i}", [32, 128], BF16) for i in range(2)]
    knT = [S(f"knT{i}", [128, 8, 128], BF16) for i in range(2)]
    vsb = [S(f"vsb{i}", [128, 1024], BF16) for i in range(2)]
    ssq = S("ssq", [128, 1], F32)
    rtq = S("rtq", [128, 1], F32)
    rstdq = S("rstdq", [128, 1], F32)
    cqn = S("cqn", [128, 384], BF16)
    sg = [S(f"sg{i}", [128, 8, 128], BF16) for i in range(2)]
    cqT = S("cqT", [128, 3, 128], BF16)
    qtok = S("qtok", [128, 16, 96], BF16)
    qa = S("qa", [128, 16, 16], F32)
    qb = S("qb", [128, 16, 16], F32)
    qT = [S(f"qT{i}", [96, 16, 128], BF16) for i in range(2)]

    P.ld(gkv[:], gkv_d, ['gkv'], 'c0')
    P.ld(gpre[:], gpre_d, ['gpre'], 'c1')
    P.ld(glat[:], glat_d, ['glat'], 'c2')
    P.ld(gq[:], gq_d, ['gq'], 'c3')
    P.ld(posi[:], pos_d, ['posi'], 'c4')
    P.ld(invf[:], invf_d.partition_broadcast(128), ['invf'], 'c5')
    P.ms('pool', identf[:], 1.0, ['identf'])
    P.add('pool', lambda e: e.affine_select(out=identf[:], in_=identf[:], pattern=[[-1, 128]], compare_op=ALU.is_equal,
                                            fill=0.0, base=0, channel_multiplier=1), r=['identf'], w=['identf'])
    P.cp('dve', identb[:], identf[:], r=['identf'], w=['identb'])
    P.cp('dve', posf[:], posi[:], r=['posi'], w=['posf'])
    P.tt('dve', ang[:], posf[:].unsqueeze(2).to_broadcast([128, NTT, 16]), invf[:].unsqueeze(1).to_broadcast([128, NTT, 16]),
         ALU.mult, r=['posf', 'invf'], w=['ang'])
    P.ts('dve', uu[:], ang[:], 1.0 / TWO_PI, None, ALU.mult, r=['ang'], w=['uu'])
    P.cp('dve', ki[:], uu[:], r=['uu'], w=['ki'])
    P.cp('dve', kf[:], ki[:], r=['ki'], w=['kf'])
    P.stt(gg[:], kf[:], -CW1, ang[:], ALU.mult, ALU.add, r=['kf', 'ang'], w=['gg'])
    P.stt(gg[:], kf[:], -CW2, gg[:], ALU.mult, ALU.add, r=['kf', 'gg'], w=['gg'])
    P.ts('dve', gg[:], gg[:], 1.0 / TWO_PI, None, ALU.mult, r=['gg'], w=['gg'])

    def wrap():
        P.ts('dve', m1[:], gg[:], 0.5, None, ALU.is_gt, r=['gg'], w=['m1'])
        P.tt('dve', gg[:], gg[:], m1[:], ALU.subtract, r=['gg', 'm1'], w=['gg'])
        P.ts('dve', m1[:], gg[:], -0.5, None, ALU.is_lt, r=['gg'], w=['m1'])
        P.tt('dve', gg[:], gg[:], m1[:], ALU.add, r=['gg', 'm1'], w=['gg'])
        P.ts('dve', gg[:], gg[:], 0.4999995, -0.4999995, ALU.min, ALU.max, r=['gg'], w=['gg'])

    wrap()
    P.actv(sinT[:], gg[:], AF.Sin, scale=TWO_PI, r=['gg'], w=['sinT'])
    P.ts('dve', gg[:], gg[:], 0.25, None, ALU.add, r=['gg'], w=['gg'])
    wrap()
    P.actv(cosT[:], gg[:], AF.Sin, scale=TWO_PI, r=['gg'], w=['cosT'])

    wi = [0]
    WK = {}

    def wload(grp, dst_ap, src_ap, ncols, gain_ap, in_view=None):
        i = wi[0] % 2
        wi[0] += 1
        key = f'W{wi[0]}'
        WK.setdefault(grp, []).append(key)
        P.ld(wst[i][:, 0:ncols], src_ap, [f'wst{i}'], f'wst{i}')
        src = wst[i][:, 0:ncols] if in_view is None else in_view(wst[i])
        if i == 0:
            if gain_ap is None:
                P.cp('dve', dst_ap, src, r=[f'wst{i}'], w=[key])
            else:
                P.ts('dve', dst_ap, src, gain_ap, None, ALU.mult, r=[f'wst{i}', 'gkv', 'gpre', 'glat', 'gq'], w=[key])
        else:
            if gain_ap is None:
                P.cp('act', dst_ap, src, r=[f'wst{i}'], w=[key])
            else:
                P.actv(dst_ap, src, AF.Copy, scale=gain_ap, r=[f'wst{i}', 'gkv', 'gpre', 'glat', 'gq'], w=[key])

    for kt in range(16):
        wload('wout', wout[:, kt, :], wout_d[kt * 128:(kt + 1) * 128, :], 1024, None)
    for kt in range(8):
        wload('wdn', wdn[:, kt, :], wdn_d[kt * 128:(kt + 1) * 128, :], 288, gkv[:, kt:kt + 1])
    for kt in range(8):
        wload('win', win[:, kt, :], win_d[kt * 128:(kt + 1) * 128, :], 1408, gpre[:, kt:kt + 1])
    for kt in range(2):
        wload('wkn', wkn[:, kt, :].rearrange("p (h c) -> p h c", h=16), wup_d[kt * 128:(kt + 1) * 128, :], 2048, glat[:, kt:kt + 1],
              in_view=lambda t: t[:, 0:2048].rearrange("p (h c) -> p h c", h=16)[:, :, 0:64])
        wload('wv', wv[:, kt, :].rearrange("p (h c) -> p h c", h=16), wup_d[kt * 128:(kt + 1) * 128, :], 2048, glat[:, kt:kt + 1],
              in_view=lambda t: t[:, 0:2048].rearrange("p (h c) -> p h c", h=16)[:, :, 64:128])
    for kt in range(3):
        wload('wuq', wuq[:, kt, 0:1024].rearrange("p (h c) -> p h c", h=16), wuq_d[kt * 128:(kt + 1) * 128, :], 1536, gq[:, kt:kt + 1],
              in_view=lambda t: t[:, 0:1536].rearrange("p (h c) -> p h c", h=16)[:, :, 0:64])
        wload('wuq', wuq[:, kt, 1024:1280].rearrange("p (h c) -> p h c", h=16), wuq_d[kt * 128:(kt + 1) * 128, :], 1536, gq[:, kt:kt + 1],
              in_view=lambda t: t[:, 0:1536].rearrange("p (h c) -> p h c", h=16)[:, :, 64:80])
        wload('wuq', wuq[:, kt, 1280:1536].rearrange("p (h c) -> p h c", h=16), wuq_d[kt * 128:(kt + 1) * 128, :], 1536, gq[:, kt:kt + 1],
              in_view=lambda t: t[:, 0:1536].rearrange("p (h c) -> p h c", h=16)[:, :, 80:96])


    def load_t(tt):
        sl = tt % 2
        P.ld(xin[sl][:], x_d[tt * 128:(tt + 1) * 128, :], [f'xin{sl}'], f'xin{sl}')
        P.ld(ynin[sl][:], yn_d[tt * 128:(tt + 1) * 128, :], [f'ynin{sl}'], f'ynin{sl}')

    load_t(0)
    for tt in range(ntt):
        sl = tt % 2
        tok = slice(tt * 128, (tt + 1) * 128)
        if tt + 1 < ntt:
            load_t(tt + 1)
        for half in range(2):
            bk, bkk = nb()
            pv = bfv(bk).rearrange("p (k t) -> p k t", k=8)
            for j in range(8):
                c = half * 8 + j
                P.tr(pv[:, j, :], ynin[sl][:, c * 128:(c + 1) * 128], identb[:], r=[f'ynin{sl}', 'identb'], w=[bkk])
            P.cp('dve' if half == 0 else 'act', ynT[:, half * 8:(half + 1) * 8, :], pv, r=[bkk], w=[f'ynT{half}'])
        for half in range(2):
            bk, bkk = nb()
            for c in range(16):
                P.mm(bk[:, :], ynT[:, c, :], wout[:, c, half * 512:(half + 1) * 512], start=(c == 0), stop=(c == 15),
                     r=['ynT0', 'ynT1', *WK['wout']], w=[bkk])
            P.tt('dve', h1[sl][:, half * 512:(half + 1) * 512], bk[:, :], xin[sl][:, half * 512:(half + 1) * 512], ALU.add,
                 r=[bkk, f'xin{sl}'], w=[f'h1_{sl}'])
        P.ld(h1_d[tok, :], h1[sl][:], w=[f'h1d{sl}'], sem=f'sth{sl}', r=[f'h1_{sl}'])
        P.actv(junk[:], h1[sl][:], AF.Square, accum=ss[:], r=[f'h1_{sl}'], w=['junk', 'ss'])
        P.actv(rt[:], ss[:], AF.Sqrt, bias=EPS, scale=1.0 / 1024, r=['ss'], w=['rt'])
        P.add('dve', lambda e: e.reciprocal(out=rstd[:], in_=rt[:]), r=['rt'], w=['rstd'])
        P.ts('dve', hnb[:], h1[sl][:], rstd[:, 0:1], None, ALU.mult, r=[f'h1_{sl}', 'rstd'], w=['hnb'])
        bk, bkk = nb()
        pv = bfv(bk).rearrange("p (k t) -> p k t", k=8)
        for kt in range(8):
            P.tr(pv[:, kt, :], hnb[:, kt * 128:(kt + 1) * 128], identb[:], r=['hnb', 'identb'], w=[bkk])
        P.cp('act', hT[:], pv, r=[bkk], w=['hT'])
        bk, bkk = nb()
        for kt in range(8):
            P.mm(bk[:, 0:288], hT[:, kt, :], wdn[:, kt, :], start=(kt == 0), stop=(kt == 7), r=['hT', *WK['wdn']], w=[bkk])
        P.actv(junk2[:, 0:256], bk[:, 0:256], AF.Square, accum=ssc[:], r=[bkk], w=['junk2', 'ssc'])
        P.actv(rtc[:], ssc[:], AF.Sqrt, bias=EPS, scale=1.0 / 256, r=['ssc'], w=['rtc'])
        P.add('dve', lambda e: e.reciprocal(out=rstdc[:], in_=rtc[:]), r=['rtc'], w=['rstdc'])
        P.tt('dve', ra[:], bk[:, 256:272], cosT[:, tt, :], ALU.mult, r=[bkk, 'cosT'], w=['ra'])
        P.tt('dve', rb[:], bk[:, 272:288], sinT[:, tt, :], ALU.mult, r=[bkk, 'sinT'], w=['rb'])
        P.tt('dve', krb[:, 0:16], ra[:], rb[:], ALU.subtract, r=['ra', 'rb'], w=['krb'])
        P.tt('dve', ra[:], bk[:, 256:272], sinT[:, tt, :], ALU.mult, r=[bkk, 'sinT', 'krb'], w=['ra'])
        P.tt('dve', rb[:], bk[:, 272:288], cosT[:, tt, :], ALU.mult, r=[bkk, 'cosT', 'krb'], w=['rb'])
        P.tt('dve', krb[:, 16:32], ra[:], rb[:], ALU.add, r=['ra', 'rb'], w=['krb'])
        P.ts('dve', ckvn[:], bk[:, 0:256], rstdc[:, 0:1], None, ALU.mult, r=[bkk, 'rstdc'], w=['ckvn'])
        bk, bkk = nb()
        pv = bfv(bk)
        for kt in range(2):
            P.tr(pv[:, kt * 128:(kt + 1) * 128], ckvn[:, kt * 128:(kt + 1) * 128], identb[:], r=['ckvn', 'identb'], w=[bkk])
        P.tr(pv[0:32, 256:384], krb[:], identb[:], r=['krb', 'identb'], w=[bkk])
        P.cp('act', ckT[:], pv[:, 0:256].rearrange("p (k t) -> p k t", k=2), r=[bkk], w=['ckT'])
        P.cp('act', krT[sl][:], pv[0:32, 256:384], r=[bkk], w=[f'krT{sl}'])
        P.ld(kr_d[:, tok], krT[sl][:], w=[f'krd{sl}'], sem=f'stkr{sl}', r=[f'krT{sl}'])
        for half in range(2):
            bk, bkk = nb()
            for j in range(4):
                pr = half * 4 + j
                for kt in range(2):
                    P.mm(bk[:, j * 128:(j + 1) * 128], wkn[:, kt, pr * 128:(pr + 1) * 128], ckT[:, kt, :], start=(kt == 0), stop=(kt == 1),
                         r=['ckT', *WK['wkn']], w=[bkk])
            P.cp('act' if half == 0 else 'dve', knT[sl][:, half * 4:(half + 1) * 4, :], bk[:, :].rearrange("p (j t) -> p j t", j=4),
                 r=[bkk], w=[f'knT{sl}'])
        P.ld(kn_d[:, :, tok], knT[sl][:], w=[f'knd{sl}'], sem=f'stkn{sl}', r=[f'knT{sl}'])
        for half in range(2):
            bk, bkk = nb()
            for kt in range(2):
                P.mm(bk[:, :], ckT[:, kt, :], wv[:, kt, half * 512:(half + 1) * 512], start=(kt == 0), stop=(kt == 1),
                     r=['ckT', *WK['wv']], w=[bkk])
            P.cp('act' if half == 0 else 'dve', vsb[sl][:, half * 512:(half + 1) * 512], bk[:, :], r=[bkk], w=[f'vsb{sl}'])
        P.ld(v_d[tok, :], vsb[sl][:], w=[f'vd{sl}'], sem=f'stv{sl}', r=[f'vsb{sl}'])
        bk, bkk = nb()
        for kt in range(8):
            P.mm(bk[:, 0:384], hT[:, kt, :], win[:, kt, 0:384], start=(kt == 0), stop=(kt == 7), r=['hT', *WK['win']], w=[bkk])
        P.actv(junk2[:], bk[:, 0:384], AF.Square, accum=ssq[:], r=[bkk], w=['junk2', 'ssq'])
        P.actv(rtq[:], ssq[:], AF.Sqrt, bias=EPS, scale=1.0 / 384, r=['ssq'], w=['rtq'])
        P.add('dve', lambda e: e.reciprocal(out=rstdq[:], in_=rtq[:]), r=['rtq'], w=['rstdq'])
        P.ts('dve', cqn[:], bk[:, 0:384], rstdq[:, 0:1], None, ALU.mult, r=[bkk, 'rstdq'], w=['cqn'])
        for half in range(2):
            bk, bkk = nb()
            for j in range(4):
                ct = half * 4 + j
                for kt in range(8):
                    P.mm(bk[:, j * 128:(j + 1) * 128], win[:, kt, 384 + ct * 128:384 + (ct + 1) * 128], hT[:, kt, :],
                         start=(kt == 0), stop=(kt == 7), r=['hT', *WK['win']], w=[bkk])
            P.actv(sg[sl][:, half * 4:(half + 1) * 4, :], bk[:, :].rearrange("p (j t) -> p j t", j=4), AF.Silu, r=[bkk], w=[f'sg{sl}'])
        P.ld(sg_d[:, :, tok], sg[sl][:], w=[f'sgd{sl}'], sem=f'stsg{sl}', r=[f'sg{sl}'])
        bk, bkk = nb()
        pv = bfv(bk)
        for kt in range(3):
            P.tr(pv[:, kt * 128:(kt + 1) * 128], cqn[:, kt * 128:(kt + 1) * 128], identb[:], r=['cqn', 'identb'], w=[bkk])
        P.cp('act', cqT[:], pv[:, 0:384].rearrange("p (k t) -> p k t", k=3), r=[bkk], w=['cqT'])
        for blk in range(2):
            bk, bkk = nb()
            for kt in range(3):
                P.mm(bk[:, :], cqT[:, kt, :], wuq[:, kt, blk * 512:(blk + 1) * 512], start=(kt == 0), stop=(kt == 2),
                     r=['cqT', *WK['wuq']], w=[bkk])
            P.cp('act', qtok[:, blk * 8:(blk + 1) * 8, 0:64], bk[:, :].rearrange("p (h c) -> p h c", h=8), r=[bkk], w=['qtok'])
        bk, bkk = nb()
        for kt in range(3):
            P.mm(bk[:, :], cqT[:, kt, :], wuq[:, kt, 1024:1536], start=(kt == 0), stop=(kt == 2), r=['cqT', *WK['wuq']], w=[bkk])
        x1 = bk[:, 0:256].rearrange("p (h c) -> p h c", h=16)
        x2 = bk[:, 256:512].rearrange("p (h c) -> p h c", h=16)
        cb_ = cosT[:, tt, :].unsqueeze(1).to_broadcast([128, 16, 16])
        sb_ = sinT[:, tt, :].unsqueeze(1).to_broadcast([128, 16, 16])
        P.tt('dve', qa[:], x1, cb_, ALU.mult, r=[bkk, 'cosT'], w=['qa'])
        P.tt('dve', qb[:], x2, sb_, ALU.mult, r=[bkk, 'sinT'], w=['qb'])
        P.tt('dve', qtok[:, :, 64:80], qa[:], qb[:], ALU.subtract, r=['qa', 'qb'], w=['qtok'])
        P.tt('dve', qa[:], x1, sb_, ALU.mult, r=[bkk, 'sinT', 'qtok'], w=['qa'])
        P.tt('dve', qb[:], x2, cb_, ALU.mult, r=[bkk, 'cosT', 'qtok'], w=['qb'])
        P.tt('dve', qtok[:, :, 80:96], qa[:], qb[:], ALU.add, r=['qa', 'qb'], w=['qtok'])
        for half in range(2):
            bk, bkk = nb()
            pv = bfv(bk)[0:96, :].rearrange("p (h t) -> p h t", h=8)
            for j in range(8):
                P.tr(pv[:, j, :], qtok[:, half * 8 + j, :], identb[:], r=['qtok', 'identb'], w=[bkk])
            P.cp('act' if half == 0 else 'dve', qT[sl][:, half * 8:(half + 1) * 8, :], pv, r=[bkk], w=[f'qT{sl}'])
        P.ld(qT_d[:, :, tok], qT[sl][:], w=[f'qd{sl}'], sem=f'stq{sl}', r=[f'qT{sl}'])
    outk = []
    for sl in range(2):
        outk += [f'h1d{sl}', f'krd{sl}', f'knd{sl}', f'vd{sl}', f'sgd{sl}', f'qd{sl}']
    P.wait_all('sp', outk)
    P.emit()
    es.close()
    return nc
SCALE = 96.0 ** -0.5
LOOKAHEAD = 2


def build_stageC(nheads=4, nchunks=16):
    nc = bass.Bass("TRN2", target_bir_lowering=False)
    SQ = 8192
    LK = 8192
    NKT = LK // 128
    chunks = list(range(nchunks))
    kT_d = nc.dram_tensor("kT", [4, 96, 8192], BF16, kind="ExternalInput").ap()
    v_d = nc.dram_tensor("v", [4, 128, 64, 64], BF16, kind="ExternalInput").ap()
    qT_d = nc.dram_tensor("qT", [4, 96, SQ], BF16, kind="ExternalInput").ap()
    sg_d = nc.dram_tensor("sg", [4, 64, SQ], BF16, kind="ExternalInput").ap()
    og_d = nc.dram_tensor("og", [64, 4, SQ], BF16, kind="ExternalOutput").ap()

    P = P2(nc)
    es = contextlib.ExitStack()

    def S(name, shape, dt):
        return es.enter_context(nc.sbuf_tensor(name, shape, dt))

    banks = [es.enter_context(nc.psum_tensor(f"bank{i}", [128, 512], F32)) for i in range(8)]
    kT = [S(f"kT{i}", [96, LK], BF16) for i in range(2)]
    vh = [S(f"vh{i}", [128, NKT, 65], BF16) for i in range(2)]
    qh = [S(f"qh{i}", [96, SQ], BF16) for i in range(2)]
    sgh = [S(f"sgh{i}", [64, SQ], BF16) for i in range(2)]
    ogs = [S(f"ogs{i}", [64, 512], BF16) for i in range(2)]
    PT = [S(f"PT{i}", [128, 512], BF16) for i in range(4)]
    rrow = S("rrow", [65, 512], F32)
    onesr = S("onesr", [65, 64], F32)
    ot = [S(f"ot{i}", [64, 512], F32) for i in range(2)]
    tn = S("tn", [64, 512], BF16)

    P.ms('pool', onesr[:], 1.0, ['onesr'])
    for i in range(2):
        P.ms('pool', vh[i][:, :, 64:65], 1.0, [f'vh{i}'])

    def load_head(h):
        i = h % 2
        half = LK // 2
        P.ld(kT[i][:, 0:half], kT_d[h, :, 0:half], [f'kT{i}a'], f'kT{i}a')
        P.ld(kT[i][:, half:LK], kT_d[h, :, half:LK], [f'kT{i}b'], f'kT{i}b', eng='act')
        P.ld(vh[i][:, :, 0:64], v_d[h, :, 0:NKT, :], [f'vh{i}'], f'vh{i}', eng='pool')
        P.ld(qh[i][:], qT_d[h, :, :], [f'qh{i}'], f'qh{i}')
        P.ld(sgh[i][:], sg_d[h, :, :], [f'sgh{i}'], f'sgh{i}')

    load_head(0)
    sctr = [0]
    epi = [None]
    for h in range(nheads):
        i = h % 2
        if epi[0] is not None:
            epi[0]()
            epi[0] = None
        if h + 1 < nheads:
            load_head(h + 1)
        KK = [f'kT{i}a', f'kT{i}b']
        for qi, cj in enumerate(chunks):
            nk = (cj + 1) * 4
            qsl = slice(qi * 512, (qi + 1) * 512)
            po = banks[4 + qi % 2]
            pok = f'B{4 + qi % 2}'
            tiles = []
            for kt in range(nk):
                d = kt - (nk - 4)
                c0 = 128 * d if d > 0 else 0
                tiles.append((kt, d, c0))

            def emit_S(kt, d, c0):
                sb = sctr[0] % 4
                sctr[0] += 1
                ps = banks[sb]
                P.mm(ps[:, c0:512], kT[i][:, kt * 128:(kt + 1) * 128], qh[i][:, qi * 512 + c0:(qi + 1) * 512],
                     r=KK + [f'qh{i}'], w=[f'B{sb}'])
                P.actv(PT[sb][:, c0:512], ps[:, c0:512], AF.Exp, scale=SCALE, r=[f'B{sb}'], w=[f'PT{sb}'])
                if d >= 0:
                    blk = PT[sb][:, c0:c0 + 128]
                    P.add('pool', (lambda blk: (lambda e: e.affine_select(out=blk, in_=blk, pattern=[[1, 128]], compare_op=ALU.is_ge,
                                                                          fill=0.0, base=0, channel_multiplier=-1)))(blk),
                          r=[f'PT{sb}'], w=[f'PT{sb}'])
                return sb

            pend = []
            for n, (kt, d, c0) in enumerate(tiles):
                pend.append((emit_S(kt, d, c0), kt, c0))
                if n == min(2, len(tiles) - 1) and epi[0] is not None:
                    epi[0]()
                    epi[0] = None
                if len(pend) > LOOKAHEAD:
                    sb, kt2, c02 = pend.pop(0)
                    P.mm(po[0:65, c02:512], vh[i][:, kt2, :], PT[sb][:, c02:512], start=(kt2 == 0), stop=(kt2 == nk - 1),
                         r=[f'vh{i}', f'PT{sb}'], w=[pok])
            while pend:
                sb, kt2, c02 = pend.pop(0)
                P.mm(po[0:65, c02:512], vh[i][:, kt2, :], PT[sb][:, c02:512], start=(kt2 == 0), stop=(kt2 == nk - 1),
                     r=[f'vh{i}', f'PT{sb}'], w=[pok])
            P.add('dve', lambda e, po=po: e.reciprocal(out=rrow[64:65, :], in_=po[64:65, :]), r=[pok], w=['rrow'])
            oti = ot[qi % 2]
            P.cp('act', oti[:], po[0:64, :], r=[pok], w=[f'ot{qi % 2}'])

            def make_epi(h=h, i=i, qi=qi, qsl=qsl, oti=oti):
                def f():
                    prb = banks[6]
                    P.mm(prb[0:64, :], onesr[64:65, :], rrow[64:65, :], r=['onesr', 'rrow'], w=['B6'])
                    P.tt('dve', tn[:], oti[:], prb[0:64, :], ALU.mult, r=[f'ot{qi % 2}', 'B6'], w=['tn'])
                    ogi = ogs[qi % 2]
                    P.tt('pool', ogi[:], tn[:], sgh[i][:, qsl], ALU.mult, r=['tn', f'sgh{i}'], w=[f'ogs{qi % 2}'])
                    P.ld(og_d[:, h, qsl], ogi[:], w=[f'ogd{qi % 2}'], sem=f'sto{qi % 2}', r=[f'ogs{qi % 2}'])
                return f
            epi[0] = make_epi()
    if epi[0] is not None:
        epi[0]()
    P.wait_all('sp', ['ogd0', 'ogd1'])
    P.emit()
    es.close()
    return nc


def build_stageD():
    nc = bass.Bass("TRN2", target_bir_lowering=False)
    og_d = nc.dram_tensor("og", [128, 8, NTOK], BF16, kind="ExternalInput").ap()
    h1_d = nc.dram_tensor("h1", [NTOK, 1024], F32, kind="ExternalInput").ap()
    wo_d = nc.dram_tensor("wo", [1024, 1024], F32, kind="ExternalInput").ap()
    gf_d = nc.dram_tensor("gf", [1, 1024], F32, kind="ExternalInput").ap()
    out_d = nc.dram_tensor("out", [NTOK, 1024], F32, kind="ExternalOutput").ap()
    P = P2(nc)
    es = contextlib.ExitStack()

    def S(name, shape, dt):
        return es.enter_context(nc.sbuf_tensor(name, shape, dt))

    banks = [es.enter_context(nc.psum_tensor(f"bank{i}", [128, 512], F32)) for i in range(8)]
    ogT = S("ogT", [128, 8, NTOK], BF16)
    wo = S("wo_s", [128, 8, 1024], BF16)
    wst = [S(f"wst{i}", [128, 1024], F32) for i in range(2)]
    gf_bc = S("gf_bc", [128, 1024], F32)
    h1t = [S(f"h1t{i}", [128, 1024], F32) for i in range(2)]
    h2 = S("h2", [128, 1024], F32)
    junk = S("junk", [128, 1024], BF16)
    ss = S("ss", [128, 1], F32)
    rt = S("rt", [128, 1], F32)
    rstd = S("rstd", [128, 1], F32)
    outt = [S(f"outt{i}", [128, 1024], F32) for i in range(2)]
    P.ld(gf_bc[:], gf_d.partition_broadcast(128), ['gf_bc'], 'c0')
    for q in range(4):
        P.ld(ogT[:, :, q * 512:(q + 1) * 512], og_d[:, :, q * 512:(q + 1) * 512], [f'ogT{q}'], f'og{q}')
    wkeys = []
    for pr in range(8):
        i = pr % 2
        P.ld(wst[i][:], wo_d[pr * 128:(pr + 1) * 128, :], [f'wst{i}'], f'wst{i}')
        P.cp('dve' if i == 0 else 'act', wo[:, pr, :], wst[i][:], r=[f'wst{i}'], w=[f'wo{pr}'])
        wkeys.append(f'wo{pr}')
    def load_h1(tt):
        P.ld(h1t[tt % 2][:], h1_d[tt * 128:(tt + 1) * 128, :], [f'h1t{tt % 2}'], f'h1t{tt % 2}')

    load_h1(0)
    for tt in range(NTT):
        sl = tt % 2
        if tt + 1 < NTT:
            load_h1(tt + 1)
        for half in range(2):
            bk = banks[(tt % 2) * 2 + half]
            for pr in range(8):
                P.mm(bk[:, :], ogT[:, pr, tt * 128:(tt + 1) * 128], wo[:, pr, half * 512:(half + 1) * 512], start=(pr == 0), stop=(pr == 7),
                     r=[f'ogT{tt // 4}', wkeys[pr]], w=[f'B{(tt % 2) * 2 + half}'])
            P.tt('dve', h2[:, half * 512:(half + 1) * 512], bk[:, :], h1t[sl][:, half * 512:(half + 1) * 512], ALU.add,
                 r=[f'B{(tt % 2) * 2 + half}', f'h1t{sl}'], w=[f'h2_{half}'])
        P.actv(junk[:], h2[:], AF.Square, accum=ss[:], r=['h2_0', 'h2_1'], w=['junk', 'ss'])
        P.actv(rt[:], ss[:], AF.Sqrt, bias=EPS, scale=1.0 / 1024, r=['ss'], w=['rt'])
        P.add('dve', lambda e: e.reciprocal(out=rstd[:], in_=rt[:]), r=['rt'], w=['rstd'])
        P.stt(outt[sl][:], h2[:], rstd[:, 0:1], gf_bc[:], ALU.mult, ALU.mult, r=['h2_0', 'h2_1', 'rstd', 'gf_bc'], w=[f'outt{sl}'])
        P.ld(out_d[tt * 128:(tt + 1) * 128, :], outt[sl][:], w=[f'od{sl}'], sem=f'sto{sl}', r=[f'outt{sl}'])
    P.wait_all('sp', ['od0', 'od1'])
    P.emit()
    es.close()
    return nc


def _prepA(inp, b, g):
    w_in = inp['ssm_w_in'][0]
    w = np.concatenate([w_in[:, 2048 + g * 512:2048 + (g + 1) * 512], w_in[:, 4096 + g * 128:4096 + (g + 1) * 128],
                        w_in[:, 4608 + g * 128:4608 + (g + 1) * 128], w_in[:, 5120 + g * 8:5120 + (g + 1) * 8],
                        w_in[:, g * 512:(g + 1) * 512]], axis=1)
    cidx = np.concatenate([np.arange(g * 512, (g + 1) * 512), 2048 + np.arange(g * 128, (g + 1) * 128),
                           2560 + np.arange(g * 128, (g + 1) * 128)])
    cwc = inp['ssm_conv_w'][0][:, cidx]
    cw = cwc.T.reshape(6, 128, 4).transpose(1, 0, 2).reshape(128, 24)
    cb = inp['ssm_conv_b'][0][cidx].reshape(6, 128).T
    hs = slice(g * 8, (g + 1) * 8)
    C = np.ascontiguousarray
    return dict(x=C(inp['x'][b]), w=C(w), gpre=C(inp['g_pre'][0].reshape(8, 128).T), cw=C(cw), cb=C(cb),
                dtb=C(inp['ssm_dt_bias'][0][hs].reshape(1, 8)), alog=C(inp['ssm_A_log'][0][hs].reshape(1, 8)),
                dsk=C(inp['ssm_D'][0][hs].reshape(1, 8)), gout=C(inp['ssm_g_out'][0][g * 512:(g + 1) * 512].reshape(1, 512)))


def _prepB(inp, yn_b, b, j):
    C = np.ascontiguousarray
    tok = slice(j * NTOK, (j + 1) * NTOK)
    pos = np.asarray(inp['positions'][b][tok]).astype(np.int32).reshape(NTT, 128).T
    return dict(x=C(inp['x'][b][tok]), yn=C(yn_b[tok]), pos=C(pos), invf=np.array(INV_FREQ, dtype=np.float32).reshape(1, 16),
                wout=C(inp['ssm_w_out'][0]), wdn=C(inp['kv_w_down']), wup=C(inp['kv_w_up']), win=C(inp['mla_w_in'][0]),
                wuq=C(inp['mla_w_uq'][0]), gkv=C(inp['kv_g_in'].reshape(8, 128).T), gpre=C(inp['g_pre'][1].reshape(8, 128).T),
                glat=C(inp['kv_g_latent'].reshape(2, 128).T), gq=C(inp['mla_g_q'][0].reshape(3, 128).T))


def kernel(**inputs):
    inp = {k: np.asarray(v) for k, v in inputs.items()}
    C = np.ascontiguousarray
    cores = list(range(8))
    ncA = build_stageA()
    rA = run_bass_kernel_spmd(ncA, [_prepA(inp, c // 4, c % 4) for c in cores], core_ids=cores).results
    yn = [np.concatenate([rA[b * 4 + g]['yn'] for g in range(4)], axis=1) for b in range(2)]
    ncB = build_stageB()
    rB = run_bass_kernel_spmd(ncB, [_prepB(inp, yn[c // 4], c // 4, c % 4) for c in cores], core_ids=cores).results
    imC = []
    for c in cores:
        b, hg = c // 4, c % 4
        kn = np.concatenate([rB[b * 4 + j]['kn'] for j in range(4)], axis=2)
        kr = np.concatenate([rB[b * 4 + j]['kr'] for j in range(4)], axis=1)
        vf = np.concatenate([rB[b * 4 + j]['v'] for j in range(4)], axis=0)
        qf = np.concatenate([rB[b * 4 + j]['qT'] for j in range(4)], axis=2)
        sf = np.concatenate([rB[b * 4 + j]['sg'] for j in range(4)], axis=2)
        kT = np.empty((4, 96, 8192), dtype=kn.dtype)
        v4 = np.empty((4, 128, 64, 64), dtype=vf.dtype)
        q4 = np.empty((4, 96, 8192), dtype=qf.dtype)
        s4 = np.empty((4, 64, 8192), dtype=sf.dtype)
        for hl in range(4):
            h = hg * 4 + hl
            kT[hl, 0:64] = kn[(h % 2) * 64:(h % 2) * 64 + 64, h // 2, :]
            kT[hl, 64:96] = kr
            v4[hl] = vf[:, h * 64:(h + 1) * 64].reshape(64, 128, 64).transpose(1, 0, 2)
            q4[hl] = qf[:, h, :]
            s4[hl] = sf[(h % 2) * 64:(h % 2) * 64 + 64, h // 2, :]
        imC.append(dict(kT=kT, v=v4, qT=q4, sg=s4))
    ncC = build_stageC()
    rC = run_bass_kernel_spmd(ncC, imC, core_ids=cores).results
    imD = []
    for c in cores:
        b, j = c // 4, c % 4
        tok = slice(j * NTOK, (j + 1) * NTOK)
        og = np.concatenate([rC[b * 4 + hg]['og'][:, :, tok] for hg in range(4)], axis=1)
        og = og.reshape(64, 8, 2, NTOK).transpose(2, 0, 1, 3).reshape(128, 8, NTOK)
        imD.append(dict(og=C(og), h1=rB[c]['h1'], wo=C(inp['mla_w_out'][0]), gf=C(inp['g_final'].reshape(1, 1024))))
    ncD = build_stageD()
    rD = run_bass_kernel_spmd(ncD, imD, core_ids=cores).results
    out = np.stack([np.concatenate([rD[b * 4 + j]['out'] for j in range(4)], axis=0) for b in range(2)], axis=0)
    return out.astype(np.float32)
```

```python
import contextlib
import math
from concourse.bass_utils import run_bass_kernel_spmd
import numpy as np
import concourse.bass as bass
import concourse.mybir as mybir

F32 = mybir.dt.float32
BF16 = mybir.dt.bfloat16
I32 = mybir.dt.int32
AF = mybir.ActivationFunctionType
ALU = mybir.AluOpType
AX = mybir.AxisListType


class Prog:
    def __init__(self, nc):
        self.nc = nc
        self.ops = []
        self.lastw = {}
        self.readers = {}
        self.dma_sems = {}

    def add(self, eng, fn, r=(), w=(), dma=None, group=False):
        deps = set()
        for k in r:
            if k in self.lastw:
                deps.add(self.lastw[k])
            if k[0] == 'B' and k[1:].isdigit():
                for j in self.readers.get(k, ()):
                    if self.ops[j]['eng'] != eng:
                        deps.add(j)
        for k in w:
            if k in self.lastw:
                deps.add(self.lastw[k])
            deps.update(self.readers.get(k, ()))
        i = len(self.ops)
        self.ops.append(dict(eng=eng, fn=fn, deps=deps, dma=dma, group=group, has_dep=False))
        for k in r:
            self.readers.setdefault(k, []).append(i)
        for k in w:
            self.lastw[k] = i
            self.readers[k] = []
        return i

    def pe(self, fn, r=(), w=()):
        return self.add('pe', fn, r, w)

    def act(self, fn, r=(), w=()):
        return self.add('act', fn, r, w)

    def dve(self, fn, r=(), w=()):
        return self.add('dve', fn, r, w)

    def pool(self, fn, r=(), w=()):
        return self.add('pool', fn, r, w)

    def dma(self, eng, fn, r=(), w=(), sem=None, group=False):
        assert sem is not None
        return self.add(eng, fn, r, w, dma=sem, group=group)

    def wait_all(self, eng, keys):
        return self.add(eng, None, r=keys, w=())

    def emit(self):
        nc = self.nc
        ops = self.ops
        engs = ['sp', 'act', 'dve', 'pool', 'pe']
        for o in ops:
            for d in o['deps']:
                if ops[d]['eng'] == 'pe' and o['eng'] == 'pe' and ops[d]['dma'] is None and o['dma'] is None:
                    continue
                ops[d]['has_dep'] = True
        esem = {e: nc.alloc_semaphore(name=f"s_{e}") for e in engs}
        group_tot = {}
        for o in ops:
            if o['dma'] is not None:
                if o['dma'] not in self.dma_sems:
                    self.dma_sems[o['dma']] = nc.alloc_semaphore(name=f"d_{o['dma']}")
                group_tot[o['dma']] = group_tot.get(o['dma'], 0) + 1
        cnt = {e: 0 for e in engs}
        dcnt = {}
        for o in ops:
            if o['fn'] is None:
                o['tok'] = None
            elif o['dma'] is not None:
                k = o['dma']
                dcnt[k] = dcnt.get(k, 0) + 1
                v = group_tot[k] if o['group'] else dcnt[k]
                o['tok'] = (('d', k), 16 * v)
            elif o['has_dep']:
                cnt[o['eng']] += 1
                o['tok'] = (('e', o['eng']), cnt[o['eng']])
            else:
                o['tok'] = None
        known = {e: {} for e in engs}
        for o in ops:
            e = o['eng']
            kn = known[e]
            waits = []
            for d in sorted(o['deps'], reverse=True):
                od = ops[d]
                if od['tok'] is None:
                    continue
                if od['eng'] == 'pe' and e == 'pe' and od['dma'] is None and o['dma'] is None:
                    continue
                s, v = od['tok']
                if kn.get(s, 0) < v:
                    waits.append((s, v))
                    kn[s] = v
                    for s2, v2 in od['clock'].items():
                        if kn.get(s2, 0) < v2:
                            kn[s2] = v2
            wm = {}
            for s, v in waits:
                wm[s] = max(wm.get(s, 0), v)
            o['waits'] = wm
            o['clock'] = dict(kn)

        def semof(s):
            return esem[s[1]] if s[0] == 'e' else self.dma_sems[s[1]]

        def run(ename, eng):
            for o in ops:
                if o['eng'] != ename:
                    continue
                for s, v in o['waits'].items():
                    eng.wait_ge(semof(s), v)
                if o['fn'] is None:
                    continue
                inst = o['fn'](eng)
                if o['tok'] is not None:
                    s, v = o['tok']
                    inst.then_inc(semof(s), 16 if s[0] == 'd' else 1)

        with nc.Block() as block:
            @block.sync
            def _(e):
                run('sp', e)

            @block.scalar
            def _(e):
                run('act', e)

            @block.vector
            def _(e):
                run('dve', e)

            @block.gpsimd
            def _(e):
                run('pool', e)

            @block.tensor
            def _(e):
                run('pe', e)
        n = {e: sum(1 for o in ops if o['eng'] == e) for e in engs}
        nw = sum(len(o['waits']) for o in ops)
        print("PROG ops", n, "waits", nw, "sems", 5 + len(self.dma_sems), flush=True)


def _kw(**k):
    return {a: b for a, b in k.items() if b is not None}


class P2(Prog):
    def mm(self, out, lhsT, rhs, start=True, stop=True, r=(), w=()):
        return self.add('pe', lambda e: e.matmul(out, lhsT=lhsT, rhs=rhs, start=start, stop=stop), r, w)

    def tr(self, out, in_, ident, r=(), w=()):
        return self.add('pe', lambda e: e.transpose(out, in_, ident), r, w)

    def actv(self, out, in_, func, bias=None, scale=None, accum=None, r=(), w=()):
        kw = _kw(bias=bias, scale=scale, accum_out=accum)
        return self.add('act', lambda e: e.activation(out=out, in_=in_, func=func, **kw), r, w)

    def ts(self, eng, out, in0, s1, s2=None, op0=ALU.mult, op1=None, r=(), w=()):
        kw = _kw(op1=op1)
        return self.add(eng, lambda e: e.tensor_scalar(out=out, in0=in0, scalar1=s1, scalar2=s2, op0=op0, **kw), r, w)

    def tt(self, eng, out, in0, in1, op, r=(), w=()):
        return self.add(eng, lambda e: e.tensor_tensor(out=out, in0=in0, in1=in1, op=op), r, w)

    def stt(self, out, in0, scalar, in1, op0, op1, r=(), w=()):
        return self.add('dve', lambda e: e.scalar_tensor_tensor(out=out, in0=in0, scalar=scalar, in1=in1, op0=op0, op1=op1), r, w)

    def cp(self, eng, out, in_, r=(), w=()):
        if eng == 'act':
            return self.add('act', lambda e: e.activation(out=out, in_=in_, func=AF.Copy), r, w)
        return self.add(eng, lambda e: e.tensor_copy(out=out, in_=in_), r, w)

    def ms(self, eng, ap, val, w=()):
        return self.add(eng, lambda e: e.memset(ap, val), (), w)

    def ld(self, out, in_, w, sem, eng='sp', group=False, r=()):
        return self.dma(eng, lambda e: e.dma_start(out=out, in_=in_), r=r, w=w, sem=sem, group=group)

SEQ = 8192
DM = 1024
NCH = SEQ // 256
EPS = 1e-6
WCOLS = 1288


def build_stageA(nch=NCH):
    nc = bass.Bass("TRN2", target_bir_lowering=False)
    x_d = nc.dram_tensor("x", [SEQ, DM], F32, kind="ExternalInput").ap()
    w_d = nc.dram_tensor("w", [DM, WCOLS], F32, kind="ExternalInput").ap()
    gpre_d = nc.dram_tensor("gpre", [128, 8], F32, kind="ExternalInput").ap()
    cw_d = nc.dram_tensor("cw", [128, 24], F32, kind="ExternalInput").ap()
    cb_d = nc.dram_tensor("cb", [128, 6], F32, kind="ExternalInput").ap()
    dtb_d = nc.dram_tensor("dtb", [1, 8], F32, kind="ExternalInput").ap()
    alog_d = nc.dram_tensor("alog", [1, 8], F32, kind="ExternalInput").ap()
    dsk_d = nc.dram_tensor("dsk", [1, 8], F32, kind="ExternalInput").ap()
    gout_d = nc.dram_tensor("gout", [1, 512], F32, kind="ExternalInput").ap()
    yn_d = nc.dram_tensor("yn", [SEQ, 512], BF16, kind="ExternalOutput").ap()

    P = P2(nc)
    es = contextlib.ExitStack()

    def S(name, shape, dt):
        return es.enter_context(nc.sbuf_tensor(name, shape, dt))

    banks = [es.enter_context(nc.psum_tensor(f"bank{i}", [128, 512], F32)) for i in range(8)]

    W = S("W", [128, 8, WCOLS], BF16)
    wst = [S(f"wst{i}", [128, WCOLS], F32) for i in range(2)]
    gpre = S("gpre_s", [128, 8], F32)
    cw = S("cw_s", [128, 24], F32)
    cb = S("cb_s", [128, 6], F32)
    dtb_bc = S("dtb_bc", [128, 8], F32)
    A_bc = S("A_bc", [128, 8], F32)
    D_bc = S("D_bc", [128, 8], F32)
    gout_bc = S("gout_bc", [128, 512], F32)
    identf = S("identf", [128, 128], F32)
    identb = S("identb", [128, 128], BF16)
    onesf = S("onesf", [128, 128], F32)
    onesb = S("onesb", [128, 128], BF16)
    trif = S("trif", [128, 128], F32)
    triw = S("triw", [128, 256], BF16)
    SU = S("SU", [128, 128], BF16)
    cdiag = S("cdiag", [128, 24, 128], BF16)
    Dident = S("Dident", [128, 8, 128], BF16)
    xin = [S(f"xin{i}", [128, 2, DM], F32) for i in range(2)]
    junk = [S(f"junk{i}", [128, DM], BF16) for i in range(2)]
    ss = S("ss", [128, 2], F32)
    rt = S("rt", [128, 2], F32)
    rstd = S("rstd", [128, 2], F32)
    hn = S("hn", [128, 2, DM], BF16)
    hnT = S("hnT", [128, 8, 256], BF16)
    ubuf = S("ubuf", [128, 6, 259], BF16)
    xc = [S(f"xc{i}", [128, 6, 256], BF16) for i in range(2)]
    xtok = S("xtok", [128, 2, 640], BF16)
    dtr = S("dtr", [128, 2, 8], F32)
    e1 = S("e1", [128, 2, 8], F32)
    dtk = [S(f"dtk{i}", [128, 2, 8], F32) for i in range(2)]
    dtA = [S(f"dtA{i}", [128, 2, 8], F32) for i in range(2)]
    cend = S("cend", [128, 8], F32)
    ecum = [S(f"ecum{i}", [128, 2, 8], F32) for i in range(2)]
    wtmp = [S(f"wtmp{i}", [128, 2, 8], F32) for i in range(2)]
    dec = [S(f"dec{i}", [128, 8], F32) for i in range(2)]
    W0 = S("W0", [128, 8, 256], BF16)
    V1 = S("V1", [128, 8, 128], BF16)
    CBm = S("CBm", [128, 384], BF16)
    xdt = S("xdt", [128, 2, 512], BF16)
    Lb = [S(f"Lb{i}", [128, 384], BF16) for i in range(2)]
    junk2b = S("junk2b", [128, 512], BF16)
    MT = S("MT", [128, 8, 384], BF16)
    state = S("state", [128, 512], F32)
    state_bf = S("state_bf", [128, 512], BF16)
    yi = S("yi", [128, 512], F32)
    t1 = S("t1", [128, 512], F32)
    ysb = S("ysb", [128, 512], F32)
    zs = [S(f"zs{i}", [128, 2, 512], F32) for i in range(2)]
    yg = [S(f"yg{i}", [128, 512], F32) for i in range(2)]
    junk2 = S("junk2", [128, 512], BF16)
    ss2 = S("ss2", [128, 2], F32)
    rt2 = S("rt2", [128, 2], F32)
    rstd2 = S("rstd2", [128, 2], F32)
    yn = [S(f"yn{i}", [128, 512], BF16) for i in range(2)]
    wx = S("wx", [128, 2, 512], BF16)

    def bfview(bank):
        return bank[:].bitcast(BF16)

    ptr = bfview(banks[0]).rearrange("p (k t) -> p k t", k=8)
    pX = banks[1]
    pCv = banks[2]
    pdtk = banks[3][:, 0:16].rearrange("p (t c) -> p t c", t=2)
    pcum = banks[3][:, 16:32].rearrange("p (t c) -> p t c", t=2)
    pce = banks[3][:, 32:40]
    pz = banks[3][:, :]
    ptx = bfview(banks[4])[:, 0:640]
    pCB = banks[4][:, 0:384]
    pseg = [banks[5][:, 0:384], banks[7][:, 0:384]]
    py = banks[6][:, :]
    pyi = banks[4][:, :]
    pst = banks[4][:, :]

    P.ld(gpre[:], gpre_d, ['gpre'], 'c0')
    P.ld(cw[:], cw_d, ['cw'], 'c1')
    P.ld(cb[:], cb_d, ['cb'], 'c2')
    P.ld(dtb_bc[:], dtb_d.partition_broadcast(128), ['dtb_bc'], 'c3')
    P.ld(A_bc[:], alog_d.partition_broadcast(128), ['A_bc'], 'c4')
    P.ld(D_bc[:], dsk_d.partition_broadcast(128), ['D_bc'], 'c5')
    P.ld(gout_bc[:], gout_d.partition_broadcast(128), ['gout_bc'], 'c6')
    P.ms('pool', identf[:], 1.0, ['identf'])
    P.add('pool', lambda e: e.affine_select(out=identf[:], in_=identf[:], pattern=[[-1, 128]], compare_op=ALU.is_equal,
                                            fill=0.0, base=0, channel_multiplier=1), r=['identf'], w=['identf'])
    P.cp('dve', identb[:], identf[:], r=['identf'], w=['identb'])
    P.ms('pool', onesf[:], 1.0, ['onesf'])
    P.ms('pool', onesb[:], 1.0, ['onesb'])
    P.ms('pool', triw[:], 1.0, ['triw'])
    P.add('pool', lambda e: e.affine_select(out=triw[:, 0:128], in_=triw[:, 0:128], pattern=[[1, 128]], compare_op=ALU.is_ge,
                                            fill=0.0, base=0, channel_multiplier=-1), r=['triw'], w=['triw'])
    P.cp('dve', trif[:], triw[:, 0:128], r=['triw'], w=['trif'])
    P.ms('pool', SU[:], 1.0, ['SU'])
    P.add('pool', lambda e: e.affine_select(out=SU[:], in_=SU[:], pattern=[[-1, 128]], compare_op=ALU.is_gt,
                                            fill=0.0, base=0, channel_multiplier=1), r=['SU'], w=['SU'])
    P.ms('pool', ubuf[:], 0.0, ['ubuf%d' % i for i in range(3)])
    P.ms('pool', state[:], 0.0, ['state'])
    P.ms('pool', state_bf[:], 0.0, ['state_bf'])
    for kt in range(8):
        P.ld(wst[kt % 2][:], w_d[kt * 128:(kt + 1) * 128, :], [f'wst{kt % 2}'], f'wst{kt % 2}')
        if kt % 2 == 0:
            P.ts('dve', W[:, kt, :], wst[kt % 2][:], gpre[:, kt:kt + 1], None, ALU.mult, r=[f'wst{kt % 2}', 'gpre'], w=[f'W{kt}'])
        else:
            P.actv(W[:, kt, :], wst[kt % 2][:], AF.Copy, scale=gpre[:, kt:kt + 1], r=[f'wst{kt % 2}', 'gpre'], w=[f'W{kt}'])
    Wk = [f'W{kt}' for kt in range(8)]
    HNT = ['hnT0', 'hnT1']
    for i in range(24):
        P.ts('dve', cdiag[:, i, :], identf[:], cw[:, i:i + 1], None, ALU.mult, r=['identf', 'cw'], w=['cdiag'])
    for h in range(8):
        P.ts('dve', Dident[:, h, :], identf[:], D_bc[:, h:h + 1], None, ALU.mult, r=['identf', 'D_bc'], w=['Dident'])
    P.actv(A_bc[:], A_bc[:], AF.Exp, r=['A_bc'], w=['A_bc'])
    P.ts('dve', A_bc[:], A_bc[:], -1.0, None, ALU.mult, r=['A_bc'], w=['A_bc'])

    def load_x(c):
        sl = c % 2
        P.ld(xin[sl][:], x_d[c * 256:(c + 1) * 256, :].rearrange("(t p) d -> p t d", p=128), [f'xin{sl}'], f'xin{sl}')

    def front(c):
        sl = c % 2
        p = c % 2
        xk = f'xin{sl}'
        if c + 1 < nch:
            load_x(c + 1)
        for t in range(2):
            P.actv(junk[t][:], xin[sl][:, t, :], AF.Square, accum=ss[:, t:t + 1], r=[xk], w=[f'ss{t}', f'junk{t}'])
        P.actv(rt[:], ss[:], AF.Ln, bias=EPS, scale=1.0 / DM, r=['ss0', 'ss1'], w=['rt'])
        P.actv(rstd[:], rt[:], AF.Exp, scale=-0.5, r=['rt'], w=['rstd'])
        for t in range(2):
            P.ts('dve', hn[:, t, :], xin[sl][:, t, :], rstd[:, t:t + 1], None, ALU.mult, r=[xk, 'rstd'], w=[f'hn{t}'])
        yield
        for t in range(2):
            for kt in range(8):
                P.tr(ptr[:, kt, :], hn[:, t, kt * 128:(kt + 1) * 128], identb[:], r=[f'hn{t}', 'identb'], w=['B0'])
            P.cp('dve' if t == 0 else 'act', hnT[:, :, t * 128:(t + 1) * 128], ptr, r=['B0'], w=[f'hnT{t}'])
            yield
        for t in range(2):
            for kt in range(8):
                P.mm(pdtk[:, t, :], hnT[:, kt, t * 128:(t + 1) * 128], W[:, kt, 768:776], start=(kt == 0), stop=(kt == 7),
                     r=HNT + [Wk[kt]], w=['B3'])
        P.tt('dve', dtr[:], pdtk, dtb_bc[:].unsqueeze(1).to_broadcast([128, 2, 8]), ALU.add, r=['B3', 'dtb_bc'], w=['dtr'])
        P.actv(e1[:], dtr[:], AF.Exp, r=['dtr'], w=['e1'])
        P.actv(dtk[p][:], e1[:], AF.Ln, bias=1.0, r=['e1'], w=[f'dtk{p}'])
        P.tt('dve', dtA[p][:], dtk[p][:], A_bc[:].unsqueeze(1).to_broadcast([128, 2, 8]), ALU.mult, r=[f'dtk{p}', 'A_bc'], w=[f'dtA{p}'])
        P.mm(pcum[:, 0, :], trif[:], dtA[p][:, 0, :], r=['trif', f'dtA{p}'], w=['B3'])
        P.mm(pcum[:, 1, :], onesf[:], dtA[p][:, 0, :], start=True, stop=False, r=['onesf', f'dtA{p}'], w=['B3'])
        P.mm(pcum[:, 1, :], trif[:], dtA[p][:, 1, :], start=False, stop=True, r=['trif', f'dtA{p}'], w=['B3'])
        P.mm(pce, onesf[:], dtA[p][:, 0, :], start=True, stop=False, r=['onesf', f'dtA{p}'], w=['B3'])
        P.mm(pce, onesf[:], dtA[p][:, 1, :], start=False, stop=True, r=['onesf', f'dtA{p}'], w=['B3'])
        P.actv(ecum[p][:], pcum, AF.Exp, r=['B3'], w=[f'ecum{p}'])
        P.actv(dec[p][:], pce, AF.Exp, r=['B3'], w=[f'dec{p}'])
        P.cp('act', cend[:], pce, r=['B3'], w=['cend'])
        P.tt('dve', wtmp[p][:], cend[:].unsqueeze(1).to_broadcast([128, 2, 8]), pcum, ALU.subtract, r=['cend', 'B3'], w=[f'wtmp{p}'])
        P.actv(wtmp[p][:], wtmp[p][:], AF.Exp, r=[f'wtmp{p}'], w=[f'wtmp{p}'])
        yield
        for pr in range(3):
            for j in range(2):
                ct = 2 * pr + j
                for kt in range(8):
                    P.mm(pX[:, j * 256:(j + 1) * 256], W[:, kt, ct * 128:(ct + 1) * 128], hnT[:, kt, :], start=(kt == 0), stop=(kt == 7),
                         r=HNT + [Wk[kt]], w=['B1'])
            P.cp('act', ubuf[:, 2 * pr:2 * pr + 2, 3:259], pX[:, :].rearrange("p (j t) -> p j t", j=2), r=['B1'], w=[f'ubuf{pr}'])
            for j in range(2):
                ct = 2 * pr + j
                for k in range(4):
                    P.mm(pCv[:, j * 256:(j + 1) * 256], cdiag[:, ct * 4 + k, :], ubuf[:, ct, k:k + 256], start=(k == 0), stop=(k == 3),
                         r=['cdiag', f'ubuf{pr}'], w=['B2'])
            for j in range(2):
                ct = 2 * pr + j
                P.actv(xc[p][:, ct, :], pCv[:, j * 256:(j + 1) * 256], AF.Silu, bias=cb[:, ct:ct + 1], r=['B2', 'cb'], w=[f'xc{p}_{ct}'])
            P.cp('pool', ubuf[:, 2 * pr:2 * pr + 2, 0:3], ubuf[:, 2 * pr:2 * pr + 2, 256:259], r=[f'ubuf{pr}'], w=[f'ubuf{pr}'])
            yield
        for t in range(2):
            for kt in range(8):
                P.mm(pz, hnT[:, kt, t * 128:(t + 1) * 128], W[:, kt, 776:1288], start=(kt == 0), stop=(kt == 7),
                     r=HNT + [Wk[kt]], w=['B3'])
            P.actv(zs[p][:, t, :], pz, AF.Silu, r=['B3'], w=[f'zs{p}_{t}'])
            yield

    def back(c):
        p = c % 2
        XC = [f'xc{p}_{ct}' for ct in range(6)]
        P.tt('dve', W0[:], triw[:].unsqueeze(1).to_broadcast([128, 8, 256]), dtA[p][:, 0, :].unsqueeze(2).to_broadcast([128, 8, 256]),
             ALU.mult, r=['triw', f'dtA{p}'], w=['W0'])
        P.tt('dve', V1[:], triw[:, 0:128].unsqueeze(1).to_broadcast([128, 8, 128]), dtA[p][:, 1, :].unsqueeze(2).to_broadcast([128, 8, 128]),
             ALU.mult, r=['triw', f'dtA{p}'], w=['V1'])
        for t in range(2):
            for ct in range(5):
                P.tr(ptx[:, ct * 128:(ct + 1) * 128], xc[p][:, ct, t * 128:(t + 1) * 128], identb[:], r=[XC[ct], 'identb'], w=['B4'])
            P.cp('dve' if t == 0 else 'act', xtok[:, t, :], ptx, r=['B4'], w=[f'xtok{t}'])
            P.tt('pool', xdt[:, t, :].rearrange("p (h c) -> p h c", h=8), xtok[:, t, 0:512].rearrange("p (h c) -> p h c", h=8),
                 dtk[p][:, t, :].unsqueeze(2).to_broadcast([128, 8, 64]), ALU.mult, r=[f'xtok{t}', f'dtk{p}'], w=[f'xdt{t}'])
        yield
        P.mm(pCB[:, 0:256], xc[p][:, 4, 0:128], xc[p][:, 5, 0:256], r=[XC[4], XC[5]], w=['B4'])
        P.mm(pCB[:, 256:384], xc[p][:, 4, 128:256], xc[p][:, 5, 128:256], r=[XC[4], XC[5]], w=['B4'])
        P.cp('act', CBm[:], pCB, r=['B4'], w=['CBm'])
        for off in (0, 256):
            blk = CBm[:, off:off + 128]
            P.add('pool', (lambda blk: (lambda e: e.affine_select(out=blk, in_=blk, pattern=[[1, 128]], compare_op=ALU.is_ge,
                                                                  fill=0.0, base=0, channel_multiplier=-1)))(blk),
                  r=['CBm'], w=['CBm'])
        yield
        for h in range(8):
            L = Lb[h % 2]
            Lk = f'Lb{h % 2}'
            ps = pseg[h % 2]
            psk = 'B5' if h % 2 == 0 else 'B7'
            P.mm(ps[:, 0:128], SU[:], W0[:, h, 0:128], start=True, stop=True, r=['SU', 'W0'], w=[psk])
            P.mm(ps[:, 128:256], SU[:], W0[:, h, 128:256], start=True, stop=False, r=['SU', 'W0'], w=[psk])
            P.mm(ps[:, 128:256], onesb[:], V1[:, h, :], start=False, stop=True, r=['onesb', 'V1'], w=[psk])
            P.mm(ps[:, 256:384], SU[:], V1[:, h, :], start=True, stop=True, r=['SU', 'V1'], w=[psk])
            P.actv(L[:], ps, AF.Exp, r=[psk], w=[Lk])
            P.tt('dve', MT[:, h, :], L[:], CBm[:], ALU.mult, r=[Lk, 'CBm'], w=[f'MT{h}'])
            if h % 2 == 1:
                yield
        for t in range(2):
            for h in range(8):
                hc = slice(h * 64, (h + 1) * 64)
                P.mm(py[:, hc], MT[:, h, t * 128:(t + 1) * 128], xdt[:, 0, hc], start=True, stop=False, r=[f'MT{h}', 'xdt0'], w=['B6'])
                if t == 1:
                    P.mm(py[:, hc], MT[:, h, 256:384], xdt[:, 1, hc], start=False, stop=False, r=[f'MT{h}', 'xdt1'], w=['B6'])
                P.mm(py[:, hc], Dident[:, h, :], xtok[:, t, hc], start=False, stop=True, r=['Dident', f'xtok{t}'], w=['B6'])
            P.mm(pyi, xc[p][:, 5, t * 128:(t + 1) * 128], state_bf[:], r=[XC[5], 'state_bf'], w=['B4'])
            P.tt('dve', t1[:].rearrange("p (h c) -> p h c", h=8), pyi.rearrange("p (h c) -> p h c", h=8),
                 ecum[p][:, t, :].unsqueeze(2).to_broadcast([128, 8, 64]), ALU.mult, r=['B4', f'ecum{p}'], w=['t1'])
            P.tt('dve', ysb[:], t1[:], py, ALU.add, r=['t1', 'B6'], w=['ysb'])
            P.tt('dve', yg[t][:], ysb[:], zs[p][:, t, :], ALU.mult, r=['ysb', f'zs{p}_{t}'], w=[f'yg{t}'])
            P.actv(junk2b[:], yg[t][:], AF.Square, accum=ss2[:, t:t + 1], r=[f'yg{t}'], w=[f'ss2_{t}', 'junk2b'])
            yield
        P.actv(rt2[:], ss2[:], AF.Ln, bias=EPS, scale=1.0 / 512, r=['ss2_0', 'ss2_1'], w=['rt2'])
        P.actv(rstd2[:], rt2[:], AF.Exp, scale=-0.5, r=['rt2'], w=['rstd2'])
        for t in range(2):
            P.stt(yn[t][:], yg[t][:], rstd2[:, t:t + 1], gout_bc[:], ALU.mult, ALU.mult, r=[f'yg{t}', 'rstd2', 'gout_bc'], w=[f'yn{t}'])
            P.ld(yn_d[c * 256 + t * 128: c * 256 + (t + 1) * 128, :], yn[t][:], w=[f'ynd{t}'], sem=f'st{t}', r=[f'yn{t}'])
        for st in range(2):
            P.tt('pool', wx[:, st, :].rearrange("p (h c) -> p h c", h=8), xdt[:, st, :].rearrange("p (h c) -> p h c", h=8),
                 wtmp[p][:, st, :].unsqueeze(2).to_broadcast([128, 8, 64]), ALU.mult, r=[f'xdt{st}', f'wtmp{p}'], w=[f'wx{st}'])
        for st in range(2):
            P.mm(pst, xtok[:, st, 512:640], wx[:, st, :], start=(st == 0), stop=(st == 1), r=[f'xtok{st}', f'wx{st}'], w=['B4'])
        P.tt('dve', state[:].rearrange("p (h c) -> p h c", h=8), state[:].rearrange("p (h c) -> p h c", h=8),
             dec[p][:].unsqueeze(2).to_broadcast([128, 8, 64]), ALU.mult, r=['state', f'dec{p}'], w=['state'])
        P.tt('dve', state[:], state[:], pst, ALU.add, r=['state', 'B4'], w=['state'])
        P.cp('act', state_bf[:], state[:], r=['state'], w=['state_bf'])
        yield

    load_x(0)
    for it in range(nch + 1):
        gens = []
        if it >= 1:
            gens.append(back(it - 1))
        if it < nch:
            gens.append(front(it))
        while gens:
            for g in list(gens):
                try:
                    next(g)
                except StopIteration:
                    gens.remove(g)
    P.wait_all('sp', ['ynd0', 'ynd1'])
    P.emit()
    es.close()
    return nc

NTOK = 2048
NTT = NTOK // 128
INV_FREQ = [float(np.float32(10000.0) ** np.float32(-(2 * i) / 32.0)) for i in range(16)]
TWO_PI = 2.0 * math.pi
CW1 = 6.28125
CW2 = TWO_PI - CW1


def build_stageB(ntt=NTT):
    nc = bass.Bass("TRN2", target_bir_lowering=False)
    x_d = nc.dram_tensor("x", [NTOK, 1024], F32, kind="ExternalInput").ap()
    yn_d = nc.dram_tensor("yn", [NTOK, 2048], BF16, kind="ExternalInput").ap()
    pos_d = nc.dram_tensor("pos", [128, NTT], I32, kind="ExternalInput").ap()
    invf_d = nc.dram_tensor("invf", [1, 16], F32, kind="ExternalInput").ap()
    wout_d = nc.dram_tensor("wout", [2048, 1024], F32, kind="ExternalInput").ap()
    wdn_d = nc.dram_tensor("wdn", [1024, 288], F32, kind="ExternalInput").ap()
    wup_d = nc.dram_tensor("wup", [256, 2048], F32, kind="ExternalInput").ap()
    win_d = nc.dram_tensor("win", [1024, 1408], F32, kind="ExternalInput").ap()
    wuq_d = nc.dram_tensor("wuq", [384, 1536], F32, kind="ExternalInput").ap()
    gkv_d = nc.dram_tensor("gkv", [128, 8], F32, kind="ExternalInput").ap()
    gpre_d = nc.dram_tensor("gpre", [128, 8], F32, kind="ExternalInput").ap()
    glat_d = nc.dram_tensor("glat", [128, 2], F32, kind="ExternalInput").ap()
    gq_d = nc.dram_tensor("gq", [128, 3], F32, kind="ExternalInput").ap()
    h1_d = nc.dram_tensor("h1", [NTOK, 1024], F32, kind="ExternalOutput").ap()
    sg_d = nc.dram_tensor("sg", [128, 8, NTOK], BF16, kind="ExternalOutput").ap()
    kn_d = nc.dram_tensor("kn", [128, 8, NTOK], BF16, kind="ExternalOutput").ap()
    kr_d = nc.dram_tensor("kr", [32, NTOK], BF16, kind="ExternalOutput").ap()
    v_d = nc.dram_tensor("v", [NTOK, 1024], BF16, kind="ExternalOutput").ap()
    qT_d = nc.dram_tensor("qT", [96, 16, NTOK], BF16, kind="ExternalOutput").ap()

    P = P2(nc)
    es = contextlib.ExitStack()

    def S(name, shape, dt):
        return es.enter_context(nc.sbuf_tensor(name, shape, dt))

    banks = [es.enter_context(nc.psum_tensor(f"bank{i}", [128, 512], F32)) for i in range(8)]
    bctr = [0]

    def nb():
        i = bctr[0] % 8
        bctr[0] += 1
        return banks[i], f'B{i}'

    def bfv(bank):
        return bank[:].bitcast(BF16)

    wout = S("wout_s", [128, 16, 1024], BF16)
    wdn = S("wdn_s", [128, 8, 288], BF16)
    wkn = S("wkn_s", [128, 2, 1024], BF16)
    wv = S("wv_s", [128, 2, 1024], BF16)
    win = S("win_s", [128, 8, 1408], BF16)
    wuq = S("wuq_s", [128, 3, 1536], BF16)
    wst = [S(f"wst{i}", [128, 2048], F32) for i in range(2)]
    gkv = S("gkv_s", [128, 8], F32)
    gpre = S("gpre_s", [128, 8], F32)
    glat = S("glat_s", [128, 2], F32)
    gq = S("gq_s", [128, 3], F32)
    identf = S("identf", [128, 128], F32)
    identb = S("identb", [128, 128], BF16)
    posi = S("posi", [128, NTT], I32)
    posf = S("posf", [128, NTT], F32)
    invf = S("invf_s", [128, 16], F32)
    ang = S("ang", [128, NTT, 16], F32)
    uu = S("uu", [128, NTT, 16], F32)
    ki = S("ki", [128, NTT, 16], I32)
    kf = S("kf", [128, NTT, 16], F32)
    gg = S("gg", [128, NTT, 16], F32)
    m1 = S("m1", [128, NTT, 16], F32)
    gc = S("gc", [128, NTT, 16], F32)
    sinT = S("sinT", [128, NTT, 16], F32)
    cosT = S("cosT", [128, NTT, 16], F32)
    xin = [S(f"xin{i}", [128, 1024], F32) for i in range(2)]
    ynin = [S(f"ynin{i}", [128, 2048], BF16) for i in range(2)]
    ynT = S("ynT", [128, 16, 128], BF16)
    h1 = [S(f"h1_{i}", [128, 1024], F32) for i in range(2)]
    junk = S("junk", [128, 1024], BF16)
    ss = S("ss", [128, 1], F32)
    rt = S("rt", [128, 1], F32)
    rstd = S("rstd", [128, 1], F32)
    hnb = S("hnb", [128, 1024], BF16)
    hT = S("hT", [128, 8, 128], BF16)
    junk2 = S("junk2", [128, 384], BF16)
    ssc = S("ssc", [128, 1], F32)
    rtc = S("rtc", [128, 1], F32)
    rstdc = S("rstdc", [128, 1], F32)
    ckvn = S("ckvn", [128, 256], BF16)
    ra = S("ra", [128, 16], F32)
    rb = S("rb", [128, 16], F32)
    krb = S("krb", [128, 32], BF16)
    ckT = S("ckT", [128, 2, 128], BF16)
    krT = [S(f"krT{i}", [32, 128], BF16) for i in range(2)]
    knT = [S(f"knT{i}", [128, 8, 128], BF16) for i in range(2)]
    vsb = [S(f"vsb{i}", [128, 1024], BF16) for i in range(2)]
    ssq = S("ssq", [128, 1], F32)
    rtq = S("rtq", [128, 1], F32)
    rstdq = S("rstdq", [128, 1], F32)
    cqn = S("cqn", [128, 384], BF16)
    sg = [S(f"sg{i}", [128, 8, 128], BF16) for i in range(2)]
    cqT = S("cqT", [128, 3, 128], BF16)
    qtok = S("qtok", [128, 16, 96], BF16)
    qa = S("qa", [128, 16, 16], F32)
    qb = S("qb", [128, 16, 16], F32)
    qT = [S(f"qT{i}", [96, 16, 128], BF16) for i in range(2)]

    P.ld(gkv[:], gkv_d, ['gkv'], 'c0')
    P.ld(gpre[:], gpre_d, ['gpre'], 'c1')
    P.ld(glat[:], glat_d, ['glat'], 'c2')
    P.ld(gq[:], gq_d, ['gq'], 'c3')
    P.ld(posi[:], pos_d, ['posi'], 'c4')
    P.ld(invf[:], invf_d.partition_broadcast(128), ['invf'], 'c5')
    P.ms('pool', identf[:], 1.0, ['identf'])
    P.add('pool', lambda e: e.affine_select(out=identf[:], in_=identf[:], pattern=[[-1, 128]], compare_op=ALU.is_equal,
                                            fill=0.0, base=0, channel_multiplier=1), r=['identf'], w=['identf'])
    P.cp('dve', identb[:], identf[:], r=['identf'], w=['identb'])
    P.cp('dve', posf[:], posi[:], r=['posi'], w=['posf'])
    P.tt('dve', ang[:], posf[:].unsqueeze(2).to_broadcast([128, NTT, 16]), invf[:].unsqueeze(1).to_broadcast([128, NTT, 16]),
         ALU.mult, r=['posf', 'invf'], w=['ang'])
    P.ts('dve', uu[:], ang[:], 1.0 / TWO_PI, None, ALU.mult, r=['ang'], w=['uu'])
    P.cp('dve', ki[:], uu[:], r=['uu'], w=['ki'])
    P.cp('dve', kf[:], ki[:], r=['ki'], w=['kf'])
    P.stt(gg[:], kf[:], -CW1, ang[:], ALU.mult, ALU.add, r=['kf', 'ang'], w=['gg'])
    P.stt(gg[:], kf[:], -CW2, gg[:], ALU.mult, ALU.add, r=['kf', 'gg'], w=['gg'])
    P.ts('dve', gg[:], gg[:], 1.0 / TWO_PI, None, ALU.mult, r=['gg'], w=['gg'])

    def wrap():
        P.ts('dve', m1[:], gg[:], 0.5, None, ALU.is_gt, r=['gg'], w=['m1'])
        P.tt('dve', gg[:], gg[:], m1[:], ALU.subtract, r=['gg', 'm1'], w=['gg'])
        P.ts('dve', m1[:], gg[:], -0.5, None, ALU.is_lt, r=['gg'], w=['m1'])
        P.tt('dve', gg[:], gg[:], m1[:], ALU.add, r=['gg', 'm1'], w=['gg'])
        P.ts('dve', gg[:], gg[:], 0.4999995, -0.4999995, ALU.min, ALU.max, r=['gg'], w=['gg'])

    wrap()
    P.actv(sinT[:], gg[:], AF.Sin, scale=TWO_PI, r=['gg'], w=['sinT'])
    P.ts('dve', gg[:], gg[:], 0.25, None, ALU.add, r=['gg'], w=['gg'])
    wrap()
    P.actv(cosT[:], gg[:], AF.Sin, scale=TWO_PI, r=['gg'], w=['cosT'])

    wi = [0]
    WK = {}

    def wload(grp, dst_ap, src_ap, ncols, gain_ap, in_view=None):
        i = wi[0] % 2
        wi[0] += 1
        key = f'W{wi[0]}'
        WK.setdefault(grp, []).append(key)
        P.ld(wst[i][:, 0:ncols], src_ap, [f'wst{i}'], f'wst{i}')
        src = wst[i][:, 0:ncols] if in_view is None else in_view(wst[i])
        if i == 0:
            if gain_ap is None:
                P.cp('dve', dst_ap, src, r=[f'wst{i}'], w=[key])
            else:
                P.ts('dve', dst_ap, src, gain_ap, None, ALU.mult, r=[f'wst{i}', 'gkv', 'gpre', 'glat', 'gq'], w=[key])
        else:
            if gain_ap is None:
                P.cp('act', dst_ap, src, r=[f'wst{i}'], w=[key])
            else:
                P.actv(dst_ap, src, AF.Copy, scale=gain_ap, r=[f'wst{i}', 'gkv', 'gpre', 'glat', 'gq'], w=[key])

    for kt in range(16):
        wload('wout', wout[:, kt, :], wout_d[kt * 128:(kt + 1) * 128, :], 1024, None)
    for kt in range(8):
        wload('wdn', wdn[:, kt, :], wdn_d[kt * 128:(kt + 1) * 128, :], 288, gkv[:, kt:kt + 1])
    for kt in range(8):
        wload('win', win[:, kt, :], win_d[kt * 128:(kt + 1) * 128, :], 1408, gpre[:, kt:kt + 1])
    for kt in range(2):
        wload('wkn', wkn[:, kt, :].rearrange("p (h c) -> p h c", h=16), wup_d[kt * 128:(kt + 1) * 128, :], 2048, glat[:, kt:kt + 1],
              in_view=lambda t: t[:, 0:2048].rearrange("p (h c) -> p h c", h=16)[:, :, 0:64])
        wload('wv', wv[:, kt, :].rearrange("p (h c) -> p h c", h=16), wup_d[kt * 128:(kt + 1) * 128, :], 2048, glat[:, kt:kt + 1],
              in_view=lambda t: t[:, 0:2048].rearrange("p (h c) -> p h c", h=16)[:, :, 64:128])
    for kt in range(3):
        wload('wuq', wuq[:, kt, 0:1024].rearrange("p (h c) -> p h c", h=16), wuq_d[kt * 128:(kt + 1) * 128, :], 1536, gq[:, kt:kt + 1],
              in_view=lambda t: t[:, 0:1536].rearrange("p (h c) -> p h c", h=16)[:, :, 0:64])
        wload('wuq', wuq[:, kt, 1024:1280].rearrange("p (h c) -> p h c", h=16), wuq_d[kt * 128:(kt + 1) * 128, :], 1536, gq[:, kt:kt + 1],
              in_view=lambda t: t[:, 0:1536].rearrange("p (h c) -> p h c", h=16)[:, :, 64:80])
        wload('wuq', wuq[:, kt, 1280:1536].rearrange("p (h c) -> p h c", h=16), wuq_d[kt * 128:(kt + 1) * 128, :], 1536, gq[:, kt:kt + 1],
              in_view=lambda t: t[:, 0:1536].rearrange("p (h c) -> p h c", h=16)[:, :, 80:96])


    def load_t(tt):
        sl = tt % 2
        P.ld(xin[sl][:], x_d[tt * 128:(tt + 1) * 128, :], [f'xin{sl}'], f'xin{sl}')
        P.ld(ynin[sl][:], yn_d[tt * 128:(tt + 1) * 128, :], [f'ynin{sl}'], f'ynin{sl}')

    load_t(0)
    for tt in range(ntt):
        sl = tt % 2
        tok = slice(tt * 128, (tt + 1) * 128)
        if tt + 1 < ntt:
            load_t(tt + 1)
        for half in range(2):
            bk, bkk = nb()
            pv = bfv(bk).rearrange("p (k t) -> p k t", k=8)
            for j in range(8):
                c = half * 8 + j
                P.tr(pv[:, j, :], ynin[sl][:, c * 128:(c + 1) * 128], identb[:], r=[f'ynin{sl}', 'identb'], w=[bkk])
            P.cp('dve' if half == 0 else 'act', ynT[:, half * 8:(half + 1) * 8, :], pv, r=[bkk], w=[f'ynT{half}'])
        for half in range(2):
            bk, bkk = nb()
            for c in range(16):
                P.mm(bk[:, :], ynT[:, c, :], wout[:, c, half * 512:(half + 1) * 512], start=(c == 0), stop=(c == 15),
                     r=['ynT0', 'ynT1', *WK['wout']], w=[bkk])
            P.tt('dve', h1[sl][:, half * 512:(half + 1) * 512], bk[:, :], xin[sl][:, half * 512:(half + 1) * 512], ALU.add,
                 r=[bkk, f'xin{sl}'], w=[f'h1_{sl}'])
        P.ld(h1_d[tok, :], h1[sl][:], w=[f'h1d{sl}'], sem=f'sth{sl}', r=[f'h1_{sl}'])
        P.actv(junk[:], h1[sl][:], AF.Square, accum=ss[:], r=[f'h1_{sl}'], w=['junk', 'ss'])
        P.actv(rt[:], ss[:], AF.Sqrt, bias=EPS, scale=1.0 / 1024, r=['ss'], w=['rt'])
        P.add('dve', lambda e: e.reciprocal(out=rstd[:], in_=rt[:]), r=['rt'], w=['rstd'])
        P.ts('dve', hnb[:], h1[sl][:], rstd[:, 0:1], None, ALU.mult, r=[f'h1_{sl}', 'rstd'], w=['hnb'])
        bk, bkk = nb()
        pv = bfv(bk).rearrange("p (k t) -> p k t", k=8)
        for kt in range(8):
            P.tr(pv[:, kt, :], hnb[:, kt * 128:(kt + 1) * 128], identb[:], r=['hnb', 'identb'], w=[bkk])
        P.cp('act', hT[:], pv, r=[bkk], w=['hT'])
        bk, bkk = nb()
        for kt in range(8):
            P.mm(bk[:, 0:288], hT[:, kt, :], wdn[:, kt, :], start=(kt == 0), stop=(kt == 7), r=['hT', *WK['wdn']], w=[bkk])
        P.actv(junk2[:, 0:256], bk[:, 0:256], AF.Square, accum=ssc[:], r=[bkk], w=['junk2', 'ssc'])
        P.actv(rtc[:], ssc[:], AF.Sqrt, bias=EPS, scale=1.0 / 256, r=['ssc'], w=['rtc'])
        P.add('dve', lambda e: e.reciprocal(out=rstdc[:], in_=rtc[:]), r=['rtc'], w=['rstdc'])
        P.tt('dve', ra[:], bk[:, 256:272], cosT[:, tt, :], ALU.mult, r=[bkk, 'cosT'], w=['ra'])
        P.tt('dve', rb[:], bk[:, 272:288], sinT[:, tt, :], ALU.mult, r=[bkk, 'sinT'], w=['rb'])
        P.tt('dve', krb[:, 0:16], ra[:], rb[:], ALU.subtract, r=['ra', 'rb'], w=['krb'])
        P.tt('dve', ra[:], bk[:, 256:272], sinT[:, tt, :], ALU.mult, r=[bkk, 'sinT', 'krb'], w=['ra'])
        P.tt('dve', rb[:], bk[:, 272:288], cosT[:, tt, :], ALU.mult, r=[bkk, 'cosT', 'krb'], w=['rb'])
        P.tt('dve', krb[:, 16:32], ra[:], rb[:], ALU.add, r=['ra', 'rb'], w=['krb'])
        P.ts('dve', ckvn[:], bk[:, 0:256], rstdc[:, 0:1], None, ALU.mult, r=[bkk, 'rstdc'], w=['ckvn'])
        bk, bkk = nb()
        pv = bfv(bk)
        for kt in range(2):
            P.tr(pv[:, kt * 128:(kt + 1) * 128], ckvn[:, kt * 128:(kt + 1) * 128], identb[:], r=['ckvn', 'identb'], w=[bkk])
        P.tr(pv[0:32, 256:384], krb[:], identb[:], r=['krb', 'identb'], w=[bkk])
        P.cp('act', ckT[:], pv[:, 0:256].rearrange("p (k t) -> p k t", k=2), r=[bkk], w=['ckT'])
        P.cp('act', krT[sl][:], pv[0:32, 256:384], r=[bkk], w=[f'krT{sl}'])
        P.ld(kr_d[:, tok], krT[sl][:], w=[f'krd{sl}'], sem=f'stkr{sl}', r=[f'krT{sl}'])
        for half in range(2):
            bk, bkk = nb()
            for j in range(4):
                pr = half * 4 + j
                for kt in range(2):
                    P.mm(bk[:, j * 128:(j + 1) * 128], wkn[:, kt, pr * 128:(pr + 1) * 128], ckT[:, kt, :], start=(kt == 0), stop=(kt == 1),
                         r=['ckT', *WK['wkn']], w=[bkk])
            P.cp('act' if half == 0 else 'dve', knT[sl][:, half * 4:(half + 1) * 4, :], bk[:, :].rearrange("p (j t) -> p j t", j=4),
                 r=[bkk], w=[f'knT{sl}'])
        P.ld(kn_d[:, :, tok], knT[sl][:], w=[f'knd{sl}'], sem=f'stkn{sl}', r=[f'knT{sl}'])
        for half in range(2):
            bk, bkk = nb()
            for kt in range(2):
                P.mm(bk[:, :], ckT[:, kt, :], wv[:, kt, half * 512:(half + 1) * 512], start=(kt == 0), stop=(kt == 1),
                     r=['ckT', *WK['wv']], w=[bkk])
            P.cp('act' if half == 0 else 'dve', vsb[sl][:, half * 512:(half + 1) * 512], bk[:, :], r=[bkk], w=[f'vsb{sl}'])
        P.ld(v_d[tok, :], vsb[sl][:], w=[f'vd{sl}'], sem=f'stv{sl}', r=[f'vsb{sl}'])
        bk, bkk = nb()
        for kt in range(8):
            P.mm(bk[:, 0:384], hT[:, kt, :], win[:, kt, 0:384], start=(kt == 0), stop=(kt == 7), r=['hT', *WK['win']], w=[bkk])
        P.actv(junk2[:], bk[:, 0:384], AF.Square, accum=ssq[:], r=[bkk], w=['junk2', 'ssq'])
        P.actv(rtq[:], ssq[:], AF.Sqrt, bias=EPS, scale=1.0 / 384, r=['ssq'], w=['rtq'])
        P.add('dve', lambda e: e.reciprocal(out=rstdq[:], in_=rtq[:]), r=['rtq'], w=['rstdq'])
        P.ts('dve', cqn[:], bk[:, 0:384], rstdq[:, 0:1], None, ALU.mult, r=[bkk, 'rstdq'], w=['cqn'])
        for half in range(2):
            bk, bkk = nb()
            for j in range(4):
                ct = half * 4 + j
                for kt in range(8):
                    P.mm(bk[:, j * 128:(j + 1) * 128], win[:, kt, 384 + ct * 128:384 + (ct + 1) * 128], hT[:, kt, :],
                         start=(kt == 0), stop=(kt == 7), r=['hT', *WK['win']], w=[bkk])
            P.actv(sg[sl][:, half * 4:(half + 1) * 4, :], bk[:, :].rearrange("p (j t) -> p j t", j=4), AF.Silu, r=[bkk], w=[f'sg{sl}'])
        P.ld(sg_d[:, :, tok], sg[sl][:], w=[f'sgd{sl}'], sem=f'stsg{sl}', r=[f'sg{sl}'])
        bk, bkk = nb()
        pv = bfv(bk)
        for kt in range(3):
            P.tr(pv[:, kt * 128:(kt + 1) * 128], cqn[:, kt * 128:(kt + 1) * 128], identb[:], r=['cqn', 'identb'], w=[bkk])
        P.cp('act', cqT[:], pv[:, 0:384].rearrange("p (k t) -> p k t", k=3), r=[bkk], w=['cqT'])
        for blk in range(2):
            bk, bkk = nb()
            for kt in range(3):
                P.mm(bk[:, :], cqT[:, kt, :], wuq[:, kt, blk * 512:(blk + 1) * 512], start=(kt == 0), stop=(kt == 2),
                     r=['cqT', *WK['wuq']], w=[bkk])
            P.cp('act', qtok[:, blk * 8:(blk + 1) * 8, 0:64], bk[:, :].rearrange("p (h c) -> p h c", h=8), r=[bkk], w=['qtok'])
        bk, bkk = nb()
        for kt in range(3):
            P.mm(bk[:, :], cqT[:, kt, :], wuq[:, kt, 1024:1536], start=(kt == 0), stop=(kt == 2), r=['cqT', *WK['wuq']], w=[bkk])
        x1 = bk[:, 0:256].rearrange("p (h c) -> p h c", h=16)
        x2 = bk[:, 256:512].rearrange("p (h c) -> p h c", h=16)
        cb_ = cosT[:, tt, :].unsqueeze(1).to_broadcast([128, 16, 16])
        sb_ = sinT[:, tt, :].unsqueeze(1).to_broadcast([128, 16, 16])
        P.tt('dve', qa[:], x1, cb_, ALU.mult, r=[bkk, 'cosT'], w=['qa'])
        P.tt('dve', qb[:], x2, sb_, ALU.mult, r=[bkk, 'sinT'], w=['qb'])
        P.tt('dve', qtok[:, :, 64:80], qa[:], qb[:], ALU.subtract, r=['qa', 'qb'], w=['qtok'])
        P.tt('dve', qa[:], x1, sb_, ALU.mult, r=[bkk, 'sinT', 'qtok'], w=['qa'])
        P.tt('dve', qb[:], x2, cb_, ALU.mult, r=[bkk, 'cosT', 'qtok'], w=['qb'])
        P.tt('dve', qtok[:, :, 80:96], qa[:], qb[:], ALU.add, r=['qa', 'qb'], w=['qtok'])
        for half in range(2):
            bk, bkk = nb()
            pv = bfv(bk)[0:96, :].rearrange("p (h t) -> p h t", h=8)
            for j in range(8):
                P.tr(pv[:, j, :], qtok[:, half * 8 + j, :], identb[:], r=['qtok', 'identb'], w=[bkk])
            P.cp('act' if half == 0 else 'dve', qT[sl][:, half * 8:(half + 1) * 8, :], pv, r=[bkk], w=[f'qT{sl}'])
        P.ld(qT_d[:, :, tok], qT[sl][:], w=[f'qd{sl}'], sem=f'stq{sl}', r=[f'qT{sl}'])
    outk = []
    for sl in range(2):
        outk += [f'h1d{sl}', f'krd{sl}', f'knd{sl}', f'vd{sl}', f'sgd{sl}', f'qd{sl}']
    P.wait_all('sp', outk)
    P.emit()
    es.close()
    return nc
SCALE = 96.0 ** -0.5
LOOKAHEAD = 2


def build_stageC(nheads=4, nchunks=16):
    nc = bass.Bass("TRN2", target_bir_lowering=False)
    SQ = 8192
    LK = 8192
    NKT = LK // 128
    chunks = list(range(nchunks))
    kT_d = nc.dram_tensor("kT", [4, 96, 8192], BF16, kind="ExternalInput").ap()
    v_d = nc.dram_tensor("v", [4, 128, 64, 64], BF16, kind="ExternalInput").ap()
    qT_d = nc.dram_tensor("qT", [4, 96, SQ], BF16, kind="ExternalInput").ap()
    sg_d = nc.dram_tensor("sg", [4, 64, SQ], BF16, kind="ExternalInput").ap()
    og_d = nc.dram_tensor("og", [64, 4, SQ], BF16, kind="ExternalOutput").ap()

    P = P2(nc)
    es = contextlib.ExitStack()

    def S(name, shape, dt):
        return es.enter_context(nc.sbuf_tensor(name, shape, dt))

    banks = [es.enter_context(nc.psum_tensor(f"bank{i}", [128, 512], F32)) for i in range(8)]
    kT = [S(f"kT{i}", [96, LK], BF16) for i in range(2)]
    vh = [S(f"vh{i}", [128, NKT, 65], BF16) for i in range(2)]
    qh = [S(f"qh{i}", [96, SQ], BF16) for i in range(2)]
    sgh = [S(f"sgh{i}", [64, SQ], BF16) for i in range(2)]
    ogs = [S(f"ogs{i}", [64, 512], BF16) for i in range(2)]
    PT = [S(f"PT{i}", [128, 512], BF16) for i in range(4)]
    rrow = S("rrow", [65, 512], F32)
    onesr = S("onesr", [65, 64], F32)
    ot = [S(f"ot{i}", [64, 512], F32) for i in range(2)]
    tn = S("tn", [64, 512], BF16)

    P.ms('pool', onesr[:], 1.0, ['onesr'])
    for i in range(2):
        P.ms('pool', vh[i][:, :, 64:65], 1.0, [f'vh{i}'])

    def load_head(h):
        i = h % 2
        half = LK // 2
        P.ld(kT[i][:, 0:half], kT_d[h, :, 0:half], [f'kT{i}a'], f'kT{i}a')
        P.ld(kT[i][:, half:LK], kT_d[h, :, half:LK], [f'kT{i}b'], f'kT{i}b', eng='act')
        P.ld(vh[i][:, :, 0:64], v_d[h, :, 0:NKT, :], [f'vh{i}'], f'vh{i}', eng='pool')
        P.ld(qh[i][:], qT_d[h, :, :], [f'qh{i}'], f'qh{i}')
        P.ld(sgh[i][:], sg_d[h, :, :], [f'sgh{i}'], f'sgh{i}')

    load_head(0)
    if nheads > 1:
        load_head(1)
    tiles = []
    cn = 0
    for h in range(nheads):
        for qi, cj in enumerate(chunks):
            nk = (cj + 1) * 4
            for kt in range(nk):
                d = kt - (nk - 4)
                c0 = 128 * d if d > 0 else 0
                tiles.append(dict(h=h, qi=qi, kt=kt, d=d, c0=c0, nk=nk, cn=cn, last_chunk=(qi == len(chunks) - 1)))
            cn += 1

    def emit_S(n, t):
        i = t['h'] % 2
        sb = n % 4
        ps = banks[sb]
        c0, kt, qi = t['c0'], t['kt'], t['qi']
        P.mm(ps[:, c0:512], kT[i][:, kt * 128:(kt + 1) * 128], qh[i][:, qi * 512 + c0:(qi + 1) * 512],
             r=[f'kT{i}a', f'kT{i}b', f'qh{i}'], w=[f'B{sb}'])
        P.actv(PT[sb][:, c0:512], ps[:, c0:512], AF.Exp, scale=SCALE, r=[f'B{sb}'], w=[f'PT{sb}'])
        if t['d'] >= 0:
            blk = PT[sb][:, c0:c0 + 128]
            P.add('pool', (lambda blk: (lambda e: e.affine_select(out=blk, in_=blk, pattern=[[1, 128]], compare_op=ALU.is_ge,
                                                                  fill=0.0, base=0, channel_multiplier=-1)))(blk),
                  r=[f'PT{sb}'], w=[f'PT{sb}'])

    def emit_PV(n, t):
        i = t['h'] % 2
        sb = n % 4
        par = t['cn'] % 2
        po = banks[4 + par]
        c0, kt = t['c0'], t['kt']
        P.mm(po[0:65, c0:512], vh[i][:, kt, :], PT[sb][:, c0:512], start=(kt == 0), stop=(kt == t['nk'] - 1),
             r=[f'vh{i}', f'PT{sb}'], w=[f'B{4 + par}'])

    def epi1(t):
        par = t['cn'] % 2
        po = banks[4 + par]
        P.add('dve', lambda e, po=po: e.reciprocal(out=rrow[64:65, :], in_=po[64:65, :]), r=[f'B{4 + par}'], w=['rrow'])
        P.cp('act', ot[par][:], po[0:64, :], r=[f'B{4 + par}'], w=[f'ot{par}'])

    def epi2(t):
        i = t['h'] % 2
        par = t['cn'] % 2
        qsl = slice(t['qi'] * 512, (t['qi'] + 1) * 512)
        prb = banks[6]
        P.mm(prb[0:64, :], onesr[64:65, :], rrow[64:65, :], r=['onesr', 'rrow'], w=['B6'])
        P.tt('dve', tn[:], ot[par][:], prb[0:64, :], ALU.mult, r=[f'ot{par}', 'B6'], w=['tn'])
        P.tt('pool', ogs[par][:], tn[:], sgh[i][:, qsl], ALU.mult, r=['tn', f'sgh{i}'], w=[f'ogs{par}'])
        P.ld(og_d[:, t['h'], qsl], ogs[par][:], w=[f'ogd{par}'], sem=f'sto{par}', r=[f'ogs{par}'])
        if t['last_chunk'] and t['h'] + 2 < nheads:
            load_head(t['h'] + 2)

    LA = 3
    DEFER = 2
    sched = {}
    NT = len(tiles)
    for n in range(NT + LA):
        if n < NT:
            emit_S(n, tiles[n])
        for t in sched.pop(n, []):
            epi2(t)
        m = n - LA
        if m >= 0:
            t = tiles[m]
            emit_PV(m, t)
            if t['kt'] == t['nk'] - 1:
                epi1(t)
                sched.setdefault(n + DEFER, []).append(t)
    for k in sorted(sched):
        for t in sched[k]:
            epi2(t)
    P.wait_all('sp', ['ogd0', 'ogd1'])
    P.emit()
    es.close()
    return nc


def build_stageD():
    nc = bass.Bass("TRN2", target_bir_lowering=False)
    og_d = nc.dram_tensor("og", [128, 8, NTOK], BF16, kind="ExternalInput").ap()
    h1_d = nc.dram_tensor("h1", [NTOK, 1024], F32, kind="ExternalInput").ap()
    wo_d = nc.dram_tensor("wo", [1024, 1024], F32, kind="ExternalInput").ap()
    gf_d = nc.dram_tensor("gf", [1, 1024], F32, kind="ExternalInput").ap()
    out_d = nc.dram_tensor("out", [NTOK, 1024], F32, kind="ExternalOutput").ap()
    P = P2(nc)
    es = contextlib.ExitStack()

    def S(name, shape, dt):
        return es.enter_context(nc.sbuf_tensor(name, shape, dt))

    banks = [es.enter_context(nc.psum_tensor(f"bank{i}", [128, 512], F32)) for i in range(8)]
    ogT = S("ogT", [128, 8, NTOK], BF16)
    wo = S("wo_s", [128, 8, 1024], BF16)
    wst = [S(f"wst{i}", [128, 1024], F32) for i in range(2)]
    gf_bc = S("gf_bc", [128, 1024], F32)
    h1t = [S(f"h1t{i}", [128, 1024], F32) for i in range(2)]
    h2 = S("h2", [128, 1024], F32)
    junk = S("junk", [128, 1024], BF16)
    ss = S("ss", [128, 1], F32)
    rt = S("rt", [128, 1], F32)
    rstd = S("rstd", [128, 1], F32)
    outt = [S(f"outt{i}", [128, 1024], F32) for i in range(2)]
    P.ld(gf_bc[:], gf_d.partition_broadcast(128), ['gf_bc'], 'c0')
    for q in range(4):
        P.ld(ogT[:, :, q * 512:(q + 1) * 512], og_d[:, :, q * 512:(q + 1) * 512], [f'ogT{q}'], f'og{q}')
    wkeys = []
    for pr in range(8):
        i = pr % 2
        P.ld(wst[i][:], wo_d[pr * 128:(pr + 1) * 128, :], [f'wst{i}'], f'wst{i}')
        P.cp('dve' if i == 0 else 'act', wo[:, pr, :], wst[i][:], r=[f'wst{i}'], w=[f'wo{pr}'])
        wkeys.append(f'wo{pr}')
    def load_h1(tt):
        P.ld(h1t[tt % 2][:], h1_d[tt * 128:(tt + 1) * 128, :], [f'h1t{tt % 2}'], f'h1t{tt % 2}')

    load_h1(0)
    for tt in range(NTT):
        sl = tt % 2
        if tt + 1 < NTT:
            load_h1(tt + 1)
        for half in range(2):
            bk = banks[(tt % 2) * 2 + half]
            for pr in range(8):
                P.mm(bk[:, :], ogT[:, pr, tt * 128:(tt + 1) * 128], wo[:, pr, half * 512:(half + 1) * 512], start=(pr == 0), stop=(pr == 7),
                     r=[f'ogT{tt // 4}', wkeys[pr]], w=[f'B{(tt % 2) * 2 + half}'])
            P.tt('dve', h2[:, half * 512:(half + 1) * 512], bk[:, :], h1t[sl][:, half * 512:(half + 1) * 512], ALU.add,
                 r=[f'B{(tt % 2) * 2 + half}', f'h1t{sl}'], w=[f'h2_{half}'])
        P.actv(junk[:], h2[:], AF.Square, accum=ss[:], r=['h2_0', 'h2_1'], w=['junk', 'ss'])
        P.actv(rt[:], ss[:], AF.Sqrt, bias=EPS, scale=1.0 / 1024, r=['ss'], w=['rt'])
        P.add('dve', lambda e: e.reciprocal(out=rstd[:], in_=rt[:]), r=['rt'], w=['rstd'])
        P.stt(outt[sl][:], h2[:], rstd[:, 0:1], gf_bc[:], ALU.mult, ALU.mult, r=['h2_0', 'h2_1', 'rstd', 'gf_bc'], w=[f'outt{sl}'])
        P.ld(out_d[tt * 128:(tt + 1) * 128, :], outt[sl][:], w=[f'od{sl}'], sem=f'sto{sl}', r=[f'outt{sl}'])
    P.wait_all('sp', ['od0', 'od1'])
    P.emit()
    es.close()
    return nc


def _prepA(inp, b, g):
    w_in = inp['ssm_w_in'][0]
    w = np.concatenate([w_in[:, 2048 + g * 512:2048 + (g + 1) * 512], w_in[:, 4096 + g * 128:4096 + (g + 1) * 128],
                        w_in[:, 4608 + g * 128:4608 + (g + 1) * 128], w_in[:, 5120 + g * 8:5120 + (g + 1) * 8],
                        w_in[:, g * 512:(g + 1) * 512]], axis=1)
    cidx = np.concatenate([np.arange(g * 512, (g + 1) * 512), 2048 + np.arange(g * 128, (g + 1) * 128),
                           2560 + np.arange(g * 128, (g + 1) * 128)])
    cwc = inp['ssm_conv_w'][0][:, cidx]
    cw = cwc.T.reshape(6, 128, 4).transpose(1, 0, 2).reshape(128, 24)
    cb = inp['ssm_conv_b'][0][cidx].reshape(6, 128).T
    hs = slice(g * 8, (g + 1) * 8)
    C = np.ascontiguousarray
    return dict(x=C(inp['x'][b]), w=C(w), gpre=C(inp['g_pre'][0].reshape(8, 128).T), cw=C(cw), cb=C(cb),
                dtb=C(inp['ssm_dt_bias'][0][hs].reshape(1, 8)), alog=C(inp['ssm_A_log'][0][hs].reshape(1, 8)),
                dsk=C(inp['ssm_D'][0][hs].reshape(1, 8)), gout=C(inp['ssm_g_out'][0][g * 512:(g + 1) * 512].reshape(1, 512)))


def _prepB(inp, yn_b, b, j):
    C = np.ascontiguousarray
    tok = slice(j * NTOK, (j + 1) * NTOK)
    pos = np.asarray(inp['positions'][b][tok]).astype(np.int32).reshape(NTT, 128).T
    return dict(x=C(inp['x'][b][tok]), yn=C(yn_b[tok]), pos=C(pos), invf=np.array(INV_FREQ, dtype=np.float32).reshape(1, 16),
                wout=C(inp['ssm_w_out'][0]), wdn=C(inp['kv_w_down']), wup=C(inp['kv_w_up']), win=C(inp['mla_w_in'][0]),
                wuq=C(inp['mla_w_uq'][0]), gkv=C(inp['kv_g_in'].reshape(8, 128).T), gpre=C(inp['g_pre'][1].reshape(8, 128).T),
                glat=C(inp['kv_g_latent'].reshape(2, 128).T), gq=C(inp['mla_g_q'][0].reshape(3, 128).T))


def kernel(**inputs):
    inp = {k: np.asarray(v) for k, v in inputs.items()}
    C = np.ascontiguousarray
    cores = list(range(8))
    ncA = build_stageA()
    rA = run_bass_kernel_spmd(ncA, [_prepA(inp, c // 4, c % 4) for c in cores], core_ids=cores).results
    yn = [np.concatenate([rA[b * 4 + g]['yn'] for g in range(4)], axis=1) for b in range(2)]
    ncB = build_stageB()
    rB = run_bass_kernel_spmd(ncB, [_prepB(inp, yn[c // 4], c // 4, c % 4) for c in cores], core_ids=cores).results
    imC = []
    for c in cores:
        b, hg = c // 4, c % 4
        kn = np.concatenate([rB[b * 4 + j]['kn'] for j in range(4)], axis=2)
        kr = np.concatenate([rB[b * 4 + j]['kr'] for j in range(4)], axis=1)
        vf = np.concatenate([rB[b * 4 + j]['v'] for j in range(4)], axis=0)
        qf = np.concatenate([rB[b * 4 + j]['qT'] for j in range(4)], axis=2)
        sf = np.concatenate([rB[b * 4 + j]['sg'] for j in range(4)], axis=2)
        kT = np.empty((4, 96, 8192), dtype=kn.dtype)
        v4 = np.empty((4, 128, 64, 64), dtype=vf.dtype)
        q4 = np.empty((4, 96, 8192), dtype=qf.dtype)
        s4 = np.empty((4, 64, 8192), dtype=sf.dtype)
        for hl in range(4):
            h = hg * 4 + hl
            kT[hl, 0:64] = kn[(h % 2) * 64:(h % 2) * 64 + 64, h // 2, :]
            kT[hl, 64:96] = kr
            v4[hl] = vf[:, h * 64:(h + 1) * 64].reshape(64, 128, 64).transpose(1, 0, 2)
            q4[hl] = qf[:, h, :]
            s4[hl] = sf[(h % 2) * 64:(h % 2) * 64 + 64, h // 2, :]
        imC.append(dict(kT=kT, v=v4, qT=q4, sg=s4))
    ncC = build_stageC()
    rC = run_bass_kernel_spmd(ncC, imC, core_ids=cores).results
    imD = []
    for c in cores:
        b, j = c // 4, c % 4
        tok = slice(j * NTOK, (j + 1) * NTOK)
        og = np.concatenate([rC[b * 4 + hg]['og'][:, :, tok] for hg in range(4)], axis=1)
        og = og.reshape(64, 8, 2, NTOK).transpose(2, 0, 1, 3).reshape(128, 8, NTOK)
        imD.append(dict(og=C(og), h1=rB[c]['h1'], wo=C(inp['mla_w_out'][0]), gf=C(inp['g_final'].reshape(1, 1024))))
    ncD = build_stageD()
    rD = run_bass_kernel_spmd(ncD, imD, core_ids=cores).results
    out = np.stack([np.concatenate([rD[b * 4 + j]['out'] for j in range(4)], axis=0) for b in range(2)], axis=0)
    return out.astype(np.float32)
```

```python
import contextlib
import math
from concourse.bass_utils import run_bass_kernel_spmd
import numpy as np
import concourse.bass as bass
import concourse.mybir as mybir

F32 = mybir.dt.float32
BF16 = mybir.dt.bfloat16
I32 = mybir.dt.int32
AF = mybir.ActivationFunctionType
ALU = mybir.AluOpType
AX = mybir.AxisListType


class Prog:
    def __init__(self, nc):
        self.nc = nc
        self.ops = []
        self.lastw = {}
        self.readers = {}
        self.dma_sems = {}

    def add(self, eng, fn, r=(), w=(), dma=None, group=False):
        deps = set()
        for k in r:
            if k in self.lastw:
                deps.add(self.lastw[k])
            if k[0] == 'B' and k[1:].isdigit():
                for j in self.readers.get(k, ()):
                    if self.ops[j]['eng'] != eng:
                        deps.add(j)
        for k in w:
            if k in self.lastw:
                deps.add(self.lastw[k])
            deps.update(self.readers.get(k, ()))
        i = len(self.ops)
        self.ops.append(dict(eng=eng, fn=fn, deps=deps, dma=dma, group=group, has_dep=False))
        for k in r:
            self.readers.setdefault(k, []).append(i)
        for k in w:
            self.lastw[k] = i
            self.readers[k] = []
        return i

    def pe(self, fn, r=(), w=()):
        return self.add('pe', fn, r, w)

    def act(self, fn, r=(), w=()):
        return self.add('act', fn, r, w)

    def dve(self, fn, r=(), w=()):
        return self.add('dve', fn, r, w)

    def pool(self, fn, r=(), w=()):
        return self.add('pool', fn, r, w)

    def dma(self, eng, fn, r=(), w=(), sem=None, group=False):
        assert sem is not None
        return self.add(eng, fn, r, w, dma=sem, group=group)

    def wait_all(self, eng, keys):
        return self.add(eng, None, r=keys, w=())

    def emit(self):
        nc = self.nc
        ops = self.ops
        engs = ['sp', 'act', 'dve', 'pool', 'pe']
        for o in ops:
            for d in o['deps']:
                if ops[d]['eng'] == 'pe' and o['eng'] == 'pe' and ops[d]['dma'] is None and o['dma'] is None:
                    continue
                ops[d]['has_dep'] = True
        esem = {e: nc.alloc_semaphore(name=f"s_{e}") for e in engs}
        group_tot = {}
        for o in ops:
            if o['dma'] is not None:
                if o['dma'] not in self.dma_sems:
                    self.dma_sems[o['dma']] = nc.alloc_semaphore(name=f"d_{o['dma']}")
                group_tot[o['dma']] = group_tot.get(o['dma'], 0) + 1
        cnt = {e: 0 for e in engs}
        dcnt = {}
        for o in ops:
            if o['fn'] is None:
                o['tok'] = None
            elif o['dma'] is not None:
                k = o['dma']
                dcnt[k] = dcnt.get(k, 0) + 1
                v = group_tot[k] if o['group'] else dcnt[k]
                o['tok'] = (('d', k), 16 * v)
            elif o['has_dep']:
                cnt[o['eng']] += 1
                o['tok'] = (('e', o['eng']), cnt[o['eng']])
            else:
                o['tok'] = None
        known = {e: {} for e in engs}
        for o in ops:
            e = o['eng']
            kn = known[e]
            waits = []
            for d in sorted(o['deps'], reverse=True):
                od = ops[d]
                if od['tok'] is None:
                    continue
                if od['eng'] == 'pe' and e == 'pe' and od['dma'] is None and o['dma'] is None:
                    continue
                s, v = od['tok']
                if kn.get(s, 0) < v:
                    waits.append((s, v))
                    kn[s] = v
                    for s2, v2 in od['clock'].items():
                        if kn.get(s2, 0) < v2:
                            kn[s2] = v2
            wm = {}
            for s, v in waits:
                wm[s] = max(wm.get(s, 0), v)
            o['waits'] = wm
            o['clock'] = dict(kn)

        def semof(s):
            return esem[s[1]] if s[0] == 'e' else self.dma_sems[s[1]]

        def run(ename, eng):
            for o in ops:
                if o['eng'] != ename:
                    continue
                for s, v in o['waits'].items():
                    eng.wait_ge(semof(s), v)
                if o['fn'] is None:
                    continue
                inst = o['fn'](eng)
                if o['tok'] is not None:
                    s, v = o['tok']
                    inst.then_inc(semof(s), 16 if s[0] == 'd' else 1)

        with nc.Block() as block:
            @block.sync
            def _(e):
                run('sp', e)

            @block.scalar
            def _(e):
                run('act', e)

            @block.vector
            def _(e):
                run('dve', e)

            @block.gpsimd
            def _(e):
                run('pool', e)

            @block.tensor
            def _(e):
                run('pe', e)
        n = {e: sum(1 for o in ops if o['eng'] == e) for e in engs}
        nw = sum(len(o['waits']) for o in ops)
        print("PROG ops", n, "waits", nw, "sems", 5 + len(self.dma_sems), flush=True)


def _kw(**k):
    return {a: b for a, b in k.items() if b is not None}


class P2(Prog):
    def mm(self, out, lhsT, rhs, start=True, stop=True, r=(), w=()):
        return self.add('pe', lambda e: e.matmul(out, lhsT=lhsT, rhs=rhs, start=start, stop=stop), r, w)

    def tr(self, out, in_, ident, r=(), w=()):
        return self.add('pe', lambda e: e.transpose(out, in_, ident), r, w)

    def actv(self, out, in_, func, bias=None, scale=None, accum=None, r=(), w=()):
        kw = _kw(bias=bias, scale=scale, accum_out=accum)
        return self.add('act', lambda e: e.activation(out=out, in_=in_, func=func, **kw), r, w)

    def ts(self, eng, out, in0, s1, s2=None, op0=ALU.mult, op1=None, r=(), w=()):
        kw = _kw(op1=op1)
        return self.add(eng, lambda e: e.tensor_scalar(out=out, in0=in0, scalar1=s1, scalar2=s2, op0=op0, **kw), r, w)

    def tt(self, eng, out, in0, in1, op, r=(), w=()):
        return self.add(eng, lambda e: e.tensor_tensor(out=out, in0=in0, in1=in1, op=op), r, w)

    def stt(self, out, in0, scalar, in1, op0, op1, r=(), w=()):
        return self.add('dve', lambda e: e.scalar_tensor_tensor(out=out, in0=in0, scalar=scalar, in1=in1, op0=op0, op1=op1), r, w)

    def cp(self, eng, out, in_, r=(), w=()):
        if eng == 'act':
            return self.add('act', lambda e: e.activation(out=out, in_=in_, func=AF.Copy), r, w)
        return self.add(eng, lambda e: e.tensor_copy(out=out, in_=in_), r, w)

    def ms(self, eng, ap, val, w=()):
        return self.add(eng, lambda e: e.memset(ap, val), (), w)

    def ld(self, out, in_, w, sem, eng='sp', group=False, r=()):
        return self.dma(eng, lambda e: e.dma_start(out=out, in_=in_), r=r, w=w, sem=sem, group=group)

SEQ = 8192
DM = 1024
NCH = SEQ // 256
EPS = 1e-6
WCOLS = 1288


def build_stageA(nch=NCH):
    nc = bass.Bass("TRN2", target_bir_lowering=False)
    x_d = nc.dram_tensor("x", [SEQ, DM], F32, kind="ExternalInput").ap()
    w_d = nc.dram_tensor("w", [DM, WCOLS], F32, kind="ExternalInput").ap()
    gpre_d = nc.dram_tensor("gpre", [128, 8], F32, kind="ExternalInput").ap()
    cw_d = nc.dram_tensor("cw", [128, 24], F32, kind="ExternalInput").ap()
    cb_d = nc.dram_tensor("cb", [128, 6], F32, kind="ExternalInput").ap()
    dtb_d = nc.dram_tensor("dtb", [1, 8], F32, kind="ExternalInput").ap()
    alog_d = nc.dram_tensor("alog", [1, 8], F32, kind="ExternalInput").ap()
    dsk_d = nc.dram_tensor("dsk", [1, 8], F32, kind="ExternalInput").ap()
    gout_d = nc.dram_tensor("gout", [1, 512], F32, kind="ExternalInput").ap()
    yn_d = nc.dram_tensor("yn", [SEQ, 512], BF16, kind="ExternalOutput").ap()

    P = P2(nc)
    es = contextlib.ExitStack()

    def S(name, shape, dt):
        return es.enter_context(nc.sbuf_tensor(name, shape, dt))

    banks = [es.enter_context(nc.psum_tensor(f"bank{i}", [128, 512], F32)) for i in range(8)]

    W = S("W", [128, 8, WCOLS], BF16)
    wst = [S(f"wst{i}", [128, WCOLS], F32) for i in range(2)]
    gpre = S("gpre_s", [128, 8], F32)
    cw = S("cw_s", [128, 24], F32)
    cb = S("cb_s", [128, 6], F32)
    dtb_bc = S("dtb_bc", [128, 8], F32)
    A_bc = S("A_bc", [128, 8], F32)
    D_bc = S("D_bc", [128, 8], F32)
    gout_bc = S("gout_bc", [128, 512], F32)
    identf = S("identf", [128, 128], F32)
    identb = S("identb", [128, 128], BF16)
    onesf = S("onesf", [128, 128], F32)
    onesb = S("onesb", [128, 128], BF16)
    trif = S("trif", [128, 128], F32)
    triw = S("triw", [128, 256], BF16)
    SU = S("SU", [128, 128], BF16)
    cdiag = S("cdiag", [128, 24, 128], BF16)
    Dident = S("Dident", [128, 8, 128], BF16)
    xin = [S(f"xin{i}", [128, 2, DM], F32) for i in range(2)]
    junk = [S(f"junk{i}", [128, DM], BF16) for i in range(2)]
    ss = S("ss", [128, 2], F32)
    rt = S("rt", [128, 2], F32)
    rstd = S("rstd", [128, 2], F32)
    hn = S("hn", [128, 2, DM], BF16)
    hnT = S("hnT", [128, 8, 256], BF16)
    ubuf = S("ubuf", [128, 6, 259], BF16)
    xc = [S(f"xc{i}", [128, 6, 256], BF16) for i in range(2)]
    xtok = S("xtok", [128, 2, 640], BF16)
    dtr = S("dtr", [128, 2, 8], F32)
    e1 = S("e1", [128, 2, 8], F32)
    dtk = [S(f"dtk{i}", [128, 2, 8], F32) for i in range(2)]
    dtA = [S(f"dtA{i}", [128, 2, 8], F32) for i in range(2)]
    cend = S("cend", [128, 8], F32)
    ecum = [S(f"ecum{i}", [128, 2, 8], F32) for i in range(2)]
    wtmp = [S(f"wtmp{i}", [128, 2, 8], F32) for i in range(2)]
    dec = [S(f"dec{i}", [128, 8], F32) for i in range(2)]
    W0 = S("W0", [128, 8, 256], BF16)
    V1 = S("V1", [128, 8, 128], BF16)
    CBm = S("CBm", [128, 384], BF16)
    xdt = S("xdt", [128, 2, 512], BF16)
    Lb = [S(f"Lb{i}", [128, 384], BF16) for i in range(2)]
    junk2b = S("junk2b", [128, 512], BF16)
    MT = S("MT", [128, 8, 384], BF16)
    state = S("state", [128, 512], F32)
    state_bf = S("state_bf", [128, 512], BF16)
    yi = S("yi", [128, 512], F32)
    t1 = S("t1", [128, 512], F32)
    ysb = S("ysb", [128, 512], F32)
    zs = [S(f"zs{i}", [128, 2, 512], F32) for i in range(2)]
    yg = [S(f"yg{i}", [128, 512], F32) for i in range(2)]
    junk2 = S("junk2", [128, 512], BF16)
    ss2 = S("ss2", [128, 2], F32)
    rt2 = S("rt2", [128, 2], F32)
    rstd2 = S("rstd2", [128, 2], F32)
    yn = [S(f"yn{i}", [128, 512], BF16) for i in range(2)]
    wx = S("wx", [128, 2, 512], BF16)

    def bfview(bank):
        return bank[:].bitcast(BF16)

    ptr = bfview(banks[0]).rearrange("p (k t) -> p k t", k=8)
    pX = banks[1]
    pCv = banks[2]
    pdtk = banks[3][:, 0:16].rearrange("p (t c) -> p t c", t=2)
    pcum = banks[3][:, 16:32].rearrange("p (t c) -> p t c", t=2)
    pce = banks[3][:, 32:40]
    pz = banks[3][:, :]
    ptx = bfview(banks[4])[:, 0:640]
    pCB = banks[4][:, 0:384]
    pseg = [banks[5][:, 0:384], banks[7][:, 0:384]]
    py = banks[6][:, :]
    pyi = banks[4][:, :]
    pst = banks[4][:, :]

    P.ld(gpre[:], gpre_d, ['gpre'], 'c0')
    P.ld(cw[:], cw_d, ['cw'], 'c1')
    P.ld(cb[:], cb_d, ['cb'], 'c2')
    P.ld(dtb_bc[:], dtb_d.partition_broadcast(128), ['dtb_bc'], 'c3')
    P.ld(A_bc[:], alog_d.partition_broadcast(128), ['A_bc'], 'c4')
    P.ld(D_bc[:], dsk_d.partition_broadcast(128), ['D_bc'], 'c5')
    P.ld(gout_bc[:], gout_d.partition_broadcast(128), ['gout_bc'], 'c6')
    P.ms('pool', identf[:], 1.0, ['identf'])
    P.add('pool', lambda e: e.affine_select(out=identf[:], in_=identf[:], pattern=[[-1, 128]], compare_op=ALU.is_equal,
                                            fill=0.0, base=0, channel_multiplier=1), r=['identf'], w=['identf'])
    P.cp('dve', identb[:], identf[:], r=['identf'], w=['identb'])
    P.ms('pool', onesf[:], 1.0, ['onesf'])
    P.ms('pool', onesb[:], 1.0, ['onesb'])
    P.ms('pool', triw[:], 1.0, ['triw'])
    P.add('pool', lambda e: e.affine_select(out=triw[:, 0:128], in_=triw[:, 0:128], pattern=[[1, 128]], compare_op=ALU.is_ge,
                                            fill=0.0, base=0, channel_multiplier=-1), r=['triw'], w=['triw'])
    P.cp('dve', trif[:], triw[:, 0:128], r=['triw'], w=['trif'])
    P.ms('pool', SU[:], 1.0, ['SU'])
    P.add('pool', lambda e: e.affine_select(out=SU[:], in_=SU[:], pattern=[[-1, 128]], compare_op=ALU.is_gt,
                                            fill=0.0, base=0, channel_multiplier=1), r=['SU'], w=['SU'])
    P.ms('pool', ubuf[:], 0.0, ['ubuf%d' % i for i in range(3)])
    P.ms('pool', state[:], 0.0, ['state'])
    P.ms('pool', state_bf[:], 0.0, ['state_bf'])
    for kt in range(8):
        P.ld(wst[kt % 2][:], w_d[kt * 128:(kt + 1) * 128, :], [f'wst{kt % 2}'], f'wst{kt % 2}')
        if kt % 2 == 0:
            P.ts('dve', W[:, kt, :], wst[kt % 2][:], gpre[:, kt:kt + 1], None, ALU.mult, r=[f'wst{kt % 2}', 'gpre'], w=[f'W{kt}'])
        else:
            P.actv(W[:, kt, :], wst[kt % 2][:], AF.Copy, scale=gpre[:, kt:kt + 1], r=[f'wst{kt % 2}', 'gpre'], w=[f'W{kt}'])
    Wk = [f'W{kt}' for kt in range(8)]
    HNT = ['hnT0', 'hnT1']
    for i in range(24):
        P.ts('dve', cdiag[:, i, :], identf[:], cw[:, i:i + 1], None, ALU.mult, r=['identf', 'cw'], w=['cdiag'])
    for h in range(8):
        P.ts('dve', Dident[:, h, :], identf[:], D_bc[:, h:h + 1], None, ALU.mult, r=['identf', 'D_bc'], w=['Dident'])
    P.actv(A_bc[:], A_bc[:], AF.Exp, r=['A_bc'], w=['A_bc'])
    P.ts('dve', A_bc[:], A_bc[:], -1.0, None, ALU.mult, r=['A_bc'], w=['A_bc'])

    def load_x(c):
        sl = c % 2
        P.ld(xin[sl][:], x_d[c * 256:(c + 1) * 256, :].rearrange("(t p) d -> p t d", p=128), [f'xin{sl}'], f'xin{sl}')

    def front(c):
        sl = c % 2
        p = c % 2
        xk = f'xin{sl}'
        if c + 1 < nch:
            load_x(c + 1)
        for t in range(2):
            P.actv(junk[t][:], xin[sl][:, t, :], AF.Square, accum=ss[:, t:t + 1], r=[xk], w=[f'ss{t}', f'junk{t}'])
        P.actv(rt[:], ss[:], AF.Ln, bias=EPS, scale=1.0 / DM, r=['ss0', 'ss1'], w=['rt'])
        P.actv(rstd[:], rt[:], AF.Exp, scale=-0.5, r=['rt'], w=['rstd'])
        for t in range(2):
            P.ts('dve', hn[:, t, :], xin[sl][:, t, :], rstd[:, t:t + 1], None, ALU.mult, r=[xk, 'rstd'], w=[f'hn{t}'])
        yield
        for t in range(2):
            for kt in range(8):
                P.tr(ptr[:, kt, :], hn[:, t, kt * 128:(kt + 1) * 128], identb[:], r=[f'hn{t}', 'identb'], w=['B0'])
            P.cp('dve' if t == 0 else 'act', hnT[:, :, t * 128:(t + 1) * 128], ptr, r=['B0'], w=[f'hnT{t}'])
            yield
        for t in range(2):
            for kt in range(8):
                P.mm(pdtk[:, t, :], hnT[:, kt, t * 128:(t + 1) * 128], W[:, kt, 768:776], start=(kt == 0), stop=(kt == 7),
                     r=HNT + [Wk[kt]], w=['B3'])
        P.tt('dve', dtr[:], pdtk, dtb_bc[:].unsqueeze(1).to_broadcast([128, 2, 8]), ALU.add, r=['B3', 'dtb_bc'], w=['dtr'])
        P.actv(e1[:], dtr[:], AF.Exp, r=['dtr'], w=['e1'])
        P.actv(dtk[p][:], e1[:], AF.Ln, bias=1.0, r=['e1'], w=[f'dtk{p}'])
        P.tt('dve', dtA[p][:], dtk[p][:], A_bc[:].unsqueeze(1).to_broadcast([128, 2, 8]), ALU.mult, r=[f'dtk{p}', 'A_bc'], w=[f'dtA{p}'])
        P.mm(pcum[:, 0, :], trif[:], dtA[p][:, 0, :], r=['trif', f'dtA{p}'], w=['B3'])
        P.mm(pcum[:, 1, :], onesf[:], dtA[p][:, 0, :], start=True, stop=False, r=['onesf', f'dtA{p}'], w=['B3'])
        P.mm(pcum[:, 1, :], trif[:], dtA[p][:, 1, :], start=False, stop=True, r=['trif', f'dtA{p}'], w=['B3'])
        P.mm(pce, onesf[:], dtA[p][:, 0, :], start=True, stop=False, r=['onesf', f'dtA{p}'], w=['B3'])
        P.mm(pce, onesf[:], dtA[p][:, 1, :], start=False, stop=True, r=['onesf', f'dtA{p}'], w=['B3'])
        P.actv(ecum[p][:], pcum, AF.Exp, r=['B3'], w=[f'ecum{p}'])
        P.actv(dec[p][:], pce, AF.Exp, r=['B3'], w=[f'dec{p}'])
        P.cp('act', cend[:], pce, r=['B3'], w=['cend'])
        P.tt('dve', wtmp[p][:], cend[:].unsqueeze(1).to_broadcast([128, 2, 8]), pcum, ALU.subtract, r=['cend', 'B3'], w=[f'wtmp{p}'])
        P.actv(wtmp[p][:], wtmp[p][:], AF.Exp, r=[f'wtmp{p}'], w=[f'wtmp{p}'])
        yield
        for pr in range(3):
            for j in range(2):
                ct = 2 * pr + j
                for kt in range(8):
                    P.mm(pX[:, j * 256:(j + 1) * 256], W[:, kt, ct * 128:(ct + 1) * 128], hnT[:, kt, :], start=(kt == 0), stop=(kt == 7),
                         r=HNT + [Wk[kt]], w=['B1'])
            P.cp('act', ubuf[:, 2 * pr:2 * pr + 2, 3:259], pX[:, :].rearrange("p (j t) -> p j t", j=2), r=['B1'], w=[f'ubuf{pr}'])
            for j in range(2):
                ct = 2 * pr + j
                for k in range(4):
                    P.mm(pCv[:, j * 256:(j + 1) * 256], cdiag[:, ct * 4 + k, :], ubuf[:, ct, k:k + 256], start=(k == 0), stop=(k == 3),
                         r=['cdiag', f'ubuf{pr}'], w=['B2'])
            for j in range(2):
                ct = 2 * pr + j
                P.actv(xc[p][:, ct, :], pCv[:, j * 256:(j + 1) * 256], AF.Silu, bias=cb[:, ct:ct + 1], r=['B2', 'cb'], w=[f'xc{p}_{ct}'])
            P.cp('pool', ubuf[:, 2 * pr:2 * pr + 2, 0:3], ubuf[:, 2 * pr:2 * pr + 2, 256:259], r=[f'ubuf{pr}'], w=[f'ubuf{pr}'])
            yield
        for t in range(2):
            for kt in range(8):
                P.mm(pz, hnT[:, kt, t * 128:(t + 1) * 128], W[:, kt, 776:1288], start=(kt == 0), stop=(kt == 7),
                     r=HNT + [Wk[kt]], w=['B3'])
            P.actv(zs[p][:, t, :], pz, AF.Silu, r=['B3'], w=[f'zs{p}_{t}'])
            yield

    def back(c):
        p = c % 2
        XC = [f'xc{p}_{ct}' for ct in range(6)]
        P.tt('dve', W0[:], triw[:].unsqueeze(1).to_broadcast([128, 8, 256]), dtA[p][:, 0, :].unsqueeze(2).to_broadcast([128, 8, 256]),
             ALU.mult, r=['triw', f'dtA{p}'], w=['W0'])
        P.tt('dve', V1[:], triw[:, 0:128].unsqueeze(1).to_broadcast([128, 8, 128]), dtA[p][:, 1, :].unsqueeze(2).to_broadcast([128, 8, 128]),
             ALU.mult, r=['triw', f'dtA{p}'], w=['V1'])
        for t in range(2):
            for ct in range(5):
                P.tr(ptx[:, ct * 128:(ct + 1) * 128], xc[p][:, ct, t * 128:(t + 1) * 128], identb[:], r=[XC[ct], 'identb'], w=['B4'])
            P.cp('dve' if t == 0 else 'act', xtok[:, t, :], ptx, r=['B4'], w=[f'xtok{t}'])
            P.tt('pool', xdt[:, t, :].rearrange("p (h c) -> p h c", h=8), xtok[:, t, 0:512].rearrange("p (h c) -> p h c", h=8),
                 dtk[p][:, t, :].unsqueeze(2).to_broadcast([128, 8, 64]), ALU.mult, r=[f'xtok{t}', f'dtk{p}'], w=[f'xdt{t}'])
        yield
        P.mm(pCB[:, 0:256], xc[p][:, 4, 0:128], xc[p][:, 5, 0:256], r=[XC[4], XC[5]], w=['B4'])
        P.mm(pCB[:, 256:384], xc[p][:, 4, 128:256], xc[p][:, 5, 128:256], r=[XC[4], XC[5]], w=['B4'])
        P.cp('act', CBm[:], pCB, r=['B4'], w=['CBm'])
        for off in (0, 256):
            blk = CBm[:, off:off + 128]
            P.add('pool', (lambda blk: (lambda e: e.affine_select(out=blk, in_=blk, pattern=[[1, 128]], compare_op=ALU.is_ge,
                                                                  fill=0.0, base=0, channel_multiplier=-1)))(blk),
                  r=['CBm'], w=['CBm'])
        yield
        for h in range(8):
            L = Lb[h % 2]
            Lk = f'Lb{h % 2}'
            ps = pseg[h % 2]
            psk = 'B5' if h % 2 == 0 else 'B7'
            P.mm(ps[:, 0:128], SU[:], W0[:, h, 0:128], start=True, stop=True, r=['SU', 'W0'], w=[psk])
            P.mm(ps[:, 128:256], SU[:], W0[:, h, 128:256], start=True, stop=False, r=['SU', 'W0'], w=[psk])
            P.mm(ps[:, 128:256], onesb[:], V1[:, h, :], start=False, stop=True, r=['onesb', 'V1'], w=[psk])
            P.mm(ps[:, 256:384], SU[:], V1[:, h, :], start=True, stop=True, r=['SU', 'V1'], w=[psk])
            P.actv(L[:], ps, AF.Exp, r=[psk], w=[Lk])
            P.tt('dve', MT[:, h, :], L[:], CBm[:], ALU.mult, r=[Lk, 'CBm'], w=[f'MT{h}'])
            if h % 2 == 1:
                yield
        for t in range(2):
            for h in range(8):
                hc = slice(h * 64, (h + 1) * 64)
                P.mm(py[:, hc], MT[:, h, t * 128:(t + 1) * 128], xdt[:, 0, hc], start=True, stop=False, r=[f'MT{h}', 'xdt0'], w=['B6'])
                if t == 1:
                    P.mm(py[:, hc], MT[:, h, 256:384], xdt[:, 1, hc], start=False, stop=False, r=[f'MT{h}', 'xdt1'], w=['B6'])
                P.mm(py[:, hc], Dident[:, h, :], xtok[:, t, hc], start=False, stop=True, r=['Dident', f'xtok{t}'], w=['B6'])
            P.mm(pyi, xc[p][:, 5, t * 128:(t + 1) * 128], state_bf[:], r=[XC[5], 'state_bf'], w=['B4'])
            P.tt('dve', t1[:].rearrange("p (h c) -> p h c", h=8), pyi.rearrange("p (h c) -> p h c", h=8),
                 ecum[p][:, t, :].unsqueeze(2).to_broadcast([128, 8, 64]), ALU.mult, r=['B4', f'ecum{p}'], w=['t1'])
            P.tt('dve', ysb[:], t1[:], py, ALU.add, r=['t1', 'B6'], w=['ysb'])
            P.tt('dve', yg[t][:], ysb[:], zs[p][:, t, :], ALU.mult, r=['ysb', f'zs{p}_{t}'], w=[f'yg{t}'])
            P.actv(junk2b[:], yg[t][:], AF.Square, accum=ss2[:, t:t + 1], r=[f'yg{t}'], w=[f'ss2_{t}', 'junk2b'])
            yield
        P.actv(rt2[:], ss2[:], AF.Ln, bias=EPS, scale=1.0 / 512, r=['ss2_0', 'ss2_1'], w=['rt2'])
        P.actv(rstd2[:], rt2[:], AF.Exp, scale=-0.5, r=['rt2'], w=['rstd2'])
        for t in range(2):
            P.stt(yn[t][:], yg[t][:], rstd2[:, t:t + 1], gout_bc[:], ALU.mult, ALU.mult, r=[f'yg{t}', 'rstd2', 'gout_bc'], w=[f'yn{t}'])
            P.ld(yn_d[c * 256 + t * 128: c * 256 + (t + 1) * 128, :], yn[t][:], w=[f'ynd{t}'], sem=f'st{t}', r=[f'yn{t}'])
        for st in range(2):
            P.tt('pool', wx[:, st, :].rearrange("p (h c) -> p h c", h=8), xdt[:, st, :].rearrange("p (h c) -> p h c", h=8),
                 wtmp[p][:, st, :].unsqueeze(2).to_broadcast([128, 8, 64]), ALU.mult, r=[f'xdt{st}', f'wtmp{p}'], w=[f'wx{st}'])
        for st in range(2):
            P.mm(pst, xtok[:, st, 512:640], wx[:, st, :], start=(st == 0), stop=(st == 1), r=[f'xtok{st}', f'wx{st}'], w=['B4'])
        P.tt('dve', state[:].rearrange("p (h c) -> p h c", h=8), state[:].rearrange("p (h c) -> p h c", h=8),
             dec[p][:].unsqueeze(2).to_broadcast([128, 8, 64]), ALU.mult, r=['state', f'dec{p}'], w=['state'])
        P.tt('dve', state[:], state[:], pst, ALU.add, r=['state', 'B4'], w=['state'])
        P.cp('act', state_bf[:], state[:], r=['state'], w=['state_bf'])
        yield

    load_x(0)
    for it in range(nch + 1):
        gens = []
        if it >= 1:
            gens.append(back(it - 1))
        if it < nch:
            gens.append(front(it))
        while gens:
            for g in list(gens):
                try:
                    next(g)
                except StopIteration:
                    gens.remove(g)
    P.wait_all('sp', ['ynd0', 'ynd1'])
    P.emit()
    es.close()
    return nc

NTOK = 2048
NTT = NTOK // 128
INV_FREQ = [float(np.float32(10000.0) ** np.float32(-(2 * i) / 32.0)) for i in range(16)]
TWO_PI = 2.0 * math.pi
CW1 = 6.28125
CW2 = TWO_PI - CW1


def build_stageB(ntt=NTT):
    nc = bass.Bass("TRN2", target_bir_lowering=False)
    x_d = nc.dram_tensor("x", [NTOK, 1024], F32, kind="ExternalInput").ap()
    yn_d = nc.dram_tensor("yn", [NTOK, 2048], BF16, kind="ExternalInput").ap()
    pos_d = nc.dram_tensor("pos", [128, NTT], I32, kind="ExternalInput").ap()
    invf_d = nc.dram_tensor("invf", [1, 16], F32, kind="ExternalInput").ap()
    wout_d = nc.dram_tensor("wout", [2048, 1024], F32, kind="ExternalInput").ap()
    wdn_d = nc.dram_tensor("wdn", [1024, 288], F32, kind="ExternalInput").ap()
    wup_d = nc.dram_tensor("wup", [256, 2048], F32, kind="ExternalInput").ap()
    win_d = nc.dram_tensor("win", [1024, 1408], F32, kind="ExternalInput").ap()
    wuq_d = nc.dram_tensor("wuq", [384, 1536], F32, kind="ExternalInput").ap()
    gkv_d = nc.dram_tensor("gkv", [128, 8], F32, kind="ExternalInput").ap()
    gpre_d = nc.dram_tensor("gpre", [128, 8], F32, kind="ExternalInput").ap()
    glat_d = nc.dram_tensor("glat", [128, 2], F32, kind="ExternalInput").ap()
    gq_d = nc.dram_tensor("gq", [128, 3], F32, kind="ExternalInput").ap()
    h1_d = nc.dram_tensor("h1", [NTOK, 1024], F32, kind="ExternalOutput").ap()
    sg_d = nc.dram_tensor("sg", [128, 8, NTOK], BF16, kind="ExternalOutput").ap()
    kn_d = nc.dram_tensor("kn", [128, 8, NTOK], BF16, kind="ExternalOutput").ap()
    kr_d = nc.dram_tensor("kr", [32, NTOK], BF16, kind="ExternalOutput").ap()
    v_d = nc.dram_tensor("v", [NTOK, 1024], BF16, kind="ExternalOutput").ap()
    qT_d = nc.dram_tensor("qT", [96, 16, NTOK], BF16, kind="ExternalOutput").ap()

    P = P2(nc)
    es = contextlib.ExitStack()

    def S(name, shape, dt):
        return es.enter_context(nc.sbuf_tensor(name, shape, dt))

    banks = [es.enter_context(nc.psum_tensor(f"bank{i}", [128, 512], F32)) for i in range(8)]
    bctr = [0, 0]
    ring = [0]

    def nb():
        r = ring[0]
        i = r * 4 + bctr[r] % 4
        bctr[r] += 1
        return banks[i], f'B{i}'

    def bfv(bank):
        return bank[:].bitcast(BF16)

    wout = S("wout_s", [128, 16, 1024], BF16)
    wdn = S("wdn_s", [128, 8, 288], BF16)
    wkn = S("wkn_s", [128, 2, 1024], BF16)
    wv = S("wv_s", [128, 2, 1024], BF16)
    win = S("win_s", [128, 8, 1408], BF16)
    wuq = S("wuq_s", [128, 3, 1536], BF16)
    wst = [S(f"wst{i}", [128, 2048], F32) for i in range(2)]
    gkv = S("gkv_s", [128, 8], F32)
    gpre = S("gpre_s", [128, 8], F32)
    glat = S("glat_s", [128, 2], F32)
    gq = S("gq_s", [128, 3], F32)
    identf = S("identf", [128, 128], F32)
    identb = S("identb", [128, 128], BF16)
    posi = S("posi", [128, NTT], I32)
    posf = S("posf", [128, NTT], F32)
    invf = S("invf_s", [128, 16], F32)
    ang = S("ang", [128, NTT, 16], F32)
    uu = S("uu", [128, NTT, 16], F32)
    ki = S("ki", [128, NTT, 16], I32)
    kf = S("kf", [128, NTT, 16], F32)
    gg = S("gg", [128, NTT, 16], F32)
    m1 = S("m1", [128, NTT, 16], F32)
    gc = S("gc", [128, NTT, 16], F32)
    sinT = S("sinT", [128, NTT, 16], F32)
    cosT = S("cosT", [128, NTT, 16], F32)
    xin = [S(f"xin{i}", [128, 1024], F32) for i in range(2)]
    ynin = [S(f"ynin{i}", [128, 2048], BF16) for i in range(2)]
    ynT = S("ynT", [128, 16, 128], BF16)
    h1 = [S(f"h1_{i}", [128, 1024], F32) for i in range(2)]
    junk = S("junk", [128, 1024], BF16)
    ss = S("ss", [128, 1], F32)
    rt = S("rt", [128, 1], F32)
    rstd = S("rstd", [128, 1], F32)
    hnb = S("hnb", [128, 1024], BF16)
    hT2 = [S(f"hT{i}", [128, 8, 128], BF16) for i in range(2)]
    junk2 = S("junk2", [128, 384], BF16)
    ssc = S("ssc", [128, 1], F32)
    rtc = S("rtc", [128, 1], F32)
    rstdc = S("rstdc", [128, 1], F32)
    ckvn = S("ckvn", [128, 256], BF16)
    ra = S("ra", [128, 16], F32)
    rb = S("rb", [128, 16], F32)
    krb = S("krb", [128, 32], BF16)
    ckT = S("ckT", [128, 2, 128], BF16)
    krT = [S(f"krT{i}", [32, 128], BF16) for i in range(2)]
    knT = [S(f"knT{i}", [128, 8, 128], BF16) for i in range(2)]
    vsb = [S(f"vsb{i}", [128, 1024], BF16) for i in range(2)]
    ssq = S("ssq", [128, 1], F32)
    rtq = S("rtq", [128, 1], F32)
    rstdq = S("rstdq", [128, 1], F32)
    cqn = S("cqn", [128, 384], BF16)
    sg = [S(f"sg{i}", [128, 8, 128], BF16) for i in range(2)]
    cqT = S("cqT", [128, 3, 128], BF16)
    qtok = S("qtok", [128, 16, 96], BF16)
    qa = S("qa", [128, 16, 16], F32)
    qb = S("qb", [128, 16, 16], F32)
    qT = [S(f"qT{i}", [96, 16, 128], BF16) for i in range(2)]

    P.ld(gkv[:], gkv_d, ['gkv'], 'c0')
    P.ld(gpre[:], gpre_d, ['gpre'], 'c1')
    P.ld(glat[:], glat_d, ['glat'], 'c2')
    P.ld(gq[:], gq_d, ['gq'], 'c3')
    P.ld(posi[:], pos_d, ['posi'], 'c4')
    P.ld(invf[:], invf_d.partition_broadcast(128), ['invf'], 'c5')
    P.ms('pool', identf[:], 1.0, ['identf'])
    P.add('pool', lambda e: e.affine_select(out=identf[:], in_=identf[:], pattern=[[-1, 128]], compare_op=ALU.is_equal,
                                            fill=0.0, base=0, channel_multiplier=1), r=['identf'], w=['identf'])
    P.cp('dve', identb[:], identf[:], r=['identf'], w=['identb'])
    P.cp('dve', posf[:], posi[:], r=['posi'], w=['posf'])
    P.tt('dve', ang[:], posf[:].unsqueeze(2).to_broadcast([128, NTT, 16]), invf[:].unsqueeze(1).to_broadcast([128, NTT, 16]),
         ALU.mult, r=['posf', 'invf'], w=['ang'])
    P.ts('dve', uu[:], ang[:], 1.0 / TWO_PI, None, ALU.mult, r=['ang'], w=['uu'])
    P.cp('dve', ki[:], uu[:], r=['uu'], w=['ki'])
    P.cp('dve', kf[:], ki[:], r=['ki'], w=['kf'])
    P.stt(gg[:], kf[:], -CW1, ang[:], ALU.mult, ALU.add, r=['kf', 'ang'], w=['gg'])
    P.stt(gg[:], kf[:], -CW2, gg[:], ALU.mult, ALU.add, r=['kf', 'gg'], w=['gg'])
    P.ts('dve', gg[:], gg[:], 1.0 / TWO_PI, None, ALU.mult, r=['gg'], w=['gg'])

    def wrap():
        P.ts('dve', m1[:], gg[:], 0.5, None, ALU.is_gt, r=['gg'], w=['m1'])
        P.tt('dve', gg[:], gg[:], m1[:], ALU.subtract, r=['gg', 'm1'], w=['gg'])
        P.ts('dve', m1[:], gg[:], -0.5, None, ALU.is_lt, r=['gg'], w=['m1'])
        P.tt('dve', gg[:], gg[:], m1[:], ALU.add, r=['gg', 'm1'], w=['gg'])
        P.ts('dve', gg[:], gg[:], 0.4999995, -0.4999995, ALU.min, ALU.max, r=['gg'], w=['gg'])

    wrap()
    P.actv(sinT[:], gg[:], AF.Sin, scale=TWO_PI, r=['gg'], w=['sinT'])
    P.ts('dve', gg[:], gg[:], 0.25, None, ALU.add, r=['gg'], w=['gg'])
    wrap()
    P.actv(cosT[:], gg[:], AF.Sin, scale=TWO_PI, r=['gg'], w=['cosT'])

    wi = [0]
    WK = {}

    def wload(grp, dst_ap, src_ap, ncols, gain_ap, in_view=None):
        i = wi[0] % 2
        wi[0] += 1
        key = f'W{wi[0]}'
        WK.setdefault(grp, []).append(key)
        P.ld(wst[i][:, 0:ncols], src_ap, [f'wst{i}'], f'wst{i}')
        src = wst[i][:, 0:ncols] if in_view is None else in_view(wst[i])
        if i == 0:
            if gain_ap is None:
                P.cp('dve', dst_ap, src, r=[f'wst{i}'], w=[key])
            else:
                P.ts('dve', dst_ap, src, gain_ap, None, ALU.mult, r=[f'wst{i}', 'gkv', 'gpre', 'glat', 'gq'], w=[key])
        else:
            if gain_ap is None:
                P.cp('act', dst_ap, src, r=[f'wst{i}'], w=[key])
            else:
                P.actv(dst_ap, src, AF.Copy, scale=gain_ap, r=[f'wst{i}', 'gkv', 'gpre', 'glat', 'gq'], w=[key])

    for kt in range(16):
        wload('wout', wout[:, kt, :], wout_d[kt * 128:(kt + 1) * 128, :], 1024, None)
    for kt in range(8):
        wload('wdn', wdn[:, kt, :], wdn_d[kt * 128:(kt + 1) * 128, :], 288, gkv[:, kt:kt + 1])
    for kt in range(8):
        wload('win', win[:, kt, :], win_d[kt * 128:(kt + 1) * 128, :], 1408, gpre[:, kt:kt + 1])
    for kt in range(2):
        wload('wkn', wkn[:, kt, :].rearrange("p (h c) -> p h c", h=16), wup_d[kt * 128:(kt + 1) * 128, :], 2048, glat[:, kt:kt + 1],
              in_view=lambda t: t[:, 0:2048].rearrange("p (h c) -> p h c", h=16)[:, :, 0:64])
        wload('wv', wv[:, kt, :].rearrange("p (h c) -> p h c", h=16), wup_d[kt * 128:(kt + 1) * 128, :], 2048, glat[:, kt:kt + 1],
              in_view=lambda t: t[:, 0:2048].rearrange("p (h c) -> p h c", h=16)[:, :, 64:128])
    for kt in range(3):
        wload('wuq', wuq[:, kt, 0:1024].rearrange("p (h c) -> p h c", h=16), wuq_d[kt * 128:(kt + 1) * 128, :], 1536, gq[:, kt:kt + 1],
              in_view=lambda t: t[:, 0:1536].rearrange("p (h c) -> p h c", h=16)[:, :, 0:64])
        wload('wuq', wuq[:, kt, 1024:1280].rearrange("p (h c) -> p h c", h=16), wuq_d[kt * 128:(kt + 1) * 128, :], 1536, gq[:, kt:kt + 1],
              in_view=lambda t: t[:, 0:1536].rearrange("p (h c) -> p h c", h=16)[:, :, 64:80])
        wload('wuq', wuq[:, kt, 1280:1536].rearrange("p (h c) -> p h c", h=16), wuq_d[kt * 128:(kt + 1) * 128, :], 1536, gq[:, kt:kt + 1],
              in_view=lambda t: t[:, 0:1536].rearrange("p (h c) -> p h c", h=16)[:, :, 80:96])


    def load_t(tt):
        sl = tt % 2
        P.ld(xin[sl][:], x_d[tt * 128:(tt + 1) * 128, :], [f'xin{sl}'], f'xin{sl}')
        P.ld(ynin[sl][:], yn_d[tt * 128:(tt + 1) * 128, :], [f'ynin{sl}'], f'ynin{sl}')

    def front(tt):
        ring[0] = 0
        sl = tt % 2
        hT = hT2[tt % 2]
        hTk = f'hT{tt % 2}'
        tok = slice(tt * 128, (tt + 1) * 128)
        if tt + 1 < ntt:
            load_t(tt + 1)
        for half in range(2):
            bk, bkk = nb()
            pv = bfv(bk).rearrange("p (k t) -> p k t", k=8)
            for j in range(8):
                c = half * 8 + j
                P.tr(pv[:, j, :], ynin[sl][:, c * 128:(c + 1) * 128], identb[:], r=[f'ynin{sl}', 'identb'], w=[bkk])
            P.cp('dve' if half == 0 else 'act', ynT[:, half * 8:(half + 1) * 8, :], pv, r=[bkk], w=[f'ynT{half}'])
        yield
        ring[0] = 0
        for half in range(2):
            bk, bkk = nb()
            for c in range(16):
                P.mm(bk[:, :], ynT[:, c, :], wout[:, c, half * 512:(half + 1) * 512], start=(c == 0), stop=(c == 15),
                     r=['ynT0', 'ynT1', *WK['wout']], w=[bkk])
            P.tt('dve', h1[sl][:, half * 512:(half + 1) * 512], bk[:, :], xin[sl][:, half * 512:(half + 1) * 512], ALU.add,
                 r=[bkk, f'xin{sl}'], w=[f'h1_{sl}'])
        P.ld(h1_d[tok, :], h1[sl][:], w=[f'h1d{sl}'], sem=f'sth{sl}', r=[f'h1_{sl}'])
        yield
        ring[0] = 0
        P.actv(junk[:], h1[sl][:], AF.Square, accum=ss[:], r=[f'h1_{sl}'], w=['junk', 'ss'])
        P.actv(rt[:], ss[:], AF.Sqrt, bias=EPS, scale=1.0 / 1024, r=['ss'], w=['rt'])
        P.add('dve', lambda e: e.reciprocal(out=rstd[:], in_=rt[:]), r=['rt'], w=['rstd'])
        P.ts('dve', hnb[:], h1[sl][:], rstd[:, 0:1], None, ALU.mult, r=[f'h1_{sl}', 'rstd'], w=['hnb'])
        bk, bkk = nb()
        pv = bfv(bk).rearrange("p (k t) -> p k t", k=8)
        for kt in range(8):
            P.tr(pv[:, kt, :], hnb[:, kt * 128:(kt + 1) * 128], identb[:], r=['hnb', 'identb'], w=[bkk])
        P.cp('act', hT[:], pv, r=[bkk], w=[hTk])
        yield

    def back(tt):
        ring[0] = 1
        sl = tt % 2
        hT = hT2[tt % 2]
        hTk = f'hT{tt % 2}'
        tok = slice(tt * 128, (tt + 1) * 128)
        bk, bkk = nb()
        for kt in range(8):
            P.mm(bk[:, 0:288], hT[:, kt, :], wdn[:, kt, :], start=(kt == 0), stop=(kt == 7), r=[hTk, *WK['wdn']], w=[bkk])
        P.actv(junk2[:, 0:256], bk[:, 0:256], AF.Square, accum=ssc[:], r=[bkk], w=['junk2', 'ssc'])
        P.actv(rtc[:], ssc[:], AF.Sqrt, bias=EPS, scale=1.0 / 256, r=['ssc'], w=['rtc'])
        P.add('dve', lambda e: e.reciprocal(out=rstdc[:], in_=rtc[:]), r=['rtc'], w=['rstdc'])
        P.tt('dve', ra[:], bk[:, 256:272], cosT[:, tt, :], ALU.mult, r=[bkk, 'cosT'], w=['ra'])
        P.tt('dve', rb[:], bk[:, 272:288], sinT[:, tt, :], ALU.mult, r=[bkk, 'sinT'], w=['rb'])
        P.tt('dve', krb[:, 0:16], ra[:], rb[:], ALU.subtract, r=['ra', 'rb'], w=['krb'])
        P.tt('dve', ra[:], bk[:, 256:272], sinT[:, tt, :], ALU.mult, r=[bkk, 'sinT', 'krb'], w=['ra'])
        P.tt('dve', rb[:], bk[:, 272:288], cosT[:, tt, :], ALU.mult, r=[bkk, 'cosT', 'krb'], w=['rb'])
        P.tt('dve', krb[:, 16:32], ra[:], rb[:], ALU.add, r=['ra', 'rb'], w=['krb'])
        P.ts('dve', ckvn[:], bk[:, 0:256], rstdc[:, 0:1], None, ALU.mult, r=[bkk, 'rstdc'], w=['ckvn'])
        bk, bkk = nb()
        pv = bfv(bk)
        for kt in range(2):
            P.tr(pv[:, kt * 128:(kt + 1) * 128], ckvn[:, kt * 128:(kt + 1) * 128], identb[:], r=['ckvn', 'identb'], w=[bkk])
        P.tr(pv[0:32, 256:384], krb[:], identb[:], r=['krb', 'identb'], w=[bkk])
        P.cp('act', ckT[:], pv[:, 0:256].rearrange("p (k t) -> p k t", k=2), r=[bkk], w=['ckT'])
        P.cp('act', krT[sl][:], pv[0:32, 256:384], r=[bkk], w=[f'krT{sl}'])
        P.ld(kr_d[:, tok], krT[sl][:], w=[f'krd{sl}'], sem=f'stkr{sl}', r=[f'krT{sl}'])
        yield
        ring[0] = 1
        for half in range(2):
            bk, bkk = nb()
            for j in range(4):
                pr = half * 4 + j
                for kt in range(2):
                    P.mm(bk[:, j * 128:(j + 1) * 128], wkn[:, kt, pr * 128:(pr + 1) * 128], ckT[:, kt, :], start=(kt == 0), stop=(kt == 1),
                         r=['ckT', *WK['wkn']], w=[bkk])
            P.cp('act' if half == 0 else 'dve', knT[sl][:, half * 4:(half + 1) * 4, :], bk[:, :].rearrange("p (j t) -> p j t", j=4),
                 r=[bkk], w=[f'knT{sl}'])
        P.ld(kn_d[:, :, tok], knT[sl][:], w=[f'knd{sl}'], sem=f'stkn{sl}', r=[f'knT{sl}'])
        yield
        ring[0] = 1
        for half in range(2):
            bk, bkk = nb()
            for kt in range(2):
                P.mm(bk[:, :], ckT[:, kt, :], wv[:, kt, half * 512:(half + 1) * 512], start=(kt == 0), stop=(kt == 1),
                     r=['ckT', *WK['wv']], w=[bkk])
            P.cp('act' if half == 0 else 'dve', vsb[sl][:, half * 512:(half + 1) * 512], bk[:, :], r=[bkk], w=[f'vsb{sl}'])
        P.ld(v_d[tok, :], vsb[sl][:], w=[f'vd{sl}'], sem=f'stv{sl}', r=[f'vsb{sl}'])
        yield
        ring[0] = 1
        bk, bkk = nb()
        for kt in range(8):
            P.mm(bk[:, 0:384], hT[:, kt, :], win[:, kt, 0:384], start=(kt == 0), stop=(kt == 7), r=[hTk, *WK['win']], w=[bkk])
        P.actv(junk2[:], bk[:, 0:384], AF.Square, accum=ssq[:], r=[bkk], w=['junk2', 'ssq'])
        P.actv(rtq[:], ssq[:], AF.Sqrt, bias=EPS, scale=1.0 / 384, r=['ssq'], w=['rtq'])
        P.add('dve', lambda e: e.reciprocal(out=rstdq[:], in_=rtq[:]), r=['rtq'], w=['rstdq'])
        P.ts('dve', cqn[:], bk[:, 0:384], rstdq[:, 0:1], None, ALU.mult, r=[bkk, 'rstdq'], w=['cqn'])
        for half in range(2):
            bk, bkk = nb()
            for j in range(4):
                ct = half * 4 + j
                for kt in range(8):
                    P.mm(bk[:, j * 128:(j + 1) * 128], win[:, kt, 384 + ct * 128:384 + (ct + 1) * 128], hT[:, kt, :],
                         start=(kt == 0), stop=(kt == 7), r=[hTk, *WK['win']], w=[bkk])
            P.actv(sg[sl][:, half * 4:(half + 1) * 4, :], bk[:, :].rearrange("p (j t) -> p j t", j=4), AF.Silu, r=[bkk], w=[f'sg{sl}'])
        P.ld(sg_d[:, :, tok], sg[sl][:], w=[f'sgd{sl}'], sem=f'stsg{sl}', r=[f'sg{sl}'])
        bk, bkk = nb()
        pv = bfv(bk)
        for kt in range(3):
            P.tr(pv[:, kt * 128:(kt + 1) * 128], cqn[:, kt * 128:(kt + 1) * 128], identb[:], r=['cqn', 'identb'], w=[bkk])
        P.cp('act', cqT[:], pv[:, 0:384].rearrange("p (k t) -> p k t", k=3), r=[bkk], w=['cqT'])
        yield
        ring[0] = 1
        for blk in range(2):
            bk, bkk = nb()
            for kt in range(3):
                P.mm(bk[:, :], cqT[:, kt, :], wuq[:, kt, blk * 512:(blk + 1) * 512], start=(kt == 0), stop=(kt == 2),
                     r=['cqT', *WK['wuq']], w=[bkk])
            P.cp('act', qtok[:, blk * 8:(blk + 1) * 8, 0:64], bk[:, :].rearrange("p (h c) -> p h c", h=8), r=[bkk], w=['qtok'])
        bk, bkk = nb()
        for kt in range(3):
            P.mm(bk[:, :], cqT[:, kt, :], wuq[:, kt, 1024:1536], start=(kt == 0), stop=(kt == 2), r=['cqT', *WK['wuq']], w=[bkk])
        x1 = bk[:, 0:256].rearrange("p (h c) -> p h c", h=16)
        x2 = bk[:, 256:512].rearrange("p (h c) -> p h c", h=16)
        cb_ = cosT[:, tt, :].unsqueeze(1).to_broadcast([128, 16, 16])
        sb_ = sinT[:, tt, :].unsqueeze(1).to_broadcast([128, 16, 16])
        P.tt('dve', qa[:], x1, cb_, ALU.mult, r=[bkk, 'cosT'], w=['qa'])
        P.tt('dve', qb[:], x2, sb_, ALU.mult, r=[bkk, 'sinT'], w=['qb'])
        P.tt('dve', qtok[:, :, 64:80], qa[:], qb[:], ALU.subtract, r=['qa', 'qb'], w=['qtok'])
        P.tt('dve', qa[:], x1, sb_, ALU.mult, r=[bkk, 'sinT', 'qtok'], w=['qa'])
        P.tt('dve', qb[:], x2, cb_, ALU.mult, r=[bkk, 'cosT', 'qtok'], w=['qb'])
        P.tt('dve', qtok[:, :, 80:96], qa[:], qb[:], ALU.add, r=['qa', 'qb'], w=['qtok'])
        yield
        ring[0] = 1
        for half in range(2):
            bk, bkk = nb()
            pv = bfv(bk)[0:96, :].rearrange("p (h t) -> p h t", h=8)
            for j in range(8):
                P.tr(pv[:, j, :], qtok[:, half * 8 + j, :], identb[:], r=['qtok', 'identb'], w=[bkk])
            P.cp('act' if half == 0 else 'dve', qT[sl][:, half * 8:(half + 1) * 8, :], pv, r=[bkk], w=[f'qT{sl}'])
        P.ld(qT_d[:, :, tok], qT[sl][:], w=[f'qd{sl}'], sem=f'stq{sl}', r=[f'qT{sl}'])
        yield

    load_t(0)
    for it in range(ntt + 1):
        gens = []
        if it >= 1:
            gens.append(back(it - 1))
        if it < ntt:
            gens.append(front(it))
        while gens:
            for g in list(gens):
                try:
                    next(g)
                except StopIteration:
                    gens.remove(g)
    outk = []
    for sl in range(2):
        outk += [f'h1d{sl}', f'krd{sl}', f'knd{sl}', f'vd{sl}', f'sgd{sl}', f'qd{sl}']
    P.wait_all('sp', outk)
    P.emit()
    es.close()
    return nc
SCALE = 96.0 ** -0.5
LOOKAHEAD = 2


def build_stageC(nheads=4, nchunks=16):
    nc = bass.Bass("TRN2", target_bir_lowering=False)
    SQ = 8192
    LK = 8192
    NKT = LK // 128
    chunks = list(range(nchunks))
    kT_d = nc.dram_tensor("kT", [4, 96, 8192], BF16, kind="ExternalInput").ap()
    v_d = nc.dram_tensor("v", [4, 128, 64, 64], BF16, kind="ExternalInput").ap()
    qT_d = nc.dram_tensor("qT", [4, 96, SQ], BF16, kind="ExternalInput").ap()
    sg_d = nc.dram_tensor("sg", [4, 64, SQ], BF16, kind="ExternalInput").ap()
    og_d = nc.dram_tensor("og", [64, 4, SQ], BF16, kind="ExternalOutput").ap()

    P = P2(nc)
    es = contextlib.ExitStack()

    def S(name, shape, dt):
        return es.enter_context(nc.sbuf_tensor(name, shape, dt))

    banks = [es.enter_context(nc.psum_tensor(f"bank{i}", [128, 512], F32)) for i in range(8)]
    kT = [S(f"kT{i}", [96, LK], BF16) for i in range(2)]
    vh = [S(f"vh{i}", [128, NKT, 65], BF16) for i in range(2)]
    qh = [S(f"qh{i}", [96, SQ], BF16) for i in range(2)]
    sgh = [S(f"sgh{i}", [64, SQ], BF16) for i in range(2)]
    ogs = [S(f"ogs{i}", [64, 512], BF16) for i in range(2)]
    PT = [S(f"PT{i}", [128, 512], BF16) for i in range(4)]
    rrow = S("rrow", [65, 512], F32)
    onesr = S("onesr", [65, 64], F32)
    ot = [S(f"ot{i}", [64, 512], F32) for i in range(2)]
    tn = S("tn", [64, 512], BF16)

    P.ms('pool', onesr[:], 1.0, ['onesr'])
    for i in range(2):
        P.ms('pool', vh[i][:, :, 64:65], 1.0, [f'vh{i}'])

    def load_head(h):
        i = h % 2
        half = LK // 2
        P.ld(kT[i][:, 0:half], kT_d[h, :, 0:half], [f'kT{i}a'], f'kT{i}a')
        P.ld(kT[i][:, half:LK], kT_d[h, :, half:LK], [f'kT{i}b'], f'kT{i}b', eng='act')
        P.ld(vh[i][:, :, 0:64], v_d[h, :, 0:NKT, :], [f'vh{i}'], f'vh{i}', eng='pool')
        P.ld(qh[i][:], qT_d[h, :, :], [f'qh{i}'], f'qh{i}')
        P.ld(sgh[i][:], sg_d[h, :, :], [f'sgh{i}'], f'sgh{i}')

    load_head(0)
    if nheads > 1:
        load_head(1)
    tiles = []
    cn = 0
    for h in range(nheads):
        for qi, cj in enumerate(chunks):
            nk = (cj + 1) * 4
            for kt in range(nk):
                d = kt - (nk - 4)
                c0 = 128 * d if d > 0 else 0
                tiles.append(dict(h=h, qi=qi, kt=kt, d=d, c0=c0, nk=nk, cn=cn, last_chunk=(qi == len(chunks) - 1)))
            cn += 1

    def emit_S(n, t):
        i = t['h'] % 2
        sb = n % 4
        ps = banks[sb]
        c0, kt, qi = t['c0'], t['kt'], t['qi']
        P.mm(ps[:, c0:512], kT[i][:, kt * 128:(kt + 1) * 128], qh[i][:, qi * 512 + c0:(qi + 1) * 512],
             r=[f'kT{i}a', f'kT{i}b', f'qh{i}'], w=[f'B{sb}'])
        P.actv(PT[sb][:, c0:512], ps[:, c0:512], AF.Exp, scale=SCALE, r=[f'B{sb}'], w=[f'PT{sb}'])
        if t['d'] >= 0:
            blk = PT[sb][:, c0:c0 + 128]
            P.add('pool', (lambda blk: (lambda e: e.affine_select(out=blk, in_=blk, pattern=[[1, 128]], compare_op=ALU.is_ge,
                                                                  fill=0.0, base=0, channel_multiplier=-1)))(blk),
                  r=[f'PT{sb}'], w=[f'PT{sb}'])

    def emit_PV(n, t):
        i = t['h'] % 2
        sb = n % 4
        par = t['cn'] % 2
        po = banks[4 + par]
        c0, kt = t['c0'], t['kt']
        P.mm(po[0:65, c0:512], vh[i][:, kt, :], PT[sb][:, c0:512], start=(kt == 0), stop=(kt == t['nk'] - 1),
             r=[f'vh{i}', f'PT{sb}'], w=[f'B{4 + par}'])

    def epi1(t):
        par = t['cn'] % 2
        po = banks[4 + par]
        P.add('dve', lambda e, po=po: e.reciprocal(out=rrow[64:65, :], in_=po[64:65, :]), r=[f'B{4 + par}'], w=['rrow'])
        P.cp('act', ot[par][:], po[0:64, :], r=[f'B{4 + par}'], w=[f'ot{par}'])

    def epi2(t):
        i = t['h'] % 2
        par = t['cn'] % 2
        qsl = slice(t['qi'] * 512, (t['qi'] + 1) * 512)
        prb = banks[6]
        P.mm(prb[0:64, :], onesr[64:65, :], rrow[64:65, :], r=['onesr', 'rrow'], w=['B6'])
        P.tt('dve', tn[:], ot[par][:], prb[0:64, :], ALU.mult, r=[f'ot{par}', 'B6'], w=['tn'])
        P.tt('pool', ogs[par][:], tn[:], sgh[i][:, qsl], ALU.mult, r=['tn', f'sgh{i}'], w=[f'ogs{par}'])
        P.ld(og_d[:, t['h'], qsl], ogs[par][:], w=[f'ogd{par}'], sem=f'sto{par}', r=[f'ogs{par}'])
        if t['last_chunk'] and t['h'] + 2 < nheads:
            load_head(t['h'] + 2)

    LA = 3
    DEFER = 2
    sched = {}
    NT = len(tiles)
    for n in range(NT + LA):
        if n < NT:
            emit_S(n, tiles[n])
        for t in sched.pop(n, []):
            epi2(t)
        m = n - LA
        if m >= 0:
            t = tiles[m]
            emit_PV(m, t)
            if t['kt'] == t['nk'] - 1:
                epi1(t)
                sched.setdefault(n + DEFER, []).append(t)
    for k in sorted(sched):
        for t in sched[k]:
            epi2(t)
    P.wait_all('sp', ['ogd0', 'ogd1'])
    P.emit()
    es.close()
    return nc


def build_stageD():
    nc = bass.Bass("TRN2", target_bir_lowering=False)
    og_d = nc.dram_tensor("og", [128, 8, NTOK], BF16, kind="ExternalInput").ap()
    h1_d = nc.dram_tensor("h1", [NTOK, 1024], F32, kind="ExternalInput").ap()
    wo_d = nc.dram_tensor("wo", [1024, 1024], F32, kind="ExternalInput").ap()
    gf_d = nc.dram_tensor("gf", [1, 1024], F32, kind="ExternalInput").ap()
    out_d = nc.dram_tensor("out", [NTOK, 1024], F32, kind="ExternalOutput").ap()
    P = P2(nc)
    es = contextlib.ExitStack()

    def S(name, shape, dt):
        return es.enter_context(nc.sbuf_tensor(name, shape, dt))

    banks = [es.enter_context(nc.psum_tensor(f"bank{i}", [128, 512], F32)) for i in range(8)]
    ogT = S("ogT", [128, 8, NTOK], BF16)
    wo = S("wo_s", [128, 8, 1024], BF16)
    wst = [S(f"wst{i}", [128, 1024], F32) for i in range(2)]
    gf_bc = S("gf_bc", [128, 1024], F32)
    h1t = [S(f"h1t{i}", [128, 1024], F32) for i in range(2)]
    h2 = S("h2", [128, 1024], F32)
    junk = S("junk", [128, 1024], BF16)
    ss = S("ss", [128, 1], F32)
    rt = S("rt", [128, 1], F32)
    rstd = S("rstd", [128, 1], F32)
    outt = [S(f"outt{i}", [128, 1024], F32) for i in range(2)]
    P.ld(gf_bc[:], gf_d.partition_broadcast(128), ['gf_bc'], 'c0')
    for q in range(4):
        P.ld(ogT[:, :, q * 512:(q + 1) * 512], og_d[:, :, q * 512:(q + 1) * 512], [f'ogT{q}'], f'og{q}')
    wkeys = []
    for pr in range(8):
        i = pr % 2
        P.ld(wst[i][:], wo_d[pr * 128:(pr + 1) * 128, :], [f'wst{i}'], f'wst{i}')
        P.cp('dve' if i == 0 else 'act', wo[:, pr, :], wst[i][:], r=[f'wst{i}'], w=[f'wo{pr}'])
        wkeys.append(f'wo{pr}')
    def load_h1(tt):
        P.ld(h1t[tt % 2][:], h1_d[tt * 128:(tt + 1) * 128, :], [f'h1t{tt % 2}'], f'h1t{tt % 2}')

    load_h1(0)
    for tt in range(NTT):
        sl = tt % 2
        if tt + 1 < NTT:
            load_h1(tt + 1)
        for half in range(2):
            bk = banks[(tt % 2) * 2 + half]
            for pr in range(8):
                P.mm(bk[:, :], ogT[:, pr, tt * 128:(tt + 1) * 128], wo[:, pr, half * 512:(half + 1) * 512], start=(pr == 0), stop=(pr == 7),
                     r=[f'ogT{tt // 4}', wkeys[pr]], w=[f'B{(tt % 2) * 2 + half}'])
            P.tt('dve', h2[:, half * 512:(half + 1) * 512], bk[:, :], h1t[sl][:, half * 512:(half + 1) * 512], ALU.add,
                 r=[f'B{(tt % 2) * 2 + half}', f'h1t{sl}'], w=[f'h2_{half}'])
        P.actv(junk[:], h2[:], AF.Square, accum=ss[:], r=['h2_0', 'h2_1'], w=['junk', 'ss'])
        P.actv(rt[:], ss[:], AF.Sqrt, bias=EPS, scale=1.0 / 1024, r=['ss'], w=['rt'])
        P.add('dve', lambda e: e.reciprocal(out=rstd[:], in_=rt[:]), r=['rt'], w=['rstd'])
        P.stt(outt[sl][:], h2[:], rstd[:, 0:1], gf_bc[:], ALU.mult, ALU.mult, r=['h2_0', 'h2_1', 'rstd', 'gf_bc'], w=[f'outt{sl}'])
        P.ld(out_d[tt * 128:(tt + 1) * 128, :], outt[sl][:], w=[f'od{sl}'], sem=f'sto{sl}', r=[f'outt{sl}'])
    P.wait_all('sp', ['od0', 'od1'])
    P.emit()
    es.close()
    return nc


def _prepA(inp, b, g):
    w_in = inp['ssm_w_in'][0]
    w = np.concatenate([w_in[:, 2048 + g * 512:2048 + (g + 1) * 512], w_in[:, 4096 + g * 128:4096 + (g + 1) * 128],
                        w_in[:, 4608 + g * 128:4608 + (g + 1) * 128], w_in[:, 5120 + g * 8:5120 + (g + 1) * 8],
                        w_in[:, g * 512:(g + 1) * 512]], axis=1)
    cidx = np.concatenate([np.arange(g * 512, (g + 1) * 512), 2048 + np.arange(g * 128, (g + 1) * 128),
                           2560 + np.arange(g * 128, (g + 1) * 128)])
    cwc = inp['ssm_conv_w'][0][:, cidx]
    cw = cwc.T.reshape(6, 128, 4).transpose(1, 0, 2).reshape(128, 24)
    cb = inp['ssm_conv_b'][0][cidx].reshape(6, 128).T
    hs = slice(g * 8, (g + 1) * 8)
    C = np.ascontiguousarray
    return dict(x=C(inp['x'][b]), w=C(w), gpre=C(inp['g_pre'][0].reshape(8, 128).T), cw=C(cw), cb=C(cb),
                dtb=C(inp['ssm_dt_bias'][0][hs].reshape(1, 8)), alog=C(inp['ssm_A_log'][0][hs].reshape(1, 8)),
                dsk=C(inp['ssm_D'][0][hs].reshape(1, 8)), gout=C(inp['ssm_g_out'][0][g * 512:(g + 1) * 512].reshape(1, 512)))


def _prepB(inp, yn_b, b, j):
    C = np.ascontiguousarray
    tok = slice(j * NTOK, (j + 1) * NTOK)
    pos = np.asarray(inp['positions'][b][tok]).astype(np.int32).reshape(NTT, 128).T
    return dict(x=C(inp['x'][b][tok]), yn=C(yn_b[tok]), pos=C(pos), invf=np.array(INV_FREQ, dtype=np.float32).reshape(1, 16),
                wout=C(inp['ssm_w_out'][0]), wdn=C(inp['kv_w_down']), wup=C(inp['kv_w_up']), win=C(inp['mla_w_in'][0]),
                wuq=C(inp['mla_w_uq'][0]), gkv=C(inp['kv_g_in'].reshape(8, 128).T), gpre=C(inp['g_pre'][1].reshape(8, 128).T),
                glat=C(inp['kv_g_latent'].reshape(2, 128).T), gq=C(inp['mla_g_q'][0].reshape(3, 128).T))


def kernel(**inputs):
    inp = {k: np.asarray(v) for k, v in inputs.items()}
    C = np.ascontiguousarray
    cores = list(range(8))
    ncA = build_stageA()
    rA = run_bass_kernel_spmd(ncA, [_prepA(inp, c // 4, c % 4) for c in cores], core_ids=cores).results
    yn = [np.concatenate([rA[b * 4 + g]['yn'] for g in range(4)], axis=1) for b in range(2)]
    ncB = build_stageB()
    rB = run_bass_kernel_spmd(ncB, [_prepB(inp, yn[c // 4], c // 4, c % 4) for c in cores], core_ids=cores).results
    imC = []
    for c in cores:
        b, hg = c // 4, c % 4
        kn = np.concatenate([rB[b * 4 + j]['kn'] for j in range(4)], axis=2)
        kr = np.concatenate([rB[b * 4 + j]['kr'] for j in range(4)], axis=1)
        vf = np.concatenate([rB[b * 4 + j]['v'] for j in range(4)], axis=0)
        qf = np.concatenate([rB[b * 4 + j]['qT'] for j in range(4)], axis=2)
        sf = np.concatenate([rB[b * 4 + j]['sg'] for j in range(4)], axis=2)
        kT = np.empty((4, 96, 8192), dtype=kn.dtype)
        v4 = np.empty((4, 128, 64, 64), dtype=vf.dtype)
        q4 = np.empty((4, 96, 8192), dtype=qf.dtype)
        s4 = np.empty((4, 64, 8192), dtype=sf.dtype)
        for hl in range(4):
            h = hg * 4 + hl
            kT[hl, 0:64] = kn[(h % 2) * 64:(h % 2) * 64 + 64, h // 2, :]
            kT[hl, 64:96] = kr
            v4[hl] = vf[:, h * 64:(h + 1) * 64].reshape(64, 128, 64).transpose(1, 0, 2)
            q4[hl] = qf[:, h, :]
            s4[hl] = sf[(h % 2) * 64:(h % 2) * 64 + 64, h // 2, :]
        imC.append(dict(kT=kT, v=v4, qT=q4, sg=s4))
    ncC = build_stageC()
    rC = run_bass_kernel_spmd(ncC, imC, core_ids=cores).results
    imD = []
    for c in cores:
        b, j = c // 4, c % 4
        tok = slice(j * NTOK, (j + 1) * NTOK)
        og = np.concatenate([rC[b * 4 + hg]['og'][:, :, tok] for hg in range(4)], axis=1)
        og = og.reshape(64, 8, 2, NTOK).transpose(2, 0, 1, 3).reshape(128, 8, NTOK)
        imD.append(dict(og=C(og), h1=rB[c]['h1'], wo=C(inp['mla_w_out'][0]), gf=C(inp['g_final'].reshape(1, 1024))))
    ncD = build_stageD()
    rD = run_bass_kernel_spmd(ncD, imD, core_ids=cores).results
    out = np.stack([np.concatenate([rD[b * 4 + j]['out'] for j in range(4)], axis=0) for b in range(2)], axis=0)
    return out.astype(np.float32)
```

```python
import contextlib
import math
from concourse.bass_utils import run_bass_kernel_spmd
import numpy as np
import concourse.bass as bass
import concourse.mybir as mybir

F32 = mybir.dt.float32
BF16 = mybir.dt.bfloat16
I32 = mybir.dt.int32
AF = mybir.ActivationFunctionType
ALU = mybir.AluOpType
AX = mybir.AxisListType


class Prog:
    def __init__(self, nc):
        self.nc = nc
        self.ops = []
        self.lastw = {}
        self.readers = {}
        self.dma_sems = {}

    def add(self, eng, fn, r=(), w=(), dma=None, group=False):
        deps = set()
        for k in r:
            if k in self.lastw:
                deps.add(self.lastw[k])
            if k[0] == 'B' and k[1:].isdigit():
                for j in self.readers.get(k, ()):
                    if self.ops[j]['eng'] != eng:
                        deps.add(j)
        for k in w:
            if k in self.lastw:
                deps.add(self.lastw[k])
            deps.update(self.readers.get(k, ()))
        i = len(self.ops)
        self.ops.append(dict(eng=eng, fn=fn, deps=deps, dma=dma, group=group, has_dep=False))
        for k in r:
            self.readers.setdefault(k, []).append(i)
        for k in w:
            self.lastw[k] = i
            self.readers[k] = []
        return i

    def pe(self, fn, r=(), w=()):
        return self.add('pe', fn, r, w)

    def act(self, fn, r=(), w=()):
        return self.add('act', fn, r, w)

    def dve(self, fn, r=(), w=()):
        return self.add('dve', fn, r, w)

    def pool(self, fn, r=(), w=()):
        return self.add('pool', fn, r, w)

    def dma(self, eng, fn, r=(), w=(), sem=None, group=False):
        assert sem is not None
        return self.add(eng, fn, r, w, dma=sem, group=group)

    def wait_all(self, eng, keys):
        return self.add(eng, None, r=keys, w=())

    def emit(self):
        nc = self.nc
        ops = self.ops
        engs = ['sp', 'act', 'dve', 'pool', 'pe']
        for o in ops:
            for d in o['deps']:
                if ops[d]['eng'] == 'pe' and o['eng'] == 'pe' and ops[d]['dma'] is None and o['dma'] is None:
                    continue
                ops[d]['has_dep'] = True
        esem = {e: nc.alloc_semaphore(name=f"s_{e}") for e in engs}
        group_tot = {}
        for o in ops:
            if o['dma'] is not None:
                if o['dma'] not in self.dma_sems:
                    self.dma_sems[o['dma']] = nc.alloc_semaphore(name=f"d_{o['dma']}")
                group_tot[o['dma']] = group_tot.get(o['dma'], 0) + 1
        cnt = {e: 0 for e in engs}
        dcnt = {}
        for o in ops:
            if o['fn'] is None:
                o['tok'] = None
            elif o['dma'] is not None:
                k = o['dma']
                dcnt[k] = dcnt.get(k, 0) + 1
                v = group_tot[k] if o['group'] else dcnt[k]
                o['tok'] = (('d', k), 16 * v)
            elif o['has_dep']:
                cnt[o['eng']] += 1
                o['tok'] = (('e', o['eng']), cnt[o['eng']])
            else:
                o['tok'] = None
        known = {e: {} for e in engs}
        for o in ops:
            e = o['eng']
            kn = known[e]
            waits = []
            for d in sorted(o['deps'], reverse=True):
                od = ops[d]
                if od['tok'] is None:
                    continue
                if od['eng'] == 'pe' and e == 'pe' and od['dma'] is None and o['dma'] is None:
                    continue
                s, v = od['tok']
                if kn.get(s, 0) < v:
                    waits.append((s, v))
                    kn[s] = v
                    for s2, v2 in od['clock'].items():
                        if kn.get(s2, 0) < v2:
                            kn[s2] = v2
            wm = {}
            for s, v in waits:
                wm[s] = max(wm.get(s, 0), v)
            o['waits'] = wm
            o['clock'] = dict(kn)

        def semof(s):
            return esem[s[1]] if s[0] == 'e' else self.dma_sems[s[1]]

        def run(ename, eng):
            for o in ops:
                if o['eng'] != ename:
                    continue
                for s, v in o['waits'].items():
                    eng.wait_ge(semof(s), v)
                if o['fn'] is None:
                    continue
                inst = o['fn'](eng)
                if o['tok'] is not None:
                    s, v = o['tok']
                    inst.then_inc(semof(s), 16 if s[0] == 'd' else 1)

        with nc.Block() as block:
            @block.sync
            def _(e):
                run('sp', e)

            @block.scalar
            def _(e):
                run('act', e)

            @block.vector
            def _(e):
                run('dve', e)

            @block.gpsimd
            def _(e):
                run('pool', e)

            @block.tensor
            def _(e):
                run('pe', e)
        n = {e: sum(1 for o in ops if o['eng'] == e) for e in engs}
        nw = sum(len(o['waits']) for o in ops)
        print("PROG ops", n, "waits", nw, "sems", 5 + len(self.dma_sems), flush=True)


def _kw(**k):
    return {a: b for a, b in k.items() if b is not None}


class P2(Prog):
    def mm(self, out, lhsT, rhs, start=True, stop=True, r=(), w=()):
        return self.add('pe', lambda e: e.matmul(out, lhsT=lhsT, rhs=rhs, start=start, stop=stop), r, w)

    def tr(self, out, in_, ident, r=(), w=()):
        return self.add('pe', lambda e: e.transpose(out, in_, ident), r, w)

    def actv(self, out, in_, func, bias=None, scale=None, accum=None, r=(), w=()):
        kw = _kw(bias=bias, scale=scale, accum_out=accum)
        return self.add('act', lambda e: e.activation(out=out, in_=in_, func=func, **kw), r, w)

    def ts(self, eng, out, in0, s1, s2=None, op0=ALU.mult, op1=None, r=(), w=()):
        kw = _kw(op1=op1)
        return self.add(eng, lambda e: e.tensor_scalar(out=out, in0=in0, scalar1=s1, scalar2=s2, op0=op0, **kw), r, w)

    def tt(self, eng, out, in0, in1, op, r=(), w=()):
        return self.add(eng, lambda e: e.tensor_tensor(out=out, in0=in0, in1=in1, op=op), r, w)

    def stt(self, out, in0, scalar, in1, op0, op1, r=(), w=()):
        return self.add('dve', lambda e: e.scalar_tensor_tensor(out=out, in0=in0, scalar=scalar, in1=in1, op0=op0, op1=op1), r, w)

    def cp(self, eng, out, in_, r=(), w=()):
        if eng == 'act':
            return self.add('act', lambda e: e.activation(out=out, in_=in_, func=AF.Copy), r, w)
        return self.add(eng, lambda e: e.tensor_copy(out=out, in_=in_), r, w)

    def ms(self, eng, ap, val, w=()):
        return self.add(eng, lambda e: e.memset(ap, val), (), w)

    def ld(self, out, in_, w, sem, eng='sp', group=False, r=()):
        return self.dma(eng, lambda e: e.dma_start(out=out, in_=in_), r=r, w=w, sem=sem, group=group)

SEQ = 8192
DM = 1024
NCH = SEQ // 256
EPS = 1e-6
WCOLS = 1288


def build_stageA(nch=NCH):
    nc = bass.Bass("TRN2", target_bir_lowering=False)
    x_d = nc.dram_tensor("x", [SEQ, DM], F32, kind="ExternalInput").ap()
    w_d = nc.dram_tensor("w", [DM, WCOLS], F32, kind="ExternalInput").ap()
    gpre_d = nc.dram_tensor("gpre", [128, 8], F32, kind="ExternalInput").ap()
    cw_d = nc.dram_tensor("cw", [128, 24], F32, kind="ExternalInput").ap()
    cb_d = nc.dram_tensor("cb", [128, 6], F32, kind="ExternalInput").ap()
    dtb_d = nc.dram_tensor("dtb", [1, 8], F32, kind="ExternalInput").ap()
    alog_d = nc.dram_tensor("alog", [1, 8], F32, kind="ExternalInput").ap()
    dsk_d = nc.dram_tensor("dsk", [1, 8], F32, kind="ExternalInput").ap()
    gout_d = nc.dram_tensor("gout", [1, 512], F32, kind="ExternalInput").ap()
    yn_d = nc.dram_tensor("yn", [SEQ, 512], BF16, kind="ExternalOutput").ap()

    P = P2(nc)
    es = contextlib.ExitStack()

    def S(name, shape, dt):
        return es.enter_context(nc.sbuf_tensor(name, shape, dt))

    banks = [es.enter_context(nc.psum_tensor(f"bank{i}", [128, 512], F32)) for i in range(8)]

    W = S("W", [128, 8, WCOLS], BF16)
    wst = [S(f"wst{i}", [128, WCOLS], F32) for i in range(2)]
    gpre = S("gpre_s", [128, 8], F32)
    cw = S("cw_s", [128, 24], F32)
    cb = S("cb_s", [128, 6], F32)
    dtb_bc = S("dtb_bc", [128, 8], F32)
    A_bc = S("A_bc", [128, 8], F32)
    D_bc = S("D_bc", [128, 8], F32)
    gout_bc = S("gout_bc", [128, 512], F32)
    identf = S("identf", [128, 128], F32)
    identb = S("identb", [128, 128], BF16)
    onesf = S("onesf", [128, 128], F32)
    onesb = S("onesb", [128, 128], BF16)
    trif = S("trif", [128, 128], F32)
    triw = S("triw", [128, 256], BF16)
    SU = S("SU", [128, 128], BF16)
    cdiag = S("cdiag", [128, 24, 128], BF16)
    Dident = S("Dident", [128, 8, 128], BF16)
    xin = [S(f"xin{i}", [128, 2, DM], F32) for i in range(2)]
    junk = [S(f"junk{i}", [128, DM], BF16) for i in range(2)]
    ss = S("ss", [128, 2], F32)
    rt = S("rt", [128, 2], F32)
    rstd = S("rstd", [128, 2], F32)
    hn = S("hn", [128, 2, DM], BF16)
    hnT = S("hnT", [128, 8, 256], BF16)
    ubuf = S("ubuf", [128, 6, 259], BF16)
    xc = [S(f"xc{i}", [128, 6, 256], BF16) for i in range(2)]
    xtok = S("xtok", [128, 2, 640], BF16)
    dtr = S("dtr", [128, 2, 8], F32)
    e1 = S("e1", [128, 2, 8], F32)
    dtk = [S(f"dtk{i}", [128, 2, 8], F32) for i in range(2)]
    dtA = [S(f"dtA{i}", [128, 2, 8], F32) for i in range(2)]
    cend = S("cend", [128, 8], F32)
    ecum = [S(f"ecum{i}", [128, 2, 8], F32) for i in range(2)]
    wtmp = [S(f"wtmp{i}", [128, 2, 8], F32) for i in range(2)]
    dec = [S(f"dec{i}", [128, 8], F32) for i in range(2)]
    W0 = S("W0", [128, 8, 256], BF16)
    V1 = S("V1", [128, 8, 128], BF16)
    CBm = S("CBm", [128, 384], BF16)
    xdt = S("xdt", [128, 2, 512], BF16)
    Lb = [S(f"Lb{i}", [128, 384], BF16) for i in range(2)]
    junk2b = S("junk2b", [128, 512], BF16)
    MT = S("MT", [128, 8, 384], BF16)
    state = S("state", [128, 512], F32)
    state_bf = S("state_bf", [128, 512], BF16)
    yi = S("yi", [128, 512], F32)
    t1 = S("t1", [128, 512], F32)
    ysb = S("ysb", [128, 512], F32)
    zs = [S(f"zs{i}", [128, 2, 512], F32) for i in range(2)]
    yg = [S(f"yg{i}", [128, 512], F32) for i in range(2)]
    junk2 = S("junk2", [128, 512], BF16)
    ss2 = S("ss2", [128, 2], F32)
    rt2 = S("rt2", [128, 2], F32)
    rstd2 = S("rstd2", [128, 2], F32)
    yn = [S(f"yn{i}", [128, 512], BF16) for i in range(2)]
    wx = S("wx", [128, 2, 512], BF16)

    def bfview(bank):
        return bank[:].bitcast(BF16)

    ptr = bfview(banks[0]).rearrange("p (k t) -> p k t", k=8)
    pX = banks[1]
    pCv = banks[2]
    pdtk = banks[3][:, 0:16].rearrange("p (t c) -> p t c", t=2)
    pcum = banks[3][:, 16:32].rearrange("p (t c) -> p t c", t=2)
    pce = banks[3][:, 32:40]
    pz = banks[3][:, :]
    ptx = bfview(banks[4])[:, 0:640]
    pCB = banks[4][:, 0:384]
    pseg = [banks[5][:, 0:384], banks[7][:, 0:384]]
    py = banks[6][:, :]
    pyi = banks[4][:, :]
    pst = banks[4][:, :]

    P.ld(gpre[:], gpre_d, ['gpre'], 'c0')
    P.ld(cw[:], cw_d, ['cw'], 'c1')
    P.ld(cb[:], cb_d, ['cb'], 'c2')
    P.ld(dtb_bc[:], dtb_d.partition_broadcast(128), ['dtb_bc'], 'c3')
    P.ld(A_bc[:], alog_d.partition_broadcast(128), ['A_bc'], 'c4')
    P.ld(D_bc[:], dsk_d.partition_broadcast(128), ['D_bc'], 'c5')
    P.ld(gout_bc[:], gout_d.partition_broadcast(128), ['gout_bc'], 'c6')
    P.ms('pool', identf[:], 1.0, ['identf'])
    P.add('pool', lambda e: e.affine_select(out=identf[:], in_=identf[:], pattern=[[-1, 128]], compare_op=ALU.is_equal,
                                            fill=0.0, base=0, channel_multiplier=1), r=['identf'], w=['identf'])
    P.cp('dve', identb[:], identf[:], r=['identf'], w=['identb'])
    P.ms('pool', onesf[:], 1.0, ['onesf'])
    P.ms('pool', onesb[:], 1.0, ['onesb'])
    P.ms('pool', triw[:], 1.0, ['triw'])
    P.add('pool', lambda e: e.affine_select(out=triw[:, 0:128], in_=triw[:, 0:128], pattern=[[1, 128]], compare_op=ALU.is_ge,
                                            fill=0.0, base=0, channel_multiplier=-1), r=['triw'], w=['triw'])
    P.cp('dve', trif[:], triw[:, 0:128], r=['triw'], w=['trif'])
    P.ms('pool', SU[:], 1.0, ['SU'])
    P.add('pool', lambda e: e.affine_select(out=SU[:], in_=SU[:], pattern=[[-1, 128]], compare_op=ALU.is_gt,
                                            fill=0.0, base=0, channel_multiplier=1), r=['SU'], w=['SU'])
    P.ms('pool', ubuf[:], 0.0, ['ubuf%d' % i for i in range(3)])
    P.ms('pool', state[:], 0.0, ['state'])
    P.ms('pool', state_bf[:], 0.0, ['state_bf'])
    for kt in range(8):
        P.ld(wst[kt % 2][:], w_d[kt * 128:(kt + 1) * 128, :], [f'wst{kt % 2}'], f'wst{kt % 2}')
        if kt % 2 == 0:
            P.ts('dve', W[:, kt, :], wst[kt % 2][:], gpre[:, kt:kt + 1], None, ALU.mult, r=[f'wst{kt % 2}', 'gpre'], w=[f'W{kt}'])
        else:
            P.actv(W[:, kt, :], wst[kt % 2][:], AF.Copy, scale=gpre[:, kt:kt + 1], r=[f'wst{kt % 2}', 'gpre'], w=[f'W{kt}'])
    Wk = [f'W{kt}' for kt in range(8)]
    HNT = ['hnT0', 'hnT1']
    for i in range(24):
        P.ts('dve', cdiag[:, i, :], identf[:], cw[:, i:i + 1], None, ALU.mult, r=['identf', 'cw'], w=['cdiag'])
    for h in range(8):
        P.ts('dve', Dident[:, h, :], identf[:], D_bc[:, h:h + 1], None, ALU.mult, r=['identf', 'D_bc'], w=['Dident'])
    P.actv(A_bc[:], A_bc[:], AF.Exp, r=['A_bc'], w=['A_bc'])
    P.ts('dve', A_bc[:], A_bc[:], -1.0, None, ALU.mult, r=['A_bc'], w=['A_bc'])

    def load_x(c):
        sl = c % 2
        P.ld(xin[sl][:], x_d[c * 256:(c + 1) * 256, :].rearrange("(t p) d -> p t d", p=128), [f'xin{sl}'], f'xin{sl}')

    def front(c):
        sl = c % 2
        p = c % 2
        xk = f'xin{sl}'
        if c + 1 < nch:
            load_x(c + 1)
        for t in range(2):
            P.actv(junk[t][:], xin[sl][:, t, :], AF.Square, accum=ss[:, t:t + 1], r=[xk], w=[f'ss{t}', f'junk{t}'])
        P.actv(rt[:], ss[:], AF.Ln, bias=EPS, scale=1.0 / DM, r=['ss0', 'ss1'], w=['rt'])
        P.actv(rstd[:], rt[:], AF.Exp, scale=-0.5, r=['rt'], w=['rstd'])
        for t in range(2):
            P.ts('dve', hn[:, t, :], xin[sl][:, t, :], rstd[:, t:t + 1], None, ALU.mult, r=[xk, 'rstd'], w=[f'hn{t}'])
        yield
        for t in range(2):
            for kt in range(8):
                P.tr(ptr[:, kt, :], hn[:, t, kt * 128:(kt + 1) * 128], identb[:], r=[f'hn{t}', 'identb'], w=['B0'])
            P.cp('dve' if t == 0 else 'act', hnT[:, :, t * 128:(t + 1) * 128], ptr, r=['B0'], w=[f'hnT{t}'])
            yield
        for t in range(2):
            for kt in range(8):
                P.mm(pdtk[:, t, :], hnT[:, kt, t * 128:(t + 1) * 128], W[:, kt, 768:776], start=(kt == 0), stop=(kt == 7),
                     r=HNT + [Wk[kt]], w=['B3'])
        P.tt('dve', dtr[:], pdtk, dtb_bc[:].unsqueeze(1).to_broadcast([128, 2, 8]), ALU.add, r=['B3', 'dtb_bc'], w=['dtr'])
        P.actv(e1[:], dtr[:], AF.Exp, r=['dtr'], w=['e1'])
        P.actv(dtk[p][:], e1[:], AF.Ln, bias=1.0, r=['e1'], w=[f'dtk{p}'])
        P.tt('dve', dtA[p][:], dtk[p][:], A_bc[:].unsqueeze(1).to_broadcast([128, 2, 8]), ALU.mult, r=[f'dtk{p}', 'A_bc'], w=[f'dtA{p}'])
        P.mm(pcum[:, 0, :], trif[:], dtA[p][:, 0, :], r=['trif', f'dtA{p}'], w=['B3'])
        P.mm(pcum[:, 1, :], onesf[:], dtA[p][:, 0, :], start=True, stop=False, r=['onesf', f'dtA{p}'], w=['B3'])
        P.mm(pcum[:, 1, :], trif[:], dtA[p][:, 1, :], start=False, stop=True, r=['trif', f'dtA{p}'], w=['B3'])
        P.mm(pce, onesf[:], dtA[p][:, 0, :], start=True, stop=False, r=['onesf', f'dtA{p}'], w=['B3'])
        P.mm(pce, onesf[:], dtA[p][:, 1, :], start=False, stop=True, r=['onesf', f'dtA{p}'], w=['B3'])
        P.actv(ecum[p][:], pcum, AF.Exp, r=['B3'], w=[f'ecum{p}'])
        P.actv(dec[p][:], pce, AF.Exp, r=['B3'], w=[f'dec{p}'])
        P.cp('act', cend[:], pce, r=['B3'], w=['cend'])
        P.tt('dve', wtmp[p][:], cend[:].unsqueeze(1).to_broadcast([128, 2, 8]), pcum, ALU.subtract, r=['cend', 'B3'], w=[f'wtmp{p}'])
        P.actv(wtmp[p][:], wtmp[p][:], AF.Exp, r=[f'wtmp{p}'], w=[f'wtmp{p}'])
        yield
        for pr in range(3):
            for j in range(2):
                ct = 2 * pr + j
                for kt in range(8):
                    P.mm(pX[:, j * 256:(j + 1) * 256], W[:, kt, ct * 128:(ct + 1) * 128], hnT[:, kt, :], start=(kt == 0), stop=(kt == 7),
                         r=HNT + [Wk[kt]], w=['B1'])
            P.cp('act', ubuf[:, 2 * pr:2 * pr + 2, 3:259], pX[:, :].rearrange("p (j t) -> p j t", j=2), r=['B1'], w=[f'ubuf{pr}'])
            for j in range(2):
                ct = 2 * pr + j
                for k in range(4):
                    P.mm(pCv[:, j * 256:(j + 1) * 256], cdiag[:, ct * 4 + k, :], ubuf[:, ct, k:k + 256], start=(k == 0), stop=(k == 3),
                         r=['cdiag', f'ubuf{pr}'], w=['B2'])
            for j in range(2):
                ct = 2 * pr + j
                P.actv(xc[p][:, ct, :], pCv[:, j * 256:(j + 1) * 256], AF.Silu, bias=cb[:, ct:ct + 1], r=['B2', 'cb'], w=[f'xc{p}_{ct}'])
            P.cp('pool', ubuf[:, 2 * pr:2 * pr + 2, 0:3], ubuf[:, 2 * pr:2 * pr + 2, 256:259], r=[f'ubuf{pr}'], w=[f'ubuf{pr}'])
            yield
        for t in range(2):
            for kt in range(8):
                P.mm(pz, hnT[:, kt, t * 128:(t + 1) * 128], W[:, kt, 776:1288], start=(kt == 0), stop=(kt == 7),
                     r=HNT + [Wk[kt]], w=['B3'])
            P.actv(zs[p][:, t, :], pz, AF.Silu, r=['B3'], w=[f'zs{p}_{t}'])
            yield

    def back(c):
        p = c % 2
        XC = [f'xc{p}_{ct}' for ct in range(6)]
        P.tt('dve', W0[:], triw[:].unsqueeze(1).to_broadcast([128, 8, 256]), dtA[p][:, 0, :].unsqueeze(2).to_broadcast([128, 8, 256]),
             ALU.mult, r=['triw', f'dtA{p}'], w=['W0'])
        P.tt('dve', V1[:], triw[:, 0:128].unsqueeze(1).to_broadcast([128, 8, 128]), dtA[p][:, 1, :].unsqueeze(2).to_broadcast([128, 8, 128]),
             ALU.mult, r=['triw', f'dtA{p}'], w=['V1'])
        for t in range(2):
            for ct in range(5):
                P.tr(ptx[:, ct * 128:(ct + 1) * 128], xc[p][:, ct, t * 128:(t + 1) * 128], identb[:], r=[XC[ct], 'identb'], w=['B4'])
            P.cp('dve' if t == 0 else 'act', xtok[:, t, :], ptx, r=['B4'], w=[f'xtok{t}'])
            P.tt('pool', xdt[:, t, :].rearrange("p (h c) -> p h c", h=8), xtok[:, t, 0:512].rearrange("p (h c) -> p h c", h=8),
                 dtk[p][:, t, :].unsqueeze(2).to_broadcast([128, 8, 64]), ALU.mult, r=[f'xtok{t}', f'dtk{p}'], w=[f'xdt{t}'])
        yield
        P.mm(pCB[:, 0:256], xc[p][:, 4, 0:128], xc[p][:, 5, 0:256], r=[XC[4], XC[5]], w=['B4'])
        P.mm(pCB[:, 256:384], xc[p][:, 4, 128:256], xc[p][:, 5, 128:256], r=[XC[4], XC[5]], w=['B4'])
        P.cp('act', CBm[:], pCB, r=['B4'], w=['CBm'])
        for off in (0, 256):
            blk = CBm[:, off:off + 128]
            P.add('pool', (lambda blk: (lambda e: e.affine_select(out=blk, in_=blk, pattern=[[1, 128]], compare_op=ALU.is_ge,
                                                                  fill=0.0, base=0, channel_multiplier=-1)))(blk),
                  r=['CBm'], w=['CBm'])
        yield
        for h in range(8):
            L = Lb[h % 2]
            Lk = f'Lb{h % 2}'
            ps = pseg[h % 2]
            psk = 'B5' if h % 2 == 0 else 'B7'
            P.mm(ps[:, 0:128], SU[:], W0[:, h, 0:128], start=True, stop=True, r=['SU', 'W0'], w=[psk])
            P.mm(ps[:, 128:256], SU[:], W0[:, h, 128:256], start=True, stop=False, r=['SU', 'W0'], w=[psk])
            P.mm(ps[:, 128:256], onesb[:], V1[:, h, :], start=False, stop=True, r=['onesb', 'V1'], w=[psk])
            P.mm(ps[:, 256:384], SU[:], V1[:, h, :], start=True, stop=True, r=['SU', 'V1'], w=[psk])
            P.actv(L[:], ps, AF.Exp, r=[psk], w=[Lk])
            P.tt('dve', MT[:, h, :], L[:], CBm[:], ALU.mult, r=[Lk, 'CBm'], w=[f'MT{h}'])
            if h % 2 == 1:
                yield
        for t in range(2):
            for h in range(8):
                hc = slice(h * 64, (h + 1) * 64)
                P.mm(py[:, hc], MT[:, h, t * 128:(t + 1) * 128], xdt[:, 0, hc], start=True, stop=False, r=[f'MT{h}', 'xdt0'], w=['B6'])
                if t == 1:
                    P.mm(py[:, hc], MT[:, h, 256:384], xdt[:, 1, hc], start=False, stop=False, r=[f'MT{h}', 'xdt1'], w=['B6'])
                P.mm(py[:, hc], Dident[:, h, :], xtok[:, t, hc], start=False, stop=True, r=['Dident', f'xtok{t}'], w=['B6'])
            P.mm(pyi, xc[p][:, 5, t * 128:(t + 1) * 128], state_bf[:], r=[XC[5], 'state_bf'], w=['B4'])
            P.tt('dve', t1[:].rearrange("p (h c) -> p h c", h=8), pyi.rearrange("p (h c) -> p h c", h=8),
                 ecum[p][:, t, :].unsqueeze(2).to_broadcast([128, 8, 64]), ALU.mult, r=['B4', f'ecum{p}'], w=['t1'])
            P.tt('dve', ysb[:], t1[:], py, ALU.add, r=['t1', 'B6'], w=['ysb'])
            P.tt('dve', yg[t][:], ysb[:], zs[p][:, t, :], ALU.mult, r=['ysb', f'zs{p}_{t}'], w=[f'yg{t}'])
            P.actv(junk2b[:], yg[t][:], AF.Square, accum=ss2[:, t:t + 1], r=[f'yg{t}'], w=[f'ss2_{t}', 'junk2b'])
            yield
        P.actv(rt2[:], ss2[:], AF.Ln, bias=EPS, scale=1.0 / 512, r=['ss2_0', 'ss2_1'], w=['rt2'])
        P.actv(rstd2[:], rt2[:], AF.Exp, scale=-0.5, r=['rt2'], w=['rstd2'])
        for t in range(2):
            P.stt(yn[t][:], yg[t][:], rstd2[:, t:t + 1], gout_bc[:], ALU.mult, ALU.mult, r=[f'yg{t}', 'rstd2', 'gout_bc'], w=[f'yn{t}'])
            P.ld(yn_d[c * 256 + t * 128: c * 256 + (t + 1) * 128, :], yn[t][:], w=[f'ynd{t}'], sem=f'st{t}', r=[f'yn{t}'])
        for st in range(2):
            P.tt('pool', wx[:, st, :].rearrange("p (h c) -> p h c", h=8), xdt[:, st, :].rearrange("p (h c) -> p h c", h=8),
                 wtmp[p][:, st, :].unsqueeze(2).to_broadcast([128, 8, 64]), ALU.mult, r=[f'xdt{st}', f'wtmp{p}'], w=[f'wx{st}'])
        for st in range(2):
            P.mm(pst, xtok[:, st, 512:640], wx[:, st, :], start=(st == 0), stop=(st == 1), r=[f'xtok{st}', f'wx{st}'], w=['B4'])
        P.tt('dve', state[:].rearrange("p (h c) -> p h c", h=8), state[:].rearrange("p (h c) -> p h c", h=8),
             dec[p][:].unsqueeze(2).to_broadcast([128, 8, 64]), ALU.mult, r=['state', f'dec{p}'], w=['state'])
        P.tt('dve', state[:], state[:], pst, ALU.add, r=['state', 'B4'], w=['state'])
        P.cp('act', state_bf[:], state[:], r=['state'], w=['state_bf'])
        yield

    load_x(0)
    for it in range(nch + 1):
        gens = []
        if it >= 1:
            gens.append(back(it - 1))
        if it < nch:
            gens.append(front(it))
        while gens:
            for g in list(gens):
                try:
                    next(g)
                except StopIteration:
                    gens.remove(g)
    P.wait_all('sp', ['ynd0', 'ynd1'])
    P.emit()
    es.close()
    return nc

NTOK = 2048
NTT = NTOK // 128
INV_FREQ = [float(np.float32(10000.0) ** np.float32(-(2 * i) / 32.0)) for i in range(16)]
TWO_PI = 2.0 * math.pi
CW1 = 6.28125
CW2 = TWO_PI - CW1


def build_stageB(ntt=NTT):
    nc = bass.Bass("TRN2", target_bir_lowering=False)
    x_d = nc.dram_tensor("x", [NTOK, 1024], F32, kind="ExternalInput").ap()
    yn_d = nc.dram_tensor("yn", [NTOK, 2048], BF16, kind="ExternalInput").ap()
    pos_d = nc.dram_tensor("pos", [128, NTT], I32, kind="ExternalInput").ap()
    invf_d = nc.dram_tensor("invf", [1, 16], F32, kind="ExternalInput").ap()
    wout_d = nc.dram_tensor("wout", [2048, 1024], F32, kind="ExternalInput").ap()
    wdn_d = nc.dram_tensor("wdn", [1024, 288], F32, kind="ExternalInput").ap()
    wup_d = nc.dram_tensor("wup", [256, 2048], F32, kind="ExternalInput").ap()
    win_d = nc.dram_tensor("win", [1024, 1408], F32, kind="ExternalInput").ap()
    wuq_d = nc.dram_tensor("wuq", [384, 1536], F32, kind="ExternalInput").ap()
    gkv_d = nc.dram_tensor("gkv", [128, 8], F32, kind="ExternalInput").ap()
    gpre_d = nc.dram_tensor("gpre", [128, 8], F32, kind="ExternalInput").ap()
    glat_d = nc.dram_tensor("glat", [128, 2], F32, kind="ExternalInput").ap()
    gq_d = nc.dram_tensor("gq", [128, 3], F32, kind="ExternalInput").ap()
    h1_d = nc.dram_tensor("h1", [NTOK, 1024], F32, kind="ExternalOutput").ap()
    sg_d = nc.dram_tensor("sg", [128, 8, NTOK], BF16, kind="ExternalOutput").ap()
    kn_d = nc.dram_tensor("kn", [128, 8, NTOK], BF16, kind="ExternalOutput").ap()
    kr_d = nc.dram_tensor("kr", [32, NTOK], BF16, kind="ExternalOutput").ap()
    v_d = nc.dram_tensor("v", [NTOK, 1024], BF16, kind="ExternalOutput").ap()
    qT_d = nc.dram_tensor("qT", [96, 16, NTOK], BF16, kind="ExternalOutput").ap()

    P = P2(nc)
    es = contextlib.ExitStack()

    def S(name, shape, dt):
        return es.enter_context(nc.sbuf_tensor(name, shape, dt))

    banks = [es.enter_context(nc.psum_tensor(f"bank{i}", [128, 512], F32)) for i in range(8)]
    bctr = [0, 0]
    ring = [0]

    def nb():
        r = ring[0]
        i = r * 4 + bctr[r] % 4
        bctr[r] += 1
        return banks[i], f'B{i}'

    def bfv(bank):
        return bank[:].bitcast(BF16)

    wout = S("wout_s", [128, 16, 1024], BF16)
    wdn = S("wdn_s", [128, 8, 288], BF16)
    wkn = S("wkn_s", [128, 2, 1024], BF16)
    wv = S("wv_s", [128, 2, 1024], BF16)
    win = S("win_s", [128, 8, 1408], BF16)
    wuq = S("wuq_s", [128, 3, 1536], BF16)
    wst = [S(f"wst{i}", [128, 2048], F32) for i in range(2)]
    gkv = S("gkv_s", [128, 8], F32)
    gpre = S("gpre_s", [128, 8], F32)
    glat = S("glat_s", [128, 2], F32)
    gq = S("gq_s", [128, 3], F32)
    identf = S("identf", [128, 128], F32)
    identb = S("identb", [128, 128], BF16)
    posi = S("posi", [128, NTT], I32)
    posf = S("posf", [128, NTT], F32)
    invf = S("invf_s", [128, 16], F32)
    ang = S("ang", [128, NTT, 16], F32)
    uu = S("uu", [128, NTT, 16], F32)
    ki = S("ki", [128, NTT, 16], I32)
    kf = S("kf", [128, NTT, 16], F32)
    gg = S("gg", [128, NTT, 16], F32)
    m1 = S("m1", [128, NTT, 16], F32)
    gc = S("gc", [128, NTT, 16], F32)
    sinT = S("sinT", [128, NTT, 16], F32)
    cosT = S("cosT", [128, NTT, 16], F32)
    xin = [S(f"xin{i}", [128, 1024], F32) for i in range(2)]
    ynin = [S(f"ynin{i}", [128, 2048], BF16) for i in range(2)]
    ynT = S("ynT", [128, 16, 128], BF16)
    h1 = [S(f"h1_{i}", [128, 1024], F32) for i in range(2)]
    junk = S("junk", [128, 1024], BF16)
    ss = S("ss", [128, 1], F32)
    rt = S("rt", [128, 1], F32)
    rstd = S("rstd", [128, 1], F32)
    hnb = S("hnb", [128, 1024], BF16)
    hT2 = [S(f"hT{i}", [128, 8, 128], BF16) for i in range(2)]
    junk2 = S("junk2", [128, 384], BF16)
    ssc = S("ssc", [128, 1], F32)
    rtc = S("rtc", [128, 1], F32)
    rstdc = S("rstdc", [128, 1], F32)
    ckvn = S("ckvn", [128, 256], BF16)
    ra = S("ra", [128, 16], F32)
    rb = S("rb", [128, 16], F32)
    krb = S("krb", [128, 32], BF16)
    ckT = S("ckT", [128, 2, 128], BF16)
    krT = [S(f"krT{i}", [32, 128], BF16) for i in range(2)]
    knT = [S(f"knT{i}", [128, 8, 128], BF16) for i in range(2)]
    vsb = [S(f"vsb{i}", [128, 1024], BF16) for i in range(2)]
    ssq = S("ssq", [128, 1], F32)
    rtq = S("rtq", [128, 1], F32)
    rstdq = S("rstdq", [128, 1], F32)
    cqn = S("cqn", [128, 384], BF16)
    sg = [S(f"sg{i}", [128, 8, 128], BF16) for i in range(2)]
    cqT = S("cqT", [128, 3, 128], BF16)
    qtok = S("qtok", [128, 16, 96], BF16)
    qa = S("qa", [128, 16, 16], F32)
    qb = S("qb", [128, 16, 16], F32)
    qT = [S(f"qT{i}", [96, 16, 128], BF16) for i in range(2)]

    P.ld(gkv[:], gkv_d, ['gkv'], 'c0')
    P.ld(gpre[:], gpre_d, ['gpre'], 'c1')
    P.ld(glat[:], glat_d, ['glat'], 'c2')
    P.ld(gq[:], gq_d, ['gq'], 'c3')
    P.ld(posi[:], pos_d, ['posi'], 'c4')
    P.ld(invf[:], invf_d.partition_broadcast(128), ['invf'], 'c5')
    P.ms('pool', identf[:], 1.0, ['identf'])
    P.add('pool', lambda e: e.affine_select(out=identf[:], in_=identf[:], pattern=[[-1, 128]], compare_op=ALU.is_equal,
                                            fill=0.0, base=0, channel_multiplier=1), r=['identf'], w=['identf'])
    P.cp('dve', identb[:], identf[:], r=['identf'], w=['identb'])
    P.cp('dve', posf[:], posi[:], r=['posi'], w=['posf'])
    P.tt('dve', ang[:], posf[:].unsqueeze(2).to_broadcast([128, NTT, 16]), invf[:].unsqueeze(1).to_broadcast([128, NTT, 16]),
         ALU.mult, r=['posf', 'invf'], w=['ang'])
    P.ts('dve', uu[:], ang[:], 1.0 / TWO_PI, None, ALU.mult, r=['ang'], w=['uu'])
    P.cp('dve', ki[:], uu[:], r=['uu'], w=['ki'])
    P.cp('dve', kf[:], ki[:], r=['ki'], w=['kf'])
    P.stt(gg[:], kf[:], -CW1, ang[:], ALU.mult, ALU.add, r=['kf', 'ang'], w=['gg'])
    P.stt(gg[:], kf[:], -CW2, gg[:], ALU.mult, ALU.add, r=['kf', 'gg'], w=['gg'])
    P.ts('dve', gg[:], gg[:], 1.0 / TWO_PI, None, ALU.mult, r=['gg'], w=['gg'])

    def wrap():
        P.ts('dve', m1[:], gg[:], 0.5, None, ALU.is_gt, r=['gg'], w=['m1'])
        P.tt('dve', gg[:], gg[:], m1[:], ALU.subtract, r=['gg', 'm1'], w=['gg'])
        P.ts('dve', m1[:], gg[:], -0.5, None, ALU.is_lt, r=['gg'], w=['m1'])
        P.tt('dve', gg[:], gg[:], m1[:], ALU.add, r=['gg', 'm1'], w=['gg'])
        P.ts('dve', gg[:], gg[:], 0.4999995, -0.4999995, ALU.min, ALU.max, r=['gg'], w=['gg'])

    wrap()
    P.actv(sinT[:], gg[:], AF.Sin, scale=TWO_PI, r=['gg'], w=['sinT'])
    P.ts('dve', gg[:], gg[:], 0.25, None, ALU.add, r=['gg'], w=['gg'])
    wrap()
    P.actv(cosT[:], gg[:], AF.Sin, scale=TWO_PI, r=['gg'], w=['cosT'])

    wi = [0]
    WK = {}

    def wload(grp, dst_ap, src_ap, ncols, gain_ap, in_view=None):
        i = wi[0] % 2
        wi[0] += 1
        key = f'W{wi[0]}'
        WK.setdefault(grp, []).append(key)
        P.ld(wst[i][:, 0:ncols], src_ap, [f'wst{i}'], f'wst{i}')
        src = wst[i][:, 0:ncols] if in_view is None else in_view(wst[i])
        if i == 0:
            if gain_ap is None:
                P.cp('dve', dst_ap, src, r=[f'wst{i}'], w=[key])
            else:
                P.ts('dve', dst_ap, src, gain_ap, None, ALU.mult, r=[f'wst{i}', 'gkv', 'gpre', 'glat', 'gq'], w=[key])
        else:
            if gain_ap is None:
                P.cp('act', dst_ap, src, r=[f'wst{i}'], w=[key])
            else:
                P.actv(dst_ap, src, AF.Copy, scale=gain_ap, r=[f'wst{i}', 'gkv', 'gpre', 'glat', 'gq'], w=[key])

    for kt in range(16):
        wload('wout', wout[:, kt, :], wout_d[kt * 128:(kt + 1) * 128, :], 1024, None)
    for kt in range(8):
        wload('wdn', wdn[:, kt, :], wdn_d[kt * 128:(kt + 1) * 128, :], 288, gkv[:, kt:kt + 1])
    for kt in range(8):
        wload('win', win[:, kt, :], win_d[kt * 128:(kt + 1) * 128, :], 1408, gpre[:, kt:kt + 1])
    for kt in range(2):
        wload('wkn', wkn[:, kt, :].rearrange("p (h c) -> p h c", h=16), wup_d[kt * 128:(kt + 1) * 128, :], 2048, glat[:, kt:kt + 1],
              in_view=lambda t: t[:, 0:2048].rearrange("p (h c) -> p h c", h=16)[:, :, 0:64])
        wload('wv', wv[:, kt, :].rearrange("p (h c) -> p h c", h=16), wup_d[kt * 128:(kt + 1) * 128, :], 2048, glat[:, kt:kt + 1],
              in_view=lambda t: t[:, 0:2048].rearrange("p (h c) -> p h c", h=16)[:, :, 64:128])
    for kt in range(3):
        wload('wuq', wuq[:, kt, 0:1024].rearrange("p (h c) -> p h c", h=16), wuq_d[kt * 128:(kt + 1) * 128, :], 1536, gq[:, kt:kt + 1],
              in_view=lambda t: t[:, 0:1536].rearrange("p (h c) -> p h c", h=16)[:, :, 0:64])
        wload('wuq', wuq[:, kt, 1024:1280].rearrange("p (h c) -> p h c", h=16), wuq_d[kt * 128:(kt + 1) * 128, :], 1536, gq[:, kt:kt + 1],
              in_view=lambda t: t[:, 0:1536].rearrange("p (h c) -> p h c", h=16)[:, :, 64:80])
        wload('wuq', wuq[:, kt, 1280:1536].rearrange("p (h c) -> p h c", h=16), wuq_d[kt * 128:(kt + 1) * 128, :], 1536, gq[:, kt:kt + 1],
              in_view=lambda t: t[:, 0:1536].rearrange("p (h c) -> p h c", h=16)[:, :, 80:96])


    def load_t(tt):
        sl = tt % 2
        P.ld(xin[sl][:], x_d[tt * 128:(tt + 1) * 128, :], [f'xin{sl}'], f'xin{sl}')
        P.ld(ynin[sl][:], yn_d[tt * 128:(tt + 1) * 128, :], [f'ynin{sl}'], f'ynin{sl}')

    def front(tt):
        ring[0] = 0
        sl = tt % 2
        hT = hT2[tt % 2]
        hTk = f'hT{tt % 2}'
        tok = slice(tt * 128, (tt + 1) * 128)
        if tt + 1 < ntt:
            load_t(tt + 1)
        for half in range(2):
            bk, bkk = nb()
            pv = bfv(bk).rearrange("p (k t) -> p k t", k=8)
            for j in range(8):
                c = half * 8 + j
                P.tr(pv[:, j, :], ynin[sl][:, c * 128:(c + 1) * 128], identb[:], r=[f'ynin{sl}', 'identb'], w=[bkk])
            P.cp('dve' if half == 0 else 'act', ynT[:, half * 8:(half + 1) * 8, :], pv, r=[bkk], w=[f'ynT{half}'])
        yield
        ring[0] = 0
        for half in range(2):
            bk, bkk = nb()
            for c in range(16):
                P.mm(bk[:, :], ynT[:, c, :], wout[:, c, half * 512:(half + 1) * 512], start=(c == 0), stop=(c == 15),
                     r=['ynT0', 'ynT1', *WK['wout']], w=[bkk])
            P.tt('dve', h1[sl][:, half * 512:(half + 1) * 512], bk[:, :], xin[sl][:, half * 512:(half + 1) * 512], ALU.add,
                 r=[bkk, f'xin{sl}'], w=[f'h1_{sl}'])
        P.ld(h1_d[tok, :], h1[sl][:], w=[f'h1d{sl}'], sem=f'sth{sl}', r=[f'h1_{sl}'])
        yield
        ring[0] = 0
        P.actv(junk[:], h1[sl][:], AF.Square, accum=ss[:], r=[f'h1_{sl}'], w=['junk', 'ss'])
        P.actv(rt[:], ss[:], AF.Sqrt, bias=EPS, scale=1.0 / 1024, r=['ss'], w=['rt'])
        P.add('dve', lambda e: e.reciprocal(out=rstd[:], in_=rt[:]), r=['rt'], w=['rstd'])
        P.ts('dve', hnb[:], h1[sl][:], rstd[:, 0:1], None, ALU.mult, r=[f'h1_{sl}', 'rstd'], w=['hnb'])
        bk, bkk = nb()
        pv = bfv(bk).rearrange("p (k t) -> p k t", k=8)
        for kt in range(8):
            P.tr(pv[:, kt, :], hnb[:, kt * 128:(kt + 1) * 128], identb[:], r=['hnb', 'identb'], w=[bkk])
        P.cp('act', hT[:], pv, r=[bkk], w=[hTk])
        yield

    def back(tt):
        ring[0] = 1
        sl = tt % 2
        hT = hT2[tt % 2]
        hTk = f'hT{tt % 2}'
        tok = slice(tt * 128, (tt + 1) * 128)
        bk, bkk = nb()
        for kt in range(8):
            P.mm(bk[:, 0:288], hT[:, kt, :], wdn[:, kt, :], start=(kt == 0), stop=(kt == 7), r=[hTk, *WK['wdn']], w=[bkk])
        P.actv(junk2[:, 0:256], bk[:, 0:256], AF.Square, accum=ssc[:], r=[bkk], w=['junk2', 'ssc'])
        P.actv(rtc[:], ssc[:], AF.Sqrt, bias=EPS, scale=1.0 / 256, r=['ssc'], w=['rtc'])
        P.add('dve', lambda e: e.reciprocal(out=rstdc[:], in_=rtc[:]), r=['rtc'], w=['rstdc'])
        P.tt('dve', ra[:], bk[:, 256:272], cosT[:, tt, :], ALU.mult, r=[bkk, 'cosT'], w=['ra'])
        P.tt('dve', rb[:], bk[:, 272:288], sinT[:, tt, :], ALU.mult, r=[bkk, 'sinT'], w=['rb'])
        P.tt('dve', krb[:, 0:16], ra[:], rb[:], ALU.subtract, r=['ra', 'rb'], w=['krb'])
        P.tt('dve', ra[:], bk[:, 256:272], sinT[:, tt, :], ALU.mult, r=[bkk, 'sinT', 'krb'], w=['ra'])
        P.tt('dve', rb[:], bk[:, 272:288], cosT[:, tt, :], ALU.mult, r=[bkk, 'cosT', 'krb'], w=['rb'])
        P.tt('dve', krb[:, 16:32], ra[:], rb[:], ALU.add, r=['ra', 'rb'], w=['krb'])
        P.ts('dve', ckvn[:], bk[:, 0:256], rstdc[:, 0:1], None, ALU.mult, r=[bkk, 'rstdc'], w=['ckvn'])
        bk, bkk = nb()
        pv = bfv(bk)
        for kt in range(2):
            P.tr(pv[:, kt * 128:(kt + 1) * 128], ckvn[:, kt * 128:(kt + 1) * 128], identb[:], r=['ckvn', 'identb'], w=[bkk])
        P.tr(pv[0:32, 256:384], krb[:], identb[:], r=['krb', 'identb'], w=[bkk])
        P.cp('act', ckT[:], pv[:, 0:256].rearrange("p (k t) -> p k t", k=2), r=[bkk], w=['ckT'])
        P.cp('act', krT[sl][:], pv[0:32, 256:384], r=[bkk], w=[f'krT{sl}'])
        P.ld(kr_d[:, tok], krT[sl][:], w=[f'krd{sl}'], sem=f'stkr{sl}', r=[f'krT{sl}'])
        yield
        ring[0] = 1
        for half in range(2):
            bk, bkk = nb()
            for j in range(4):
                pr = half * 4 + j
                for kt in range(2):
                    P.mm(bk[:, j * 128:(j + 1) * 128], wkn[:, kt, pr * 128:(pr + 1) * 128], ckT[:, kt, :], start=(kt == 0), stop=(kt == 1),
                         r=['ckT', *WK['wkn']], w=[bkk])
            P.cp('act' if half == 0 else 'dve', knT[sl][:, half * 4:(half + 1) * 4, :], bk[:, :].rearrange("p (j t) -> p j t", j=4),
                 r=[bkk], w=[f'knT{sl}'])
        P.ld(kn_d[:, :, tok], knT[sl][:], w=[f'knd{sl}'], sem=f'stkn{sl}', r=[f'knT{sl}'])
        yield
        ring[0] = 1
        for half in range(2):
            bk, bkk = nb()
            for kt in range(2):
                P.mm(bk[:, :], ckT[:, kt, :], wv[:, kt, half * 512:(half + 1) * 512], start=(kt == 0), stop=(kt == 1),
                     r=['ckT', *WK['wv']], w=[bkk])
            P.cp('act' if half == 0 else 'dve', vsb[sl][:, half * 512:(half + 1) * 512], bk[:, :], r=[bkk], w=[f'vsb{sl}'])
        P.ld(v_d[tok, :], vsb[sl][:], w=[f'vd{sl}'], sem=f'stv{sl}', r=[f'vsb{sl}'])
        yield
        ring[0] = 1
        bk, bkk = nb()
        for kt in range(8):
            P.mm(bk[:, 0:384], hT[:, kt, :], win[:, kt, 0:384], start=(kt == 0), stop=(kt == 7), r=[hTk, *WK['win']], w=[bkk])
        P.actv(junk2[:], bk[:, 0:384], AF.Square, accum=ssq[:], r=[bkk], w=['junk2', 'ssq'])
        P.actv(rtq[:], ssq[:], AF.Sqrt, bias=EPS, scale=1.0 / 384, r=['ssq'], w=['rtq'])
        P.add('dve', lambda e: e.reciprocal(out=rstdq[:], in_=rtq[:]), r=['rtq'], w=['rstdq'])
        P.ts('dve', cqn[:], bk[:, 0:384], rstdq[:, 0:1], None, ALU.mult, r=[bkk, 'rstdq'], w=['cqn'])
        for half in range(2):
            bk, bkk = nb()
            for j in range(4):
                ct = half * 4 + j
                for kt in range(8):
                    P.mm(bk[:, j * 128:(j + 1) * 128], win[:, kt, 384 + ct * 128:384 + (ct + 1) * 128], hT[:, kt, :],
                         start=(kt == 0), stop=(kt == 7), r=[hTk, *WK['win']], w=[bkk])
            P.actv(sg[sl][:, half * 4:(half + 1) * 4, :], bk[:, :].rearrange("p (j t) -> p j t", j=4), AF.Silu, r=[bkk], w=[f'sg{sl}'])
        P.ld(sg_d[:, :, tok], sg[sl][:], w=[f'sgd{sl}'], sem=f'stsg{sl}', r=[f'sg{sl}'])
        bk, bkk = nb()
        pv = bfv(bk)
        for kt in range(3):
            P.tr(pv[:, kt * 128:(kt + 1) * 128], cqn[:, kt * 128:(kt + 1) * 128], identb[:], r=['cqn', 'identb'], w=[bkk])
        P.cp('act', cqT[:], pv[:, 0:384].rearrange("p (k t) -> p k t", k=3), r=[bkk], w=['cqT'])
        yield
        ring[0] = 1
        for blk in range(2):
            bk, bkk = nb()
            for kt in range(3):
                P.mm(bk[:, :], cqT[:, kt, :], wuq[:, kt, blk * 512:(blk + 1) * 512], start=(kt == 0), stop=(kt == 2),
                     r=['cqT', *WK['wuq']], w=[bkk])
            P.cp('act', qtok[:, blk * 8:(blk + 1) * 8, 0:64], bk[:, :].rearrange("p (h c) -> p h c", h=8), r=[bkk], w=['qtok'])
        bk, bkk = nb()
        for kt in range(3):
            P.mm(bk[:, :], cqT[:, kt, :], wuq[:, kt, 1024:1536], start=(kt == 0), stop=(kt == 2), r=['cqT', *WK['wuq']], w=[bkk])
        x1 = bk[:, 0:256].rearrange("p (h c) -> p h c", h=16)
        x2 = bk[:, 256:512].rearrange("p (h c) -> p h c", h=16)
        cb_ = cosT[:, tt, :].unsqueeze(1).to_broadcast([128, 16, 16])
        sb_ = sinT[:, tt, :].unsqueeze(1).to_broadcast([128, 16, 16])
        P.tt('dve', qa[:], x1, cb_, ALU.mult, r=[bkk, 'cosT'], w=['qa'])
        P.tt('dve', qb[:], x2, sb_, ALU.mult, r=[bkk, 'sinT'], w=['qb'])
        P.tt('dve', qtok[:, :, 64:80], qa[:], qb[:], ALU.subtract, r=['qa', 'qb'], w=['qtok'])
        P.tt('dve', qa[:], x1, sb_, ALU.mult, r=[bkk, 'sinT', 'qtok'], w=['qa'])
        P.tt('dve', qb[:], x2, cb_, ALU.mult, r=[bkk, 'cosT', 'qtok'], w=['qb'])
        P.tt('dve', qtok[:, :, 80:96], qa[:], qb[:], ALU.add, r=['qa', 'qb'], w=['qtok'])
        yield
        ring[0] = 1
        for half in range(2):
            bk, bkk = nb()
            pv = bfv(bk)[0:96, :].rearrange("p (h t) -> p h t", h=8)
            for j in range(8):
                P.tr(pv[:, j, :], qtok[:, half * 8 + j, :], identb[:], r=['qtok', 'identb'], w=[bkk])
            P.cp('act' if half == 0 else 'dve', qT[sl][:, half * 8:(half + 1) * 8, :], pv, r=[bkk], w=[f'qT{sl}'])
        P.ld(qT_d[:, :, tok], qT[sl][:], w=[f'qd{sl}'], sem=f'stq{sl}', r=[f'qT{sl}'])
        yield

    load_t(0)
    for it in range(ntt + 1):
        gens = []
        if it >= 1:
            gens.append(back(it - 1))
        if it < ntt:
            gens.append(front(it))
        while gens:
            for g in list(gens):
                try:
                    next(g)
                except StopIteration:
                    gens.remove(g)
    outk = []
    for sl in range(2):
        outk += [f'h1d{sl}', f'krd{sl}', f'knd{sl}', f'vd{sl}', f'sgd{sl}', f'qd{sl}']
    P.wait_all('sp', outk)
    P.emit()
    es.close()
    return nc
SCALE = 96.0 ** -0.5
LOOKAHEAD = 2


def build_stageC(nheads=2, nchunks=16):
    nc = bass.Bass("TRN2", target_bir_lowering=False)
    SQ = 8192
    LK = 8192
    NKT = LK // 128
    chunks = list(range(nchunks))
    kT_d = nc.dram_tensor("kT", [nheads, 96, 8192], BF16, kind="ExternalInput").ap()
    v_d = nc.dram_tensor("v", [nheads, 128, 64, 64], BF16, kind="ExternalInput").ap()
    qT_d = nc.dram_tensor("qT", [nheads, 96, SQ], BF16, kind="ExternalInput").ap()
    sg_d = nc.dram_tensor("sg", [nheads, 128, 64, 64], BF16, kind="ExternalInput").ap()
    og_d = nc.dram_tensor("og", [nheads, SQ, 64], BF16, kind="ExternalOutput").ap()

    P = P2(nc)
    es = contextlib.ExitStack()

    def S(name, shape, dt):
        return es.enter_context(nc.sbuf_tensor(name, shape, dt))

    banks = [es.enter_context(nc.psum_tensor(f"bank{i}", [128, 512], F32)) for i in range(8)]
    kT = [S(f"kT{i}", [96, LK], BF16) for i in range(2)]
    vh = [S(f"vh{i}", [128, NKT, 65], BF16) for i in range(2)]
    qh = [S(f"qh{i}", [96, SQ], BF16) for i in range(2)]
    sgh = [S(f"sgh{i}", [128, 64, 64], BF16) for i in range(2)]
    ogs = [S(f"ogs{i}", [128, 4, 64], BF16) for i in range(2)]
    rr = [S(f"rr{i}", [128, 4], F32) for i in range(2)]
    PT = [S(f"PT{i}", [128, 512], BF16) for i in range(4)]

    for i in range(2):
        P.ms('pool', vh[i][:, :, 64:65], 1.0, [f'vh{i}'])

    def load_head(h):
        i = h % 2
        half = LK // 2
        P.ld(kT[i][:, 0:half], kT_d[h, :, 0:half], [f'kT{i}a'], f'kT{i}a')
        P.ld(kT[i][:, half:LK], kT_d[h, :, half:LK], [f'kT{i}b'], f'kT{i}b', eng='act')
        P.ld(vh[i][:, :, 0:64], v_d[h, :, 0:NKT, :], [f'vh{i}'], f'vh{i}', eng='pool')
        P.ld(qh[i][:], qT_d[h, :, :], [f'qh{i}'], f'qh{i}')
        P.ld(sgh[i][:], sg_d[h, :, :, :], [f'sgh{i}'], f'sgh{i}')

    load_head(0)
    if nheads > 1:
        load_head(1)
    tiles = []
    cn = 0
    for h in range(nheads):
        for qi, cj in enumerate(chunks):
            nk = (cj + 1) * 4
            for kt in range(nk):
                d = kt - (nk - 4)
                c0 = 128 * d if d > 0 else 0
                tiles.append(dict(h=h, qi=qi, kt=kt, d=d, c0=c0, nk=nk, cn=cn, last_chunk=(qi == len(chunks) - 1)))
            cn += 1

    def emit_S(n, t):
        i = t['h'] % 2
        sb = n % 4
        ps = banks[sb]
        c0, kt, qi = t['c0'], t['kt'], t['qi']
        P.mm(ps[:, c0:512], kT[i][:, kt * 128:(kt + 1) * 128], qh[i][:, qi * 512 + c0:(qi + 1) * 512],
             r=[f'kT{i}a', f'kT{i}b', f'qh{i}'], w=[f'B{sb}'])
        P.actv(PT[sb][:, c0:512], ps[:, c0:512], AF.Exp, scale=SCALE, r=[f'B{sb}'], w=[f'PT{sb}'])
        if t['d'] >= 0:
            blk = PT[sb][:, c0:c0 + 128]
            P.add('pool', (lambda blk: (lambda e: e.affine_select(out=blk, in_=blk, pattern=[[1, 128]], compare_op=ALU.is_ge,
                                                                  fill=0.0, base=0, channel_multiplier=-1)))(blk),
                  r=[f'PT{sb}'], w=[f'PT{sb}'])

    def emit_PV(n, t):
        i = t['h'] % 2
        sb = n % 4
        par = t['cn'] % 2
        po = banks[4 + par]
        kt, d, nk = t['kt'], t['d'], t['nk']
        for qt in range(4):
            if d > qt:
                continue
            last = (kt == nk - 4 + qt)
            P.add('pe', (lambda po=po, sb=sb, qt=qt, kt=kt, i=i, last=last:
                         (lambda e: e.matmul(po[:, qt * 65:(qt + 1) * 65], lhsT=PT[sb][:, qt * 128:(qt + 1) * 128], rhs=vh[i][:, kt, :],
                                             start=(kt == 0 and qt == 0), stop=last, skip_group_check=True)))(),
                  r=[f'vh{i}', f'PT{sb}'], w=[f'B{4 + par}'])

    def epi1(t):
        i = t['h'] % 2
        par = t['cn'] % 2
        po = banks[4 + par]
        pv = po[:, 0:260].rearrange("p (q c) -> p q c", q=4)
        P.add('dve', lambda e, pv=pv, par=par: e.reciprocal(out=rr[par][:], in_=pv[:, :, 64]), r=[f'B{4 + par}'], w=[f'rr{par}'])
        for qt in range(4):
            tile_i = t['qi'] * 4 + qt
            P.stt(ogs[par][:, qt, :], pv[:, qt, 0:64], rr[par][:, qt:qt + 1], sgh[i][:, tile_i, :], ALU.mult, ALU.mult,
                  r=[f'B{4 + par}', f'rr{par}', f'sgh{i}'], w=[f'ogs{par}'])
        P.ld(og_d[t['h'], t['qi'] * 512:(t['qi'] + 1) * 512, :].rearrange("(q p) d -> p q d", p=128), ogs[par][:],
             w=[f'ogd{par}'], sem=f'sto{par}', r=[f'ogs{par}'])

    def epi2(t):
        if t['last_chunk'] and t['h'] + 2 < nheads:
            load_head(t['h'] + 2)

    LA = 3
    DEFER = 2
    sched = {}
    NT = len(tiles)
    for n in range(NT + LA):
        if n < NT:
            emit_S(n, tiles[n])
        for t in sched.pop(n, []):
            epi2(t)
        m = n - LA
        if m >= 0:
            t = tiles[m]
            emit_PV(m, t)
            if t['kt'] == t['nk'] - 1:
                epi1(t)
                sched.setdefault(n + DEFER, []).append(t)
    for k in sorted(sched):
        for t in sched[k]:
            epi2(t)
    P.wait_all('sp', ['ogd0', 'ogd1'])
    P.emit()
    es.close()
    return nc


def build_stageD():
    nc = bass.Bass("TRN2", target_bir_lowering=False)
    og_d = nc.dram_tensor("og", [128, 8, NTOK], BF16, kind="ExternalInput").ap()
    h1_d = nc.dram_tensor("h1", [NTOK, 1024], F32, kind="ExternalInput").ap()
    wo_d = nc.dram_tensor("wo", [1024, 1024], F32, kind="ExternalInput").ap()
    gf_d = nc.dram_tensor("gf", [1, 1024], F32, kind="ExternalInput").ap()
    out_d = nc.dram_tensor("out", [NTOK, 1024], F32, kind="ExternalOutput").ap()
    P = P2(nc)
    es = contextlib.ExitStack()

    def S(name, shape, dt):
        return es.enter_context(nc.sbuf_tensor(name, shape, dt))

    banks = [es.enter_context(nc.psum_tensor(f"bank{i}", [128, 512], F32)) for i in range(8)]
    ogT = S("ogT", [128, 8, NTOK], BF16)
    wo = S("wo_s", [128, 8, 1024], BF16)
    wst = [S(f"wst{i}", [128, 1024], F32) for i in range(2)]
    gf_bc = S("gf_bc", [128, 1024], F32)
    h1t = [S(f"h1t{i}", [128, 1024], F32) for i in range(2)]
    h2 = S("h2", [128, 1024], F32)
    junk = S("junk", [128, 1024], BF16)
    ss = S("ss", [128, 1], F32)
    rt = S("rt", [128, 1], F32)
    rstd = S("rstd", [128, 1], F32)
    outt = [S(f"outt{i}", [128, 1024], F32) for i in range(2)]
    P.ld(gf_bc[:], gf_d.partition_broadcast(128), ['gf_bc'], 'c0')
    for q in range(4):
        P.ld(ogT[:, :, q * 512:(q + 1) * 512], og_d[:, :, q * 512:(q + 1) * 512], [f'ogT{q}'], f'og{q}')
    wkeys = []
    for pr in range(8):
        i = pr % 2
        P.ld(wst[i][:], wo_d[pr * 128:(pr + 1) * 128, :], [f'wst{i}'], f'wst{i}')
        P.cp('dve' if i == 0 else 'act', wo[:, pr, :], wst[i][:], r=[f'wst{i}'], w=[f'wo{pr}'])
        wkeys.append(f'wo{pr}')
    def load_h1(tt):
        P.ld(h1t[tt % 2][:], h1_d[tt * 128:(tt + 1) * 128, :], [f'h1t{tt % 2}'], f'h1t{tt % 2}')

    load_h1(0)
    for tt in range(NTT):
        sl = tt % 2
        if tt + 1 < NTT:
            load_h1(tt + 1)
        for half in range(2):
            bk = banks[(tt % 2) * 2 + half]
            for pr in range(8):
                P.mm(bk[:, :], ogT[:, pr, tt * 128:(tt + 1) * 128], wo[:, pr, half * 512:(half + 1) * 512], start=(pr == 0), stop=(pr == 7),
                     r=[f'ogT{tt // 4}', wkeys[pr]], w=[f'B{(tt % 2) * 2 + half}'])
            P.tt('dve', h2[:, half * 512:(half + 1) * 512], bk[:, :], h1t[sl][:, half * 512:(half + 1) * 512], ALU.add,
                 r=[f'B{(tt % 2) * 2 + half}', f'h1t{sl}'], w=[f'h2_{half}'])
        P.actv(junk[:], h2[:], AF.Square, accum=ss[:], r=['h2_0', 'h2_1'], w=['junk', 'ss'])
        P.actv(rt[:], ss[:], AF.Sqrt, bias=EPS, scale=1.0 / 1024, r=['ss'], w=['rt'])
        P.add('dve', lambda e: e.reciprocal(out=rstd[:], in_=rt[:]), r=['rt'], w=['rstd'])
        P.stt(outt[sl][:], h2[:], rstd[:, 0:1], gf_bc[:], ALU.mult, ALU.mult, r=['h2_0', 'h2_1', 'rstd', 'gf_bc'], w=[f'outt{sl}'])
        P.ld(out_d[tt * 128:(tt + 1) * 128, :], outt[sl][:], w=[f'od{sl}'], sem=f'sto{sl}', r=[f'outt{sl}'])
    P.wait_all('sp', ['od0', 'od1'])
    P.emit()
    es.close()
    return nc


def _prepA(inp, b, g):
    w_in = inp['ssm_w_in'][0]
    w = np.concatenate([w_in[:, 2048 + g * 512:2048 + (g + 1) * 512], w_in[:, 4096 + g * 128:4096 + (g + 1) * 128],
                        w_in[:, 4608 + g * 128:4608 + (g + 1) * 128], w_in[:, 5120 + g * 8:5120 + (g + 1) * 8],
                        w_in[:, g * 512:(g + 1) * 512]], axis=1)
    cidx = np.concatenate([np.arange(g * 512, (g + 1) * 512), 2048 + np.arange(g * 128, (g + 1) * 128),
                           2560 + np.arange(g * 128, (g + 1) * 128)])
    cwc = inp['ssm_conv_w'][0][:, cidx]
    cw = cwc.T.reshape(6, 128, 4).transpose(1, 0, 2).reshape(128, 24)
    cb = inp['ssm_conv_b'][0][cidx].reshape(6, 128).T
    hs = slice(g * 8, (g + 1) * 8)
    C = np.ascontiguousarray
    return dict(x=C(inp['x'][b]), w=C(w), gpre=C(inp['g_pre'][0].reshape(8, 128).T), cw=C(cw), cb=C(cb),
                dtb=C(inp['ssm_dt_bias'][0][hs].reshape(1, 8)), alog=C(inp['ssm_A_log'][0][hs].reshape(1, 8)),
                dsk=C(inp['ssm_D'][0][hs].reshape(1, 8)), gout=C(inp['ssm_g_out'][0][g * 512:(g + 1) * 512].reshape(1, 512)))


def _prepB(inp, yn_b, b, j):
    C = np.ascontiguousarray
    tok = slice(j * NTOK, (j + 1) * NTOK)
    pos = np.asarray(inp['positions'][b][tok]).astype(np.int32).reshape(NTT, 128).T
    return dict(x=C(inp['x'][b][tok]), yn=C(yn_b[tok]), pos=C(pos), invf=np.array(INV_FREQ, dtype=np.float32).reshape(1, 16),
                wout=C(inp['ssm_w_out'][0]), wdn=C(inp['kv_w_down']), wup=C(inp['kv_w_up']), win=C(inp['mla_w_in'][0]),
                wuq=C(inp['mla_w_uq'][0]), gkv=C(inp['kv_g_in'].reshape(8, 128).T), gpre=C(inp['g_pre'][1].reshape(8, 128).T),
                glat=C(inp['kv_g_latent'].reshape(2, 128).T), gq=C(inp['mla_g_q'][0].reshape(3, 128).T))


def kernel(**inputs):
    inp = {k: np.asarray(v) for k, v in inputs.items()}
    C = np.ascontiguousarray
    cores = list(range(8))
    ncA = build_stageA()
    rA = run_bass_kernel_spmd(ncA, [_prepA(inp, c // 4, c % 4) for c in cores], core_ids=cores).results
    yn = [np.concatenate([rA[b * 4 + g]['yn'] for g in range(4)], axis=1) for b in range(2)]
    ncB = build_stageB()
    rB = run_bass_kernel_spmd(ncB, [_prepB(inp, yn[c // 4], c // 4, c % 4) for c in cores], core_ids=cores).results
    knf, krf, vff, qff, sff = [], [], [], [], []
    for b in range(2):
        knf.append(np.concatenate([rB[b * 4 + j]['kn'] for j in range(4)], axis=2))
        krf.append(np.concatenate([rB[b * 4 + j]['kr'] for j in range(4)], axis=1))
        vff.append(np.concatenate([rB[b * 4 + j]['v'] for j in range(4)], axis=0))
        qff.append(np.concatenate([rB[b * 4 + j]['qT'] for j in range(4)], axis=2))
        sff.append(np.concatenate([rB[b * 4 + j]['sg'] for j in range(4)], axis=2))
    ncC = build_stageC(2)
    og_heads = {}
    for part in range(2):
        imC = []
        for c in cores:
            b, hg = c // 4, c % 4
            kT = np.empty((2, 96, 8192), dtype=knf[b].dtype)
            v4 = np.empty((2, 128, 64, 64), dtype=vff[b].dtype)
            q4 = np.empty((2, 96, 8192), dtype=qff[b].dtype)
            s4 = np.empty((2, 128, 64, 64), dtype=sff[b].dtype)
            for hl in range(2):
                h = hg * 4 + part * 2 + hl
                kT[hl, 0:64] = knf[b][(h % 2) * 64:(h % 2) * 64 + 64, h // 2, :]
                kT[hl, 64:96] = krf[b]
                v4[hl] = vff[b][:, h * 64:(h + 1) * 64].reshape(64, 128, 64).transpose(1, 0, 2)
                q4[hl] = qff[b][:, h, :]
                s4[hl] = sff[b][(h % 2) * 64:(h % 2) * 64 + 64, h // 2, :].T.reshape(64, 128, 64).transpose(1, 0, 2)
            imC.append(dict(kT=kT, v=v4, qT=q4, sg=s4))
        rC = run_bass_kernel_spmd(ncC, imC, core_ids=cores).results
        for c in cores:
            b, hg = c // 4, c % 4
            for hl in range(2):
                og_heads[(b, hg * 4 + part * 2 + hl)] = rC[c]['og'][hl]
    imD = []
    for c in cores:
        b, j = c // 4, c % 4
        tok = slice(j * NTOK, (j + 1) * NTOK)
        og = np.empty((128, 8, NTOK), dtype=og_heads[(0, 0)].dtype)
        for h in range(16):
            og[(h % 2) * 64:(h % 2) * 64 + 64, h // 2, :] = og_heads[(b, h)][tok].T
        imD.append(dict(og=og, h1=rB[c]['h1'], wo=C(inp['mla_w_out'][0]), gf=C(inp['g_final'].reshape(1, 1024))))
    ncD = build_stageD()
    rD = run_bass_kernel_spmd(ncD, imD, core_ids=cores).results
    out = np.stack([np.concatenate([rD[b * 4 + j]['out'] for j in range(4)], axis=0) for b in range(2)], axis=0)
    return out.astype(np.float32)
```

```python
import contextlib
import math
from concourse.bass_utils import run_bass_kernel_spmd
import numpy as np
import concourse.bass as bass
import concourse.mybir as mybir

F32 = mybir.dt.float32
BF16 = mybir.dt.bfloat16
I32 = mybir.dt.int32
AF = mybir.ActivationFunctionType
ALU = mybir.AluOpType
AX = mybir.AxisListType


class Prog:
    def __init__(self, nc):
        self.nc = nc
        self.ops = []
        self.lastw = {}
        self.readers = {}
        self.dma_sems = {}

    def add(self, eng, fn, r=(), w=(), dma=None, group=False):
        deps = set()
        for k in r:
            if k in self.lastw:
                deps.add(self.lastw[k])
            if k[0] == 'B' and k[1:].isdigit():
                for j in self.readers.get(k, ()):
                    if self.ops[j]['eng'] != eng:
                        deps.add(j)
        for k in w:
            if k in self.lastw:
                deps.add(self.lastw[k])
            deps.update(self.readers.get(k, ()))
        i = len(self.ops)
        self.ops.append(dict(eng=eng, fn=fn, deps=deps, dma=dma, group=group, has_dep=False))
        for k in r:
            self.readers.setdefault(k, []).append(i)
        for k in w:
            self.lastw[k] = i
            self.readers[k] = []
        return i

    def pe(self, fn, r=(), w=()):
        return self.add('pe', fn, r, w)

    def act(self, fn, r=(), w=()):
        return self.add('act', fn, r, w)

    def dve(self, fn, r=(), w=()):
        return self.add('dve', fn, r, w)

    def pool(self, fn, r=(), w=()):
        return self.add('pool', fn, r, w)

    def dma(self, eng, fn, r=(), w=(), sem=None, group=False):
        assert sem is not None
        return self.add(eng, fn, r, w, dma=sem, group=group)

    def wait_all(self, eng, keys):
        return self.add(eng, None, r=keys, w=())

    def emit(self):
        nc = self.nc
        ops = self.ops
        engs = ['sp', 'act', 'dve', 'pool', 'pe']
        for o in ops:
            for d in o['deps']:
                if ops[d]['eng'] == 'pe' and o['eng'] == 'pe' and ops[d]['dma'] is None and o['dma'] is None:
                    continue
                ops[d]['has_dep'] = True
        esem = {e: nc.alloc_semaphore(name=f"s_{e}") for e in engs}
        group_tot = {}
        for o in ops:
            if o['dma'] is not None:
                if o['dma'] not in self.dma_sems:
                    self.dma_sems[o['dma']] = nc.alloc_semaphore(name=f"d_{o['dma']}")
                group_tot[o['dma']] = group_tot.get(o['dma'], 0) + 1
        cnt = {e: 0 for e in engs}
        dcnt = {}
        for o in ops:
            if o['fn'] is None:
                o['tok'] = None
            elif o['dma'] is not None:
                k = o['dma']
                dcnt[k] = dcnt.get(k, 0) + 1
                v = group_tot[k] if o['group'] else dcnt[k]
                o['tok'] = (('d', k), 16 * v)
            elif o['has_dep']:
                cnt[o['eng']] += 1
                o['tok'] = (('e', o['eng']), cnt[o['eng']])
            else:
                o['tok'] = None
        known = {e: {} for e in engs}
        for o in ops:
            e = o['eng']
            kn = known[e]
            waits = []
            for d in sorted(o['deps'], reverse=True):
                od = ops[d]
                if od['tok'] is None:
                    continue
                if od['eng'] == 'pe' and e == 'pe' and od['dma'] is None and o['dma'] is None:
                    continue
                s, v = od['tok']
                if kn.get(s, 0) < v:
                    waits.append((s, v))
                    kn[s] = v
                    for s2, v2 in od['clock'].items():
                        if kn.get(s2, 0) < v2:
                            kn[s2] = v2
            wm = {}
            for s, v in waits:
                wm[s] = max(wm.get(s, 0), v)
            o['waits'] = wm
            o['clock'] = dict(kn)

        def semof(s):
            return esem[s[1]] if s[0] == 'e' else self.dma_sems[s[1]]

        def run(ename, eng):
            for o in ops:
                if o['eng'] != ename:
                    continue
                for s, v in o['waits'].items():
                    eng.wait_ge(semof(s), v)
                if o['fn'] is None:
                    continue
                inst = o['fn'](eng)
                if o['tok'] is not None:
                    s, v = o['tok']
                    inst.then_inc(semof(s), 16 if s[0] == 'd' else 1)

        with nc.Block() as block:
            @block.sync
            def _(e):
                run('sp', e)

            @block.scalar
            def _(e):
                run('act', e)

            @block.vector
            def _(e):
                run('dve', e)

            @block.gpsimd
            def _(e):
                run('pool', e)

            @block.tensor
            def _(e):
                run('pe', e)
        n = {e: sum(1 for o in ops if o['eng'] == e) for e in engs}
        nw = sum(len(o['waits']) for o in ops)
        print("PROG ops", n, "waits", nw, "sems", 5 + len(self.dma_sems), flush=True)


def _kw(**k):
    return {a: b for a, b in k.items() if b is not None}


class P2(Prog):
    def mm(self, out, lhsT, rhs, start=True, stop=True, r=(), w=()):
        return self.add('pe', lambda e: e.matmul(out, lhsT=lhsT, rhs=rhs, start=start, stop=stop), r, w)

    def tr(self, out, in_, ident, r=(), w=()):
        return self.add('pe', lambda e: e.transpose(out, in_, ident), r, w)

    def actv(self, out, in_, func, bias=None, scale=None, accum=None, r=(), w=()):
        kw = _kw(bias=bias, scale=scale, accum_out=accum)
        return self.add('act', lambda e: e.activation(out=out, in_=in_, func=func, **kw), r, w)

    def ts(self, eng, out, in0, s1, s2=None, op0=ALU.mult, op1=None, r=(), w=()):
        kw = _kw(op1=op1)
        return self.add(eng, lambda e: e.tensor_scalar(out=out, in0=in0, scalar1=s1, scalar2=s2, op0=op0, **kw), r, w)

    def tt(self, eng, out, in0, in1, op, r=(), w=()):
        return self.add(eng, lambda e: e.tensor_tensor(out=out, in0=in0, in1=in1, op=op), r, w)

    def stt(self, out, in0, scalar, in1, op0, op1, r=(), w=()):
        return self.add('dve', lambda e: e.scalar_tensor_tensor(out=out, in0=in0, scalar=scalar, in1=in1, op0=op0, op1=op1), r, w)

    def cp(self, eng, out, in_, r=(), w=()):
        if eng == 'act':
            return self.add('act', lambda e: e.activation(out=out, in_=in_, func=AF.Copy), r, w)
        return self.add(eng, lambda e: e.tensor_copy(out=out, in_=in_), r, w)

    def ms(self, eng, ap, val, w=()):
        return self.add(eng, lambda e: e.memset(ap, val), (), w)

    def ld(self, out, in_, w, sem, eng='sp', group=False, r=()):
        return self.dma(eng, lambda e: e.dma_start(out=out, in_=in_), r=r, w=w, sem=sem, group=group)

SEQ = 8192
DM = 1024
NCH = SEQ // 256
EPS = 1e-6
WCOLS = 1288


def build_stageA(nch=NCH):
    nc = bass.Bass("TRN2", target_bir_lowering=False)
    x_d = nc.dram_tensor("x", [SEQ, DM], F32, kind="ExternalInput").ap()
    w_d = nc.dram_tensor("w", [DM, WCOLS], F32, kind="ExternalInput").ap()
    gpre_d = nc.dram_tensor("gpre", [128, 8], F32, kind="ExternalInput").ap()
    cw_d = nc.dram_tensor("cw", [128, 24], F32, kind="ExternalInput").ap()
    cb_d = nc.dram_tensor("cb", [128, 6], F32, kind="ExternalInput").ap()
    dtb_d = nc.dram_tensor("dtb", [1, 8], F32, kind="ExternalInput").ap()
    alog_d = nc.dram_tensor("alog", [1, 8], F32, kind="ExternalInput").ap()
    dsk_d = nc.dram_tensor("dsk", [1, 8], F32, kind="ExternalInput").ap()
    gout_d = nc.dram_tensor("gout", [1, 512], F32, kind="ExternalInput").ap()
    yn_d = nc.dram_tensor("yn", [SEQ, 512], BF16, kind="ExternalOutput").ap()

    P = P2(nc)
    es = contextlib.ExitStack()

    def S(name, shape, dt):
        return es.enter_context(nc.sbuf_tensor(name, shape, dt))

    banks = [es.enter_context(nc.psum_tensor(f"bank{i}", [128, 512], F32)) for i in range(8)]

    W = S("W", [128, 8, WCOLS], BF16)
    wst = [S(f"wst{i}", [128, WCOLS], F32) for i in range(2)]
    gpre = S("gpre_s", [128, 8], F32)
    cw = S("cw_s", [128, 24], F32)
    cb = S("cb_s", [128, 6], F32)
    dtb_bc = S("dtb_bc", [128, 8], F32)
    A_bc = S("A_bc", [128, 8], F32)
    D_bc = S("D_bc", [128, 8], F32)
    gout_bc = S("gout_bc", [128, 512], F32)
    identf = S("identf", [128, 128], F32)
    identb = S("identb", [128, 128], BF16)
    onesf = S("onesf", [128, 128], F32)
    onesb = S("onesb", [128, 128], BF16)
    trif = S("trif", [128, 128], F32)
    triw = S("triw", [128, 256], BF16)
    SU = S("SU", [128, 128], BF16)
    cdiag = S("cdiag", [128, 24, 128], BF16)
    Dident = S("Dident", [128, 8, 128], BF16)
    xin = [S(f"xin{i}", [128, 2, DM], F32) for i in range(2)]
    junk = [S(f"junk{i}", [128, DM], BF16) for i in range(2)]
    ss = S("ss", [128, 2], F32)
    rt = S("rt", [128, 2], F32)
    rstd = S("rstd", [128, 2], F32)
    hn = S("hn", [128, 2, DM], BF16)
    hnT = S("hnT", [128, 8, 256], BF16)
    ubuf = S("ubuf", [128, 6, 259], BF16)
    xc = [S(f"xc{i}", [128, 6, 256], BF16) for i in range(2)]
    xtok = S("xtok", [128, 2, 640], BF16)
    dtr = S("dtr", [128, 2, 8], F32)
    e1 = S("e1", [128, 2, 8], F32)
    dtk = [S(f"dtk{i}", [128, 2, 8], F32) for i in range(2)]
    dtA = [S(f"dtA{i}", [128, 2, 8], F32) for i in range(2)]
    cend = S("cend", [128, 8], F32)
    ecum = [S(f"ecum{i}", [128, 2, 8], F32) for i in range(2)]
    wtmp = [S(f"wtmp{i}", [128, 2, 8], F32) for i in range(2)]
    dec = [S(f"dec{i}", [128, 8], F32) for i in range(2)]
    W0 = S("W0", [128, 8, 256], BF16)
    V1 = S("V1", [128, 8, 128], BF16)
    CBm = S("CBm", [128, 384], BF16)
    xdt = S("xdt", [128, 2, 512], BF16)
    Lb = [S(f"Lb{i}", [128, 384], BF16) for i in range(2)]
    junk2b = S("junk2b", [128, 512], BF16)
    MT = S("MT", [128, 8, 384], BF16)
    state = S("state", [128, 512], F32)
    state_bf = S("state_bf", [128, 512], BF16)
    yi = S("yi", [128, 512], F32)
    t1 = S("t1", [128, 512], F32)
    ysb = S("ysb", [128, 512], F32)
    zs = [S(f"zs{i}", [128, 2, 512], F32) for i in range(2)]
    yg = [S(f"yg{i}", [128, 512], F32) for i in range(2)]
    junk2 = S("junk2", [128, 512], BF16)
    ss2 = S("ss2", [128, 2], F32)
    rt2 = S("rt2", [128, 2], F32)
    rstd2 = S("rstd2", [128, 2], F32)
    yn = [S(f"yn{i}", [128, 512], BF16) for i in range(2)]
    wx = S("wx", [128, 2, 512], BF16)

    def bfview(bank):
        return bank[:].bitcast(BF16)

    ptr = bfview(banks[0]).rearrange("p (k t) -> p k t", k=8)
    pX = banks[1]
    pCv = banks[2]
    pdtk = banks[3][:, 0:16].rearrange("p (t c) -> p t c", t=2)
    pcum = banks[3][:, 16:32].rearrange("p (t c) -> p t c", t=2)
    pce = banks[3][:, 32:40]
    pz = banks[3][:, :]
    ptx = bfview(banks[4])[:, 0:640]
    pCB = banks[4][:, 0:384]
    pseg = [banks[5][:, 0:384], banks[7][:, 0:384]]
    py = banks[6][:, :]
    pyi = banks[4][:, :]
    pst = banks[4][:, :]

    P.ld(gpre[:], gpre_d, ['gpre'], 'c0')
    P.ld(cw[:], cw_d, ['cw'], 'c1')
    P.ld(cb[:], cb_d, ['cb'], 'c2')
    P.ld(dtb_bc[:], dtb_d.partition_broadcast(128), ['dtb_bc'], 'c3')
    P.ld(A_bc[:], alog_d.partition_broadcast(128), ['A_bc'], 'c4')
    P.ld(D_bc[:], dsk_d.partition_broadcast(128), ['D_bc'], 'c5')
    P.ld(gout_bc[:], gout_d.partition_broadcast(128), ['gout_bc'], 'c6')
    P.ms('pool', identf[:], 1.0, ['identf'])
    P.add('pool', lambda e: e.affine_select(out=identf[:], in_=identf[:], pattern=[[-1, 128]], compare_op=ALU.is_equal,
                                            fill=0.0, base=0, channel_multiplier=1), r=['identf'], w=['identf'])
    P.cp('dve', identb[:], identf[:], r=['identf'], w=['identb'])
    P.ms('pool', onesf[:], 1.0, ['onesf'])
    P.ms('pool', onesb[:], 1.0, ['onesb'])
    P.ms('pool', triw[:], 1.0, ['triw'])
    P.add('pool', lambda e: e.affine_select(out=triw[:, 0:128], in_=triw[:, 0:128], pattern=[[1, 128]], compare_op=ALU.is_ge,
                                            fill=0.0, base=0, channel_multiplier=-1), r=['triw'], w=['triw'])
    P.cp('dve', trif[:], triw[:, 0:128], r=['triw'], w=['trif'])
    P.ms('pool', SU[:], 1.0, ['SU'])
    P.add('pool', lambda e: e.affine_select(out=SU[:], in_=SU[:], pattern=[[-1, 128]], compare_op=ALU.is_gt,
                                            fill=0.0, base=0, channel_multiplier=1), r=['SU'], w=['SU'])
    P.ms('pool', ubuf[:], 0.0, ['ubuf%d' % i for i in range(3)])
    P.ms('pool', state[:], 0.0, ['state'])
    P.ms('pool', state_bf[:], 0.0, ['state_bf'])
    for kt in range(8):
        P.ld(wst[kt % 2][:], w_d[kt * 128:(kt + 1) * 128, :], [f'wst{kt % 2}'], f'wst{kt % 2}')
        if kt % 2 == 0:
            P.ts('dve', W[:, kt, :], wst[kt % 2][:], gpre[:, kt:kt + 1], None, ALU.mult, r=[f'wst{kt % 2}', 'gpre'], w=[f'W{kt}'])
        else:
            P.actv(W[:, kt, :], wst[kt % 2][:], AF.Copy, scale=gpre[:, kt:kt + 1], r=[f'wst{kt % 2}', 'gpre'], w=[f'W{kt}'])
    Wk = [f'W{kt}' for kt in range(8)]
    HNT = ['hnT0', 'hnT1']
    for i in range(24):
        P.ts('dve', cdiag[:, i, :], identf[:], cw[:, i:i + 1], None, ALU.mult, r=['identf', 'cw'], w=['cdiag'])
    for h in range(8):
        P.ts('dve', Dident[:, h, :], identf[:], D_bc[:, h:h + 1], None, ALU.mult, r=['identf', 'D_bc'], w=['Dident'])
    P.actv(A_bc[:], A_bc[:], AF.Exp, r=['A_bc'], w=['A_bc'])
    P.ts('dve', A_bc[:], A_bc[:], -1.0, None, ALU.mult, r=['A_bc'], w=['A_bc'])

    def load_x(c):
        sl = c % 2
        P.ld(xin[sl][:], x_d[c * 256:(c + 1) * 256, :].rearrange("(t p) d -> p t d", p=128), [f'xin{sl}'], f'xin{sl}')

    def front(c):
        sl = c % 2
        p = c % 2
        xk = f'xin{sl}'
        if c + 1 < nch:
            load_x(c + 1)
        for t in range(2):
            P.actv(junk[t][:], xin[sl][:, t, :], AF.Square, accum=ss[:, t:t + 1], r=[xk], w=[f'ss{t}', f'junk{t}'])
        P.actv(rt[:], ss[:], AF.Ln, bias=EPS, scale=1.0 / DM, r=['ss0', 'ss1'], w=['rt'])
        P.actv(rstd[:], rt[:], AF.Exp, scale=-0.5, r=['rt'], w=['rstd'])
        for t in range(2):
            P.ts('dve', hn[:, t, :], xin[sl][:, t, :], rstd[:, t:t + 1], None, ALU.mult, r=[xk, 'rstd'], w=[f'hn{t}'])
        yield
        for t in range(2):
            for kt in range(8):
                P.tr(ptr[:, kt, :], hn[:, t, kt * 128:(kt + 1) * 128], identb[:], r=[f'hn{t}', 'identb'], w=['B0'])
            P.cp('dve' if t == 0 else 'act', hnT[:, :, t * 128:(t + 1) * 128], ptr, r=['B0'], w=[f'hnT{t}'])
            yield
        for t in range(2):
            for kt in range(8):
                P.mm(pdtk[:, t, :], hnT[:, kt, t * 128:(t + 1) * 128], W[:, kt, 768:776], start=(kt == 0), stop=(kt == 7),
                     r=HNT + [Wk[kt]], w=['B3'])
        P.tt('dve', dtr[:], pdtk, dtb_bc[:].unsqueeze(1).to_broadcast([128, 2, 8]), ALU.add, r=['B3', 'dtb_bc'], w=['dtr'])
        P.actv(e1[:], dtr[:], AF.Exp, r=['dtr'], w=['e1'])
        P.actv(dtk[p][:], e1[:], AF.Ln, bias=1.0, r=['e1'], w=[f'dtk{p}'])
        P.tt('dve', dtA[p][:], dtk[p][:], A_bc[:].unsqueeze(1).to_broadcast([128, 2, 8]), ALU.mult, r=[f'dtk{p}', 'A_bc'], w=[f'dtA{p}'])
        P.mm(pcum[:, 0, :], trif[:], dtA[p][:, 0, :], r=['trif', f'dtA{p}'], w=['B3'])
        P.mm(pcum[:, 1, :], onesf[:], dtA[p][:, 0, :], start=True, stop=False, r=['onesf', f'dtA{p}'], w=['B3'])
        P.mm(pcum[:, 1, :], trif[:], dtA[p][:, 1, :], start=False, stop=True, r=['trif', f'dtA{p}'], w=['B3'])
        P.mm(pce, onesf[:], dtA[p][:, 0, :], start=True, stop=False, r=['onesf', f'dtA{p}'], w=['B3'])
        P.mm(pce, onesf[:], dtA[p][:, 1, :], start=False, stop=True, r=['onesf', f'dtA{p}'], w=['B3'])
        P.actv(ecum[p][:], pcum, AF.Exp, r=['B3'], w=[f'ecum{p}'])
        P.actv(dec[p][:], pce, AF.Exp, r=['B3'], w=[f'dec{p}'])
        P.cp('act', cend[:], pce, r=['B3'], w=['cend'])
        P.tt('dve', wtmp[p][:], cend[:].unsqueeze(1).to_broadcast([128, 2, 8]), pcum, ALU.subtract, r=['cend', 'B3'], w=[f'wtmp{p}'])
        P.actv(wtmp[p][:], wtmp[p][:], AF.Exp, r=[f'wtmp{p}'], w=[f'wtmp{p}'])
        yield
        for pr in range(3):
            for j in range(2):
                ct = 2 * pr + j
                for kt in range(8):
                    P.mm(pX[:, j * 256:(j + 1) * 256], W[:, kt, ct * 128:(ct + 1) * 128], hnT[:, kt, :], start=(kt == 0), stop=(kt == 7),
                         r=HNT + [Wk[kt]], w=['B1'])
            P.cp('dve', ubuf[:, 2 * pr:2 * pr + 2, 3:259], pX[:, :].rearrange("p (j t) -> p j t", j=2), r=['B1'], w=[f'ubuf{pr}'])
            for j in range(2):
                ct = 2 * pr + j
                for k in range(4):
                    P.mm(pCv[:, j * 256:(j + 1) * 256], cdiag[:, ct * 4 + k, :], ubuf[:, ct, k:k + 256], start=(k == 0), stop=(k == 3),
                         r=['cdiag', f'ubuf{pr}'], w=['B2'])
            for j in range(2):
                ct = 2 * pr + j
                P.actv(xc[p][:, ct, :], pCv[:, j * 256:(j + 1) * 256], AF.Silu, bias=cb[:, ct:ct + 1], r=['B2', 'cb'], w=[f'xc{p}_{ct}'])
            P.cp('pool', ubuf[:, 2 * pr:2 * pr + 2, 0:3], ubuf[:, 2 * pr:2 * pr + 2, 256:259], r=[f'ubuf{pr}'], w=[f'ubuf{pr}'])
            yield
        for t in range(2):
            for kt in range(8):
                P.mm(pz, hnT[:, kt, t * 128:(t + 1) * 128], W[:, kt, 776:1288], start=(kt == 0), stop=(kt == 7),
                     r=HNT + [Wk[kt]], w=['B3'])
            P.actv(zs[p][:, t, :], pz, AF.Silu, r=['B3'], w=[f'zs{p}_{t}'])
            yield
    def back(c):
        p = c % 2
        XC = [f'xc{p}_{ct}' for ct in range(6)]
        P.tt('dve', W0[:], triw[:].unsqueeze(1).to_broadcast([128, 8, 256]), dtA[p][:, 0, :].unsqueeze(2).to_broadcast([128, 8, 256]),
             ALU.mult, r=['triw', f'dtA{p}'], w=['W0'])
        P.tt('dve', V1[:], triw[:, 0:128].unsqueeze(1).to_broadcast([128, 8, 128]), dtA[p][:, 1, :].unsqueeze(2).to_broadcast([128, 8, 128]),
             ALU.mult, r=['triw', f'dtA{p}'], w=['V1'])
        for t in range(2):
            for ct in range(5):
                P.tr(ptx[:, ct * 128:(ct + 1) * 128], xc[p][:, ct, t * 128:(t + 1) * 128], identb[:], r=[XC[ct], 'identb'], w=['B4'])
            P.cp('dve' if t == 0 else 'act', xtok[:, t, :], ptx, r=['B4'], w=[f'xtok{t}'])
            P.tt('pool', xdt[:, t, :].rearrange("p (h c) -> p h c", h=8), xtok[:, t, 0:512].rearrange("p (h c) -> p h c", h=8),
                 dtk[p][:, t, :].unsqueeze(2).to_broadcast([128, 8, 64]), ALU.mult, r=[f'xtok{t}', f'dtk{p}'], w=[f'xdt{t}'])
        yield
        P.mm(pCB[:, 0:256], xc[p][:, 4, 0:128], xc[p][:, 5, 0:256], r=[XC[4], XC[5]], w=['B4'])
        P.mm(pCB[:, 256:384], xc[p][:, 4, 128:256], xc[p][:, 5, 128:256], r=[XC[4], XC[5]], w=['B4'])
        P.cp('act', CBm[:], pCB, r=['B4'], w=['CBm'])
        for off in (0, 256):
            blk = CBm[:, off:off + 128]
            P.add('pool', (lambda blk: (lambda e: e.affine_select(out=blk, in_=blk, pattern=[[1, 128]], compare_op=ALU.is_ge,
                                                                  fill=0.0, base=0, channel_multiplier=-1)))(blk),
                  r=['CBm'], w=['CBm'])
        yield
        for h in range(8):
            L = Lb[h % 2]
            Lk = f'Lb{h % 2}'
            ps = pseg[h % 2]
            psk = 'B5' if h % 2 == 0 else 'B7'
            P.mm(ps[:, 0:128], SU[:], W0[:, h, 0:128], start=True, stop=True, r=['SU', 'W0'], w=[psk])
            P.mm(ps[:, 128:256], SU[:], W0[:, h, 128:256], start=True, stop=False, r=['SU', 'W0'], w=[psk])
            P.mm(ps[:, 128:256], onesb[:], V1[:, h, :], start=False, stop=True, r=['onesb', 'V1'], w=[psk])
            P.mm(ps[:, 256:384], SU[:], V1[:, h, :], start=True, stop=True, r=['SU', 'V1'], w=[psk])
            P.actv(L[:], ps, AF.Exp, r=[psk], w=[Lk])
            P.tt('dve', MT[:, h, :], L[:], CBm[:], ALU.mult, r=[Lk, 'CBm'], w=[f'MT{h}'])
            if h % 2 == 1:
                yield
        for t in range(2):
            for h in range(8):
                hc = slice(h * 64, (h + 1) * 64)
                P.mm(py[:, hc], MT[:, h, t * 128:(t + 1) * 128], xdt[:, 0, hc], start=True, stop=False, r=[f'MT{h}', 'xdt0'], w=['B6'])
                if t == 1:
                    P.mm(py[:, hc], MT[:, h, 256:384], xdt[:, 1, hc], start=False, stop=False, r=[f'MT{h}', 'xdt1'], w=['B6'])
                P.mm(py[:, hc], Dident[:, h, :], xtok[:, t, hc], start=False, stop=True, r=['Dident', f'xtok{t}'], w=['B6'])
            P.mm(pyi, xc[p][:, 5, t * 128:(t + 1) * 128], state_bf[:], r=[XC[5], 'state_bf'], w=['B4'])
            P.tt('dve', t1[:].rearrange("p (h c) -> p h c", h=8), pyi.rearrange("p (h c) -> p h c", h=8),
                 ecum[p][:, t, :].unsqueeze(2).to_broadcast([128, 8, 64]), ALU.mult, r=['B4', f'ecum{p}'], w=['t1'])
            P.tt('dve', ysb[:], t1[:], py, ALU.add, r=['t1', 'B6'], w=['ysb'])
            P.tt('dve', yg[t][:], ysb[:], zs[p][:, t, :], ALU.mult, r=['ysb', f'zs{p}_{t}'], w=[f'yg{t}'])
            P.actv(junk2b[:], yg[t][:], AF.Square, accum=ss2[:, t:t + 1], r=[f'yg{t}'], w=[f'ss2_{t}', 'junk2b'])
            yield
        P.actv(rt2[:], ss2[:], AF.Ln, bias=EPS, scale=1.0 / 512, r=['ss2_0', 'ss2_1'], w=['rt2'])
        P.actv(rstd2[:], rt2[:], AF.Exp, scale=-0.5, r=['rt2'], w=['rstd2'])
        for t in range(2):
            P.stt(yn[t][:], yg[t][:], rstd2[:, t:t + 1], gout_bc[:], ALU.mult, ALU.mult, r=[f'yg{t}', 'rstd2', 'gout_bc'], w=[f'yn{t}'])
            P.ld(yn_d[c * 256 + t * 128: c * 256 + (t + 1) * 128, :], yn[t][:], w=[f'ynd{t}'], sem=f'st{t}', r=[f'yn{t}'])
        for st in range(2):
            P.tt('pool', wx[:, st, :].rearrange("p (h c) -> p h c", h=8), xdt[:, st, :].rearrange("p (h c) -> p h c", h=8),
                 wtmp[p][:, st, :].unsqueeze(2).to_broadcast([128, 8, 64]), ALU.mult, r=[f'xdt{st}', f'wtmp{p}'], w=[f'wx{st}'])
        for st in range(2):
            P.mm(pst, xtok[:, st, 512:640], wx[:, st, :], start=(st == 0), stop=(st == 1), r=[f'xtok{st}', f'wx{st}'], w=['B4'])
        P.tt('dve', state[:].rearrange("p (h c) -> p h c", h=8), state[:].rearrange("p (h c) -> p h c", h=8),
             dec[p][:].unsqueeze(2).to_broadcast([128, 8, 64]), ALU.mult, r=['state', f'dec{p}'], w=['state'])
        P.tt('dve', state[:], state[:], pst, ALU.add, r=['state', 'B4'], w=['state'])
        P.cp('act', state_bf[:], state[:], r=['state'], w=['state_bf'])
        yield

    load_x(0)
    for it in range(nch + 1):
        gens = []
        if it >= 1:
            gens.append(back(it - 1))
        if it < nch:
            gens.append(front(it))
        while gens:
            for g in list(gens):
                try:
                    next(g)
                except StopIteration:
                    gens.remove(g)
    P.wait_all('sp', ['ynd0', 'ynd1'])
    P.emit()
    es.close()
    return nc

NTOK = 2048
NTT = NTOK // 128
INV_FREQ = [float(np.float32(10000.0) ** np.float32(-(2 * i) / 32.0)) for i in range(16)]
TWO_PI = 2.0 * math.pi
CW1 = 6.28125
CW2 = TWO_PI - CW1


def build_stageB(ntt=NTT):
    nc = bass.Bass("TRN2", target_bir_lowering=False)
    x_d = nc.dram_tensor("x", [NTOK, 1024], F32, kind="ExternalInput").ap()
    yn_d = nc.dram_tensor("yn", [NTOK, 2048], BF16, kind="ExternalInput").ap()
    pos_d = nc.dram_tensor("pos", [128, NTT], I32, kind="ExternalInput").ap()
    invf_d = nc.dram_tensor("invf", [1, 16], F32, kind="ExternalInput").ap()
    wout_d = nc.dram_tensor("wout", [2048, 1024], F32, kind="ExternalInput").ap()
    wdn_d = nc.dram_tensor("wdn", [1024, 288], F32, kind="ExternalInput").ap()
    wup_d = nc.dram_tensor("wup", [256, 2048], F32, kind="ExternalInput").ap()
    win_d = nc.dram_tensor("win", [1024, 1408], F32, kind="ExternalInput").ap()
    wuq_d = nc.dram_tensor("wuq", [384, 1536], F32, kind="ExternalInput").ap()
    gkv_d = nc.dram_tensor("gkv", [128, 8], F32, kind="ExternalInput").ap()
    gpre_d = nc.dram_tensor("gpre", [128, 8], F32, kind="ExternalInput").ap()
    glat_d = nc.dram_tensor("glat", [128, 2], F32, kind="ExternalInput").ap()
    gq_d = nc.dram_tensor("gq", [128, 3], F32, kind="ExternalInput").ap()
    h1_d = nc.dram_tensor("h1", [NTOK, 1024], F32, kind="ExternalOutput").ap()
    sg_d = nc.dram_tensor("sg", [128, 8, NTOK], BF16, kind="ExternalOutput").ap()
    kn_d = nc.dram_tensor("kn", [128, 8, NTOK], BF16, kind="ExternalOutput").ap()
    kr_d = nc.dram_tensor("kr", [32, NTOK], BF16, kind="ExternalOutput").ap()
    v_d = nc.dram_tensor("v", [NTOK, 1024], BF16, kind="ExternalOutput").ap()
    qT_d = nc.dram_tensor("qT", [96, 16, NTOK], BF16, kind="ExternalOutput").ap()

    P = P2(nc)
    es = contextlib.ExitStack()

    def S(name, shape, dt):
        return es.enter_context(nc.sbuf_tensor(name, shape, dt))

    banks = [es.enter_context(nc.psum_tensor(f"bank{i}", [128, 512], F32)) for i in range(8)]
    bctr = [0, 0]
    ring = [0]

    def nb():
        r = ring[0]
        i = r * 4 + bctr[r] % 4
        bctr[r] += 1
        return banks[i], f'B{i}'

    def bfv(bank):
        return bank[:].bitcast(BF16)

    wout = S("wout_s", [128, 16, 1024], BF16)
    wdn = S("wdn_s", [128, 8, 288], BF16)
    wkn = S("wkn_s", [128, 2, 1024], BF16)
    wv = S("wv_s", [128, 2, 1024], BF16)
    win = S("win_s", [128, 8, 1408], BF16)
    wuq = S("wuq_s", [128, 3, 1536], BF16)
    wst = [S(f"wst{i}", [128, 2048], F32) for i in range(4)]
    gkv = S("gkv_s", [128, 8], F32)
    gpre = S("gpre_s", [128, 8], F32)
    glat = S("glat_s", [128, 2], F32)
    gq = S("gq_s", [128, 3], F32)
    identf = S("identf", [128, 128], F32)
    identb = S("identb", [128, 128], BF16)
    posi = S("posi", [128, NTT], I32)
    posf = S("posf", [128, NTT], F32)
    invf = S("invf_s", [128, 16], F32)
    ang = S("ang", [128, NTT, 16], F32)
    uu = S("uu", [128, NTT, 16], F32)
    ki = S("ki", [128, NTT, 16], I32)
    kf = S("kf", [128, NTT, 16], F32)
    gg = S("gg", [128, NTT, 16], F32)
    m1 = S("m1", [128, NTT, 16], F32)
    gc = S("gc", [128, NTT, 16], F32)
    sinT = S("sinT", [128, NTT, 16], F32)
    cosT = S("cosT", [128, NTT, 16], F32)
    xin = [S(f"xin{i}", [128, 1024], F32) for i in range(2)]
    ynin = [S(f"ynin{i}", [128, 2048], BF16) for i in range(2)]
    ynT = S("ynT", [128, 16, 128], BF16)
    h1 = [S(f"h1_{i}", [128, 1024], F32) for i in range(2)]
    junk = S("junk", [128, 1024], BF16)
    ss = S("ss", [128, 1], F32)
    rt = S("rt", [128, 1], F32)
    rstd = S("rstd", [128, 1], F32)
    hnb = S("hnb", [128, 1024], BF16)
    hT2 = [S(f"hT{i}", [128, 8, 128], BF16) for i in range(2)]
    junk2 = S("junk2", [128, 384], BF16)
    ssc = S("ssc", [128, 1], F32)
    rtc = S("rtc", [128, 1], F32)
    rstdc = S("rstdc", [128, 1], F32)
    ckvn = S("ckvn", [128, 256], BF16)
    ra = S("ra", [128, 16], F32)
    rb = S("rb", [128, 16], F32)
    krb = S("krb", [128, 32], BF16)
    ckT = S("ckT", [128, 2, 128], BF16)
    krT = [S(f"krT{i}", [32, 128], BF16) for i in range(2)]
    knT = [S(f"knT{i}", [128, 8, 128], BF16) for i in range(2)]
    vsb = [S(f"vsb{i}", [128, 1024], BF16) for i in range(2)]
    ssq = S("ssq", [128, 1], F32)
    rtq = S("rtq", [128, 1], F32)
    rstdq = S("rstdq", [128, 1], F32)
    cqn = S("cqn", [128, 384], BF16)
    sg = [S(f"sg{i}", [128, 8, 128], BF16) for i in range(2)]
    cqT = S("cqT", [128, 3, 128], BF16)
    qtok = S("qtok", [128, 16, 96], BF16)
    qa = S("qa", [128, 16, 16], F32)
    qb = S("qb", [128, 16, 16], F32)
    qT = [S(f"qT{i}", [96, 16, 128], BF16) for i in range(2)]

    P.ld(gkv[:], gkv_d, ['gkv'], 'c0')
    P.ld(gpre[:], gpre_d, ['gpre'], 'c1')
    P.ld(glat[:], glat_d, ['glat'], 'c2')
    P.ld(gq[:], gq_d, ['gq'], 'c3')
    P.ld(posi[:], pos_d, ['posi'], 'c4')
    P.ld(invf[:], invf_d.partition_broadcast(128), ['invf'], 'c5')
    P.ms('pool', identf[:], 1.0, ['identf'])
    P.add('pool', lambda e: e.affine_select(out=identf[:], in_=identf[:], pattern=[[-1, 128]], compare_op=ALU.is_equal,
                                            fill=0.0, base=0, channel_multiplier=1), r=['identf'], w=['identf'])
    P.cp('dve', identb[:], identf[:], r=['identf'], w=['identb'])
    P.cp('dve', posf[:], posi[:], r=['posi'], w=['posf'])
    P.tt('dve', ang[:], posf[:].unsqueeze(2).to_broadcast([128, NTT, 16]), invf[:].unsqueeze(1).to_broadcast([128, NTT, 16]),
         ALU.mult, r=['posf', 'invf'], w=['ang'])
    P.ts('dve', uu[:], ang[:], 1.0 / TWO_PI, None, ALU.mult, r=['ang'], w=['uu'])
    P.cp('dve', ki[:], uu[:], r=['uu'], w=['ki'])
    P.cp('dve', kf[:], ki[:], r=['ki'], w=['kf'])
    P.stt(gg[:], kf[:], -CW1, ang[:], ALU.mult, ALU.add, r=['kf', 'ang'], w=['gg'])
    P.stt(gg[:], kf[:], -CW2, gg[:], ALU.mult, ALU.add, r=['kf', 'gg'], w=['gg'])
    P.ts('dve', gg[:], gg[:], 1.0 / TWO_PI, None, ALU.mult, r=['gg'], w=['gg'])

    def wrap():
        P.ts('dve', m1[:], gg[:], 0.5, None, ALU.is_gt, r=['gg'], w=['m1'])
        P.tt('dve', gg[:], gg[:], m1[:], ALU.subtract, r=['gg', 'm1'], w=['gg'])
        P.ts('dve', m1[:], gg[:], -0.5, None, ALU.is_lt, r=['gg'], w=['m1'])
        P.tt('dve', gg[:], gg[:], m1[:], ALU.add, r=['gg', 'm1'], w=['gg'])
        P.ts('dve', gg[:], gg[:], 0.4999995, -0.4999995, ALU.min, ALU.max, r=['gg'], w=['gg'])

    wrap()
    P.actv(sinT[:], gg[:], AF.Sin, scale=TWO_PI, r=['gg'], w=['sinT'])
    P.ts('dve', gg[:], gg[:], 0.25, None, ALU.add, r=['gg'], w=['gg'])
    wrap()
    P.actv(cosT[:], gg[:], AF.Sin, scale=TWO_PI, r=['gg'], w=['cosT'])

    wi = [0]
    WK = {}
    NST = 4

    def wload(src_ap, ncols, convs):
        i = wi[0] % NST
        wi[0] += 1
        P.ld(wst[i][:, 0:ncols], src_ap, [f'wst{i}'], f'wst{i}', eng=('sp', 'act', 'pool')[wi[0] % 3] if False else 'sp')
        for n, (grp, dst_ap, gain_ap, in_view) in enumerate(convs):
            key = f'W{wi[0]}_{n}'
            WK.setdefault(grp, []).append(key)
            src = wst[i][:, 0:ncols] if in_view is None else in_view(wst[i])
            if (wi[0] + n) % 2 == 0:
                if gain_ap is None:
                    P.cp('dve', dst_ap, src, r=[f'wst{i}'], w=[key])
                else:
                    P.ts('dve', dst_ap, src, gain_ap, None, ALU.mult, r=[f'wst{i}', 'gkv', 'gpre', 'glat', 'gq'], w=[key])
            else:
                if gain_ap is None:
                    P.cp('act', dst_ap, src, r=[f'wst{i}'], w=[key])
                else:
                    P.actv(dst_ap, src, AF.Copy, scale=gain_ap, r=[f'wst{i}', 'gkv', 'gpre', 'glat', 'gq'], w=[key])

    hv = lambda lo, hi, n, w: (lambda t: t[:, 0:n].rearrange("p (h c) -> p h c", h=16)[:, :, lo:hi])
    for kt in range(16):
        wload(wout_d[kt * 128:(kt + 1) * 128, :], 1024, [('wout', wout[:, kt, :], None, None)])
    for kt in range(8):
        wload(wdn_d[kt * 128:(kt + 1) * 128, :], 288, [('wdn', wdn[:, kt, :], gkv[:, kt:kt + 1], None)])
    for kt in range(8):
        wload(win_d[kt * 128:(kt + 1) * 128, :], 1408, [('win', win[:, kt, :], gpre[:, kt:kt + 1], None)])
    for kt in range(2):
        wload(wup_d[kt * 128:(kt + 1) * 128, :], 2048, [
            ('wkn', wkn[:, kt, :].rearrange("p (h c) -> p h c", h=16), glat[:, kt:kt + 1], hv(0, 64, 2048, 128)),
            ('wv', wv[:, kt, :].rearrange("p (h c) -> p h c", h=16), glat[:, kt:kt + 1], hv(64, 128, 2048, 128))])
    for kt in range(3):
        wload(wuq_d[kt * 128:(kt + 1) * 128, :], 1536, [
            ('wuq', wuq[:, kt, 0:1024].rearrange("p (h c) -> p h c", h=16), gq[:, kt:kt + 1], hv(0, 64, 1536, 96)),
            ('wuq', wuq[:, kt, 1024:1280].rearrange("p (h c) -> p h c", h=16), gq[:, kt:kt + 1], hv(64, 80, 1536, 96)),
            ('wuq', wuq[:, kt, 1280:1536].rearrange("p (h c) -> p h c", h=16), gq[:, kt:kt + 1], hv(80, 96, 1536, 96))])

    def load_t(tt):
        sl = tt % 2
        P.ld(xin[sl][:], x_d[tt * 128:(tt + 1) * 128, :], [f'xin{sl}'], f'xin{sl}')
        P.ld(ynin[sl][:], yn_d[tt * 128:(tt + 1) * 128, :], [f'ynin{sl}'], f'ynin{sl}')

    def front(tt):
        ring[0] = 0
        sl = tt % 2
        hT = hT2[tt % 2]
        hTk = f'hT{tt % 2}'
        tok = slice(tt * 128, (tt + 1) * 128)
        if tt + 1 < ntt:
            load_t(tt + 1)
        for half in range(2):
            bk, bkk = nb()
            pv = bfv(bk).rearrange("p (k t) -> p k t", k=8)
            for j in range(8):
                c = half * 8 + j
                P.tr(pv[:, j, :], ynin[sl][:, c * 128:(c + 1) * 128], identb[:], r=[f'ynin{sl}', 'identb'], w=[bkk])
            P.cp('dve' if half == 0 else 'act', ynT[:, half * 8:(half + 1) * 8, :], pv, r=[bkk], w=[f'ynT{half}'])
        yield
        ring[0] = 0
        for half in range(2):
            bk, bkk = nb()
            for c in range(16):
                P.mm(bk[:, :], ynT[:, c, :], wout[:, c, half * 512:(half + 1) * 512], start=(c == 0), stop=(c == 15),
                     r=['ynT0', 'ynT1', *WK['wout']], w=[bkk])
            P.tt('dve', h1[sl][:, half * 512:(half + 1) * 512], bk[:, :], xin[sl][:, half * 512:(half + 1) * 512], ALU.add,
                 r=[bkk, f'xin{sl}'], w=[f'h1_{sl}'])
        P.ld(h1_d[tok, :], h1[sl][:], w=[f'h1d{sl}'], sem=f'sth{sl}', r=[f'h1_{sl}'])
        yield
        ring[0] = 0
        P.actv(junk[:], h1[sl][:], AF.Square, accum=ss[:], r=[f'h1_{sl}'], w=['junk', 'ss'])
        P.actv(rt[:], ss[:], AF.Sqrt, bias=EPS, scale=1.0 / 1024, r=['ss'], w=['rt'])
        P.add('dve', lambda e: e.reciprocal(out=rstd[:], in_=rt[:]), r=['rt'], w=['rstd'])
        P.ts('dve', hnb[:], h1[sl][:], rstd[:, 0:1], None, ALU.mult, r=[f'h1_{sl}', 'rstd'], w=['hnb'])
        bk, bkk = nb()
        pv = bfv(bk).rearrange("p (k t) -> p k t", k=8)
        for kt in range(8):
            P.tr(pv[:, kt, :], hnb[:, kt * 128:(kt + 1) * 128], identb[:], r=['hnb', 'identb'], w=[bkk])
        P.cp('act', hT[:], pv, r=[bkk], w=[hTk])
        yield

    def back(tt):
        ring[0] = 1
        sl = tt % 2
        hT = hT2[tt % 2]
        hTk = f'hT{tt % 2}'
        tok = slice(tt * 128, (tt + 1) * 128)
        bk, bkk = nb()
        for kt in range(8):
            P.mm(bk[:, 0:288], hT[:, kt, :], wdn[:, kt, :], start=(kt == 0), stop=(kt == 7), r=[hTk, *WK['wdn']], w=[bkk])
        P.actv(junk2[:, 0:256], bk[:, 0:256], AF.Square, accum=ssc[:], r=[bkk], w=['junk2', 'ssc'])
        P.actv(rtc[:], ssc[:], AF.Sqrt, bias=EPS, scale=1.0 / 256, r=['ssc'], w=['rtc'])
        P.add('dve', lambda e: e.reciprocal(out=rstdc[:], in_=rtc[:]), r=['rtc'], w=['rstdc'])
        P.tt('dve', ra[:], bk[:, 256:272], cosT[:, tt, :], ALU.mult, r=[bkk, 'cosT'], w=['ra'])
        P.tt('dve', rb[:], bk[:, 272:288], sinT[:, tt, :], ALU.mult, r=[bkk, 'sinT'], w=['rb'])
        P.tt('dve', krb[:, 0:16], ra[:], rb[:], ALU.subtract, r=['ra', 'rb'], w=['krb'])
        P.tt('dve', ra[:], bk[:, 256:272], sinT[:, tt, :], ALU.mult, r=[bkk, 'sinT', 'krb'], w=['ra'])
        P.tt('dve', rb[:], bk[:, 272:288], cosT[:, tt, :], ALU.mult, r=[bkk, 'cosT', 'krb'], w=['rb'])
        P.tt('dve', krb[:, 16:32], ra[:], rb[:], ALU.add, r=['ra', 'rb'], w=['krb'])
        P.ts('dve', ckvn[:], bk[:, 0:256], rstdc[:, 0:1], None, ALU.mult, r=[bkk, 'rstdc'], w=['ckvn'])
        bk, bkk = nb()
        pv = bfv(bk)
        for kt in range(2):
            P.tr(pv[:, kt * 128:(kt + 1) * 128], ckvn[:, kt * 128:(kt + 1) * 128], identb[:], r=['ckvn', 'identb'], w=[bkk])
        P.tr(pv[0:32, 256:384], krb[:], identb[:], r=['krb', 'identb'], w=[bkk])
        P.cp('act', ckT[:], pv[:, 0:256].rearrange("p (k t) -> p k t", k=2), r=[bkk], w=['ckT'])
        P.cp('act', krT[sl][:], pv[0:32, 256:384], r=[bkk], w=[f'krT{sl}'])
        P.ld(kr_d[:, tok], krT[sl][:], w=[f'krd{sl}'], sem=f'stkr{sl}', r=[f'krT{sl}'])
        yield
        ring[0] = 1
        for half in range(2):
            bk, bkk = nb()
            for j in range(4):
                pr = half * 4 + j
                for kt in range(2):
                    P.mm(bk[:, j * 128:(j + 1) * 128], wkn[:, kt, pr * 128:(pr + 1) * 128], ckT[:, kt, :], start=(kt == 0), stop=(kt == 1),
                         r=['ckT', *WK['wkn']], w=[bkk])
            P.cp('act' if half == 0 else 'dve', knT[sl][:, half * 4:(half + 1) * 4, :], bk[:, :].rearrange("p (j t) -> p j t", j=4),
                 r=[bkk], w=[f'knT{sl}'])
        P.ld(kn_d[:, :, tok], knT[sl][:], w=[f'knd{sl}'], sem=f'stkn{sl}', r=[f'knT{sl}'])
        yield
        ring[0] = 1
        for half in range(2):
            bk, bkk = nb()
            for kt in range(2):
                P.mm(bk[:, :], ckT[:, kt, :], wv[:, kt, half * 512:(half + 1) * 512], start=(kt == 0), stop=(kt == 1),
                     r=['ckT', *WK['wv']], w=[bkk])
            P.cp('act' if half == 0 else 'dve', vsb[sl][:, half * 512:(half + 1) * 512], bk[:, :], r=[bkk], w=[f'vsb{sl}'])
        P.ld(v_d[tok, :], vsb[sl][:], w=[f'vd{sl}'], sem=f'stv{sl}', r=[f'vsb{sl}'])
        yield
        ring[0] = 1
        bk, bkk = nb()
        for kt in range(8):
            P.mm(bk[:, 0:384], hT[:, kt, :], win[:, kt, 0:384], start=(kt == 0), stop=(kt == 7), r=[hTk, *WK['win']], w=[bkk])
        P.actv(junk2[:], bk[:, 0:384], AF.Square, accum=ssq[:], r=[bkk], w=['junk2', 'ssq'])
        P.actv(rtq[:], ssq[:], AF.Sqrt, bias=EPS, scale=1.0 / 384, r=['ssq'], w=['rtq'])
        P.add('dve', lambda e: e.reciprocal(out=rstdq[:], in_=rtq[:]), r=['rtq'], w=['rstdq'])
        P.ts('dve', cqn[:], bk[:, 0:384], rstdq[:, 0:1], None, ALU.mult, r=[bkk, 'rstdq'], w=['cqn'])
        for half in range(2):
            bk, bkk = nb()
            for j in range(4):
                ct = half * 4 + j
                for kt in range(8):
                    P.mm(bk[:, j * 128:(j + 1) * 128], win[:, kt, 384 + ct * 128:384 + (ct + 1) * 128], hT[:, kt, :],
                         start=(kt == 0), stop=(kt == 7), r=[hTk, *WK['win']], w=[bkk])
            P.actv(sg[sl][:, half * 4:(half + 1) * 4, :], bk[:, :].rearrange("p (j t) -> p j t", j=4), AF.Silu, r=[bkk], w=[f'sg{sl}'])
        P.ld(sg_d[:, :, tok], sg[sl][:], w=[f'sgd{sl}'], sem=f'stsg{sl}', r=[f'sg{sl}'])
        bk, bkk = nb()
        pv = bfv(bk)
        for kt in range(3):
            P.tr(pv[:, kt * 128:(kt + 1) * 128], cqn[:, kt * 128:(kt + 1) * 128], identb[:], r=['cqn', 'identb'], w=[bkk])
        P.cp('act', cqT[:], pv[:, 0:384].rearrange("p (k t) -> p k t", k=3), r=[bkk], w=['cqT'])
        yield
        ring[0] = 1
        for blk in range(2):
            bk, bkk = nb()
            for kt in range(3):
                P.mm(bk[:, :], cqT[:, kt, :], wuq[:, kt, blk * 512:(blk + 1) * 512], start=(kt == 0), stop=(kt == 2),
                     r=['cqT', *WK['wuq']], w=[bkk])
            P.cp('act', qtok[:, blk * 8:(blk + 1) * 8, 0:64], bk[:, :].rearrange("p (h c) -> p h c", h=8), r=[bkk], w=['qtok'])
        bk, bkk = nb()
        for kt in range(3):
            P.mm(bk[:, :], cqT[:, kt, :], wuq[:, kt, 1024:1536], start=(kt == 0), stop=(kt == 2), r=['cqT', *WK['wuq']], w=[bkk])
        x1 = bk[:, 0:256].rearrange("p (h c) -> p h c", h=16)
        x2 = bk[:, 256:512].rearrange("p (h c) -> p h c", h=16)
        cb_ = cosT[:, tt, :].unsqueeze(1).to_broadcast([128, 16, 16])
        sb_ = sinT[:, tt, :].unsqueeze(1).to_broadcast([128, 16, 16])
        P.tt('dve', qa[:], x1, cb_, ALU.mult, r=[bkk, 'cosT'], w=['qa'])
        P.tt('dve', qb[:], x2, sb_, ALU.mult, r=[bkk, 'sinT'], w=['qb'])
        P.tt('dve', qtok[:, :, 64:80], qa[:], qb[:], ALU.subtract, r=['qa', 'qb'], w=['qtok'])
        P.tt('dve', qa[:], x1, sb_, ALU.mult, r=[bkk, 'sinT', 'qtok'], w=['qa'])
        P.tt('dve', qb[:], x2, cb_, ALU.mult, r=[bkk, 'cosT', 'qtok'], w=['qb'])
        P.tt('dve', qtok[:, :, 80:96], qa[:], qb[:], ALU.add, r=['qa', 'qb'], w=['qtok'])
        yield
        ring[0] = 1
        for half in range(2):
            bk, bkk = nb()
            pv = bfv(bk)[0:96, :].rearrange("p (h t) -> p h t", h=8)
            for j in range(8):
                P.tr(pv[:, j, :], qtok[:, half * 8 + j, :], identb[:], r=['qtok', 'identb'], w=[bkk])
            P.cp('act' if half == 0 else 'dve', qT[sl][:, half * 8:(half + 1) * 8, :], pv, r=[bkk], w=[f'qT{sl}'])
        P.ld(qT_d[:, :, tok], qT[sl][:], w=[f'qd{sl}'], sem=f'stq{sl}', r=[f'qT{sl}'])
        yield

    load_t(0)
    for it in range(ntt + 1):
        gens = []
        if it >= 1:
            gens.append(back(it - 1))
        if it < ntt:
            gens.append(front(it))
        while gens:
            for g in list(gens):
                try:
                    next(g)
                except StopIteration:
                    gens.remove(g)
    outk = []
    for sl in range(2):
        outk += [f'h1d{sl}', f'krd{sl}', f'knd{sl}', f'vd{sl}', f'sgd{sl}', f'qd{sl}']
    P.wait_all('sp', outk)
    P.emit()
    es.close()
    return nc
SCALE = 96.0 ** -0.5
LOOKAHEAD = 2


def build_stageC(nheads=2, nchunks=16):
    nc = bass.Bass("TRN2", target_bir_lowering=False)
    SQ = 8192
    LK = 8192
    NKT = LK // 128
    chunks = list(range(nchunks))
    kT_d = nc.dram_tensor("kT", [nheads, 96, 8192], BF16, kind="ExternalInput").ap()
    v_d = nc.dram_tensor("v", [nheads, 128, 64, 64], BF16, kind="ExternalInput").ap()
    qT_d = nc.dram_tensor("qT", [nheads, 96, SQ], BF16, kind="ExternalInput").ap()
    sg_d = nc.dram_tensor("sg", [nheads, 128, 64, 64], BF16, kind="ExternalInput").ap()
    og_d = nc.dram_tensor("og", [nheads, SQ, 64], BF16, kind="ExternalOutput").ap()

    P = P2(nc)
    es = contextlib.ExitStack()

    def S(name, shape, dt):
        return es.enter_context(nc.sbuf_tensor(name, shape, dt))

    banks = [es.enter_context(nc.psum_tensor(f"bank{i}", [128, 512], F32)) for i in range(8)]
    kT = [S(f"kT{i}", [96, LK], BF16) for i in range(2)]
    vh = [S(f"vh{i}", [128, NKT, 65], BF16) for i in range(2)]
    qh = [S(f"qh{i}", [96, SQ], BF16) for i in range(2)]
    sgh = [S(f"sgh{i}", [128, 64, 64], BF16) for i in range(2)]
    ogs = [S(f"ogs{i}", [128, 4, 64], BF16) for i in range(2)]
    rr = [S(f"rr{i}", [128, 4], F32) for i in range(2)]
    PT = [S(f"PT{i}", [128, 512], BF16) for i in range(4)]

    for i in range(2):
        P.ms('pool', vh[i][:, :, 64:65], 1.0, [f'vh{i}'])

    def load_head(h):
        i = h % 2
        half = LK // 2
        P.ld(kT[i][:, 0:half], kT_d[h, :, 0:half], [f'kT{i}a'], f'kT{i}a')
        P.ld(kT[i][:, half:LK], kT_d[h, :, half:LK], [f'kT{i}b'], f'kT{i}b', eng='act')
        P.ld(vh[i][:, :, 0:64], v_d[h, :, 0:NKT, :], [f'vh{i}'], f'vh{i}', eng='pool')
        P.ld(qh[i][:], qT_d[h, :, :], [f'qh{i}'], f'qh{i}')
        P.ld(sgh[i][:], sg_d[h, :, :, :], [f'sgh{i}'], f'sgh{i}')

    load_head(0)
    if nheads > 1:
        load_head(1)
    tiles = []
    cn = 0
    for h in range(nheads):
        for qi, cj in enumerate(chunks):
            nk = (cj + 1) * 4
            for kt in range(nk):
                d = kt - (nk - 4)
                c0 = 128 * d if d > 0 else 0
                tiles.append(dict(h=h, qi=qi, kt=kt, d=d, c0=c0, nk=nk, cn=cn, last_chunk=(qi == len(chunks) - 1)))
            cn += 1

    def emit_S(n, t):
        i = t['h'] % 2
        sb = n % 4
        ps = banks[sb]
        c0, kt, qi = t['c0'], t['kt'], t['qi']
        P.mm(ps[:, c0:512], kT[i][:, kt * 128:(kt + 1) * 128], qh[i][:, qi * 512 + c0:(qi + 1) * 512],
             r=[f'kT{i}a', f'kT{i}b', f'qh{i}'], w=[f'B{sb}'])
        P.actv(PT[sb][:, c0:512], ps[:, c0:512], AF.Exp, scale=SCALE, r=[f'B{sb}'], w=[f'PT{sb}'])
        if t['d'] >= 0:
            blk = PT[sb][:, c0:c0 + 128]
            P.add('pool', (lambda blk: (lambda e: e.affine_select(out=blk, in_=blk, pattern=[[1, 128]], compare_op=ALU.is_ge,
                                                                  fill=0.0, base=0, channel_multiplier=-1)))(blk),
                  r=[f'PT{sb}'], w=[f'PT{sb}'])

    def emit_PV(n, t):
        i = t['h'] % 2
        sb = n % 4
        par = t['cn'] % 2
        po = banks[4 + par]
        kt, d, nk = t['kt'], t['d'], t['nk']
        for qt in range(4):
            if d > qt:
                continue
            last = (kt == nk - 4 + qt)
            P.add('pe', (lambda po=po, sb=sb, qt=qt, kt=kt, i=i, last=last:
                         (lambda e: e.matmul(po[:, qt * 65:(qt + 1) * 65], lhsT=PT[sb][:, qt * 128:(qt + 1) * 128], rhs=vh[i][:, kt, :],
                                             start=(kt == 0 and qt == 0), stop=last, skip_group_check=True)))(),
                  r=[f'vh{i}', f'PT{sb}'], w=[f'B{4 + par}'])

    def epi1(t):
        i = t['h'] % 2
        par = t['cn'] % 2
        po = banks[4 + par]
        pv = po[:, 0:260].rearrange("p (q c) -> p q c", q=4)
        P.add('dve', lambda e, pv=pv, par=par: e.reciprocal(out=rr[par][:], in_=pv[:, :, 64]), r=[f'B{4 + par}'], w=[f'rr{par}'])
        for qt in range(4):
            tile_i = t['qi'] * 4 + qt
            P.stt(ogs[par][:, qt, :], pv[:, qt, 0:64], rr[par][:, qt:qt + 1], sgh[i][:, tile_i, :], ALU.mult, ALU.mult,
                  r=[f'B{4 + par}', f'rr{par}', f'sgh{i}'], w=[f'ogs{par}'])
        P.ld(og_d[t['h'], t['qi'] * 512:(t['qi'] + 1) * 512, :].rearrange("(q p) d -> p q d", p=128), ogs[par][:],
             w=[f'ogd{par}'], sem=f'sto{par}', r=[f'ogs{par}'])

    def epi2(t):
        if t['last_chunk'] and t['h'] + 2 < nheads:
            load_head(t['h'] + 2)

    LA = 3
    DEFER = 2
    sched = {}
    NT = len(tiles)
    for n in range(NT + LA):
        if n < NT:
            emit_S(n, tiles[n])
        for t in sched.pop(n, []):
            epi2(t)
        m = n - LA
        if m >= 0:
            t = tiles[m]
            emit_PV(m, t)
            if t['kt'] == t['nk'] - 1:
                epi1(t)
                sched.setdefault(n + DEFER, []).append(t)
    for k in sorted(sched):
        for t in sched[k]:
            epi2(t)
    P.wait_all('sp', ['ogd0', 'ogd1'])
    P.emit()
    es.close()
    return nc


def build_stageD():
    nc = bass.Bass("TRN2", target_bir_lowering=False)
    og_d = nc.dram_tensor("og", [128, 8, NTOK], BF16, kind="ExternalInput").ap()
    h1_d = nc.dram_tensor("h1", [NTOK, 1024], F32, kind="ExternalInput").ap()
    wo_d = nc.dram_tensor("wo", [1024, 1024], F32, kind="ExternalInput").ap()
    gf_d = nc.dram_tensor("gf", [1, 1024], F32, kind="ExternalInput").ap()
    out_d = nc.dram_tensor("out", [NTOK, 1024], F32, kind="ExternalOutput").ap()
    P = P2(nc)
    es = contextlib.ExitStack()

    def S(name, shape, dt):
        return es.enter_context(nc.sbuf_tensor(name, shape, dt))

    banks = [es.enter_context(nc.psum_tensor(f"bank{i}", [128, 512], F32)) for i in range(8)]
    ogT = S("ogT", [128, 8, NTOK], BF16)
    wo = S("wo_s", [128, 8, 1024], BF16)
    wst = [S(f"wst{i}", [128, 1024], F32) for i in range(2)]
    gf_bc = S("gf_bc", [128, 1024], F32)
    h1t = [S(f"h1t{i}", [128, 1024], F32) for i in range(2)]
    h2 = S("h2", [128, 1024], F32)
    junk = S("junk", [128, 1024], BF16)
    ss = S("ss", [128, 1], F32)
    rt = S("rt", [128, 1], F32)
    rstd = S("rstd", [128, 1], F32)
    outt = [S(f"outt{i}", [128, 1024], F32) for i in range(2)]
    P.ld(gf_bc[:], gf_d.partition_broadcast(128), ['gf_bc'], 'c0')
    for q in range(4):
        P.ld(ogT[:, :, q * 512:(q + 1) * 512], og_d[:, :, q * 512:(q + 1) * 512], [f'ogT{q}'], f'og{q}')
    wkeys = []
    for pr in range(8):
        i = pr % 2
        P.ld(wst[i][:], wo_d[pr * 128:(pr + 1) * 128, :], [f'wst{i}'], f'wst{i}')
        P.cp('dve' if i == 0 else 'act', wo[:, pr, :], wst[i][:], r=[f'wst{i}'], w=[f'wo{pr}'])
        wkeys.append(f'wo{pr}')
    def load_h1(tt):
        P.ld(h1t[tt % 2][:], h1_d[tt * 128:(tt + 1) * 128, :], [f'h1t{tt % 2}'], f'h1t{tt % 2}')

    load_h1(0)
    for tt in range(NTT):
        sl = tt % 2
        if tt + 1 < NTT:
            load_h1(tt + 1)
        for half in range(2):
            bk = banks[(tt % 2) * 2 + half]
            for pr in range(8):
                P.mm(bk[:, :], ogT[:, pr, tt * 128:(tt + 1) * 128], wo[:, pr, half * 512:(half + 1) * 512], start=(pr == 0), stop=(pr == 7),
                     r=[f'ogT{tt // 4}', wkeys[pr]], w=[f'B{(tt % 2) * 2 + half}'])
            P.tt('dve', h2[:, half * 512:(half + 1) * 512], bk[:, :], h1t[sl][:, half * 512:(half + 1) * 512], ALU.add,
                 r=[f'B{(tt % 2) * 2 + half}', f'h1t{sl}'], w=[f'h2_{half}'])
        P.actv(junk[:], h2[:], AF.Square, accum=ss[:], r=['h2_0', 'h2_1'], w=['junk', 'ss'])
        P.actv(rt[:], ss[:], AF.Sqrt, bias=EPS, scale=1.0 / 1024, r=['ss'], w=['rt'])
        P.add('dve', lambda e: e.reciprocal(out=rstd[:], in_=rt[:]), r=['rt'], w=['rstd'])
        P.stt(outt[sl][:], h2[:], rstd[:, 0:1], gf_bc[:], ALU.mult, ALU.mult, r=['h2_0', 'h2_1', 'rstd', 'gf_bc'], w=[f'outt{sl}'])
        P.ld(out_d[tt * 128:(tt + 1) * 128, :], outt[sl][:], w=[f'od{sl}'], sem=f'sto{sl}', r=[f'outt{sl}'])
    P.wait_all('sp', ['od0', 'od1'])
    P.emit()
    es.close()
    return nc


def _prepA(inp, b, g):
    w_in = inp['ssm_w_in'][0]
    w = np.concatenate([w_in[:, 2048 + g * 512:2048 + (g + 1) * 512], w_in[:, 4096 + g * 128:4096 + (g + 1) * 128],
                        w_in[:, 4608 + g * 128:4608 + (g + 1) * 128], w_in[:, 5120 + g * 8:5120 + (g + 1) * 8],
                        w_in[:, g * 512:(g + 1) * 512]], axis=1)
    cidx = np.concatenate([np.arange(g * 512, (g + 1) * 512), 2048 + np.arange(g * 128, (g + 1) * 128),
                           2560 + np.arange(g * 128, (g + 1) * 128)])
    cwc = inp['ssm_conv_w'][0][:, cidx]
    cw = cwc.T.reshape(6, 128, 4).transpose(1, 0, 2).reshape(128, 24)
    cb = inp['ssm_conv_b'][0][cidx].reshape(6, 128).T
    hs = slice(g * 8, (g + 1) * 8)
    C = np.ascontiguousarray
    return dict(x=C(inp['x'][b]), w=C(w), gpre=C(inp['g_pre'][0].reshape(8, 128).T), cw=C(cw), cb=C(cb),
                dtb=C(inp['ssm_dt_bias'][0][hs].reshape(1, 8)), alog=C(inp['ssm_A_log'][0][hs].reshape(1, 8)),
                dsk=C(inp['ssm_D'][0][hs].reshape(1, 8)), gout=C(inp['ssm_g_out'][0][g * 512:(g + 1) * 512].reshape(1, 512)))


def _prepB(inp, yn_b, b, j):
    C = np.ascontiguousarray
    tok = slice(j * NTOK, (j + 1) * NTOK)
    pos = np.asarray(inp['positions'][b][tok]).astype(np.int32).reshape(NTT, 128).T
    return dict(x=C(inp['x'][b][tok]), yn=C(yn_b[tok]), pos=C(pos), invf=np.array(INV_FREQ, dtype=np.float32).reshape(1, 16),
                wout=C(inp['ssm_w_out'][0]), wdn=C(inp['kv_w_down']), wup=C(inp['kv_w_up']), win=C(inp['mla_w_in'][0]),
                wuq=C(inp['mla_w_uq'][0]), gkv=C(inp['kv_g_in'].reshape(8, 128).T), gpre=C(inp['g_pre'][1].reshape(8, 128).T),
                glat=C(inp['kv_g_latent'].reshape(2, 128).T), gq=C(inp['mla_g_q'][0].reshape(3, 128).T))


def kernel(**inputs):
    inp = {k: np.asarray(v) for k, v in inputs.items()}
    C = np.ascontiguousarray
    cores = list(range(8))
    ncA = build_stageA()
    rA = run_bass_kernel_spmd(ncA, [_prepA(inp, c // 4, c % 4) for c in cores], core_ids=cores).results
    yn = [np.concatenate([rA[b * 4 + g]['yn'] for g in range(4)], axis=1) for b in range(2)]
    ncB = build_stageB()
    rB = run_bass_kernel_spmd(ncB, [_prepB(inp, yn[c // 4], c // 4, c % 4) for c in cores], core_ids=cores).results
    knf, krf, vff, qff, sff = [], [], [], [], []
    for b in range(2):
        knf.append(np.concatenate([rB[b * 4 + j]['kn'] for j in range(4)], axis=2))
        krf.append(np.concatenate([rB[b * 4 + j]['kr'] for j in range(4)], axis=1))
        vff.append(np.concatenate([rB[b * 4 + j]['v'] for j in range(4)], axis=0))
        qff.append(np.concatenate([rB[b * 4 + j]['qT'] for j in range(4)], axis=2))
        sff.append(np.concatenate([rB[b * 4 + j]['sg'] for j in range(4)], axis=2))
    ncC = build_stageC(2)
    og_heads = {}
    for part in range(2):
        imC = []
        for c in cores:
            b, hg = c // 4, c % 4
            kT = np.empty((2, 96, 8192), dtype=knf[b].dtype)
            v4 = np.empty((2, 128, 64, 64), dtype=vff[b].dtype)
            q4 = np.empty((2, 96, 8192), dtype=qff[b].dtype)
            s4 = np.empty((2, 128, 64, 64), dtype=sff[b].dtype)
            for hl in range(2):
                h = hg * 4 + part * 2 + hl
                kT[hl, 0:64] = knf[b][(h % 2) * 64:(h % 2) * 64 + 64, h // 2, :]
                kT[hl, 64:96] = krf[b]
                v4[hl] = vff[b][:, h * 64:(h + 1) * 64].reshape(64, 128, 64).transpose(1, 0, 2)
                q4[hl] = qff[b][:, h, :]
                s4[hl] = sff[b][(h % 2) * 64:(h % 2) * 64 + 64, h // 2, :].T.reshape(64, 128, 64).transpose(1, 0, 2)
            imC.append(dict(kT=kT, v=v4, qT=q4, sg=s4))
        rC = run_bass_kernel_spmd(ncC, imC, core_ids=cores).results
        for c in cores:
            b, hg = c // 4, c % 4
            for hl in range(2):
                og_heads[(b, hg * 4 + part * 2 + hl)] = rC[c]['og'][hl]
    imD = []
    for c in cores:
        b, j = c // 4, c % 4
        tok = slice(j * NTOK, (j + 1) * NTOK)
        og = np.empty((128, 8, NTOK), dtype=og_heads[(0, 0)].dtype)
        for h in range(16):
            og[(h % 2) * 64:(h % 2) * 64 + 64, h // 2, :] = og_heads[(b, h)][tok].T
        imD.append(dict(og=og, h1=rB[c]['h1'], wo=C(inp['mla_w_out'][0]), gf=C(inp['g_final'].reshape(1, 1024))))
    ncD = build_stageD()
    rD = run_bass_kernel_spmd(ncD, imD, core_ids=cores).results
    out = np.stack([np.concatenate([rD[b * 4 + j]['out'] for j in range(4)], axis=0) for b in range(2)], axis=0)
    return out.astype(np.float32)
```

```python
import contextlib
import math
from concourse.bass_utils import run_bass_kernel_spmd
import numpy as np
import concourse.bass as bass
import concourse.mybir as mybir

F32 = mybir.dt.float32
BF16 = mybir.dt.bfloat16
I32 = mybir.dt.int32
AF = mybir.ActivationFunctionType
ALU = mybir.AluOpType
AX = mybir.AxisListType


class Prog:
    def __init__(self, nc):
        self.nc = nc
        self.ops = []
        self.lastw = {}
        self.readers = {}
        self.dma_sems = {}

    def add(self, eng, fn, r=(), w=(), dma=None, group=False):
        deps = set()
        for k in r:
            if k in self.lastw:
                deps.add(self.lastw[k])
            if k[0] == 'B' and k[1:].isdigit():
                for j in self.readers.get(k, ()):
                    if self.ops[j]['eng'] != eng:
                        deps.add(j)
        for k in w:
            if k in self.lastw:
                deps.add(self.lastw[k])
            deps.update(self.readers.get(k, ()))
        i = len(self.ops)
        self.ops.append(dict(eng=eng, fn=fn, deps=deps, dma=dma, group=group, has_dep=False))
        for k in r:
            self.readers.setdefault(k, []).append(i)
        for k in w:
            self.lastw[k] = i
            self.readers[k] = []
        return i

    def pe(self, fn, r=(), w=()):
        return self.add('pe', fn, r, w)

    def act(self, fn, r=(), w=()):
        return self.add('act', fn, r, w)

    def dve(self, fn, r=(), w=()):
        return self.add('dve', fn, r, w)

    def pool(self, fn, r=(), w=()):
        return self.add('pool', fn, r, w)

    def dma(self, eng, fn, r=(), w=(), sem=None, group=False):
        assert sem is not None
        return self.add(eng, fn, r, w, dma=sem, group=group)

    def wait_all(self, eng, keys):
        return self.add(eng, None, r=keys, w=())

    def emit(self):
        nc = self.nc
        ops = self.ops
        engs = ['sp', 'act', 'dve', 'pool', 'pe']
        for o in ops:
            for d in o['deps']:
                if ops[d]['eng'] == 'pe' and o['eng'] == 'pe' and ops[d]['dma'] is None and o['dma'] is None:
                    continue
                ops[d]['has_dep'] = True
        esem = {e: nc.alloc_semaphore(name=f"s_{e}") for e in engs}
        group_tot = {}
        for o in ops:
            if o['dma'] is not None:
                if o['dma'] not in self.dma_sems:
                    self.dma_sems[o['dma']] = nc.alloc_semaphore(name=f"d_{o['dma']}")
                group_tot[o['dma']] = group_tot.get(o['dma'], 0) + 1
        cnt = {e: 0 for e in engs}
        dcnt = {}
        for o in ops:
            if o['fn'] is None:
                o['tok'] = None
            elif o['dma'] is not None:
                k = o['dma']
                dcnt[k] = dcnt.get(k, 0) + 1
                v = group_tot[k] if o['group'] else dcnt[k]
                o['tok'] = (('d', k), 16 * v)
            elif o['has_dep']:
                cnt[o['eng']] += 1
                o['tok'] = (('e', o['eng']), cnt[o['eng']])
            else:
                o['tok'] = None
        known = {e: {} for e in engs}
        for o in ops:
            e = o['eng']
            kn = known[e]
            waits = []
            for d in sorted(o['deps'], reverse=True):
                od = ops[d]
                if od['tok'] is None:
                    continue
                if od['eng'] == 'pe' and e == 'pe' and od['dma'] is None and o['dma'] is None:
                    continue
                s, v = od['tok']
                if kn.get(s, 0) < v:
                    waits.append((s, v))
                    kn[s] = v
                    for s2, v2 in od['clock'].items():
                        if kn.get(s2, 0) < v2:
                            kn[s2] = v2
            wm = {}
            for s, v in waits:
                wm[s] = max(wm.get(s, 0), v)
            o['waits'] = wm
            o['clock'] = dict(kn)

        def semof(s):
            return esem[s[1]] if s[0] == 'e' else self.dma_sems[s[1]]

        def run(ename, eng):
            for o in ops:
                if o['eng'] != ename:
                    continue
                for s, v in o['waits'].items():
                    eng.wait_ge(semof(s), v)
                if o['fn'] is None:
                    continue
                inst = o['fn'](eng)
                if o['tok'] is not None:
                    s, v = o['tok']
                    inst.then_inc(semof(s), 16 if s[0] == 'd' else 1)

        with nc.Block() as block:
            @block.sync
            def _(e):
                run('sp', e)

            @block.scalar
            def _(e):
                run('act', e)

            @block.vector
            def _(e):
                run('dve', e)

            @block.gpsimd
            def _(e):
                run('pool', e)

            @block.tensor
            def _(e):
                run('pe', e)
        n = {e: sum(1 for o in ops if o['eng'] == e) for e in engs}
        nw = sum(len(o['waits']) for o in ops)
        print("PROG ops", n, "waits", nw, "sems", 5 + len(self.dma_sems), flush=True)


def _kw(**k):
    return {a: b for a, b in k.items() if b is not None}


class P2(Prog):
    def mm(self, out, lhsT, rhs, start=True, stop=True, r=(), w=()):
        return self.add('pe', lambda e: e.matmul(out, lhsT=lhsT, rhs=rhs, start=start, stop=stop), r, w)

    def tr(self, out, in_, ident, r=(), w=()):
        return self.add('pe', lambda e: e.transpose(out, in_, ident), r, w)

    def actv(self, out, in_, func, bias=None, scale=None, accum=None, r=(), w=()):
        kw = _kw(bias=bias, scale=scale, accum_out=accum)
        return self.add('act', lambda e: e.activation(out=out, in_=in_, func=func, **kw), r, w)

    def ts(self, eng, out, in0, s1, s2=None, op0=ALU.mult, op1=None, r=(), w=()):
        kw = _kw(op1=op1)
        return self.add(eng, lambda e: e.tensor_scalar(out=out, in0=in0, scalar1=s1, scalar2=s2, op0=op0, **kw), r, w)

    def tt(self, eng, out, in0, in1, op, r=(), w=()):
        return self.add(eng, lambda e: e.tensor_tensor(out=out, in0=in0, in1=in1, op=op), r, w)

    def stt(self, out, in0, scalar, in1, op0, op1, r=(), w=()):
        return self.add('dve', lambda e: e.scalar_tensor_tensor(out=out, in0=in0, scalar=scalar, in1=in1, op0=op0, op1=op1), r, w)

    def cp(self, eng, out, in_, r=(), w=()):
        if eng == 'act':
            return self.add('act', lambda e: e.activation(out=out, in_=in_, func=AF.Copy), r, w)
        return self.add(eng, lambda e: e.tensor_copy(out=out, in_=in_), r, w)

    def ms(self, eng, ap, val, w=()):
        return self.add(eng, lambda e: e.memset(ap, val), (), w)

    def ld(self, out, in_, w, sem, eng='sp', group=False, r=()):
        return self.dma(eng, lambda e: e.dma_start(out=out, in_=in_), r=r, w=w, sem=sem, group=group)

SEQ = 8192
DM = 1024
NCH = SEQ // 256
EPS = 1e-6
WCOLS = 1288


def build_stageA(nch=NCH):
    nc = bass.Bass("TRN2", target_bir_lowering=False)
    x_d = nc.dram_tensor("x", [SEQ, DM], F32, kind="ExternalInput").ap()
    w_d = nc.dram_tensor("w", [DM, WCOLS], F32, kind="ExternalInput").ap()
    gpre_d = nc.dram_tensor("gpre", [128, 8], F32, kind="ExternalInput").ap()
    cw_d = nc.dram_tensor("cw", [128, 24], F32, kind="ExternalInput").ap()
    cb_d = nc.dram_tensor("cb", [128, 6], F32, kind="ExternalInput").ap()
    dtb_d = nc.dram_tensor("dtb", [1, 8], F32, kind="ExternalInput").ap()
    alog_d = nc.dram_tensor("alog", [1, 8], F32, kind="ExternalInput").ap()
    dsk_d = nc.dram_tensor("dsk", [1, 8], F32, kind="ExternalInput").ap()
    gout_d = nc.dram_tensor("gout", [1, 512], F32, kind="ExternalInput").ap()
    yn_d = nc.dram_tensor("yn", [SEQ, 512], BF16, kind="ExternalOutput").ap()

    P = P2(nc)
    es = contextlib.ExitStack()

    def S(name, shape, dt):
        return es.enter_context(nc.sbuf_tensor(name, shape, dt))

    banks = [es.enter_context(nc.psum_tensor(f"bank{i}", [128, 512], F32)) for i in range(8)]

    W = S("W", [128, 8, WCOLS], BF16)
    wst = [S(f"wst{i}", [128, WCOLS], F32) for i in range(2)]
    gpre = S("gpre_s", [128, 8], F32)
    cw = S("cw_s", [128, 24], F32)
    cb = S("cb_s", [128, 6], F32)
    dtb_bc = S("dtb_bc", [128, 8], F32)
    A_bc = S("A_bc", [128, 8], F32)
    D_bc = S("D_bc", [128, 8], F32)
    gout_bc = S("gout_bc", [128, 512], F32)
    identf = S("identf", [128, 128], F32)
    identb = S("identb", [128, 128], BF16)
    onesf = S("onesf", [128, 128], F32)
    onesb = S("onesb", [128, 128], BF16)
    trif = S("trif", [128, 128], F32)
    triw = S("triw", [128, 256], BF16)
    SU = S("SU", [128, 128], BF16)
    cdiag = S("cdiag", [128, 24, 128], BF16)
    Dident = S("Dident", [128, 8, 128], BF16)
    xin = [S(f"xin{i}", [128, 2, DM], F32) for i in range(2)]
    junk = [S(f"junk{i}", [128, DM], BF16) for i in range(2)]
    ss = S("ss", [128, 2], F32)
    rt = S("rt", [128, 2], F32)
    rstd = S("rstd", [128, 2], F32)
    hn = S("hn", [128, 2, DM], BF16)
    hnT = S("hnT", [128, 8, 256], BF16)
    ubuf = S("ubuf", [128, 6, 259], BF16)
    xc = [S(f"xc{i}", [128, 6, 256], BF16) for i in range(2)]
    xtok = S("xtok", [128, 2, 640], BF16)
    dtr = S("dtr", [128, 2, 8], F32)
    e1 = S("e1", [128, 2, 8], F32)
    dtk = [S(f"dtk{i}", [128, 2, 8], F32) for i in range(2)]
    dtA = [S(f"dtA{i}", [128, 2, 8], F32) for i in range(2)]
    cend = S("cend", [128, 8], F32)
    ecum = [S(f"ecum{i}", [128, 2, 8], F32) for i in range(2)]
    wtmp = [S(f"wtmp{i}", [128, 2, 8], F32) for i in range(2)]
    dec = [S(f"dec{i}", [128, 8], F32) for i in range(2)]
    W0 = S("W0", [128, 8, 256], BF16)
    V1 = S("V1", [128, 8, 128], BF16)
    CBm = S("CBm", [128, 384], BF16)
    xdt = S("xdt", [128, 2, 512], BF16)
    Lb = [S(f"Lb{i}", [128, 384], BF16) for i in range(2)]
    junk2b = S("junk2b", [128, 512], BF16)
    MT = S("MT", [128, 8, 384], BF16)
    state = S("state", [128, 512], F32)
    state_bf = S("state_bf", [128, 512], BF16)
    yi = S("yi", [128, 512], F32)
    t1 = S("t1", [128, 512], F32)
    ysb = S("ysb", [128, 512], F32)
    zs = [S(f"zs{i}", [128, 2, 512], F32) for i in range(2)]
    yg = [S(f"yg{i}", [128, 512], F32) for i in range(2)]
    junk2 = S("junk2", [128, 512], BF16)
    ss2 = S("ss2", [128, 2], F32)
    rt2 = S("rt2", [128, 2], F32)
    rstd2 = S("rstd2", [128, 2], F32)
    yn = [S(f"yn{i}", [128, 512], BF16) for i in range(2)]
    wx = S("wx", [128, 2, 512], BF16)

    def bfview(bank):
        return bank[:].bitcast(BF16)

    ptr = bfview(banks[0]).rearrange("p (k t) -> p k t", k=8)
    pX = banks[1]
    pCv = banks[2]
    pdtk = banks[3][:, 0:16].rearrange("p (t c) -> p t c", t=2)
    pcum = banks[3][:, 16:32].rearrange("p (t c) -> p t c", t=2)
    pce = banks[3][:, 32:40]
    pz = banks[3][:, :]
    ptx = bfview(banks[4])[:, 0:640]
    pCB = banks[4][:, 0:384]
    pseg = [banks[5][:, 0:384], banks[7][:, 0:384]]
    py = banks[6][:, :]
    pyi = banks[4][:, :]
    pst = banks[4][:, :]

    P.ld(gpre[:], gpre_d, ['gpre'], 'c0')
    P.ld(cw[:], cw_d, ['cw'], 'c1')
    P.ld(cb[:], cb_d, ['cb'], 'c2')
    P.ld(dtb_bc[:], dtb_d.partition_broadcast(128), ['dtb_bc'], 'c3')
    P.ld(A_bc[:], alog_d.partition_broadcast(128), ['A_bc'], 'c4')
    P.ld(D_bc[:], dsk_d.partition_broadcast(128), ['D_bc'], 'c5')
    P.ld(gout_bc[:], gout_d.partition_broadcast(128), ['gout_bc'], 'c6')
    P.ms('pool', identf[:], 1.0, ['identf'])
    P.add('pool', lambda e: e.affine_select(out=identf[:], in_=identf[:], pattern=[[-1, 128]], compare_op=ALU.is_equal,
                                            fill=0.0, base=0, channel_multiplier=1), r=['identf'], w=['identf'])
    P.cp('dve', identb[:], identf[:], r=['identf'], w=['identb'])
    P.ms('pool', onesf[:], 1.0, ['onesf'])
    P.ms('pool', onesb[:], 1.0, ['onesb'])
    P.ms('pool', triw[:], 1.0, ['triw'])
    P.add('pool', lambda e: e.affine_select(out=triw[:, 0:128], in_=triw[:, 0:128], pattern=[[1, 128]], compare_op=ALU.is_ge,
                                            fill=0.0, base=0, channel_multiplier=-1), r=['triw'], w=['triw'])
    P.cp('dve', trif[:], triw[:, 0:128], r=['triw'], w=['trif'])
    P.ms('pool', SU[:], 1.0, ['SU'])
    P.add('pool', lambda e: e.affine_select(out=SU[:], in_=SU[:], pattern=[[-1, 128]], compare_op=ALU.is_gt,
                                            fill=0.0, base=0, channel_multiplier=1), r=['SU'], w=['SU'])
    P.ms('pool', ubuf[:], 0.0, ['ubuf%d' % i for i in range(3)])
    P.ms('pool', state[:], 0.0, ['state'])
    P.ms('pool', state_bf[:], 0.0, ['state_bf'])
    for kt in range(8):
        P.ld(wst[kt % 2][:], w_d[kt * 128:(kt + 1) * 128, :], [f'wst{kt % 2}'], f'wst{kt % 2}')
        if kt % 2 == 0:
            P.ts('dve', W[:, kt, :], wst[kt % 2][:], gpre[:, kt:kt + 1], None, ALU.mult, r=[f'wst{kt % 2}', 'gpre'], w=[f'W{kt}'])
        else:
            P.actv(W[:, kt, :], wst[kt % 2][:], AF.Copy, scale=gpre[:, kt:kt + 1], r=[f'wst{kt % 2}', 'gpre'], w=[f'W{kt}'])
    Wk = [f'W{kt}' for kt in range(8)]
    HNT = ['hnT0', 'hnT1']
    for i in range(24):
        P.ts('dve', cdiag[:, i, :], identf[:], cw[:, i:i + 1], None, ALU.mult, r=['identf', 'cw'], w=['cdiag'])
    for h in range(8):
        P.ts('dve', Dident[:, h, :], identf[:], D_bc[:, h:h + 1], None, ALU.mult, r=['identf', 'D_bc'], w=['Dident'])
    P.actv(A_bc[:], A_bc[:], AF.Exp, r=['A_bc'], w=['A_bc'])
    P.ts('dve', A_bc[:], A_bc[:], -1.0, None, ALU.mult, r=['A_bc'], w=['A_bc'])

    def load_x(c):
        sl = c % 2
        P.ld(xin[sl][:], x_d[c * 256:(c + 1) * 256, :].rearrange("(t p) d -> p t d", p=128), [f'xin{sl}'], f'xin{sl}')

    def front(c):
        sl = c % 2
        p = c % 2
        xk = f'xin{sl}'
        if c + 1 < nch:
            load_x(c + 1)
        for t in range(2):
            P.actv(junk[t][:], xin[sl][:, t, :], AF.Square, accum=ss[:, t:t + 1], r=[xk], w=[f'ss{t}', f'junk{t}'])
        P.actv(rt[:], ss[:], AF.Ln, bias=EPS, scale=1.0 / DM, r=['ss0', 'ss1'], w=['rt'])
        P.actv(rstd[:], rt[:], AF.Exp, scale=-0.5, r=['rt'], w=['rstd'])
        for t in range(2):
            P.ts('dve', hn[:, t, :], xin[sl][:, t, :], rstd[:, t:t + 1], None, ALU.mult, r=[xk, 'rstd'], w=[f'hn{t}'])
        yield
        for t in range(2):
            for kt in range(8):
                P.tr(ptr[:, kt, :], hn[:, t, kt * 128:(kt + 1) * 128], identb[:], r=[f'hn{t}', 'identb'], w=['B0'])
            P.cp('dve' if t == 0 else 'act', hnT[:, :, t * 128:(t + 1) * 128], ptr, r=['B0'], w=[f'hnT{t}'])
            yield
        for t in range(2):
            for kt in range(8):
                P.mm(pdtk[:, t, :], hnT[:, kt, t * 128:(t + 1) * 128], W[:, kt, 768:776], start=(kt == 0), stop=(kt == 7),
                     r=HNT + [Wk[kt]], w=['B3'])
        P.tt('dve', dtr[:], pdtk, dtb_bc[:].unsqueeze(1).to_broadcast([128, 2, 8]), ALU.add, r=['B3', 'dtb_bc'], w=['dtr'])
        P.actv(e1[:], dtr[:], AF.Exp, r=['dtr'], w=['e1'])
        P.actv(dtk[p][:], e1[:], AF.Ln, bias=1.0, r=['e1'], w=[f'dtk{p}'])
        P.tt('dve', dtA[p][:], dtk[p][:], A_bc[:].unsqueeze(1).to_broadcast([128, 2, 8]), ALU.mult, r=[f'dtk{p}', 'A_bc'], w=[f'dtA{p}'])
        P.mm(pcum[:, 0, :], trif[:], dtA[p][:, 0, :], r=['trif', f'dtA{p}'], w=['B3'])
        P.mm(pcum[:, 1, :], onesf[:], dtA[p][:, 0, :], start=True, stop=False, r=['onesf', f'dtA{p}'], w=['B3'])
        P.mm(pcum[:, 1, :], trif[:], dtA[p][:, 1, :], start=False, stop=True, r=['trif', f'dtA{p}'], w=['B3'])
        P.mm(pce, onesf[:], dtA[p][:, 0, :], start=True, stop=False, r=['onesf', f'dtA{p}'], w=['B3'])
        P.mm(pce, onesf[:], dtA[p][:, 1, :], start=False, stop=True, r=['onesf', f'dtA{p}'], w=['B3'])
        P.actv(ecum[p][:], pcum, AF.Exp, r=['B3'], w=[f'ecum{p}'])
        P.actv(dec[p][:], pce, AF.Exp, r=['B3'], w=[f'dec{p}'])
        P.cp('act', cend[:], pce, r=['B3'], w=['cend'])
        P.tt('dve', wtmp[p][:], cend[:].unsqueeze(1).to_broadcast([128, 2, 8]), pcum, ALU.subtract, r=['cend', 'B3'], w=[f'wtmp{p}'])
        P.actv(wtmp[p][:], wtmp[p][:], AF.Exp, r=[f'wtmp{p}'], w=[f'wtmp{p}'])
        yield
        for pr in range(3):
            for j in range(2):
                ct = 2 * pr + j
                for kt in range(8):
                    P.mm(pX[:, j * 256:(j + 1) * 256], W[:, kt, ct * 128:(ct + 1) * 128], hnT[:, kt, :], start=(kt == 0), stop=(kt == 7),
                         r=HNT + [Wk[kt]], w=['B1'])
            P.cp('dve', ubuf[:, 2 * pr:2 * pr + 2, 3:259], pX[:, :].rearrange("p (j t) -> p j t", j=2), r=['B1'], w=[f'ubuf{pr}'])
            for j in range(2):
                ct = 2 * pr + j
                for k in range(4):
                    P.mm(pCv[:, j * 256:(j + 1) * 256], cdiag[:, ct * 4 + k, :], ubuf[:, ct, k:k + 256], start=(k == 0), stop=(k == 3),
                         r=['cdiag', f'ubuf{pr}'], w=['B2'])
            for j in range(2):
                ct = 2 * pr + j
                P.actv(xc[p][:, ct, :], pCv[:, j * 256:(j + 1) * 256], AF.Silu, bias=cb[:, ct:ct + 1], r=['B2', 'cb'], w=[f'xc{p}_{ct}'])
            P.cp('pool', ubuf[:, 2 * pr:2 * pr + 2, 0:3], ubuf[:, 2 * pr:2 * pr + 2, 256:259], r=[f'ubuf{pr}'], w=[f'ubuf{pr}'])
            yield
        for t in range(2):
            for kt in range(8):
                P.mm(pz, hnT[:, kt, t * 128:(t + 1) * 128], W[:, kt, 776:1288], start=(kt == 0), stop=(kt == 7),
                     r=HNT + [Wk[kt]], w=['B3'])
            P.actv(zs[p][:, t, :], pz, AF.Silu, r=['B3'], w=[f'zs{p}_{t}'])
            yield
    def back(c):
        p = c % 2
        XC = [f'xc{p}_{ct}' for ct in range(6)]
        P.tt('dve', W0[:], triw[:].unsqueeze(1).to_broadcast([128, 8, 256]), dtA[p][:, 0, :].unsqueeze(2).to_broadcast([128, 8, 256]),
             ALU.mult, r=['triw', f'dtA{p}'], w=['W0'])
        P.tt('dve', V1[:], triw[:, 0:128].unsqueeze(1).to_broadcast([128, 8, 128]), dtA[p][:, 1, :].unsqueeze(2).to_broadcast([128, 8, 128]),
             ALU.mult, r=['triw', f'dtA{p}'], w=['V1'])
        for t in range(2):
            for ct in range(5):
                P.tr(ptx[:, ct * 128:(ct + 1) * 128], xc[p][:, ct, t * 128:(t + 1) * 128], identb[:], r=[XC[ct], 'identb'], w=['B4'])
            P.cp('dve' if t == 0 else 'act', xtok[:, t, :], ptx, r=['B4'], w=[f'xtok{t}'])
            P.tt('pool', xdt[:, t, :].rearrange("p (h c) -> p h c", h=8), xtok[:, t, 0:512].rearrange("p (h c) -> p h c", h=8),
                 dtk[p][:, t, :].unsqueeze(2).to_broadcast([128, 8, 64]), ALU.mult, r=[f'xtok{t}', f'dtk{p}'], w=[f'xdt{t}'])
        yield
        P.mm(pCB[:, 0:256], xc[p][:, 4, 0:128], xc[p][:, 5, 0:256], r=[XC[4], XC[5]], w=['B4'])
        P.mm(pCB[:, 256:384], xc[p][:, 4, 128:256], xc[p][:, 5, 128:256], r=[XC[4], XC[5]], w=['B4'])
        P.cp('act', CBm[:], pCB, r=['B4'], w=['CBm'])
        for off in (0, 256):
            blk = CBm[:, off:off + 128]
            P.add('pool', (lambda blk: (lambda e: e.affine_select(out=blk, in_=blk, pattern=[[1, 128]], compare_op=ALU.is_ge,
                                                                  fill=0.0, base=0, channel_multiplier=-1)))(blk),
                  r=['CBm'], w=['CBm'])
        yield
        for h in range(8):
            L = Lb[h % 2]
            Lk = f'Lb{h % 2}'
            ps = pseg[h % 2]
            psk = 'B5' if h % 2 == 0 else 'B7'
            P.mm(ps[:, 0:128], SU[:], W0[:, h, 0:128], start=True, stop=True, r=['SU', 'W0'], w=[psk])
            P.mm(ps[:, 128:256], SU[:], W0[:, h, 128:256], start=True, stop=False, r=['SU', 'W0'], w=[psk])
            P.mm(ps[:, 128:256], onesb[:], V1[:, h, :], start=False, stop=True, r=['onesb', 'V1'], w=[psk])
            P.mm(ps[:, 256:384], SU[:], V1[:, h, :], start=True, stop=True, r=['SU', 'V1'], w=[psk])
            P.actv(L[:], ps, AF.Exp, r=[psk], w=[Lk])
            P.tt('dve', MT[:, h, :], L[:], CBm[:], ALU.mult, r=[Lk, 'CBm'], w=[f'MT{h}'])
            if h % 4 == 3:
                yield
        for t in range(2):
            for h in range(8):
                hc = slice(h * 64, (h + 1) * 64)
                P.mm(py[:, hc], MT[:, h, t * 128:(t + 1) * 128], xdt[:, 0, hc], start=True, stop=False, r=[f'MT{h}', 'xdt0'], w=['B6'])
                if t == 1:
                    P.mm(py[:, hc], MT[:, h, 256:384], xdt[:, 1, hc], start=False, stop=False, r=[f'MT{h}', 'xdt1'], w=['B6'])
                P.mm(py[:, hc], Dident[:, h, :], xtok[:, t, hc], start=False, stop=True, r=['Dident', f'xtok{t}'], w=['B6'])
            P.mm(pyi, xc[p][:, 5, t * 128:(t + 1) * 128], state_bf[:], r=[XC[5], 'state_bf'], w=['B4'])
            P.tt('dve', t1[:].rearrange("p (h c) -> p h c", h=8), pyi.rearrange("p (h c) -> p h c", h=8),
                 ecum[p][:, t, :].unsqueeze(2).to_broadcast([128, 8, 64]), ALU.mult, r=['B4', f'ecum{p}'], w=['t1'])
            P.tt('dve', ysb[:], t1[:], py, ALU.add, r=['t1', 'B6'], w=['ysb'])
            P.tt('dve', yg[t][:], ysb[:], zs[p][:, t, :], ALU.mult, r=['ysb', f'zs{p}_{t}'], w=[f'yg{t}'])
            P.actv(junk2b[:], yg[t][:], AF.Square, accum=ss2[:, t:t + 1], r=[f'yg{t}'], w=[f'ss2_{t}', 'junk2b'])
            yield
        yield
        yield
        P.actv(rt2[:], ss2[:], AF.Ln, bias=EPS, scale=1.0 / 512, r=['ss2_0', 'ss2_1'], w=['rt2'])
        P.actv(rstd2[:], rt2[:], AF.Exp, scale=-0.5, r=['rt2'], w=['rstd2'])
        for t in range(2):
            P.stt(yn[t][:], yg[t][:], rstd2[:, t:t + 1], gout_bc[:], ALU.mult, ALU.mult, r=[f'yg{t}', 'rstd2', 'gout_bc'], w=[f'yn{t}'])
            P.ld(yn_d[c * 256 + t * 128: c * 256 + (t + 1) * 128, :], yn[t][:], w=[f'ynd{t}'], sem=f'st{t}', r=[f'yn{t}'])
        for st in range(2):
            P.tt('pool', wx[:, st, :].rearrange("p (h c) -> p h c", h=8), xdt[:, st, :].rearrange("p (h c) -> p h c", h=8),
                 wtmp[p][:, st, :].unsqueeze(2).to_broadcast([128, 8, 64]), ALU.mult, r=[f'xdt{st}', f'wtmp{p}'], w=[f'wx{st}'])
        for st in range(2):
            P.mm(pst, xtok[:, st, 512:640], wx[:, st, :], start=(st == 0), stop=(st == 1), r=[f'xtok{st}', f'wx{st}'], w=['B4'])
        P.tt('dve', state[:].rearrange("p (h c) -> p h c", h=8), state[:].rearrange("p (h c) -> p h c", h=8),
             dec[p][:].unsqueeze(2).to_broadcast([128, 8, 64]), ALU.mult, r=['state', f'dec{p}'], w=['state'])
        P.tt('dve', state[:], state[:], pst, ALU.add, r=['state', 'B4'], w=['state'])
        P.cp('act', state_bf[:], state[:], r=['state'], w=['state_bf'])
        yield

    load_x(0)
    for it in range(nch + 1):
        gens = []
        if it >= 1:
            gens.append(back(it - 1))
        if it < nch:
            gens.append(front(it))
        while gens:
            for g in list(gens):
                try:
                    next(g)
                except StopIteration:
                    gens.remove(g)
    P.wait_all('sp', ['ynd0', 'ynd1'])
    P.emit()
    es.close()
    return nc

NTOK = 2048
NTT = NTOK // 128
INV_FREQ = [float(np.float32(10000.0) ** np.float32(-(2 * i) / 32.0)) for i in range(16)]
TWO_PI = 2.0 * math.pi
CW1 = 6.28125
CW2 = TWO_PI - CW1


def build_stageB(ntt=NTT):
    nc = bass.Bass("TRN2", target_bir_lowering=False)
    x_d = nc.dram_tensor("x", [NTOK, 1024], F32, kind="ExternalInput").ap()
    yn_d = nc.dram_tensor("yn", [NTOK, 2048], BF16, kind="ExternalInput").ap()
    pos_d = nc.dram_tensor("pos", [128, NTT], I32, kind="ExternalInput").ap()
    invf_d = nc.dram_tensor("invf", [1, 16], F32, kind="ExternalInput").ap()
    wout_d = nc.dram_tensor("wout", [2048, 1024], F32, kind="ExternalInput").ap()
    wdn_d = nc.dram_tensor("wdn", [1024, 288], F32, kind="ExternalInput").ap()
    wup_d = nc.dram_tensor("wup", [256, 2048], F32, kind="ExternalInput").ap()
    win_d = nc.dram_tensor("win", [1024, 1408], F32, kind="ExternalInput").ap()
    wuq_d = nc.dram_tensor("wuq", [384, 1536], F32, kind="ExternalInput").ap()
    gkv_d = nc.dram_tensor("gkv", [128, 8], F32, kind="ExternalInput").ap()
    gpre_d = nc.dram_tensor("gpre", [128, 8], F32, kind="ExternalInput").ap()
    glat_d = nc.dram_tensor("glat", [128, 2], F32, kind="ExternalInput").ap()
    gq_d = nc.dram_tensor("gq", [128, 3], F32, kind="ExternalInput").ap()
    h1_d = nc.dram_tensor("h1", [NTOK, 1024], F32, kind="ExternalOutput").ap()
    sg_d = nc.dram_tensor("sg", [128, 8, NTOK], BF16, kind="ExternalOutput").ap()
    kn_d = nc.dram_tensor("kn", [128, 8, NTOK], BF16, kind="ExternalOutput").ap()
    kr_d = nc.dram_tensor("kr", [32, NTOK], BF16, kind="ExternalOutput").ap()
    v_d = nc.dram_tensor("v", [NTOK, 1024], BF16, kind="ExternalOutput").ap()
    qT_d = nc.dram_tensor("qT", [96, 16, NTOK], BF16, kind="ExternalOutput").ap()

    P = P2(nc)
    es = contextlib.ExitStack()

    def S(name, shape, dt):
        return es.enter_context(nc.sbuf_tensor(name, shape, dt))

    banks = [es.enter_context(nc.psum_tensor(f"bank{i}", [128, 512], F32)) for i in range(8)]
    bctr = [0, 0]
    ring = [0]

    def nb():
        r = ring[0]
        i = r * 4 + bctr[r] % 4
        bctr[r] += 1
        return banks[i], f'B{i}'

    def bfv(bank):
        return bank[:].bitcast(BF16)

    wout = S("wout_s", [128, 16, 1024], BF16)
    wdn = S("wdn_s", [128, 8, 288], BF16)
    wkn = S("wkn_s", [128, 2, 1024], BF16)
    wv = S("wv_s", [128, 2, 1024], BF16)
    win = S("win_s", [128, 8, 1408], BF16)
    wuq = S("wuq_s", [128, 3, 1536], BF16)
    wst = [S(f"wst{i}", [128, 2048], F32) for i in range(4)]
    gkv = S("gkv_s", [128, 8], F32)
    gpre = S("gpre_s", [128, 8], F32)
    glat = S("glat_s", [128, 2], F32)
    gq = S("gq_s", [128, 3], F32)
    identf = S("identf", [128, 128], F32)
    identb = S("identb", [128, 128], BF16)
    posi = S("posi", [128, NTT], I32)
    posf = S("posf", [128, NTT], F32)
    invf = S("invf_s", [128, 16], F32)
    ang = S("ang", [128, NTT, 16], F32)
    uu = S("uu", [128, NTT, 16], F32)
    ki = S("ki", [128, NTT, 16], I32)
    kf = S("kf", [128, NTT, 16], F32)
    gg = S("gg", [128, NTT, 16], F32)
    m1 = S("m1", [128, NTT, 16], F32)
    gc = S("gc", [128, NTT, 16], F32)
    sinT = S("sinT", [128, NTT, 16], F32)
    cosT = S("cosT", [128, NTT, 16], F32)
    xin = [S(f"xin{i}", [128, 1024], F32) for i in range(2)]
    ynin = [S(f"ynin{i}", [128, 2048], BF16) for i in range(2)]
    ynT = S("ynT", [128, 16, 128], BF16)
    h1 = [S(f"h1_{i}", [128, 1024], F32) for i in range(2)]
    junk = S("junk", [128, 1024], BF16)
    ss = S("ss", [128, 1], F32)
    rt = S("rt", [128, 1], F32)
    rstd = S("rstd", [128, 1], F32)
    hnb = S("hnb", [128, 1024], BF16)
    hT2 = [S(f"hT{i}", [128, 8, 128], BF16) for i in range(2)]
    junk2 = S("junk2", [128, 384], BF16)
    ssc = S("ssc", [128, 1], F32)
    rtc = S("rtc", [128, 1], F32)
    rstdc = S("rstdc", [128, 1], F32)
    ckvn = S("ckvn", [128, 256], BF16)
    ra = S("ra", [128, 16], F32)
    rb = S("rb", [128, 16], F32)
    krb = S("krb", [128, 32], BF16)
    ckT = S("ckT", [128, 2, 128], BF16)
    krT = [S(f"krT{i}", [32, 128], BF16) for i in range(2)]
    knT = [S(f"knT{i}", [128, 8, 128], BF16) for i in range(2)]
    vsb = [S(f"vsb{i}", [128, 1024], BF16) for i in range(2)]
    ssq = S("ssq", [128, 1], F32)
    rtq = S("rtq", [128, 1], F32)
    rstdq = S("rstdq", [128, 1], F32)
    cqn = S("cqn", [128, 384], BF16)
    sg = [S(f"sg{i}", [128, 8, 128], BF16) for i in range(2)]
    cqT = S("cqT", [128, 3, 128], BF16)
    qtok = S("qtok", [128, 16, 96], BF16)
    qa = S("qa", [128, 16, 16], F32)
    qb = S("qb", [128, 16, 16], F32)
    qT = [S(f"qT{i}", [96, 16, 128], BF16) for i in range(2)]

    P.ld(gkv[:], gkv_d, ['gkv'], 'c0')
    P.ld(gpre[:], gpre_d, ['gpre'], 'c1')
    P.ld(glat[:], glat_d, ['glat'], 'c2')
    P.ld(gq[:], gq_d, ['gq'], 'c3')
    P.ld(posi[:], pos_d, ['posi'], 'c4')
    P.ld(invf[:], invf_d.partition_broadcast(128), ['invf'], 'c5')
    P.ms('pool', identf[:], 1.0, ['identf'])
    P.add('pool', lambda e: e.affine_select(out=identf[:], in_=identf[:], pattern=[[-1, 128]], compare_op=ALU.is_equal,
                                            fill=0.0, base=0, channel_multiplier=1), r=['identf'], w=['identf'])
    P.cp('dve', identb[:], identf[:], r=['identf'], w=['identb'])
    P.cp('dve', posf[:], posi[:], r=['posi'], w=['posf'])
    P.tt('dve', ang[:], posf[:].unsqueeze(2).to_broadcast([128, NTT, 16]), invf[:].unsqueeze(1).to_broadcast([128, NTT, 16]),
         ALU.mult, r=['posf', 'invf'], w=['ang'])
    P.ts('dve', uu[:], ang[:], 1.0 / TWO_PI, None, ALU.mult, r=['ang'], w=['uu'])
    P.cp('dve', ki[:], uu[:], r=['uu'], w=['ki'])
    P.cp('dve', kf[:], ki[:], r=['ki'], w=['kf'])
    P.stt(gg[:], kf[:], -CW1, ang[:], ALU.mult, ALU.add, r=['kf', 'ang'], w=['gg'])
    P.stt(gg[:], kf[:], -CW2, gg[:], ALU.mult, ALU.add, r=['kf', 'gg'], w=['gg'])
    P.ts('dve', gg[:], gg[:], 1.0 / TWO_PI, None, ALU.mult, r=['gg'], w=['gg'])

    def wrap():
        P.ts('dve', m1[:], gg[:], 0.5, None, ALU.is_gt, r=['gg'], w=['m1'])
        P.tt('dve', gg[:], gg[:], m1[:], ALU.subtract, r=['gg', 'm1'], w=['gg'])
        P.ts('dve', m1[:], gg[:], -0.5, None, ALU.is_lt, r=['gg'], w=['m1'])
        P.tt('dve', gg[:], gg[:], m1[:], ALU.add, r=['gg', 'm1'], w=['gg'])
        P.ts('dve', gg[:], gg[:], 0.4999995, -0.4999995, ALU.min, ALU.max, r=['gg'], w=['gg'])

    wrap()
    P.actv(sinT[:], gg[:], AF.Sin, scale=TWO_PI, r=['gg'], w=['sinT'])
    P.ts('dve', gg[:], gg[:], 0.25, None, ALU.add, r=['gg'], w=['gg'])
    wrap()
    P.actv(cosT[:], gg[:], AF.Sin, scale=TWO_PI, r=['gg'], w=['cosT'])

    wi = [0]
    WK = {}
    NST = 4

    def wload(src_ap, ncols, convs):
        i = wi[0] % NST
        wi[0] += 1
        P.ld(wst[i][:, 0:ncols], src_ap, [f'wst{i}'], f'wst{i}', eng=('sp', 'act', 'pool')[wi[0] % 3] if False else 'sp')
        for n, (grp, dst_ap, gain_ap, in_view) in enumerate(convs):
            key = f'W{wi[0]}_{n}'
            WK.setdefault(grp, []).append(key)
            src = wst[i][:, 0:ncols] if in_view is None else in_view(wst[i])
            if (wi[0] + n) % 2 == 0:
                if gain_ap is None:
                    P.cp('dve', dst_ap, src, r=[f'wst{i}'], w=[key])
                else:
                    P.ts('dve', dst_ap, src, gain_ap, None, ALU.mult, r=[f'wst{i}', 'gkv', 'gpre', 'glat', 'gq'], w=[key])
            else:
                if gain_ap is None:
                    P.cp('act', dst_ap, src, r=[f'wst{i}'], w=[key])
                else:
                    P.actv(dst_ap, src, AF.Copy, scale=gain_ap, r=[f'wst{i}', 'gkv', 'gpre', 'glat', 'gq'], w=[key])

    hv = lambda lo, hi, n, w: (lambda t: t[:, 0:n].rearrange("p (h c) -> p h c", h=16)[:, :, lo:hi])
    for kt in range(16):
        wload(wout_d[kt * 128:(kt + 1) * 128, :], 1024, [('wout', wout[:, kt, :], None, None)])
    for kt in range(8):
        wload(wdn_d[kt * 128:(kt + 1) * 128, :], 288, [('wdn', wdn[:, kt, :], gkv[:, kt:kt + 1], None)])
    for kt in range(8):
        wload(win_d[kt * 128:(kt + 1) * 128, :], 1408, [('win', win[:, kt, :], gpre[:, kt:kt + 1], None)])
    for kt in range(2):
        wload(wup_d[kt * 128:(kt + 1) * 128, :], 2048, [
            ('wkn', wkn[:, kt, :].rearrange("p (h c) -> p h c", h=16), glat[:, kt:kt + 1], hv(0, 64, 2048, 128)),
            ('wv', wv[:, kt, :].rearrange("p (h c) -> p h c", h=16), glat[:, kt:kt + 1], hv(64, 128, 2048, 128))])
    for kt in range(3):
        wload(wuq_d[kt * 128:(kt + 1) * 128, :], 1536, [
            ('wuq', wuq[:, kt, 0:1024].rearrange("p (h c) -> p h c", h=16), gq[:, kt:kt + 1], hv(0, 64, 1536, 96)),
            ('wuq', wuq[:, kt, 1024:1280].rearrange("p (h c) -> p h c", h=16), gq[:, kt:kt + 1], hv(64, 80, 1536, 96)),
            ('wuq', wuq[:, kt, 1280:1536].rearrange("p (h c) -> p h c", h=16), gq[:, kt:kt + 1], hv(80, 96, 1536, 96))])

    def load_t(tt):
        sl = tt % 2
        P.ld(xin[sl][:], x_d[tt * 128:(tt + 1) * 128, :], [f'xin{sl}'], f'xin{sl}')
        P.ld(ynin[sl][:], yn_d[tt * 128:(tt + 1) * 128, :], [f'ynin{sl}'], f'ynin{sl}')

    def front(tt):
        ring[0] = 0
        sl = tt % 2
        hT = hT2[tt % 2]
        hTk = f'hT{tt % 2}'
        tok = slice(tt * 128, (tt + 1) * 128)
        if tt + 1 < ntt:
            load_t(tt + 1)
        for half in range(2):
            bk, bkk = nb()
            pv = bfv(bk).rearrange("p (k t) -> p k t", k=8)
            for j in range(8):
                c = half * 8 + j
                P.tr(pv[:, j, :], ynin[sl][:, c * 128:(c + 1) * 128], identb[:], r=[f'ynin{sl}', 'identb'], w=[bkk])
            P.cp('dve' if half == 0 else 'act', ynT[:, half * 8:(half + 1) * 8, :], pv, r=[bkk], w=[f'ynT{half}'])
        yield
        ring[0] = 0
        for half in range(2):
            bk, bkk = nb()
            for c in range(16):
                P.mm(bk[:, :], ynT[:, c, :], wout[:, c, half * 512:(half + 1) * 512], start=(c == 0), stop=(c == 15),
                     r=['ynT0', 'ynT1', *WK['wout']], w=[bkk])
            P.tt('dve', h1[sl][:, half * 512:(half + 1) * 512], bk[:, :], xin[sl][:, half * 512:(half + 1) * 512], ALU.add,
                 r=[bkk, f'xin{sl}'], w=[f'h1_{sl}'])
        P.ld(h1_d[tok, :], h1[sl][:], w=[f'h1d{sl}'], sem=f'sth{sl}', r=[f'h1_{sl}'])
        yield
        ring[0] = 0
        P.actv(junk[:], h1[sl][:], AF.Square, accum=ss[:], r=[f'h1_{sl}'], w=['junk', 'ss'])
        P.actv(rt[:], ss[:], AF.Sqrt, bias=EPS, scale=1.0 / 1024, r=['ss'], w=['rt'])
        P.add('dve', lambda e: e.reciprocal(out=rstd[:], in_=rt[:]), r=['rt'], w=['rstd'])
        P.ts('dve', hnb[:], h1[sl][:], rstd[:, 0:1], None, ALU.mult, r=[f'h1_{sl}', 'rstd'], w=['hnb'])
        bk, bkk = nb()
        pv = bfv(bk).rearrange("p (k t) -> p k t", k=8)
        for kt in range(8):
            P.tr(pv[:, kt, :], hnb[:, kt * 128:(kt + 1) * 128], identb[:], r=['hnb', 'identb'], w=[bkk])
        P.cp('act', hT[:], pv, r=[bkk], w=[hTk])
        yield

    def back(tt):
        ring[0] = 1
        sl = tt % 2
        hT = hT2[tt % 2]
        hTk = f'hT{tt % 2}'
        tok = slice(tt * 128, (tt + 1) * 128)
        bk, bkk = nb()
        for kt in range(8):
            P.mm(bk[:, 0:288], hT[:, kt, :], wdn[:, kt, :], start=(kt == 0), stop=(kt == 7), r=[hTk, *WK['wdn']], w=[bkk])
        P.actv(junk2[:, 0:256], bk[:, 0:256], AF.Square, accum=ssc[:], r=[bkk], w=['junk2', 'ssc'])
        P.actv(rtc[:], ssc[:], AF.Sqrt, bias=EPS, scale=1.0 / 256, r=['ssc'], w=['rtc'])
        P.add('dve', lambda e: e.reciprocal(out=rstdc[:], in_=rtc[:]), r=['rtc'], w=['rstdc'])
        P.tt('dve', ra[:], bk[:, 256:272], cosT[:, tt, :], ALU.mult, r=[bkk, 'cosT'], w=['ra'])
        P.tt('dve', rb[:], bk[:, 272:288], sinT[:, tt, :], ALU.mult, r=[bkk, 'sinT'], w=['rb'])
        P.tt('dve', krb[:, 0:16], ra[:], rb[:], ALU.subtract, r=['ra', 'rb'], w=['krb'])
        P.tt('dve', ra[:], bk[:, 256:272], sinT[:, tt, :], ALU.mult, r=[bkk, 'sinT', 'krb'], w=['ra'])
        P.tt('dve', rb[:], bk[:, 272:288], cosT[:, tt, :], ALU.mult, r=[bkk, 'cosT', 'krb'], w=['rb'])
        P.tt('dve', krb[:, 16:32], ra[:], rb[:], ALU.add, r=['ra', 'rb'], w=['krb'])
        P.ts('dve', ckvn[:], bk[:, 0:256], rstdc[:, 0:1], None, ALU.mult, r=[bkk, 'rstdc'], w=['ckvn'])
        bk, bkk = nb()
        pv = bfv(bk)
        for kt in range(2):
            P.tr(pv[:, kt * 128:(kt + 1) * 128], ckvn[:, kt * 128:(kt + 1) * 128], identb[:], r=['ckvn', 'identb'], w=[bkk])
        P.tr(pv[0:32, 256:384], krb[:], identb[:], r=['krb', 'identb'], w=[bkk])
        P.cp('act', ckT[:], pv[:, 0:256].rearrange("p (k t) -> p k t", k=2), r=[bkk], w=['ckT'])
        P.cp('act', krT[sl][:], pv[0:32, 256:384], r=[bkk], w=[f'krT{sl}'])
        P.ld(kr_d[:, tok], krT[sl][:], w=[f'krd{sl}'], sem=f'stkr{sl}', r=[f'krT{sl}'])
        yield
        ring[0] = 1
        for half in range(2):
            bk, bkk = nb()
            for j in range(4):
                pr = half * 4 + j
                for kt in range(2):
                    P.mm(bk[:, j * 128:(j + 1) * 128], wkn[:, kt, pr * 128:(pr + 1) * 128], ckT[:, kt, :], start=(kt == 0), stop=(kt == 1),
                         r=['ckT', *WK['wkn']], w=[bkk])
            P.cp('act' if half == 0 else 'dve', knT[sl][:, half * 4:(half + 1) * 4, :], bk[:, :].rearrange("p (j t) -> p j t", j=4),
                 r=[bkk], w=[f'knT{sl}'])
        P.ld(kn_d[:, :, tok], knT[sl][:], w=[f'knd{sl}'], sem=f'stkn{sl}', r=[f'knT{sl}'])
        yield
        ring[0] = 1
        for half in range(2):
            bk, bkk = nb()
            for kt in range(2):
                P.mm(bk[:, :], ckT[:, kt, :], wv[:, kt, half * 512:(half + 1) * 512], start=(kt == 0), stop=(kt == 1),
                     r=['ckT', *WK['wv']], w=[bkk])
            P.cp('act' if half == 0 else 'dve', vsb[sl][:, half * 512:(half + 1) * 512], bk[:, :], r=[bkk], w=[f'vsb{sl}'])
        P.ld(v_d[tok, :], vsb[sl][:], w=[f'vd{sl}'], sem=f'stv{sl}', r=[f'vsb{sl}'])
        yield
        ring[0] = 1
        bk, bkk = nb()
        for kt in range(8):
            P.mm(bk[:, 0:384], hT[:, kt, :], win[:, kt, 0:384], start=(kt == 0), stop=(kt == 7), r=[hTk, *WK['win']], w=[bkk])
        P.actv(junk2[:], bk[:, 0:384], AF.Square, accum=ssq[:], r=[bkk], w=['junk2', 'ssq'])
        P.actv(rtq[:], ssq[:], AF.Sqrt, bias=EPS, scale=1.0 / 384, r=['ssq'], w=['rtq'])
        P.add('dve', lambda e: e.reciprocal(out=rstdq[:], in_=rtq[:]), r=['rtq'], w=['rstdq'])
        P.ts('dve', cqn[:], bk[:, 0:384], rstdq[:, 0:1], None, ALU.mult, r=[bkk, 'rstdq'], w=['cqn'])
        for half in range(2):
            bk, bkk = nb()
            for j in range(4):
                ct = half * 4 + j
                for kt in range(8):
                    P.mm(bk[:, j * 128:(j + 1) * 128], win[:, kt, 384 + ct * 128:384 + (ct + 1) * 128], hT[:, kt, :],
                         start=(kt == 0), stop=(kt == 7), r=[hTk, *WK['win']], w=[bkk])
            P.actv(sg[sl][:, half * 4:(half + 1) * 4, :], bk[:, :].rearrange("p (j t) -> p j t", j=4), AF.Silu, r=[bkk], w=[f'sg{sl}'])
        P.ld(sg_d[:, :, tok], sg[sl][:], w=[f'sgd{sl}'], sem=f'stsg{sl}', r=[f'sg{sl}'])
        bk, bkk = nb()
        pv = bfv(bk)
        for kt in range(3):
            P.tr(pv[:, kt * 128:(kt + 1) * 128], cqn[:, kt * 128:(kt + 1) * 128], identb[:], r=['cqn', 'identb'], w=[bkk])
        P.cp('act', cqT[:], pv[:, 0:384].rearrange("p (k t) -> p k t", k=3), r=[bkk], w=['cqT'])
        yield
        ring[0] = 1
        for blk in range(2):
            bk, bkk = nb()
            for kt in range(3):
                P.mm(bk[:, :], cqT[:, kt, :], wuq[:, kt, blk * 512:(blk + 1) * 512], start=(kt == 0), stop=(kt == 2),
                     r=['cqT', *WK['wuq']], w=[bkk])
            P.cp('act', qtok[:, blk * 8:(blk + 1) * 8, 0:64], bk[:, :].rearrange("p (h c) -> p h c", h=8), r=[bkk], w=['qtok'])
        bk, bkk = nb()
        for kt in range(3):
            P.mm(bk[:, :], cqT[:, kt, :], wuq[:, kt, 1024:1536], start=(kt == 0), stop=(kt == 2), r=['cqT', *WK['wuq']], w=[bkk])
        x1 = bk[:, 0:256].rearrange("p (h c) -> p h c", h=16)
        x2 = bk[:, 256:512].rearrange("p (h c) -> p h c", h=16)
        cb_ = cosT[:, tt, :].unsqueeze(1).to_broadcast([128, 16, 16])
        sb_ = sinT[:, tt, :].unsqueeze(1).to_broadcast([128, 16, 16])
        P.tt('dve', qa[:], x1, cb_, ALU.mult, r=[bkk, 'cosT'], w=['qa'])
        P.tt('dve', qb[:], x2, sb_, ALU.mult, r=[bkk, 'sinT'], w=['qb'])
        P.tt('dve', qtok[:, :, 64:80], qa[:], qb[:], ALU.subtract, r=['qa', 'qb'], w=['qtok'])
        P.tt('dve', qa[:], x1, sb_, ALU.mult, r=[bkk, 'sinT', 'qtok'], w=['qa'])
        P.tt('dve', qb[:], x2, cb_, ALU.mult, r=[bkk, 'cosT', 'qtok'], w=['qb'])
        P.tt('dve', qtok[:, :, 80:96], qa[:], qb[:], ALU.add, r=['qa', 'qb'], w=['qtok'])
        yield
        ring[0] = 1
        for half in range(2):
            bk, bkk = nb()
            pv = bfv(bk)[0:96, :].rearrange("p (h t) -> p h t", h=8)
            for j in range(8):
                P.tr(pv[:, j, :], qtok[:, half * 8 + j, :], identb[:], r=['qtok', 'identb'], w=[bkk])
            P.cp('act' if half == 0 else 'dve', qT[sl][:, half * 8:(half + 1) * 8, :], pv, r=[bkk], w=[f'qT{sl}'])
        P.ld(qT_d[:, :, tok], qT[sl][:], w=[f'qd{sl}'], sem=f'stq{sl}', r=[f'qT{sl}'])
        yield

    load_t(0)
    for it in range(ntt + 1):
        gens = []
        if it >= 1:
            gens.append(back(it - 1))
        if it < ntt:
            gens.append(front(it))
        while gens:
            for g in list(gens):
                try:
                    next(g)
                except StopIteration:
                    gens.remove(g)
    outk = []
    for sl in range(2):
        outk += [f'h1d{sl}', f'krd{sl}', f'knd{sl}', f'vd{sl}', f'sgd{sl}', f'qd{sl}']
    P.wait_all('sp', outk)
    P.emit()
    es.close()
    return nc
SCALE = 96.0 ** -0.5
LOOKAHEAD = 2


def build_stageC(nheads=2, nchunks=16):
    nc = bass.Bass("TRN2", target_bir_lowering=False)
    SQ = 8192
    LK = 8192
    NKT = LK // 128
    chunks = list(range(nchunks))
    kT_d = nc.dram_tensor("kT", [nheads, 96, 8192], BF16, kind="ExternalInput").ap()
    v_d = nc.dram_tensor("v", [nheads, 128, 64, 64], BF16, kind="ExternalInput").ap()
    qT_d = nc.dram_tensor("qT", [nheads, 96, SQ], BF16, kind="ExternalInput").ap()
    sg_d = nc.dram_tensor("sg", [nheads, 128, 64, 64], BF16, kind="ExternalInput").ap()
    og_d = nc.dram_tensor("og", [nheads, SQ, 64], BF16, kind="ExternalOutput").ap()

    P = P2(nc)
    es = contextlib.ExitStack()

    def S(name, shape, dt):
        return es.enter_context(nc.sbuf_tensor(name, shape, dt))

    banks = [es.enter_context(nc.psum_tensor(f"bank{i}", [128, 512], F32)) for i in range(8)]
    kT = [S(f"kT{i}", [96, LK], BF16) for i in range(2)]
    vh = [S(f"vh{i}", [128, NKT, 65], BF16) for i in range(2)]
    qh = [S(f"qh{i}", [96, SQ], BF16) for i in range(2)]
    sgh = [S(f"sgh{i}", [128, 64, 64], BF16) for i in range(2)]
    ogs = [S(f"ogs{i}", [128, 4, 64], BF16) for i in range(2)]
    rr = [S(f"rr{i}", [128, 4], F32) for i in range(2)]
    PT = [S(f"PT{i}", [128, 512], BF16) for i in range(4)]

    for i in range(2):
        P.ms('pool', vh[i][:, :, 64:65], 1.0, [f'vh{i}'])

    def load_head(h):
        i = h % 2
        half = LK // 2
        P.ld(kT[i][:, 0:half], kT_d[h, :, 0:half], [f'kT{i}a'], f'kT{i}a')
        P.ld(kT[i][:, half:LK], kT_d[h, :, half:LK], [f'kT{i}b'], f'kT{i}b', eng='act')
        P.ld(vh[i][:, :, 0:64], v_d[h, :, 0:NKT, :], [f'vh{i}'], f'vh{i}', eng='pool')
        P.ld(qh[i][:], qT_d[h, :, :], [f'qh{i}'], f'qh{i}')
        P.ld(sgh[i][:], sg_d[h, :, :, :], [f'sgh{i}'], f'sgh{i}')

    load_head(0)
    if nheads > 1:
        load_head(1)
    tiles = []
    cn = 0
    for h in range(nheads):
        for qi, cj in enumerate(chunks):
            nk = (cj + 1) * 4
            for kt in range(nk):
                d = kt - (nk - 4)
                c0 = 128 * d if d > 0 else 0
                tiles.append(dict(h=h, qi=qi, kt=kt, d=d, c0=c0, nk=nk, cn=cn, last_chunk=(qi == len(chunks) - 1)))
            cn += 1

    def emit_S(n, t):
        i = t['h'] % 2
        sb = n % 4
        ps = banks[sb]
        c0, kt, qi = t['c0'], t['kt'], t['qi']
        P.mm(ps[:, c0:512], kT[i][:, kt * 128:(kt + 1) * 128], qh[i][:, qi * 512 + c0:(qi + 1) * 512],
             r=[f'kT{i}a', f'kT{i}b', f'qh{i}'], w=[f'B{sb}'])
        P.actv(PT[sb][:, c0:512], ps[:, c0:512], AF.Exp, scale=SCALE, r=[f'B{sb}'], w=[f'PT{sb}'])
        if t['d'] >= 0:
            blk = PT[sb][:, c0:c0 + 128]
            P.add('pool', (lambda blk: (lambda e: e.affine_select(out=blk, in_=blk, pattern=[[1, 128]], compare_op=ALU.is_ge,
                                                                  fill=0.0, base=0, channel_multiplier=-1)))(blk),
                  r=[f'PT{sb}'], w=[f'PT{sb}'])

    def emit_PV(n, t):
        i = t['h'] % 2
        sb = n % 4
        par = t['cn'] % 2
        po = banks[4 + par]
        kt, d, nk = t['kt'], t['d'], t['nk']
        for qt in range(4):
            if d > qt:
                continue
            last = (kt == nk - 4 + qt)
            P.add('pe', (lambda po=po, sb=sb, qt=qt, kt=kt, i=i, last=last:
                         (lambda e: e.matmul(po[:, qt * 65:(qt + 1) * 65], lhsT=PT[sb][:, qt * 128:(qt + 1) * 128], rhs=vh[i][:, kt, :],
                                             start=(kt == 0 and qt == 0), stop=last, skip_group_check=True)))(),
                  r=[f'vh{i}', f'PT{sb}'], w=[f'B{4 + par}'])

    def epi1(t):
        i = t['h'] % 2
        par = t['cn'] % 2
        po = banks[4 + par]
        pv = po[:, 0:260].rearrange("p (q c) -> p q c", q=4)
        P.add('dve', lambda e, pv=pv, par=par: e.reciprocal(out=rr[par][:], in_=pv[:, :, 64]), r=[f'B{4 + par}'], w=[f'rr{par}'])
        for qt in range(4):
            tile_i = t['qi'] * 4 + qt
            P.stt(ogs[par][:, qt, :], pv[:, qt, 0:64], rr[par][:, qt:qt + 1], sgh[i][:, tile_i, :], ALU.mult, ALU.mult,
                  r=[f'B{4 + par}', f'rr{par}', f'sgh{i}'], w=[f'ogs{par}'])
        P.ld(og_d[t['h'], t['qi'] * 512:(t['qi'] + 1) * 512, :].rearrange("(q p) d -> p q d", p=128), ogs[par][:],
             w=[f'ogd{par}'], sem=f'sto{par}', r=[f'ogs{par}'])

    def epi2(t):
        if t['last_chunk'] and t['h'] + 2 < nheads:
            load_head(t['h'] + 2)

    LA = 3
    DEFER = 2
    sched = {}
    NT = len(tiles)
    for n in range(NT + LA):
        if n < NT:
            emit_S(n, tiles[n])
        for t in sched.pop(n, []):
            epi2(t)
        m = n - LA
        if m >= 0:
            t = tiles[m]
            emit_PV(m, t)
            if t['kt'] == t['nk'] - 1:
                epi1(t)
                sched.setdefault(n + DEFER, []).append(t)
    for k in sorted(sched):
        for t in sched[k]:
            epi2(t)
    P.wait_all('sp', ['ogd0', 'ogd1'])
    P.emit()
    es.close()
    return nc


def build_stageD():
    nc = bass.Bass("TRN2", target_bir_lowering=False)
    og_d = nc.dram_tensor("og", [128, 8, NTOK], BF16, kind="ExternalInput").ap()
    h1_d = nc.dram_tensor("h1", [NTOK, 1024], F32, kind="ExternalInput").ap()
    wo_d = nc.dram_tensor("wo", [1024, 1024], F32, kind="ExternalInput").ap()
    gf_d = nc.dram_tensor("gf", [1, 1024], F32, kind="ExternalInput").ap()
    out_d = nc.dram_tensor("out", [NTOK, 1024], F32, kind="ExternalOutput").ap()
    P = P2(nc)
    es = contextlib.ExitStack()

    def S(name, shape, dt):
        return es.enter_context(nc.sbuf_tensor(name, shape, dt))

    banks = [es.enter_context(nc.psum_tensor(f"bank{i}", [128, 512], F32)) for i in range(8)]
    ogT = S("ogT", [128, 8, NTOK], BF16)
    wo = S("wo_s", [128, 8, 1024], BF16)
    wst = [S(f"wst{i}", [128, 1024], F32) for i in range(2)]
    gf_bc = S("gf_bc", [128, 1024], F32)
    h1t = [S(f"h1t{i}", [128, 1024], F32) for i in range(2)]
    h2 = S("h2", [128, 1024], F32)
    junk = S("junk", [128, 1024], BF16)
    ss = S("ss", [128, 1], F32)
    rt = S("rt", [128, 1], F32)
    rstd = S("rstd", [128, 1], F32)
    outt = [S(f"outt{i}", [128, 1024], F32) for i in range(2)]
    P.ld(gf_bc[:], gf_d.partition_broadcast(128), ['gf_bc'], 'c0')
    for q in range(4):
        P.ld(ogT[:, :, q * 512:(q + 1) * 512], og_d[:, :, q * 512:(q + 1) * 512], [f'ogT{q}'], f'og{q}')
    wkeys = []
    for pr in range(8):
        i = pr % 2
        P.ld(wst[i][:], wo_d[pr * 128:(pr + 1) * 128, :], [f'wst{i}'], f'wst{i}')
        P.cp('dve' if i == 0 else 'act', wo[:, pr, :], wst[i][:], r=[f'wst{i}'], w=[f'wo{pr}'])
        wkeys.append(f'wo{pr}')
    def load_h1(tt):
        P.ld(h1t[tt % 2][:], h1_d[tt * 128:(tt + 1) * 128, :], [f'h1t{tt % 2}'], f'h1t{tt % 2}')

    load_h1(0)
    for tt in range(NTT):
        sl = tt % 2
        if tt + 1 < NTT:
            load_h1(tt + 1)
        for half in range(2):
            bk = banks[(tt % 2) * 2 + half]
            for pr in range(8):
                P.mm(bk[:, :], ogT[:, pr, tt * 128:(tt + 1) * 128], wo[:, pr, half * 512:(half + 1) * 512], start=(pr == 0), stop=(pr == 7),
                     r=[f'ogT{tt // 4}', wkeys[pr]], w=[f'B{(tt % 2) * 2 + half}'])
            P.tt('dve', h2[:, half * 512:(half + 1) * 512], bk[:, :], h1t[sl][:, half * 512:(half + 1) * 512], ALU.add,
                 r=[f'B{(tt % 2) * 2 + half}', f'h1t{sl}'], w=[f'h2_{half}'])
        P.actv(junk[:], h2[:], AF.Square, accum=ss[:], r=['h2_0', 'h2_1'], w=['junk', 'ss'])
        P.actv(rt[:], ss[:], AF.Sqrt, bias=EPS, scale=1.0 / 1024, r=['ss'], w=['rt'])
        P.add('dve', lambda e: e.reciprocal(out=rstd[:], in_=rt[:]), r=['rt'], w=['rstd'])
        P.stt(outt[sl][:], h2[:], rstd[:, 0:1], gf_bc[:], ALU.mult, ALU.mult, r=['h2_0', 'h2_1', 'rstd', 'gf_bc'], w=[f'outt{sl}'])
        P.ld(out_d[tt * 128:(tt + 1) * 128, :], outt[sl][:], w=[f'od{sl}'], sem=f'sto{sl}', r=[f'outt{sl}'])
    P.wait_all('sp', ['od0', 'od1'])
    P.emit()
    es.close()
    return nc


def _prepA(inp, b, g):
    w_in = inp['ssm_w_in'][0]
    w = np.concatenate([w_in[:, 2048 + g * 512:2048 + (g + 1) * 512], w_in[:, 4096 + g * 128:4096 + (g + 1) * 128],
                        w_in[:, 4608 + g * 128:4608 + (g + 1) * 128], w_in[:, 5120 + g * 8:5120 + (g + 1) * 8],
                        w_in[:, g * 512:(g + 1) * 512]], axis=1)
    cidx = np.concatenate([np.arange(g * 512, (g + 1) * 512), 2048 + np.arange(g * 128, (g + 1) * 128),
                           2560 + np.arange(g * 128, (g + 1) * 128)])
    cwc = inp['ssm_conv_w'][0][:, cidx]
    cw = cwc.T.reshape(6, 128, 4).transpose(1, 0, 2).reshape(128, 24)
    cb = inp['ssm_conv_b'][0][cidx].reshape(6, 128).T
    hs = slice(g * 8, (g + 1) * 8)
    C = np.ascontiguousarray
    return dict(x=C(inp['x'][b]), w=C(w), gpre=C(inp['g_pre'][0].reshape(8, 128).T), cw=C(cw), cb=C(cb),
                dtb=C(inp['ssm_dt_bias'][0][hs].reshape(1, 8)), alog=C(inp['ssm_A_log'][0][hs].reshape(1, 8)),
                dsk=C(inp['ssm_D'][0][hs].reshape(1, 8)), gout=C(inp['ssm_g_out'][0][g * 512:(g + 1) * 512].reshape(1, 512)))


def _prepB(inp, yn_b, b, j):
    C = np.ascontiguousarray
    tok = slice(j * NTOK, (j + 1) * NTOK)
    pos = np.asarray(inp['positions'][b][tok]).astype(np.int32).reshape(NTT, 128).T
    return dict(x=C(inp['x'][b][tok]), yn=C(yn_b[tok]), pos=C(pos), invf=np.array(INV_FREQ, dtype=np.float32).reshape(1, 16),
                wout=C(inp['ssm_w_out'][0]), wdn=C(inp['kv_w_down']), wup=C(inp['kv_w_up']), win=C(inp['mla_w_in'][0]),
                wuq=C(inp['mla_w_uq'][0]), gkv=C(inp['kv_g_in'].reshape(8, 128).T), gpre=C(inp['g_pre'][1].reshape(8, 128).T),
                glat=C(inp['kv_g_latent'].reshape(2, 128).T), gq=C(inp['mla_g_q'][0].reshape(3, 128).T))


def kernel(**inputs):
    inp = {k: np.asarray(v) for k, v in inputs.items()}
    C = np.ascontiguousarray
    cores = list(range(8))
    ncA = build_stageA()
    rA = run_bass_kernel_spmd(ncA, [_prepA(inp, c // 4, c % 4) for c in cores], core_ids=cores).results
    yn = [np.concatenate([rA[b * 4 + g]['yn'] for g in range(4)], axis=1) for b in range(2)]
    ncB = build_stageB()
    rB = run_bass_kernel_spmd(ncB, [_prepB(inp, yn[c // 4], c // 4, c % 4) for c in cores], core_ids=cores).results
    knf, krf, vff, qff, sff = [], [], [], [], []
    for b in range(2):
        knf.append(np.concatenate([rB[b * 4 + j]['kn'] for j in range(4)], axis=2))
        krf.append(np.concatenate([rB[b * 4 + j]['kr'] for j in range(4)], axis=1))
        vff.append(np.concatenate([rB[b * 4 + j]['v'] for j in range(4)], axis=0))
        qff.append(np.concatenate([rB[b * 4 + j]['qT'] for j in range(4)], axis=2))
        sff.append(np.concatenate([rB[b * 4 + j]['sg'] for j in range(4)], axis=2))
    ncC = build_stageC(2)
    og_heads = {}
    for part in range(2):
        imC = []
        for c in cores:
            b, hg = c // 4, c % 4
            kT = np.empty((2, 96, 8192), dtype=knf[b].dtype)
            v4 = np.empty((2, 128, 64, 64), dtype=vff[b].dtype)
            q4 = np.empty((2, 96, 8192), dtype=qff[b].dtype)
            s4 = np.empty((2, 128, 64, 64), dtype=sff[b].dtype)
            for hl in range(2):
                h = hg * 4 + part * 2 + hl
                kT[hl, 0:64] = knf[b][(h % 2) * 64:(h % 2) * 64 + 64, h // 2, :]
                kT[hl, 64:96] = krf[b]
                v4[hl] = vff[b][:, h * 64:(h + 1) * 64].reshape(64, 128, 64).transpose(1, 0, 2)
                q4[hl] = qff[b][:, h, :]
                s4[hl] = sff[b][(h % 2) * 64:(h % 2) * 64 + 64, h // 2, :].T.reshape(64, 128, 64).transpose(1, 0, 2)
            imC.append(dict(kT=kT, v=v4, qT=q4, sg=s4))
        rC = run_bass_kernel_spmd(ncC, imC, core_ids=cores).results
        for c in cores:
            b, hg = c // 4, c % 4
            for hl in range(2):
                og_heads[(b, hg * 4 + part * 2 + hl)] = rC[c]['og'][hl]
    imD = []
    for c in cores:
        b, j = c // 4, c % 4
        tok = slice(j * NTOK, (j + 1) * NTOK)
        og = np.empty((128, 8, NTOK), dtype=og_heads[(0, 0)].dtype)
        for h in range(16):
            og[(h % 2) * 64:(h % 2) * 64 + 64, h // 2, :] = og_heads[(b, h)][tok].T
        imD.append(dict(og=og, h1=rB[c]['h1'], wo=C(inp['mla_w_out'][0]), gf=C(inp['g_final'].reshape(1, 1024))))
    ncD = build_stageD()
    rD = run_bass_kernel_spmd(ncD, imD, core_ids=cores).results
    out = np.stack([np.concatenate([rD[b * 4 + j]['out'] for j in range(4)], axis=0) for b in range(2)], axis=0)
    return out.astype(np.float32)
```

```python
import contextlib
import math
from concourse.bass_utils import run_bass_kernel_spmd
import numpy as np
import concourse.bass as bass
import concourse.mybir as mybir

F32 = mybir.dt.float32
BF16 = mybir.dt.bfloat16
I32 = mybir.dt.int32
AF = mybir.ActivationFunctionType
ALU = mybir.AluOpType
AX = mybir.AxisListType


class Prog:
    def __init__(self, nc):
        self.nc = nc
        self.ops = []
        self.lastw = {}
        self.readers = {}
        self.dma_sems = {}

    def add(self, eng, fn, r=(), w=(), dma=None, group=False):
        deps = set()
        for k in r:
            if k in self.lastw:
                deps.add(self.lastw[k])
            if k[0] == 'B' and k[1:].isdigit():
                for j in self.readers.get(k, ()):
                    if self.ops[j]['eng'] != eng:
                        deps.add(j)
        for k in w:
            if k in self.lastw:
                deps.add(self.lastw[k])
            deps.update(self.readers.get(k, ()))
        i = len(self.ops)
        self.ops.append(dict(eng=eng, fn=fn, deps=deps, dma=dma, group=group, has_dep=False))
        for k in r:
            self.readers.setdefault(k, []).append(i)
        for k in w:
            self.lastw[k] = i
            self.readers[k] = []
        return i

    def pe(self, fn, r=(), w=()):
        return self.add('pe', fn, r, w)

    def act(self, fn, r=(), w=()):
        return self.add('act', fn, r, w)

    def dve(self, fn, r=(), w=()):
        return self.add('dve', fn, r, w)

    def pool(self, fn, r=(), w=()):
        return self.add('pool', fn, r, w)

    def dma(self, eng, fn, r=(), w=(), sem=None, group=False):
        assert sem is not None
        return self.add(eng, fn, r, w, dma=sem, group=group)

    def wait_all(self, eng, keys):
        return self.add(eng, None, r=keys, w=())

    def emit(self):
        nc = self.nc
        ops = self.ops
        engs = ['sp', 'act', 'dve', 'pool', 'pe']
        for o in ops:
            for d in o['deps']:
                if ops[d]['eng'] == 'pe' and o['eng'] == 'pe' and ops[d]['dma'] is None and o['dma'] is None:
                    continue
                ops[d]['has_dep'] = True
        esem = {e: nc.alloc_semaphore(name=f"s_{e}") for e in engs}
        group_tot = {}
        for o in ops:
            if o['dma'] is not None:
                if o['dma'] not in self.dma_sems:
                    self.dma_sems[o['dma']] = nc.alloc_semaphore(name=f"d_{o['dma']}")
                group_tot[o['dma']] = group_tot.get(o['dma'], 0) + 1
        cnt = {e: 0 for e in engs}
        dcnt = {}
        for o in ops:
            if o['fn'] is None:
                o['tok'] = None
            elif o['dma'] is not None:
                k = o['dma']
                dcnt[k] = dcnt.get(k, 0) + 1
                v = group_tot[k] if o['group'] else dcnt[k]
                o['tok'] = (('d', k), 16 * v)
            elif o['has_dep']:
                cnt[o['eng']] += 1
                o['tok'] = (('e', o['eng']), cnt[o['eng']])
            else:
                o['tok'] = None
        known = {e: {} for e in engs}
        for o in ops:
            e = o['eng']
            kn = known[e]
            waits = []
            for d in sorted(o['deps'], reverse=True):
                od = ops[d]
                if od['tok'] is None:
                    continue
                if od['eng'] == 'pe' and e == 'pe' and od['dma'] is None and o['dma'] is None:
                    continue
                s, v = od['tok']
                if kn.get(s, 0) < v:
                    waits.append((s, v))
                    kn[s] = v
                    for s2, v2 in od['clock'].items():
                        if kn.get(s2, 0) < v2:
                            kn[s2] = v2
            wm = {}
            for s, v in waits:
                wm[s] = max(wm.get(s, 0), v)
            o['waits'] = wm
            o['clock'] = dict(kn)

        def semof(s):
            return esem[s[1]] if s[0] == 'e' else self.dma_sems[s[1]]

        def run(ename, eng):
            for o in ops:
                if o['eng'] != ename:
                    continue
                for s, v in o['waits'].items():
                    eng.wait_ge(semof(s), v)
                if o['fn'] is None:
                    continue
                inst = o['fn'](eng)
                if o['tok'] is not None:
                    s, v = o['tok']
                    inst.then_inc(semof(s), 16 if s[0] == 'd' else 1)

        with nc.Block() as block:
            @block.sync
            def _(e):
                run('sp', e)

            @block.scalar
            def _(e):
                run('act', e)

            @block.vector
            def _(e):
                run('dve', e)

            @block.gpsimd
            def _(e):
                run('pool', e)

            @block.tensor
            def _(e):
                run('pe', e)
        n = {e: sum(1 for o in ops if o['eng'] == e) for e in engs}
        nw = sum(len(o['waits']) for o in ops)
        print("PROG ops", n, "waits", nw, "sems", 5 + len(self.dma_sems), flush=True)


def _kw(**k):
    return {a: b for a, b in k.items() if b is not None}


class P2(Prog):
    def mm(self, out, lhsT, rhs, start=True, stop=True, r=(), w=()):
        return self.add('pe', lambda e: e.matmul(out, lhsT=lhsT, rhs=rhs, start=start, stop=stop), r, w)

    def tr(self, out, in_, ident, r=(), w=()):
        return self.add('pe', lambda e: e.transpose(out, in_, ident), r, w)

    def actv(self, out, in_, func, bias=None, scale=None, accum=None, r=(), w=()):
        kw = _kw(bias=bias, scale=scale, accum_out=accum)
        return self.add('act', lambda e: e.activation(out=out, in_=in_, func=func, **kw), r, w)

    def ts(self, eng, out, in0, s1, s2=None, op0=ALU.mult, op1=None, r=(), w=()):
        kw = _kw(op1=op1)
        return self.add(eng, lambda e: e.tensor_scalar(out=out, in0=in0, scalar1=s1, scalar2=s2, op0=op0, **kw), r, w)

    def tt(self, eng, out, in0, in1, op, r=(), w=()):
        return self.add(eng, lambda e: e.tensor_tensor(out=out, in0=in0, in1=in1, op=op), r, w)

    def stt(self, out, in0, scalar, in1, op0, op1, r=(), w=()):
        return self.add('dve', lambda e: e.scalar_tensor_tensor(out=out, in0=in0, scalar=scalar, in1=in1, op0=op0, op1=op1), r, w)

    def cp(self, eng, out, in_, r=(), w=()):
        if eng == 'act':
            return self.add('act', lambda e: e.activation(out=out, in_=in_, func=AF.Copy), r, w)
        return self.add(eng, lambda e: e.tensor_copy(out=out, in_=in_), r, w)

    def ms(self, eng, ap, val, w=()):
        return self.add(eng, lambda e: e.memset(ap, val), (), w)

    def ld(self, out, in_, w, sem, eng='sp', group=False, r=()):
        return self.dma(eng, lambda e: e.dma_start(out=out, in_=in_), r=r, w=w, sem=sem, group=group)

SEQ = 8192
DM = 1024
NCH = SEQ // 256
EPS = 1e-6
WCOLS = 1288


def build_stageA(nch=NCH):
    nc = bass.Bass("TRN2", target_bir_lowering=False)
    x_d = nc.dram_tensor("x", [SEQ, DM], F32, kind="ExternalInput").ap()
    w_d = nc.dram_tensor("w", [DM, WCOLS], F32, kind="ExternalInput").ap()
    gpre_d = nc.dram_tensor("gpre", [128, 8], F32, kind="ExternalInput").ap()
    cw_d = nc.dram_tensor("cw", [128, 24], F32, kind="ExternalInput").ap()
    cb_d = nc.dram_tensor("cb", [128, 6], F32, kind="ExternalInput").ap()
    dtb_d = nc.dram_tensor("dtb", [1, 8], F32, kind="ExternalInput").ap()
    alog_d = nc.dram_tensor("alog", [1, 8], F32, kind="ExternalInput").ap()
    dsk_d = nc.dram_tensor("dsk", [1, 8], F32, kind="ExternalInput").ap()
    gout_d = nc.dram_tensor("gout", [1, 512], F32, kind="ExternalInput").ap()
    yn_d = nc.dram_tensor("yn", [SEQ, 512], BF16, kind="ExternalOutput").ap()

    P = P2(nc)
    es = contextlib.ExitStack()

    def S(name, shape, dt):
        return es.enter_context(nc.sbuf_tensor(name, shape, dt))

    banks = [es.enter_context(nc.psum_tensor(f"bank{i}", [128, 512], F32)) for i in range(8)]

    W = S("W", [128, 8, WCOLS], BF16)
    wst = [S(f"wst{i}", [128, WCOLS], F32) for i in range(2)]
    gpre = S("gpre_s", [128, 8], F32)
    cw = S("cw_s", [128, 24], F32)
    cb = S("cb_s", [128, 6], F32)
    dtb_bc = S("dtb_bc", [128, 8], F32)
    A_bc = S("A_bc", [128, 8], F32)
    D_bc = S("D_bc", [128, 8], F32)
    gout_bc = S("gout_bc", [128, 512], F32)
    identf = S("identf", [128, 128], F32)
    identb = S("identb", [128, 128], BF16)
    onesf = S("onesf", [128, 128], F32)
    onesb = S("onesb", [128, 128], BF16)
    trif = S("trif", [128, 128], F32)
    triw = S("triw", [128, 256], BF16)
    SU = S("SU", [128, 128], BF16)
    cdiag = S("cdiag", [128, 24, 128], BF16)
    Dident = S("Dident", [128, 8, 128], BF16)
    xin = [S(f"xin{i}", [128, 2, DM], F32) for i in range(2)]
    junk = [S(f"junk{i}", [128, DM], BF16) for i in range(2)]
    ss = S("ss", [128, 2], F32)
    rt = S("rt", [128, 2], F32)
    rstd = S("rstd", [128, 2], F32)
    hn = S("hn", [128, 2, DM], BF16)
    hnT = S("hnT", [128, 8, 256], BF16)
    ubuf = S("ubuf", [128, 6, 259], BF16)
    xc = [S(f"xc{i}", [128, 6, 256], BF16) for i in range(2)]
    xtok = S("xtok", [128, 2, 640], BF16)
    dtr = S("dtr", [128, 2, 8], F32)
    e1 = S("e1", [128, 2, 8], F32)
    dtk = [S(f"dtk{i}", [128, 2, 8], F32) for i in range(2)]
    dtA = [S(f"dtA{i}", [128, 2, 8], F32) for i in range(2)]
    cend = S("cend", [128, 8], F32)
    ecum = [S(f"ecum{i}", [128, 2, 8], F32) for i in range(2)]
    wtmp = [S(f"wtmp{i}", [128, 2, 8], F32) for i in range(2)]
    dec = [S(f"dec{i}", [128, 8], F32) for i in range(2)]
    W0 = S("W0", [128, 8, 256], BF16)
    V1 = S("V1", [128, 8, 128], BF16)
    CBm = S("CBm", [128, 384], BF16)
    xdt = S("xdt", [128, 2, 512], BF16)
    Lb = [S(f"Lb{i}", [128, 384], BF16) for i in range(2)]
    junk2b = S("junk2b", [128, 512], BF16)
    MT = S("MT", [128, 8, 384], BF16)
    state = S("state", [128, 512], F32)
    state_bf = S("state_bf", [128, 512], BF16)
    yi = S("yi", [128, 512], F32)
    t1 = S("t1", [128, 512], F32)
    ysb = S("ysb", [128, 512], F32)
    zs = [S(f"zs{i}", [128, 2, 512], F32) for i in range(2)]
    yg = [S(f"yg{i}", [128, 512], F32) for i in range(2)]
    junk2 = S("junk2", [128, 512], BF16)
    ss2 = S("ss2", [128, 2], F32)
    rt2 = S("rt2", [128, 2], F32)
    rstd2 = S("rstd2", [128, 2], F32)
    yn = [S(f"yn{i}", [128, 512], BF16) for i in range(2)]
    wx = S("wx", [128, 2, 512], BF16)

    def bfview(bank):
        return bank[:].bitcast(BF16)

    ptr = bfview(banks[0]).rearrange("p (k t) -> p k t", k=8)
    pX = banks[1]
    pCv = banks[2]
    pdtk = banks[3][:, 0:16].rearrange("p (t c) -> p t c", t=2)
    pcum = banks[3][:, 16:32].rearrange("p (t c) -> p t c", t=2)
    pce = banks[3][:, 32:40]
    pz = banks[3][:, :]
    ptx = bfview(banks[4])[:, 0:640]
    pCB = banks[4][:, 0:384]
    pseg = [banks[5][:, 0:384], banks[7][:, 0:384]]
    py = banks[6][:, :]
    pyi = banks[4][:, :]
    pst = banks[4][:, :]

    P.ld(gpre[:], gpre_d, ['gpre'], 'c0')
    P.ld(cw[:], cw_d, ['cw'], 'c1')
    P.ld(cb[:], cb_d, ['cb'], 'c2')
    P.ld(dtb_bc[:], dtb_d.partition_broadcast(128), ['dtb_bc'], 'c3')
    P.ld(A_bc[:], alog_d.partition_broadcast(128), ['A_bc'], 'c4')
    P.ld(D_bc[:], dsk_d.partition_broadcast(128), ['D_bc'], 'c5')
    P.ld(gout_bc[:], gout_d.partition_broadcast(128), ['gout_bc'], 'c6')
    P.ms('pool', identf[:], 1.0, ['identf'])
    P.add('pool', lambda e: e.affine_select(out=identf[:], in_=identf[:], pattern=[[-1, 128]], compare_op=ALU.is_equal,
                                            fill=0.0, base=0, channel_multiplier=1), r=['identf'], w=['identf'])
    P.cp('dve', identb[:], identf[:], r=['identf'], w=['identb'])
    P.ms('pool', onesf[:], 1.0, ['onesf'])
    P.ms('pool', onesb[:], 1.0, ['onesb'])
    P.ms('pool', triw[:], 1.0, ['triw'])
    P.add('pool', lambda e: e.affine_select(out=triw[:, 0:128], in_=triw[:, 0:128], pattern=[[1, 128]], compare_op=ALU.is_ge,
                                            fill=0.0, base=0, channel_multiplier=-1), r=['triw'], w=['triw'])
    P.cp('dve', trif[:], triw[:, 0:128], r=['triw'], w=['trif'])
    P.ms('pool', SU[:], 1.0, ['SU'])
    P.add('pool', lambda e: e.affine_select(out=SU[:], in_=SU[:], pattern=[[-1, 128]], compare_op=ALU.is_gt,
                                            fill=0.0, base=0, channel_multiplier=1), r=['SU'], w=['SU'])
    P.ms('pool', ubuf[:], 0.0, ['ubuf%d' % i for i in range(3)])
    P.ms('pool', state[:], 0.0, ['state'])
    P.ms('pool', state_bf[:], 0.0, ['state_bf'])
    for kt in range(8):
        P.ld(wst[kt % 2][:], w_d[kt * 128:(kt + 1) * 128, :], [f'wst{kt % 2}'], f'wst{kt % 2}')
        if kt % 2 == 0:
            P.ts('dve', W[:, kt, :], wst[kt % 2][:], gpre[:, kt:kt + 1], None, ALU.mult, r=[f'wst{kt % 2}', 'gpre'], w=[f'W{kt}'])
        else:
            P.actv(W[:, kt, :], wst[kt % 2][:], AF.Copy, scale=gpre[:, kt:kt + 1], r=[f'wst{kt % 2}', 'gpre'], w=[f'W{kt}'])
    Wk = [f'W{kt}' for kt in range(8)]
    HNT = ['hnT0', 'hnT1']
    for i in range(24):
        P.ts('dve', cdiag[:, i, :], identf[:], cw[:, i:i + 1], None, ALU.mult, r=['identf', 'cw'], w=['cdiag'])
    for h in range(8):
        P.ts('dve', Dident[:, h, :], identf[:], D_bc[:, h:h + 1], None, ALU.mult, r=['identf', 'D_bc'], w=['Dident'])
    P.actv(A_bc[:], A_bc[:], AF.Exp, r=['A_bc'], w=['A_bc'])
    P.ts('dve', A_bc[:], A_bc[:], -1.0, None, ALU.mult, r=['A_bc'], w=['A_bc'])

    def load_x(c):
        sl = c % 2
        P.ld(xin[sl][:], x_d[c * 256:(c + 1) * 256, :].rearrange("(t p) d -> p t d", p=128), [f'xin{sl}'], f'xin{sl}')

    def front(c):
        sl = c % 2
        p = c % 2
        xk = f'xin{sl}'
        if c + 1 < nch:
            load_x(c + 1)
        for t in range(2):
            P.actv(junk[t][:], xin[sl][:, t, :], AF.Square, accum=ss[:, t:t + 1], r=[xk], w=[f'ss{t}', f'junk{t}'])
        P.actv(rt[:], ss[:], AF.Ln, bias=EPS, scale=1.0 / DM, r=['ss0', 'ss1'], w=['rt'])
        P.actv(rstd[:], rt[:], AF.Exp, scale=-0.5, r=['rt'], w=['rstd'])
        for t in range(2):
            P.ts('dve', hn[:, t, :], xin[sl][:, t, :], rstd[:, t:t + 1], None, ALU.mult, r=[xk, 'rstd'], w=[f'hn{t}'])
        yield
        for t in range(2):
            for kt in range(8):
                P.tr(ptr[:, kt, :], hn[:, t, kt * 128:(kt + 1) * 128], identb[:], r=[f'hn{t}', 'identb'], w=['B0'])
            P.cp('dve' if t == 0 else 'act', hnT[:, :, t * 128:(t + 1) * 128], ptr, r=['B0'], w=[f'hnT{t}'])
            yield
        for t in range(2):
            for kt in range(8):
                P.mm(pdtk[:, t, :], hnT[:, kt, t * 128:(t + 1) * 128], W[:, kt, 768:776], start=(kt == 0), stop=(kt == 7),
                     r=HNT + [Wk[kt]], w=['B3'])
        P.tt('dve', dtr[:], pdtk, dtb_bc[:].unsqueeze(1).to_broadcast([128, 2, 8]), ALU.add, r=['B3', 'dtb_bc'], w=['dtr'])
        P.actv(e1[:], dtr[:], AF.Exp, r=['dtr'], w=['e1'])
        P.actv(dtk[p][:], e1[:], AF.Ln, bias=1.0, r=['e1'], w=[f'dtk{p}'])
        P.tt('dve', dtA[p][:], dtk[p][:], A_bc[:].unsqueeze(1).to_broadcast([128, 2, 8]), ALU.mult, r=[f'dtk{p}', 'A_bc'], w=[f'dtA{p}'])
        P.mm(pcum[:, 0, :], trif[:], dtA[p][:, 0, :], r=['trif', f'dtA{p}'], w=['B3'])
        P.mm(pcum[:, 1, :], onesf[:], dtA[p][:, 0, :], start=True, stop=False, r=['onesf', f'dtA{p}'], w=['B3'])
        P.mm(pcum[:, 1, :], trif[:], dtA[p][:, 1, :], start=False, stop=True, r=['trif', f'dtA{p}'], w=['B3'])
        P.mm(pce, onesf[:], dtA[p][:, 0, :], start=True, stop=False, r=['onesf', f'dtA{p}'], w=['B3'])
        P.mm(pce, onesf[:], dtA[p][:, 1, :], start=False, stop=True, r=['onesf', f'dtA{p}'], w=['B3'])
        P.actv(ecum[p][:], pcum, AF.Exp, r=['B3'], w=[f'ecum{p}'])
        P.actv(dec[p][:], pce, AF.Exp, r=['B3'], w=[f'dec{p}'])
        P.cp('act', cend[:], pce, r=['B3'], w=['cend'])
        P.tt('dve', wtmp[p][:], cend[:].unsqueeze(1).to_broadcast([128, 2, 8]), pcum, ALU.subtract, r=['cend', 'B3'], w=[f'wtmp{p}'])
        P.actv(wtmp[p][:], wtmp[p][:], AF.Exp, r=[f'wtmp{p}'], w=[f'wtmp{p}'])
        yield
        for pr in range(3):
            for j in range(2):
                ct = 2 * pr + j
                for kt in range(8):
                    P.mm(pX[:, j * 256:(j + 1) * 256], W[:, kt, ct * 128:(ct + 1) * 128], hnT[:, kt, :], start=(kt == 0), stop=(kt == 7),
                         r=HNT + [Wk[kt]], w=['B1'])
            P.cp('dve', ubuf[:, 2 * pr:2 * pr + 2, 3:259], pX[:, :].rearrange("p (j t) -> p j t", j=2), r=['B1'], w=[f'ubuf{pr}'])
            for j in range(2):
                ct = 2 * pr + j
                for k in range(4):
                    P.mm(pCv[:, j * 256:(j + 1) * 256], cdiag[:, ct * 4 + k, :], ubuf[:, ct, k:k + 256], start=(k == 0), stop=(k == 3),
                         r=['cdiag', f'ubuf{pr}'], w=['B2'])
            for j in range(2):
                ct = 2 * pr + j
                P.actv(xc[p][:, ct, :], pCv[:, j * 256:(j + 1) * 256], AF.Silu, bias=cb[:, ct:ct + 1], r=['B2', 'cb'], w=[f'xc{p}_{ct}'])
            P.cp('pool', ubuf[:, 2 * pr:2 * pr + 2, 0:3], ubuf[:, 2 * pr:2 * pr + 2, 256:259], r=[f'ubuf{pr}'], w=[f'ubuf{pr}'])
            yield
        for t in range(2):
            for kt in range(8):
                P.mm(pz, hnT[:, kt, t * 128:(t + 1) * 128], W[:, kt, 776:1288], start=(kt == 0), stop=(kt == 7),
                     r=HNT + [Wk[kt]], w=['B3'])
            P.actv(zs[p][:, t, :], pz, AF.Silu, r=['B3'], w=[f'zs{p}_{t}'])
            yield
    def back(c):
        p = c % 2
        XC = [f'xc{p}_{ct}' for ct in range(6)]
        P.tt('dve', W0[:], triw[:].unsqueeze(1).to_broadcast([128, 8, 256]), dtA[p][:, 0, :].unsqueeze(2).to_broadcast([128, 8, 256]),
             ALU.mult, r=['triw', f'dtA{p}'], w=['W0'])
        P.tt('dve', V1[:], triw[:, 0:128].unsqueeze(1).to_broadcast([128, 8, 128]), dtA[p][:, 1, :].unsqueeze(2).to_broadcast([128, 8, 128]),
             ALU.mult, r=['triw', f'dtA{p}'], w=['V1'])
        for t in range(2):
            for ct in range(5):
                P.tr(ptx[:, ct * 128:(ct + 1) * 128], xc[p][:, ct, t * 128:(t + 1) * 128], identb[:], r=[XC[ct], 'identb'], w=['B4'])
            P.cp('dve' if t == 0 else 'act', xtok[:, t, :], ptx, r=['B4'], w=[f'xtok{t}'])
            P.tt('pool', xdt[:, t, :].rearrange("p (h c) -> p h c", h=8), xtok[:, t, 0:512].rearrange("p (h c) -> p h c", h=8),
                 dtk[p][:, t, :].unsqueeze(2).to_broadcast([128, 8, 64]), ALU.mult, r=[f'xtok{t}', f'dtk{p}'], w=[f'xdt{t}'])
        yield
        P.mm(pCB[:, 0:256], xc[p][:, 4, 0:128], xc[p][:, 5, 0:256], r=[XC[4], XC[5]], w=['B4'])
        P.mm(pCB[:, 256:384], xc[p][:, 4, 128:256], xc[p][:, 5, 128:256], r=[XC[4], XC[5]], w=['B4'])
        P.cp('act', CBm[:], pCB, r=['B4'], w=['CBm'])
        for off in (0, 256):
            blk = CBm[:, off:off + 128]
            P.add('pool', (lambda blk: (lambda e: e.affine_select(out=blk, in_=blk, pattern=[[1, 128]], compare_op=ALU.is_ge,
                                                                  fill=0.0, base=0, channel_multiplier=-1)))(blk),
                  r=['CBm'], w=['CBm'])
        yield
        for h in range(8):
            L = Lb[h % 2]
            Lk = f'Lb{h % 2}'
            ps = pseg[h % 2]
            psk = 'B5' if h % 2 == 0 else 'B7'
            P.mm(ps[:, 0:128], SU[:], W0[:, h, 0:128], start=True, stop=True, r=['SU', 'W0'], w=[psk])
            P.mm(ps[:, 128:256], SU[:], W0[:, h, 128:256], start=True, stop=False, r=['SU', 'W0'], w=[psk])
            P.mm(ps[:, 128:256], onesb[:], V1[:, h, :], start=False, stop=True, r=['onesb', 'V1'], w=[psk])
            P.mm(ps[:, 256:384], SU[:], V1[:, h, :], start=True, stop=True, r=['SU', 'V1'], w=[psk])
            P.actv(L[:], ps, AF.Exp, r=[psk], w=[Lk])
            P.tt('dve', MT[:, h, :], L[:], CBm[:], ALU.mult, r=[Lk, 'CBm'], w=[f'MT{h}'])
            if h % 4 == 3:
                yield
        for t in range(2):
            for h in range(8):
                hc = slice(h * 64, (h + 1) * 64)
                P.mm(py[:, hc], MT[:, h, t * 128:(t + 1) * 128], xdt[:, 0, hc], start=True, stop=False, r=[f'MT{h}', 'xdt0'], w=['B6'])
                if t == 1:
                    P.mm(py[:, hc], MT[:, h, 256:384], xdt[:, 1, hc], start=False, stop=False, r=[f'MT{h}', 'xdt1'], w=['B6'])
                P.mm(py[:, hc], Dident[:, h, :], xtok[:, t, hc], start=False, stop=True, r=['Dident', f'xtok{t}'], w=['B6'])
            P.mm(pyi, xc[p][:, 5, t * 128:(t + 1) * 128], state_bf[:], r=[XC[5], 'state_bf'], w=['B4'])
            P.tt('dve', t1[:].rearrange("p (h c) -> p h c", h=8), pyi.rearrange("p (h c) -> p h c", h=8),
                 ecum[p][:, t, :].unsqueeze(2).to_broadcast([128, 8, 64]), ALU.mult, r=['B4', f'ecum{p}'], w=['t1'])
            P.tt('dve', ysb[:], t1[:], py, ALU.add, r=['t1', 'B6'], w=['ysb'])
            P.tt('dve', yg[t][:], ysb[:], zs[p][:, t, :], ALU.mult, r=['ysb', f'zs{p}_{t}'], w=[f'yg{t}'])
            P.actv(junk2b[:], yg[t][:], AF.Square, accum=ss2[:, t:t + 1], r=[f'yg{t}'], w=[f'ss2_{t}', 'junk2b'])
            yield
        yield
        yield
        P.actv(rt2[:], ss2[:], AF.Ln, bias=EPS, scale=1.0 / 512, r=['ss2_0', 'ss2_1'], w=['rt2'])
        P.actv(rstd2[:], rt2[:], AF.Exp, scale=-0.5, r=['rt2'], w=['rstd2'])
        for t in range(2):
            P.stt(yn[t][:], yg[t][:], rstd2[:, t:t + 1], gout_bc[:], ALU.mult, ALU.mult, r=[f'yg{t}', 'rstd2', 'gout_bc'], w=[f'yn{t}'])
            P.ld(yn_d[c * 256 + t * 128: c * 256 + (t + 1) * 128, :], yn[t][:], w=[f'ynd{t}'], sem=f'st{t}', r=[f'yn{t}'])
        for st in range(2):
            P.tt('pool', wx[:, st, :].rearrange("p (h c) -> p h c", h=8), xdt[:, st, :].rearrange("p (h c) -> p h c", h=8),
                 wtmp[p][:, st, :].unsqueeze(2).to_broadcast([128, 8, 64]), ALU.mult, r=[f'xdt{st}', f'wtmp{p}'], w=[f'wx{st}'])
        for st in range(2):
            P.mm(pst, xtok[:, st, 512:640], wx[:, st, :], start=(st == 0), stop=(st == 1), r=[f'xtok{st}', f'wx{st}'], w=['B4'])
        P.tt('dve', state[:].rearrange("p (h c) -> p h c", h=8), state[:].rearrange("p (h c) -> p h c", h=8),
             dec[p][:].unsqueeze(2).to_broadcast([128, 8, 64]), ALU.mult, r=['state', f'dec{p}'], w=['state'])
        P.tt('dve', state[:], state[:], pst, ALU.add, r=['state', 'B4'], w=['state'])
        P.cp('act', state_bf[:], state[:], r=['state'], w=['state_bf'])
        yield

    load_x(0)
    for it in range(nch + 1):
        gens = []
        if it >= 1:
            gens.append(back(it - 1))
        if it < nch:
            gens.append(front(it))
        while gens:
            for g in list(gens):
                try:
                    next(g)
                except StopIteration:
                    gens.remove(g)
    P.wait_all('sp', ['ynd0', 'ynd1'])
    P.emit()
    es.close()
    return nc

NTOK = 2048
NTT = NTOK // 128
INV_FREQ = [float(np.float32(10000.0) ** np.float32(-(2 * i) / 32.0)) for i in range(16)]
TWO_PI = 2.0 * math.pi
CW1 = 6.28125
CW2 = TWO_PI - CW1


def build_stageB(ntt=NTT):
    nc = bass.Bass("TRN2", target_bir_lowering=False)
    x_d = nc.dram_tensor("x", [NTOK, 1024], F32, kind="ExternalInput").ap()
    yn_d = nc.dram_tensor("yn", [NTOK, 2048], BF16, kind="ExternalInput").ap()
    pos_d = nc.dram_tensor("pos", [128, NTT], I32, kind="ExternalInput").ap()
    invf_d = nc.dram_tensor("invf", [1, 16], F32, kind="ExternalInput").ap()
    wout_d = nc.dram_tensor("wout", [2048, 1024], F32, kind="ExternalInput").ap()
    wdn_d = nc.dram_tensor("wdn", [1024, 288], F32, kind="ExternalInput").ap()
    wup_d = nc.dram_tensor("wup", [256, 2048], F32, kind="ExternalInput").ap()
    win_d = nc.dram_tensor("win", [1024, 1408], F32, kind="ExternalInput").ap()
    wuq_d = nc.dram_tensor("wuq", [384, 1536], F32, kind="ExternalInput").ap()
    gkv_d = nc.dram_tensor("gkv", [128, 8], F32, kind="ExternalInput").ap()
    gpre_d = nc.dram_tensor("gpre", [128, 8], F32, kind="ExternalInput").ap()
    glat_d = nc.dram_tensor("glat", [128, 2], F32, kind="ExternalInput").ap()
    gq_d = nc.dram_tensor("gq", [128, 3], F32, kind="ExternalInput").ap()
    h1_d = nc.dram_tensor("h1", [NTOK, 1024], F32, kind="ExternalOutput").ap()
    sg_d = nc.dram_tensor("sg", [128, 8, NTOK], BF16, kind="ExternalOutput").ap()
    kn_d = nc.dram_tensor("kn", [128, 8, NTOK], BF16, kind="ExternalOutput").ap()
    kr_d = nc.dram_tensor("kr", [32, NTOK], BF16, kind="ExternalOutput").ap()
    v_d = nc.dram_tensor("v", [NTOK, 1024], BF16, kind="ExternalOutput").ap()
    qT_d = nc.dram_tensor("qT", [96, 16, NTOK], BF16, kind="ExternalOutput").ap()

    P = P2(nc)
    es = contextlib.ExitStack()

    def S(name, shape, dt):
        return es.enter_context(nc.sbuf_tensor(name, shape, dt))

    banks = [es.enter_context(nc.psum_tensor(f"bank{i}", [128, 512], F32)) for i in range(8)]
    bctr = [0, 0]
    ring = [0]

    def nb():
        r = ring[0]
        i = r * 4 + bctr[r] % 4
        bctr[r] += 1
        return banks[i], f'B{i}'

    def bfv(bank):
        return bank[:].bitcast(BF16)

    wout = S("wout_s", [128, 16, 1024], BF16)
    wdn = S("wdn_s", [128, 8, 288], BF16)
    wkn = S("wkn_s", [128, 2, 1024], BF16)
    wv = S("wv_s", [128, 2, 1024], BF16)
    win = S("win_s", [128, 8, 1408], BF16)
    wuq = S("wuq_s", [128, 3, 1536], BF16)
    wst = [S(f"wst{i}", [128, 2048], F32) for i in range(4)]
    gkv = S("gkv_s", [128, 8], F32)
    gpre = S("gpre_s", [128, 8], F32)
    glat = S("glat_s", [128, 2], F32)
    gq = S("gq_s", [128, 3], F32)
    identf = S("identf", [128, 128], F32)
    identb = S("identb", [128, 128], BF16)
    posi = S("posi", [128, NTT], I32)
    posf = S("posf", [128, NTT], F32)
    invf = S("invf_s", [128, 16], F32)
    ang = S("ang", [128, NTT, 16], F32)
    uu = S("uu", [128, NTT, 16], F32)
    ki = S("ki", [128, NTT, 16], I32)
    kf = S("kf", [128, NTT, 16], F32)
    gg = S("gg", [128, NTT, 16], F32)
    m1 = S("m1", [128, NTT, 16], F32)
    gc = S("gc", [128, NTT, 16], F32)
    sinT = S("sinT", [128, NTT, 16], F32)
    cosT = S("cosT", [128, NTT, 16], F32)
    xin = [S(f"xin{i}", [128, 1024], F32) for i in range(2)]
    ynin = [S(f"ynin{i}", [128, 2048], BF16) for i in range(2)]
    ynT = S("ynT", [128, 16, 128], BF16)
    h1 = [S(f"h1_{i}", [128, 1024], F32) for i in range(2)]
    junk = S("junk", [128, 1024], BF16)
    ss = S("ss", [128, 1], F32)
    rt = S("rt", [128, 1], F32)
    rstd = S("rstd", [128, 1], F32)
    hnb = S("hnb", [128, 1024], BF16)
    hT2 = [S(f"hT{i}", [128, 8, 128], BF16) for i in range(2)]
    junk2 = S("junk2", [128, 384], BF16)
    ssc = S("ssc", [128, 1], F32)
    rtc = S("rtc", [128, 1], F32)
    rstdc = S("rstdc", [128, 1], F32)
    ckvn = S("ckvn", [128, 256], BF16)
    ra = S("ra", [128, 16], F32)
    rb = S("rb", [128, 16], F32)
    krb = S("krb", [128, 32], BF16)
    ckT = S("ckT", [128, 2, 128], BF16)
    krT = [S(f"krT{i}", [32, 128], BF16) for i in range(2)]
    knT = [S(f"knT{i}", [128, 8, 128], BF16) for i in range(2)]
    vsb = [S(f"vsb{i}", [128, 1024], BF16) for i in range(2)]
    ssq = S("ssq", [128, 1], F32)
    rtq = S("rtq", [128, 1], F32)
    rstdq = S("rstdq", [128, 1], F32)
    cqn = S("cqn", [128, 384], BF16)
    sg = [S(f"sg{i}", [128, 8, 128], BF16) for i in range(2)]
    cqT = S("cqT", [128, 3, 128], BF16)
    qtok = S("qtok", [128, 16, 96], BF16)
    qa = S("qa", [128, 16, 16], F32)
    qb = S("qb", [128, 16, 16], F32)
    qT = [S(f"qT{i}", [96, 16, 128], BF16) for i in range(2)]

    P.ld(gkv[:], gkv_d, ['gkv'], 'c0')
    P.ld(gpre[:], gpre_d, ['gpre'], 'c1')
    P.ld(glat[:], glat_d, ['glat'], 'c2')
    P.ld(gq[:], gq_d, ['gq'], 'c3')
    P.ld(posi[:], pos_d, ['posi'], 'c4')
    P.ld(invf[:], invf_d.partition_broadcast(128), ['invf'], 'c5')
    P.ms('pool', identf[:], 1.0, ['identf'])
    P.add('pool', lambda e: e.affine_select(out=identf[:], in_=identf[:], pattern=[[-1, 128]], compare_op=ALU.is_equal,
                                            fill=0.0, base=0, channel_multiplier=1), r=['identf'], w=['identf'])
    P.cp('dve', identb[:], identf[:], r=['identf'], w=['identb'])
    P.cp('dve', posf[:], posi[:], r=['posi'], w=['posf'])
    P.tt('dve', ang[:], posf[:].unsqueeze(2).to_broadcast([128, NTT, 16]), invf[:].unsqueeze(1).to_broadcast([128, NTT, 16]),
         ALU.mult, r=['posf', 'invf'], w=['ang'])
    P.ts('dve', uu[:], ang[:], 1.0 / TWO_PI, None, ALU.mult, r=['ang'], w=['uu'])
    P.cp('dve', ki[:], uu[:], r=['uu'], w=['ki'])
    P.cp('dve', kf[:], ki[:], r=['ki'], w=['kf'])
    P.stt(gg[:], kf[:], -CW1, ang[:], ALU.mult, ALU.add, r=['kf', 'ang'], w=['gg'])
    P.stt(gg[:], kf[:], -CW2, gg[:], ALU.mult, ALU.add, r=['kf', 'gg'], w=['gg'])
    P.ts('dve', gg[:], gg[:], 1.0 / TWO_PI, None, ALU.mult, r=['gg'], w=['gg'])

    def wrap():
        P.ts('dve', m1[:], gg[:], 0.5, None, ALU.is_gt, r=['gg'], w=['m1'])
        P.tt('dve', gg[:], gg[:], m1[:], ALU.subtract, r=['gg', 'm1'], w=['gg'])
        P.ts('dve', m1[:], gg[:], -0.5, None, ALU.is_lt, r=['gg'], w=['m1'])
        P.tt('dve', gg[:], gg[:], m1[:], ALU.add, r=['gg', 'm1'], w=['gg'])
        P.ts('dve', gg[:], gg[:], 0.4999995, -0.4999995, ALU.min, ALU.max, r=['gg'], w=['gg'])

    wrap()
    P.actv(sinT[:], gg[:], AF.Sin, scale=TWO_PI, r=['gg'], w=['sinT'])
    P.ts('dve', gg[:], gg[:], 0.25, None, ALU.add, r=['gg'], w=['gg'])
    wrap()
    P.actv(cosT[:], gg[:], AF.Sin, scale=TWO_PI, r=['gg'], w=['cosT'])

    wi = [0]
    WK = {}
    NST = 4

    def wload(src_ap, ncols, convs):
        i = wi[0] % NST
        wi[0] += 1
        P.ld(wst[i][:, 0:ncols], src_ap, [f'wst{i}'], f'wst{i}', eng=('sp', 'act', 'pool')[wi[0] % 3] if False else 'sp')
        for n, (grp, dst_ap, gain_ap, in_view) in enumerate(convs):
            key = f'W{wi[0]}_{n}'
            WK.setdefault(grp, []).append(key)
            src = wst[i][:, 0:ncols] if in_view is None else in_view(wst[i])
            if (wi[0] + n) % 2 == 0:
                if gain_ap is None:
                    P.cp('dve', dst_ap, src, r=[f'wst{i}'], w=[key])
                else:
                    P.ts('dve', dst_ap, src, gain_ap, None, ALU.mult, r=[f'wst{i}', 'gkv', 'gpre', 'glat', 'gq'], w=[key])
            else:
                if gain_ap is None:
                    P.cp('act', dst_ap, src, r=[f'wst{i}'], w=[key])
                else:
                    P.actv(dst_ap, src, AF.Copy, scale=gain_ap, r=[f'wst{i}', 'gkv', 'gpre', 'glat', 'gq'], w=[key])

    hv = lambda lo, hi, n, w: (lambda t: t[:, 0:n].rearrange("p (h c) -> p h c", h=16)[:, :, lo:hi])
    for kt in range(16):
        wload(wout_d[kt * 128:(kt + 1) * 128, :], 1024, [('wout', wout[:, kt, :], None, None)])
    for kt in range(8):
        wload(wdn_d[kt * 128:(kt + 1) * 128, :], 288, [('wdn', wdn[:, kt, :], gkv[:, kt:kt + 1], None)])
    for kt in range(8):
        wload(win_d[kt * 128:(kt + 1) * 128, :], 1408, [('win', win[:, kt, :], gpre[:, kt:kt + 1], None)])
    for kt in range(2):
        wload(wup_d[kt * 128:(kt + 1) * 128, :], 2048, [
            ('wkn', wkn[:, kt, :].rearrange("p (h c) -> p h c", h=16), glat[:, kt:kt + 1], hv(0, 64, 2048, 128)),
            ('wv', wv[:, kt, :].rearrange("p (h c) -> p h c", h=16), glat[:, kt:kt + 1], hv(64, 128, 2048, 128))])
    for kt in range(3):
        wload(wuq_d[kt * 128:(kt + 1) * 128, :], 1536, [
            ('wuq', wuq[:, kt, 0:1024].rearrange("p (h c) -> p h c", h=16), gq[:, kt:kt + 1], hv(0, 64, 1536, 96)),
            ('wuq', wuq[:, kt, 1024:1280].rearrange("p (h c) -> p h c", h=16), gq[:, kt:kt + 1], hv(64, 80, 1536, 96)),
            ('wuq', wuq[:, kt, 1280:1536].rearrange("p (h c) -> p h c", h=16), gq[:, kt:kt + 1], hv(80, 96, 1536, 96))])

    def load_t(tt):
        sl = tt % 2
        P.ld(xin[sl][:], x_d[tt * 128:(tt + 1) * 128, :], [f'xin{sl}'], f'xin{sl}')
        P.ld(ynin[sl][:], yn_d[tt * 128:(tt + 1) * 128, :], [f'ynin{sl}'], f'ynin{sl}')

    def front(tt):
        ring[0] = 0
        sl = tt % 2
        hT = hT2[tt % 2]
        hTk = f'hT{tt % 2}'
        tok = slice(tt * 128, (tt + 1) * 128)
        if tt + 1 < ntt:
            load_t(tt + 1)
        for half in range(2):
            bk, bkk = nb()
            pv = bfv(bk).rearrange("p (k t) -> p k t", k=8)
            for j in range(8):
                c = half * 8 + j
                P.tr(pv[:, j, :], ynin[sl][:, c * 128:(c + 1) * 128], identb[:], r=[f'ynin{sl}', 'identb'], w=[bkk])
            P.cp('dve' if half == 0 else 'act', ynT[:, half * 8:(half + 1) * 8, :], pv, r=[bkk], w=[f'ynT{half}'])
        yield
        ring[0] = 0
        for half in range(2):
            bk, bkk = nb()
            for c in range(16):
                P.mm(bk[:, :], ynT[:, c, :], wout[:, c, half * 512:(half + 1) * 512], start=(c == 0), stop=(c == 15),
                     r=['ynT0', 'ynT1', *WK['wout']], w=[bkk])
            P.tt('dve', h1[sl][:, half * 512:(half + 1) * 512], bk[:, :], xin[sl][:, half * 512:(half + 1) * 512], ALU.add,
                 r=[bkk, f'xin{sl}'], w=[f'h1_{sl}'])
        P.ld(h1_d[tok, :], h1[sl][:], w=[f'h1d{sl}'], sem=f'sth{sl}', r=[f'h1_{sl}'])
        yield
        ring[0] = 0
        P.actv(junk[:], h1[sl][:], AF.Square, accum=ss[:], r=[f'h1_{sl}'], w=['junk', 'ss'])
        P.actv(rt[:], ss[:], AF.Sqrt, bias=EPS, scale=1.0 / 1024, r=['ss'], w=['rt'])
        P.add('dve', lambda e: e.reciprocal(out=rstd[:], in_=rt[:]), r=['rt'], w=['rstd'])
        P.ts('dve', hnb[:], h1[sl][:], rstd[:, 0:1], None, ALU.mult, r=[f'h1_{sl}', 'rstd'], w=['hnb'])
        bk, bkk = nb()
        pv = bfv(bk).rearrange("p (k t) -> p k t", k=8)
        for kt in range(8):
            P.tr(pv[:, kt, :], hnb[:, kt * 128:(kt + 1) * 128], identb[:], r=['hnb', 'identb'], w=[bkk])
        P.cp('act', hT[:], pv, r=[bkk], w=[hTk])
        yield

    def back(tt):
        ring[0] = 1
        sl = tt % 2
        hT = hT2[tt % 2]
        hTk = f'hT{tt % 2}'
        tok = slice(tt * 128, (tt + 1) * 128)
        bk, bkk = nb()
        for kt in range(8):
            P.mm(bk[:, 0:288], hT[:, kt, :], wdn[:, kt, :], start=(kt == 0), stop=(kt == 7), r=[hTk, *WK['wdn']], w=[bkk])
        P.actv(junk2[:, 0:256], bk[:, 0:256], AF.Square, accum=ssc[:], r=[bkk], w=['junk2', 'ssc'])
        P.actv(rtc[:], ssc[:], AF.Sqrt, bias=EPS, scale=1.0 / 256, r=['ssc'], w=['rtc'])
        P.add('dve', lambda e: e.reciprocal(out=rstdc[:], in_=rtc[:]), r=['rtc'], w=['rstdc'])
        P.tt('dve', ra[:], bk[:, 256:272], cosT[:, tt, :], ALU.mult, r=[bkk, 'cosT'], w=['ra'])
        P.tt('dve', rb[:], bk[:, 272:288], sinT[:, tt, :], ALU.mult, r=[bkk, 'sinT'], w=['rb'])
        P.tt('dve', krb[:, 0:16], ra[:], rb[:], ALU.subtract, r=['ra', 'rb'], w=['krb'])
        P.tt('dve', ra[:], bk[:, 256:272], sinT[:, tt, :], ALU.mult, r=[bkk, 'sinT', 'krb'], w=['ra'])
        P.tt('dve', rb[:], bk[:, 272:288], cosT[:, tt, :], ALU.mult, r=[bkk, 'cosT', 'krb'], w=['rb'])
        P.tt('dve', krb[:, 16:32], ra[:], rb[:], ALU.add, r=['ra', 'rb'], w=['krb'])
        P.ts('dve', ckvn[:], bk[:, 0:256], rstdc[:, 0:1], None, ALU.mult, r=[bkk, 'rstdc'], w=['ckvn'])
        bk, bkk = nb()
        pv = bfv(bk)
        for kt in range(2):
            P.tr(pv[:, kt * 128:(kt + 1) * 128], ckvn[:, kt * 128:(kt + 1) * 128], identb[:], r=['ckvn', 'identb'], w=[bkk])
        P.tr(pv[0:32, 256:384], krb[:], identb[:], r=['krb', 'identb'], w=[bkk])
        P.cp('act', ckT[:], pv[:, 0:256].rearrange("p (k t) -> p k t", k=2), r=[bkk], w=['ckT'])
        P.cp('act', krT[sl][:], pv[0:32, 256:384], r=[bkk], w=[f'krT{sl}'])
        P.ld(kr_d[:, tok], krT[sl][:], w=[f'krd{sl}'], sem=f'stkr{sl}', r=[f'krT{sl}'])
        yield
        ring[0] = 1
        for half in range(2):
            bk, bkk = nb()
            for j in range(4):
                pr = half * 4 + j
                for kt in range(2):
                    P.mm(bk[:, j * 128:(j + 1) * 128], wkn[:, kt, pr * 128:(pr + 1) * 128], ckT[:, kt, :], start=(kt == 0), stop=(kt == 1),
                         r=['ckT', *WK['wkn']], w=[bkk])
            P.cp('act' if half == 0 else 'dve', knT[sl][:, half * 4:(half + 1) * 4, :], bk[:, :].rearrange("p (j t) -> p j t", j=4),
                 r=[bkk], w=[f'knT{sl}'])
        P.ld(kn_d[:, :, tok], knT[sl][:], w=[f'knd{sl}'], sem=f'stkn{sl}', r=[f'knT{sl}'])
        for half in range(2):
            bk, bkk = nb()
            for kt in range(2):
                P.mm(bk[:, :], ckT[:, kt, :], wv[:, kt, half * 512:(half + 1) * 512], start=(kt == 0), stop=(kt == 1),
                     r=['ckT', *WK['wv']], w=[bkk])
            P.cp('act' if half == 0 else 'dve', vsb[sl][:, half * 512:(half + 1) * 512], bk[:, :], r=[bkk], w=[f'vsb{sl}'])
        P.ld(v_d[tok, :], vsb[sl][:], w=[f'vd{sl}'], sem=f'stv{sl}', r=[f'vsb{sl}'])
        yield
        ring[0] = 1
        bk, bkk = nb()
        for kt in range(8):
            P.mm(bk[:, 0:384], hT[:, kt, :], win[:, kt, 0:384], start=(kt == 0), stop=(kt == 7), r=[hTk, *WK['win']], w=[bkk])
        P.actv(junk2[:], bk[:, 0:384], AF.Square, accum=ssq[:], r=[bkk], w=['junk2', 'ssq'])
        P.actv(rtq[:], ssq[:], AF.Sqrt, bias=EPS, scale=1.0 / 384, r=['ssq'], w=['rtq'])
        P.add('dve', lambda e: e.reciprocal(out=rstdq[:], in_=rtq[:]), r=['rtq'], w=['rstdq'])
        P.ts('dve', cqn[:], bk[:, 0:384], rstdq[:, 0:1], None, ALU.mult, r=[bkk, 'rstdq'], w=['cqn'])
        for half in range(2):
            bk, bkk = nb()
            for j in range(4):
                ct = half * 4 + j
                for kt in range(8):
                    P.mm(bk[:, j * 128:(j + 1) * 128], win[:, kt, 384 + ct * 128:384 + (ct + 1) * 128], hT[:, kt, :],
                         start=(kt == 0), stop=(kt == 7), r=[hTk, *WK['win']], w=[bkk])
            P.actv(sg[sl][:, half * 4:(half + 1) * 4, :], bk[:, :].rearrange("p (j t) -> p j t", j=4), AF.Silu, r=[bkk], w=[f'sg{sl}'])
        P.ld(sg_d[:, :, tok], sg[sl][:], w=[f'sgd{sl}'], sem=f'stsg{sl}', r=[f'sg{sl}'])
        bk, bkk = nb()
        pv = bfv(bk)
        for kt in range(3):
            P.tr(pv[:, kt * 128:(kt + 1) * 128], cqn[:, kt * 128:(kt + 1) * 128], identb[:], r=['cqn', 'identb'], w=[bkk])
        P.cp('act', cqT[:], pv[:, 0:384].rearrange("p (k t) -> p k t", k=3), r=[bkk], w=['cqT'])
        yield
        ring[0] = 1
        for blk in range(2):
            bk, bkk = nb()
            for kt in range(3):
                P.mm(bk[:, :], cqT[:, kt, :], wuq[:, kt, blk * 512:(blk + 1) * 512], start=(kt == 0), stop=(kt == 2),
                     r=['cqT', *WK['wuq']], w=[bkk])
            P.cp('act', qtok[:, blk * 8:(blk + 1) * 8, 0:64], bk[:, :].rearrange("p (h c) -> p h c", h=8), r=[bkk], w=['qtok'])
        bk, bkk = nb()
        for kt in range(3):
            P.mm(bk[:, :], cqT[:, kt, :], wuq[:, kt, 1024:1536], start=(kt == 0), stop=(kt == 2), r=['cqT', *WK['wuq']], w=[bkk])
        x1 = bk[:, 0:256].rearrange("p (h c) -> p h c", h=16)
        x2 = bk[:, 256:512].rearrange("p (h c) -> p h c", h=16)
        cb_ = cosT[:, tt, :].unsqueeze(1).to_broadcast([128, 16, 16])
        sb_ = sinT[:, tt, :].unsqueeze(1).to_broadcast([128, 16, 16])
        P.tt('dve', qa[:], x1, cb_, ALU.mult, r=[bkk, 'cosT'], w=['qa'])
        P.tt('dve', qb[:], x2, sb_, ALU.mult, r=[bkk, 'sinT'], w=['qb'])
        P.tt('dve', qtok[:, :, 64:80], qa[:], qb[:], ALU.subtract, r=['qa', 'qb'], w=['qtok'])
        P.tt('dve', qa[:], x1, sb_, ALU.mult, r=[bkk, 'sinT', 'qtok'], w=['qa'])
        P.tt('dve', qb[:], x2, cb_, ALU.mult, r=[bkk, 'cosT', 'qtok'], w=['qb'])
        P.tt('dve', qtok[:, :, 80:96], qa[:], qb[:], ALU.add, r=['qa', 'qb'], w=['qtok'])
        for half in range(2):
            bk, bkk = nb()
            pv = bfv(bk)[0:96, :].rearrange("p (h t) -> p h t", h=8)
            for j in range(8):
                P.tr(pv[:, j, :], qtok[:, half * 8 + j, :], identb[:], r=['qtok', 'identb'], w=[bkk])
            P.cp('act' if half == 0 else 'dve', qT[sl][:, half * 8:(half + 1) * 8, :], pv, r=[bkk], w=[f'qT{sl}'])
        P.ld(qT_d[:, :, tok], qT[sl][:], w=[f'qd{sl}'], sem=f'stq{sl}', r=[f'qT{sl}'])
        yield

    load_t(0)
    for it in range(ntt + 1):
        gens = []
        if it >= 1:
            gens.append(back(it - 1))
        if it < ntt:
            gens.append(front(it))
        while gens:
            for g in list(gens):
                try:
                    next(g)
                except StopIteration:
                    gens.remove(g)
    outk = []
    for sl in range(2):
        outk += [f'h1d{sl}', f'krd{sl}', f'knd{sl}', f'vd{sl}', f'sgd{sl}', f'qd{sl}']
    P.wait_all('sp', outk)
    P.emit()
    es.close()
    return nc
SCALE = 96.0 ** -0.5
LOOKAHEAD = 2


def build_stageC(nheads=2, nchunks=16):
    nc = bass.Bass("TRN2", target_bir_lowering=False)
    SQ = 8192
    LK = 8192
    NKT = LK // 128
    chunks = list(range(nchunks))
    kT_d = nc.dram_tensor("kT", [nheads, 96, 8192], BF16, kind="ExternalInput").ap()
    v_d = nc.dram_tensor("v", [nheads, 128, 64, 64], BF16, kind="ExternalInput").ap()
    qT_d = nc.dram_tensor("qT", [nheads, 96, SQ], BF16, kind="ExternalInput").ap()
    sg_d = nc.dram_tensor("sg", [nheads, 128, 64, 64], BF16, kind="ExternalInput").ap()
    og_d = nc.dram_tensor("og", [nheads, SQ, 64], BF16, kind="ExternalOutput").ap()

    P = P2(nc)
    es = contextlib.ExitStack()

    def S(name, shape, dt):
        return es.enter_context(nc.sbuf_tensor(name, shape, dt))

    banks = [es.enter_context(nc.psum_tensor(f"bank{i}", [128, 512], F32)) for i in range(8)]
    kT = [S(f"kT{i}", [96, LK], BF16) for i in range(2)]
    vh = [S(f"vh{i}", [128, NKT, 65], BF16) for i in range(2)]
    qh = [S(f"qh{i}", [96, SQ], BF16) for i in range(2)]
    sgh = [S(f"sgh{i}", [128, 64, 64], BF16) for i in range(2)]
    ogs = [S(f"ogs{i}", [128, 4, 64], BF16) for i in range(2)]
    rr = [S(f"rr{i}", [128, 4], F32) for i in range(2)]
    PT = [S(f"PT{i}", [128, 512], BF16) for i in range(4)]

    for i in range(2):
        P.ms('pool', vh[i][:, :, 64:65], 1.0, [f'vh{i}'])

    def load_head(h):
        i = h % 2
        half = LK // 2
        P.ld(kT[i][:, 0:half], kT_d[h, :, 0:half], [f'kT{i}a'], f'kT{i}a')
        P.ld(kT[i][:, half:LK], kT_d[h, :, half:LK], [f'kT{i}b'], f'kT{i}b', eng='act')
        P.ld(vh[i][:, :, 0:64], v_d[h, :, 0:NKT, :], [f'vh{i}'], f'vh{i}', eng='pool')
        P.ld(qh[i][:], qT_d[h, :, :], [f'qh{i}'], f'qh{i}')
        P.ld(sgh[i][:], sg_d[h, :, :, :], [f'sgh{i}'], f'sgh{i}')

    load_head(0)
    if nheads > 1:
        load_head(1)
    tiles = []
    cn = 0
    for h in range(nheads):
        for qi, cj in enumerate(chunks):
            nk = (cj + 1) * 4
            for kt in range(nk):
                d = kt - (nk - 4)
                c0 = 128 * d if d > 0 else 0
                tiles.append(dict(h=h, qi=qi, kt=kt, d=d, c0=c0, nk=nk, cn=cn, last_chunk=(qi == len(chunks) - 1)))
            cn += 1

    def emit_S(n, t):
        i = t['h'] % 2
        sb = n % 4
        ps = banks[sb]
        c0, kt, qi = t['c0'], t['kt'], t['qi']
        P.mm(ps[:, c0:512], kT[i][:, kt * 128:(kt + 1) * 128], qh[i][:, qi * 512 + c0:(qi + 1) * 512],
             r=[f'kT{i}a', f'kT{i}b', f'qh{i}'], w=[f'B{sb}'])
        P.actv(PT[sb][:, c0:512], ps[:, c0:512], AF.Exp, scale=SCALE, r=[f'B{sb}'], w=[f'PT{sb}'])
        if t['d'] >= 0:
            blk = PT[sb][:, c0:c0 + 128]
            P.add('pool', (lambda blk: (lambda e: e.affine_select(out=blk, in_=blk, pattern=[[1, 128]], compare_op=ALU.is_ge,
                                                                  fill=0.0, base=0, channel_multiplier=-1)))(blk),
                  r=[f'PT{sb}'], w=[f'PT{sb}'])

    def emit_PV(n, t):
        i = t['h'] % 2
        sb = n % 4
        par = t['cn'] % 2
        po = banks[4 + par]
        kt, d, nk = t['kt'], t['d'], t['nk']
        for qt in range(4):
            if d > qt:
                continue
            last = (kt == nk - 4 + qt)
            P.add('pe', (lambda po=po, sb=sb, qt=qt, kt=kt, i=i, last=last:
                         (lambda e: e.matmul(po[:, qt * 65:(qt + 1) * 65], lhsT=PT[sb][:, qt * 128:(qt + 1) * 128], rhs=vh[i][:, kt, :],
                                             start=(kt == 0 and qt == 0), stop=last, skip_group_check=True)))(),
                  r=[f'vh{i}', f'PT{sb}'], w=[f'B{4 + par}'])

    def epi1(t):
        i = t['h'] % 2
        par = t['cn'] % 2
        po = banks[4 + par]
        pv = po[:, 0:260].rearrange("p (q c) -> p q c", q=4)
        P.add('dve', lambda e, pv=pv, par=par: e.reciprocal(out=rr[par][:], in_=pv[:, :, 64]), r=[f'B{4 + par}'], w=[f'rr{par}'])
        for qt in range(4):
            tile_i = t['qi'] * 4 + qt
            P.stt(ogs[par][:, qt, :], pv[:, qt, 0:64], rr[par][:, qt:qt + 1], sgh[i][:, tile_i, :], ALU.mult, ALU.mult,
                  r=[f'B{4 + par}', f'rr{par}', f'sgh{i}'], w=[f'ogs{par}'])
        P.ld(og_d[t['h'], t['qi'] * 512:(t['qi'] + 1) * 512, :].rearrange("(q p) d -> p q d", p=128), ogs[par][:],
             w=[f'ogd{par}'], sem=f'sto{par}', r=[f'ogs{par}'])

    def epi2(t):
        if t['last_chunk'] and t['h'] + 2 < nheads:
            load_head(t['h'] + 2)

    LA = 3
    DEFER = 2
    sched = {}
    NT = len(tiles)
    for n in range(NT + LA):
        if n < NT:
            emit_S(n, tiles[n])
        for t in sched.pop(n, []):
            epi2(t)
        m = n - LA
        if m >= 0:
            t = tiles[m]
            emit_PV(m, t)
            if t['kt'] == t['nk'] - 1:
                epi1(t)
                sched.setdefault(n + DEFER, []).append(t)
    for k in sorted(sched):
        for t in sched[k]:
            epi2(t)
    P.wait_all('sp', ['ogd0', 'ogd1'])
    P.emit()
    es.close()
    return nc


def build_stageD():
    nc = bass.Bass("TRN2", target_bir_lowering=False)
    og_d = nc.dram_tensor("og", [128, 8, NTOK], BF16, kind="ExternalInput").ap()
    h1_d = nc.dram_tensor("h1", [NTOK, 1024], F32, kind="ExternalInput").ap()
    wo_d = nc.dram_tensor("wo", [1024, 1024], F32, kind="ExternalInput").ap()
    gf_d = nc.dram_tensor("gf", [1, 1024], F32, kind="ExternalInput").ap()
    out_d = nc.dram_tensor("out", [NTOK, 1024], F32, kind="ExternalOutput").ap()
    P = P2(nc)
    es = contextlib.ExitStack()

    def S(name, shape, dt):
        return es.enter_context(nc.sbuf_tensor(name, shape, dt))

    banks = [es.enter_context(nc.psum_tensor(f"bank{i}", [128, 512], F32)) for i in range(8)]
    ogT = S("ogT", [128, 8, NTOK], BF16)
    wo = S("wo_s", [128, 8, 1024], BF16)
    wst = [S(f"wst{i}", [128, 1024], F32) for i in range(2)]
    gf_bc = S("gf_bc", [128, 1024], F32)
    h1t = [S(f"h1t{i}", [128, 1024], F32) for i in range(2)]
    h2 = S("h2", [128, 1024], F32)
    junk = S("junk", [128, 1024], BF16)
    ss = S("ss", [128, 1], F32)
    rt = S("rt", [128, 1], F32)
    rstd = S("rstd", [128, 1], F32)
    outt = [S(f"outt{i}", [128, 1024], F32) for i in range(2)]
    P.ld(gf_bc[:], gf_d.partition_broadcast(128), ['gf_bc'], 'c0')
    for q in range(4):
        P.ld(ogT[:, :, q * 512:(q + 1) * 512], og_d[:, :, q * 512:(q + 1) * 512], [f'ogT{q}'], f'og{q}')
    wkeys = []
    for pr in range(8):
        i = pr % 2
        P.ld(wst[i][:], wo_d[pr * 128:(pr + 1) * 128, :], [f'wst{i}'], f'wst{i}')
        P.cp('dve' if i == 0 else 'act', wo[:, pr, :], wst[i][:], r=[f'wst{i}'], w=[f'wo{pr}'])
        wkeys.append(f'wo{pr}')
    def load_h1(tt):
        P.ld(h1t[tt % 2][:], h1_d[tt * 128:(tt + 1) * 128, :], [f'h1t{tt % 2}'], f'h1t{tt % 2}')

    load_h1(0)
    for tt in range(NTT):
        sl = tt % 2
        if tt + 1 < NTT:
            load_h1(tt + 1)
        for half in range(2):
            bk = banks[(tt % 2) * 2 + half]
            for pr in range(8):
                P.mm(bk[:, :], ogT[:, pr, tt * 128:(tt + 1) * 128], wo[:, pr, half * 512:(half + 1) * 512], start=(pr == 0), stop=(pr == 7),
                     r=[f'ogT{tt // 4}', wkeys[pr]], w=[f'B{(tt % 2) * 2 + half}'])
            P.tt('dve', h2[:, half * 512:(half + 1) * 512], bk[:, :], h1t[sl][:, half * 512:(half + 1) * 512], ALU.add,
                 r=[f'B{(tt % 2) * 2 + half}', f'h1t{sl}'], w=[f'h2_{half}'])
        P.actv(junk[:], h2[:], AF.Square, accum=ss[:], r=['h2_0', 'h2_1'], w=['junk', 'ss'])
        P.actv(rt[:], ss[:], AF.Sqrt, bias=EPS, scale=1.0 / 1024, r=['ss'], w=['rt'])
        P.add('dve', lambda e: e.reciprocal(out=rstd[:], in_=rt[:]), r=['rt'], w=['rstd'])
        P.stt(outt[sl][:], h2[:], rstd[:, 0:1], gf_bc[:], ALU.mult, ALU.mult, r=['h2_0', 'h2_1', 'rstd', 'gf_bc'], w=[f'outt{sl}'])
        P.ld(out_d[tt * 128:(tt + 1) * 128, :], outt[sl][:], w=[f'od{sl}'], sem=f'sto{sl}', r=[f'outt{sl}'])
    P.wait_all('sp', ['od0', 'od1'])
    P.emit()
    es.close()
    return nc


def _prepA(inp, b, g):
    w_in = inp['ssm_w_in'][0]
    w = np.concatenate([w_in[:, 2048 + g * 512:2048 + (g + 1) * 512], w_in[:, 4096 + g * 128:4096 + (g + 1) * 128],
                        w_in[:, 4608 + g * 128:4608 + (g + 1) * 128], w_in[:, 5120 + g * 8:5120 + (g + 1) * 8],
                        w_in[:, g * 512:(g + 1) * 512]], axis=1)
    cidx = np.concatenate([np.arange(g * 512, (g + 1) * 512), 2048 + np.arange(g * 128, (g + 1) * 128),
                           2560 + np.arange(g * 128, (g + 1) * 128)])
    cwc = inp['ssm_conv_w'][0][:, cidx]
    cw = cwc.T.reshape(6, 128, 4).transpose(1, 0, 2).reshape(128, 24)
    cb = inp['ssm_conv_b'][0][cidx].reshape(6, 128).T
    hs = slice(g * 8, (g + 1) * 8)
    C = np.ascontiguousarray
    return dict(x=C(inp['x'][b]), w=C(w), gpre=C(inp['g_pre'][0].reshape(8, 128).T), cw=C(cw), cb=C(cb),
                dtb=C(inp['ssm_dt_bias'][0][hs].reshape(1, 8)), alog=C(inp['ssm_A_log'][0][hs].reshape(1, 8)),
                dsk=C(inp['ssm_D'][0][hs].reshape(1, 8)), gout=C(inp['ssm_g_out'][0][g * 512:(g + 1) * 512].reshape(1, 512)))


def _prepB(inp, yn_b, b, j):
    C = np.ascontiguousarray
    tok = slice(j * NTOK, (j + 1) * NTOK)
    pos = np.asarray(inp['positions'][b][tok]).astype(np.int32).reshape(NTT, 128).T
    return dict(x=C(inp['x'][b][tok]), yn=C(yn_b[tok]), pos=C(pos), invf=np.array(INV_FREQ, dtype=np.float32).reshape(1, 16),
                wout=C(inp['ssm_w_out'][0]), wdn=C(inp['kv_w_down']), wup=C(inp['kv_w_up']), win=C(inp['mla_w_in'][0]),
                wuq=C(inp['mla_w_uq'][0]), gkv=C(inp['kv_g_in'].reshape(8, 128).T), gpre=C(inp['g_pre'][1].reshape(8, 128).T),
                glat=C(inp['kv_g_latent'].reshape(2, 128).T), gq=C(inp['mla_g_q'][0].reshape(3, 128).T))


def kernel(**inputs):
    inp = {k: np.asarray(v) for k, v in inputs.items()}
    C = np.ascontiguousarray
    cores = list(range(8))
    ncA = build_stageA()
    rA = run_bass_kernel_spmd(ncA, [_prepA(inp, c // 4, c % 4) for c in cores], core_ids=cores).results
    yn = [np.concatenate([rA[b * 4 + g]['yn'] for g in range(4)], axis=1) for b in range(2)]
    ncB = build_stageB()
    rB = run_bass_kernel_spmd(ncB, [_prepB(inp, yn[c // 4], c // 4, c % 4) for c in cores], core_ids=cores).results
    knf, krf, vff, qff, sff = [], [], [], [], []
    for b in range(2):
        knf.append(np.concatenate([rB[b * 4 + j]['kn'] for j in range(4)], axis=2))
        krf.append(np.concatenate([rB[b * 4 + j]['kr'] for j in range(4)], axis=1))
        vff.append(np.concatenate([rB[b * 4 + j]['v'] for j in range(4)], axis=0))
        qff.append(np.concatenate([rB[b * 4 + j]['qT'] for j in range(4)], axis=2))
        sff.append(np.concatenate([rB[b * 4 + j]['sg'] for j in range(4)], axis=2))
    ncC = build_stageC(2)
    og_heads = {}
    for part in range(2):
        imC = []
        for c in cores:
            b, hg = c // 4, c % 4
            kT = np.empty((2, 96, 8192), dtype=knf[b].dtype)
            v4 = np.empty((2, 128, 64, 64), dtype=vff[b].dtype)
            q4 = np.empty((2, 96, 8192), dtype=qff[b].dtype)
            s4 = np.empty((2, 128, 64, 64), dtype=sff[b].dtype)
            for hl in range(2):
                h = hg * 4 + part * 2 + hl
                kT[hl, 0:64] = knf[b][(h % 2) * 64:(h % 2) * 64 + 64, h // 2, :]
                kT[hl, 64:96] = krf[b]
                v4[hl] = vff[b][:, h * 64:(h + 1) * 64].reshape(64, 128, 64).transpose(1, 0, 2)
                q4[hl] = qff[b][:, h, :]
                s4[hl] = sff[b][(h % 2) * 64:(h % 2) * 64 + 64, h // 2, :].T.reshape(64, 128, 64).transpose(1, 0, 2)
            imC.append(dict(kT=kT, v=v4, qT=q4, sg=s4))
        rC = run_bass_kernel_spmd(ncC, imC, core_ids=cores).results
        for c in cores:
            b, hg = c // 4, c % 4
            for hl in range(2):
                og_heads[(b, hg * 4 + part * 2 + hl)] = rC[c]['og'][hl]
    imD = []
    for c in cores:
        b, j = c // 4, c % 4
        tok = slice(j * NTOK, (j + 1) * NTOK)
        og = np.empty((128, 8, NTOK), dtype=og_heads[(0, 0)].dtype)
        for h in range(16):
            og[(h % 2) * 64:(h % 2) * 64 + 64, h // 2, :] = og_heads[(b, h)][tok].T
        imD.append(dict(og=og, h1=rB[c]['h1'], wo=C(inp['mla_w_out'][0]), gf=C(inp['g_final'].reshape(1, 1024))))
    ncD = build_stageD()
    rD = run_bass_kernel_spmd(ncD, imD, core_ids=cores).results
    out = np.stack([np.concatenate([rD[b * 4 + j]['out'] for j in range(4)], axis=0) for b in range(2)], axis=0)
    return out.astype(np.float32)
```
